# Optimizing a Trainium2 kernel written in Bass

```python
import math
import jax, jax.numpy as jnp
from jax import lax
import numpy as np

D_MODEL = 1024
BATCH = 1
SEQ = 16384
DEPTH = 2

GRID_W = 64
CTX_LEN = 256
NA_HEADS = 8
NA_HEAD_DIM = 64
NA_WIN_H = 8
NA_WIN_W = 16
DN_HEADS = 4
DN_HEAD_DIM = 128
DN_CONV = 5
DN_CHUNK = 64
FN_GROUPS = 4
FN_GROUP_DIM = 128
N_BRANCH = 3
MLP_HIDDEN = 4 * D_MODEL
ROPE_BASE = 10000.0
EPS = 1e-6

NA_W = NA_HEADS * NA_HEAD_DIM
DN_W = DN_HEADS * DN_HEAD_DIM
FN_W = FN_GROUPS * FN_GROUP_DIM
CTX_SIDE_W = 2 * NA_W + 2 * DN_W + 4 * DN_HEADS
IN_W = CTX_SIDE_W + NA_W + 2 * DN_W + FN_W + N_BRANCH * D_MODEL
SIDE_SIZES = (NA_W, NA_W, DN_W, DN_W, 2 * DN_HEADS, 2 * DN_HEADS)
REST_SIZES = (NA_W, DN_W, DN_W, FN_W, N_BRANCH * D_MODEL)

kernel_name = 'hybrid_na_deltanet_fnet_diffusion_block'


def _split(p, sizes):
    return jnp.split(p, np.cumsum(sizes)[:-1].tolist(), axis=-1)


def _rmsnorm(x, g):
    xf = x.astype(jnp.float32)
    y = xf * lax.rsqrt(jnp.mean(xf * xf, axis=-1, keepdims=True) + EPS)
    return (y * g.astype(jnp.float32)).astype(x.dtype)


def _l2norm(x):
    xf = x.astype(jnp.float32)
    return xf * lax.rsqrt(jnp.sum(xf * xf, axis=-1, keepdims=True) + EPS)


def _heads(u, d):
    return u.reshape(u.shape[:2] + (-1, d))


def _bhtd(a):
    return jnp.transpose(a, (0, 2, 1, 3))


def _short_conv(u, w):
    k, ch = w.shape
    y = lax.conv_general_dilated(u, w[:, None, :].astype(u.dtype), window_strides=(1,),
                                 padding=[(k // 2, k // 2)], dimension_numbers=('NWC', 'WIO', 'NWC'),
                                 feature_group_count=ch)
    return jax.nn.silu(y)


def _axial_rope(x):
    t_len, d = x.shape[1], x.shape[-1]
    half = d // 2
    t = jnp.arange(t_len, dtype=jnp.int32)
    pos = jnp.stack([t // GRID_W, t % GRID_W], axis=-1).astype(jnp.float32)
    inv = 1.0 / (ROPE_BASE ** (jnp.arange(0, half, 2, dtype=jnp.float32) / half))
    ang = pos[:, :, None] * inv[None, None, :]
    ang = jnp.concatenate([ang, ang], axis=-1)
    cos, sin = jnp.cos(ang)[None, :, None], jnp.sin(ang)[None, :, None]
    xa = x.reshape(x.shape[:-1] + (2, half))
    rot = jnp.concatenate([-xa[..., half // 2:], xa[..., :half // 2]], axis=-1)
    return (xa * cos + rot * sin).reshape(x.shape)


def _neighbourhood_attention(q, k, v, kc, vc, rpb):
    b, t_len, h, dh = q.shape
    rows = t_len // GRID_W
    kh, kw = min(NA_WIN_H, rows), min(NA_WIN_W, GRID_W)
    qg, kg, vg = (a.reshape(b, rows, GRID_W, h, dh) for a in (q, k, v))
    col = np.arange(GRID_W)
    col_idx = np.clip(col - kw // 2, 0, GRID_W - kw)[:, None] + np.arange(kw)[None, :]
    col_off = col_idx - col[:, None] + NA_WIN_W - 1
    rpb_c = rpb[:, :, col_off]

    def row_block(r):
        rs = jnp.clip(r - kh // 2, 0, rows - kh)
        q_r = lax.dynamic_index_in_dim(qg, r, axis=1, keepdims=False)
        k_win = lax.dynamic_slice_in_dim(kg, rs, kh, axis=1)[:, :, col_idx]
        v_win = lax.dynamic_slice_in_dim(vg, rs, kh, axis=1)[:, :, col_idx]
        bias = jnp.take(rpb_c, rs + jnp.arange(kh) - r + NA_WIN_H - 1, axis=1)
        s_loc = (jnp.einsum('bwhd,biwjhd->bhwij', q_r, k_win).astype(jnp.float32)
                 + jnp.transpose(bias, (0, 2, 1, 3)).astype(jnp.float32)[None])
        s_ctx = jnp.einsum('bwhd,bshd->bhws', q_r, kc).astype(jnp.float32)
        s = jnp.concatenate([s_loc.reshape(b, h, GRID_W, kh * kw), s_ctx], axis=-1)
        p = jax.nn.softmax(s, axis=-1).astype(v.dtype)
        p_loc = p[..., :kh * kw].reshape(b, h, GRID_W, kh, kw)
        return (jnp.einsum('bhwij,biwjhd->bwhd', p_loc, v_win)
                + jnp.einsum('bhws,bshd->bwhd', p[..., kh * kw:], vc))

    o = lax.map(row_block, jnp.arange(rows))
    return jnp.moveaxis(o, 0, 1).reshape(b, t_len, h * dh)


def _dense_attn(q, k, v):
    s = jnp.einsum('bqhd,bkhd->bhqk', q, k).astype(jnp.float32)
    p = jax.nn.softmax(s, axis=-1).astype(v.dtype)
    o = jnp.einsum('bhqk,bkhd->bqhd', p, v)
    return o.reshape(o.shape[:2] + (-1,))


def _gated_delta_chunked(q, k, v, g, beta, s0):
    b, h, t_len, dk = k.shape
    dv = v.shape[-1]
    c = DN_CHUNK
    n = t_len // c
    ch = lambda a: a.reshape((b, h, n, c) + a.shape[3:])
    k, v, g, beta = ch(k), ch(v), ch(g), ch(beta)
    g = jnp.cumsum(g, axis=-1)
    lower = jnp.tril(jnp.ones((c, c), bool))
    strict = jnp.tril(jnp.ones((c, c), bool), -1)
    decay = jnp.where(lower, jnp.exp(jnp.where(lower, g[..., :, None] - g[..., None, :], 0.0)), 0.0)
    kb = k * beta[..., None]
    a_mat = jnp.where(strict, jnp.einsum('bhncd,bhnsd->bhncs', kb, k) * decay, 0.0) + jnp.eye(c, dtype=jnp.float32)
    rhs = jnp.concatenate([v * beta[..., None], kb * jnp.exp(g)[..., None]], axis=-1)
    sol = lax.linalg.triangular_solve(a_mat, rhs, left_side=True, lower=True)
    u, w = sol[..., :dv], sol[..., dv:]
    g_last = g[..., -1]
    k_tail = k * jnp.exp(g_last[..., None] - g)[..., None]
    xs = [u, w, k_tail, jnp.exp(g_last)]
    if q is not None:
        qs = ch(q) * (dk ** -0.5)
        xs += [qs * jnp.exp(g)[..., None], jnp.einsum('bhncd,bhnsd->bhncs', qs, k) * decay]
    xs = tuple(jnp.moveaxis(a, 2, 0) for a in xs)

    def step(s, inp):
        u_c, w_c, kt_c, dl_c = inp[:4]
        v_new = u_c - jnp.einsum('bhcd,bhde->bhce', w_c, s)
        s_next = s * dl_c[..., None, None] + jnp.einsum('bhcd,bhce->bhde', kt_c, v_new)
        if q is None:
            return s_next, None
        qg_c, qk_c = inp[4:]
        o = jnp.einsum('bhcd,bhde->bhce', qg_c, s) + jnp.einsum('bhcs,bhse->bhce', qk_c, v_new)
        return s_next, o

    s_fin, o = lax.scan(step, s0, xs)
    if q is None:
        return s_fin, None
    return s_fin, jnp.moveaxis(o, 0, 2).reshape(b, h, t_len, dv)


def _delta_direction(ctx, lat, reverse):
    flip = (lambda a: a if a is None else jnp.flip(a, axis=2)) if reverse else (lambda a: a)
    s0 = jnp.zeros(ctx[1].shape[:2] + (ctx[1].shape[-1], ctx[2].shape[-1]), jnp.float32)
    s_ctx, o_ctx = _gated_delta_chunked(*map(flip, ctx), s0)
    _, o_lat = _gated_delta_chunked(*map(flip, lat), s_ctx)
    return flip(o_ctx), flip(o_lat)


def _decay_gates(a, bt, a_log, dt_bias):
    b, t_len, _ = a.shape
    a = a.astype(jnp.float32).reshape(b, t_len, 2, DN_HEADS)
    bt = bt.astype(jnp.float32).reshape(b, t_len, 2, DN_HEADS)
    g = -jnp.exp(a_log.astype(jnp.float32)) * jax.nn.softplus(a + dt_bias.astype(jnp.float32))
    beta = jax.nn.sigmoid(bt)
    return jnp.transpose(g, (2, 0, 3, 1)), jnp.transpose(beta, (2, 0, 3, 1))


def _gated_head_norm(o, z, w):
    o = jnp.transpose(o, (0, 2, 1, 3))
    o = o * lax.rsqrt(jnp.mean(o * o, axis=-1, keepdims=True) + EPS) * w.astype(jnp.float32)
    y = o * jax.nn.silu(_heads(z, DN_HEAD_DIM).astype(jnp.float32))
    return y.reshape(y.shape[:2] + (-1,)).astype(z.dtype)


def _fourier(u):
    b, t_len, _ = u.shape
    ug = u.astype(jnp.float32).reshape(b, t_len, FN_GROUPS, FN_GROUP_DIM)
    f = jnp.fft.fft2(ug, axes=(1, 3), norm='ortho').real
    return f.reshape(b, t_len, FN_W).astype(u.dtype)


def _merge(o_na, o_dn, o_fn, gate_logits, w_na_o, w_dn_o, w_fn, w_out):
    g = jax.nn.sigmoid(gate_logits.astype(jnp.float32)).astype(o_na.dtype)
    g_na, g_dn, g_fn = jnp.split(g, N_BRANCH, axis=-1)
    y = g_na * (o_na @ w_na_o) + g_dn * (o_dn @ w_dn_o) + g_fn * (o_fn @ w_fn)
    return y @ w_out


def _token_mixer(h_l, h_c, w_in, conv_w, a_log, dt_bias, dn_norm, rpb, w_na_o, w_dn_o, w_fn, w_out, ctx_out):
    (na_k, na_v, dn_k, dn_v, dn_a, dn_b, na_q, dn_q, dn_z, fn_u, gate) = _split(h_l @ w_in, SIDE_SIZES + REST_SIZES)
    cp = _split(h_c @ (w_in if ctx_out else w_in[:, :CTX_SIDE_W]), SIDE_SIZES + (REST_SIZES if ctx_out else ()))
    na_kc, na_vc, dn_kc, dn_vc, dn_ac, dn_bc = cp[:6]
    scale = NA_HEAD_DIM ** -0.5

    kc_na, vc_na = _heads(na_kc, NA_HEAD_DIM), _heads(na_vc, NA_HEAD_DIM)
    o_na = _neighbourhood_attention(_heads(na_q, NA_HEAD_DIM) * scale, _heads(na_k, NA_HEAD_DIM),
                                    _heads(na_v, NA_HEAD_DIM), kc_na, vc_na, rpb)

    q_l, k_l, v_l = _split(_short_conv(jnp.concatenate([dn_q, dn_k, dn_v], axis=-1), conv_w), (DN_W,) * 3)
    lat_qkv = (_bhtd(_axial_rope(_l2norm(_heads(q_l, DN_HEAD_DIM)))),
               _bhtd(_axial_rope(_l2norm(_heads(k_l, DN_HEAD_DIM)))),
               _bhtd(_heads(v_l, DN_HEAD_DIM).astype(jnp.float32)))
    if ctx_out:
        q_c, k_c, v_c = _split(_short_conv(jnp.concatenate([cp[7], dn_kc, dn_vc], axis=-1), conv_w), (DN_W,) * 3)
        q_c = _bhtd(_l2norm(_heads(q_c, DN_HEAD_DIM)))
    else:
        k_c, v_c = _split(_short_conv(jnp.concatenate([dn_kc, dn_vc], axis=-1), conv_w[:, DN_W:]), (DN_W,) * 2)
        q_c = None
    ctx_qkv = (q_c, _bhtd(_l2norm(_heads(k_c, DN_HEAD_DIM))), _bhtd(_heads(v_c, DN_HEAD_DIM).astype(jnp.float32)))
    g_l, be_l = _decay_gates(dn_a, dn_b, a_log, dt_bias)
    g_c, be_c = _decay_gates(dn_ac, dn_bc, a_log, dt_bias)
    oc_f, ol_f = _delta_direction(ctx_qkv + (g_c[0], be_c[0]), lat_qkv + (g_l[0], be_l[0]), reverse=False)
    oc_b, ol_b = _delta_direction(ctx_qkv + (g_c[1], be_c[1]), lat_qkv + (g_l[1], be_l[1]), reverse=True)
    o_dn = _gated_head_norm(ol_f + ol_b, dn_z, dn_norm)

    y_l = _merge(o_na, o_dn, _fourier(fn_u), gate, w_na_o, w_dn_o, w_fn, w_out)
    if not ctx_out:
        return y_l, None
    o_na_c = _dense_attn(_heads(cp[6], NA_HEAD_DIM) * scale, kc_na, vc_na)
    o_dn_c = _gated_head_norm(oc_f + oc_b, cp[8], dn_norm)
    y_c = _merge(o_na_c, o_dn_c, _fourier(cp[9]), cp[10], w_na_o, w_dn_o, w_fn, w_out)
    return y_l, y_c


def _sq_relu_mlp(h, w1, w2):
    return jnp.square(jax.nn.relu(h @ w1)) @ w2


def _layer(x, xc, cs, ccs, w_ada, b_ada, norm1, w_in, conv_w, a_log, dt_bias, dn_norm, rpb,
           w_na_o, w_dn_o, w_fn, w_out, norm2, w_mlp1, w_mlp2, ctx_out):
    d = D_MODEL
    sh1, sc1, gt1, sh2, sc2, gt2 = jnp.split((cs @ w_ada + b_ada)[:, None, :], 6, axis=-1)
    n_mod = 6 if ctx_out else 2
    cmod = jnp.split((ccs @ w_ada[:, :n_mod * d] + b_ada[:n_mod * d])[None, None, :], n_mod, axis=-1)
    h_l = _rmsnorm(x, norm1) * (1 + sc1) + sh1
    h_c = _rmsnorm(xc, norm1) * (1 + cmod[1]) + cmod[0]
    y_l, y_c = _token_mixer(h_l, h_c, w_in, conv_w, a_log, dt_bias, dn_norm, rpb,
                            w_na_o, w_dn_o, w_fn, w_out, ctx_out)
    x = x + gt1 * y_l
    x = x + gt2 * _sq_relu_mlp(_rmsnorm(x, norm2) * (1 + sc2) + sh2, w_mlp1, w_mlp2)
    if ctx_out:
        xc = xc + cmod[2] * y_c
        xc = xc + cmod[5] * _sq_relu_mlp(_rmsnorm(xc, norm2) * (1 + cmod[4]) + cmod[3], w_mlp1, w_mlp2)
    return x, xc


def setup_inputs(seed: int = 0) -> dict:
    key = jax.random.key(seed)
    ks = jax.random.split(key, 24)
    f32 = jnp.float32
    d = D_MODEL
    nrm = lambda k, shape, s: jax.random.normal(k, shape, f32) * s
    a_init = jax.random.uniform(ks[0], (DEPTH, 2, DN_HEADS), f32, 1.0, 16.0)
    dt = jnp.exp(jax.random.uniform(ks[1], (DEPTH, 2, DN_HEADS), f32, math.log(1e-3), math.log(1e-1)))
    return {
        'x': nrm(ks[2], (BATCH, SEQ, d), 1.0),
        'c': nrm(ks[3], (BATCH, d), 1.0),
        'ctx': nrm(ks[4], (BATCH, CTX_LEN, d), 1.0),
        'c_ctx': nrm(ks[5], (d,), 1.0),
        'w_ada': nrm(ks[6], (DEPTH, d, 6 * d), 0.5 * d ** -0.5),
        'b_ada': nrm(ks[7], (DEPTH, 6 * d), 0.02),
        'norm1': 1.0 + nrm(ks[8], (DEPTH, d), 0.02),
        'w_in': nrm(ks[9], (DEPTH, d, IN_W), d ** -0.5),
        'conv_w': nrm(ks[10], (DEPTH, DN_CONV, 3 * DN_W), DN_CONV ** -0.5),
        'a_log': jnp.log(a_init),
        'dt_bias': dt + jnp.log(-jnp.expm1(-dt)),
        'dn_norm': 1.0 + nrm(ks[11], (DEPTH, DN_HEAD_DIM), 0.02),
        'rpb': nrm(ks[12], (DEPTH, NA_HEADS, 2 * NA_WIN_H - 1, 2 * NA_WIN_W - 1), 0.1),
        'w_na_o': nrm(ks[13], (DEPTH, NA_W, d), NA_W ** -0.5),
        'w_dn_o': nrm(ks[14], (DEPTH, DN_W, d), DN_W ** -0.5),
        'w_fn': nrm(ks[15], (DEPTH, FN_W, d), FN_W ** -0.5),
        'w_out': nrm(ks[16], (DEPTH, d, d), d ** -0.5),
        'norm2': 1.0 + nrm(ks[17], (DEPTH, d), 0.02),
        'w_mlp1': nrm(ks[18], (DEPTH, d, MLP_HIDDEN), d ** -0.5),
        'w_mlp2': nrm(ks[19], (DEPTH, MLP_HIDDEN, d), MLP_HIDDEN ** -0.5),
        'norm_f': 1.0 + nrm(ks[20], (d,), 0.02),
    }


def reference(x, c, ctx, c_ctx, w_ada, b_ada, norm1, w_in, conv_w, a_log, dt_bias, dn_norm, rpb,
              w_na_o, w_dn_o, w_fn, w_out, norm2, w_mlp1, w_mlp2, norm_f):
    cs = jax.nn.silu(c)
    ccs = jax.nn.silu(c_ctx)
    xc = ctx
    for l in range(DEPTH):
        x, xc = _layer(x, xc, cs, ccs, w_ada[l], b_ada[l], norm1[l], w_in[l], conv_w[l], a_log[l], dt_bias[l],
                       dn_norm[l], rpb[l], w_na_o[l], w_dn_o[l], w_fn[l], w_out[l], norm2[l], w_mlp1[l], w_mlp2[l],
                       ctx_out=(l < DEPTH - 1))
    return _rmsnorm(x, norm_f)
```

```python
import math
import numpy as np
import concourse.bass as bass
import concourse.mybir as mybir
from concourse.bass_utils import run_bass_kernel_spmd

F32 = mybir.dt.float32
AF = mybir.ActivationFunctionType
ALU = mybir.AluOpType
AX = mybir.AxisListType

D = 1024
T = 16384
CT = 256
NT = T + CT
GW = 64
EPS = 1e-6
NTAB = 15


class Dep:
    __slots__ = ("name", "w", "r")

    def __init__(self, name=""):
        self.name = name
        self.w = None
        self.r = []


class Prog:
    ENGS = ("pe", "act", "dve", "pool", "sp")

    def __init__(self, nc, self_sync=True):
        self.nc = nc
        self.q = {e: [] for e in self.ENGS}
        self.cnt = {}
        self.known = {e: {} for e in self.ENGS}
        self.sems = {}
        self.self_sync = self_sync
        self._uid = 0
        self._stack = []
        self._semstack = []
        self._smap = {}
        self._scope_mark = 0
        self.n_inst = 0
        for e in self.ENGS:
            self._mksem("E_" + e)

    def _mksem(self, key):
        cm = self.nc.semaphore(key)
        h = cm.__enter__()
        self._semstack.append(cm)
        self.sems[key] = h
        self.cnt[key] = 0
        return key

    _mksem_global = _mksem

    def sbuf(self, name, shape, dt=F32):
        self._uid += 1
        cm = self.nc.sbuf_tensor(f"{name}_u{self._uid}", list(shape), dt)
        t = cm.__enter__()
        self._stack.append(cm)
        return t

    def psum(self, name, shape, dt=F32):
        cm = self.nc.psum_tensor(name, list(shape), dt)
        t = cm.__enter__()
        self._stack.append(cm)
        return t

    def close(self):
        while self._stack:
            self._stack.pop().__exit__(None, None, None)
        while self._semstack:
            self._semstack.pop().__exit__(None, None, None)

    def _waits(self, eng, reads, writes):
        need = {}

        def add(ev):
            if ev is None:
                return
            if isinstance(ev, dict):
                for k, v in ev.items():
                    if need.get(k, 0) < v:
                        need[k] = v
                return
            k, v = ev
            if need.get(k, 0) < v:
                need[k] = v
        for d in reads:
            add(d.w)
        for d in writes:
            add(d.w)
            for ev in d.r:
                add(ev)
        out = []
        own = "E_" + eng
        for k, v in need.items():
            if k == own and (eng == "pe" or not self.self_sync):
                continue
            if self.known[eng].get(k, 0) >= v:
                continue
            self.known[eng][k] = v
            out.append((k, v))
        return out

    def _emit(self, eng, fn, reads, writes, semkey, inc, track_w=True):
        waits = self._waits(eng, reads, writes if track_w else [])
        self.cnt[semkey] += inc
        ev = (semkey, self.cnt[semkey])
        for d in writes:
            d.w = ev
            d.r = []
        for d in reads:
            if d not in writes:
                d.r.append(ev)
                if len(d.r) > 64:
                    d.r = d.r[-64:]
        self.q[eng].append((waits, fn, semkey, inc))
        self.n_inst += 1 + len(waits)

    def scope_begin(self):
        self.barrier()
        self._scope_mark = len(self._stack)
        self._smap = {}

    def scope_end(self):
        self.barrier()
        while len(self._stack) > self._scope_mark:
            self._stack.pop().__exit__(None, None, None)
        self._smap = {}

    def barrier(self):
        for eng in self.ENGS:
            waits = []
            for key, v in self.cnt.items():
                if v == 0 or key == "E_" + eng:
                    continue
                if self.known[eng].get(key, 0) >= v:
                    continue
                self.known[eng][key] = v
                waits.append((key, v))
            if waits:
                self.q[eng].append((waits, None, None, 0))
                self.n_inst += len(waits)

    def dsem(self, key):
        m = self._smap
        if key not in m:
            phys = f"dma{len(m)}"
            if phys not in self.sems:
                self._mksem_global(phys)
            m[key] = phys
        return m[key]

    def op(self, eng, fn, reads=(), writes=()):
        self._emit(eng, fn, list(reads), list(writes), "E_" + eng, 1)

    def dma(self, eng, out, in_, reads=(), writes=(), sem=None, track_w=True, **kw):
        sem = self.dsem(sem)
        self._emit(eng, lambda e: e.dma_start(out=out, in_=in_, **kw), list(reads), list(writes), sem, 16,
                   track_w=track_w)

    def store(self, eng, out, in_, reads, ddep, **kw):
        sem = "st_" + reads[0].name
        self.dma(eng, out, in_, reads=reads, writes=[], sem=sem, **kw)
        sem = self.dsem(sem)
        if not isinstance(ddep.w, dict):
            ddep.w = {}
        ddep.w[sem] = self.cnt[sem]

    def build(self, final_deps=()):
        nc = self.nc
        fw = self._waits("sp", list(final_deps), [])
        sems = self.sems
        q = self.q

        def replay(eh, items, extra=()):
            for waits, fn, semkey, inc in items:
                for k, v in waits:
                    eh.wait_ge(sems[k], v)
                if fn is not None:
                    fn(eh).then_inc(sems[semkey], inc)
            for k, v in extra:
                eh.wait_ge(sems[k], v)

        with nc.Block() as block:
            @block.tensor
            def _(e):
                replay(e, q["pe"])

            @block.scalar
            def _(e):
                replay(e, q["act"])

            @block.vector
            def _(e):
                replay(e, q["dve"])

            @block.gpsimd
            def _(e):
                replay(e, q["pool"])

            @block.sync
            def _(e):
                replay(e, q["sp"], fw)
        self.close()


class Ring:
    def __init__(self, p, name, shape, n=2):
        self.t = [p.sbuf(f"{name}{i}", shape) for i in range(n)]
        self.d = [Dep(f"{name}{i}") for i in range(n)]
        self.i = 0
        self.n = n
        self.name = name

    def next(self):
        i = self.i
        self.i = (i + 1) % self.n
        return self.t[i], self.d[i], f"ld_{self.name}{i}"


_o = 0
COLS = {}
for _n, _s in (("na_k", 512), ("na_v", 512), ("dn_k", 512), ("dn_v", 512), ("dn_a", 8), ("dn_b", 8),
               ("na_q", 512), ("dn_q", 512), ("dn_z", 512), ("fn_u", 512), ("gate", 3072)):
    COLS[_n] = (_o, _o + _s)
    _o += _s
IN_W = _o
FB_GROUPS = (("na_q", 4), ("na_k", 4), ("dn_q", 4), ("dn_k", 4), ("dn_v", 4), ("fn_u", 4), ("gate", 24))
FB_CHUNKS = [(g, i) for g, n in FB_GROUPS for i in range(n)]
NFB = len(FB_CHUNKS)
FA_COLS = np.concatenate([np.arange(*COLS["na_v"]), np.arange(*COLS["dn_z"]),
                          np.arange(*COLS["dn_a"]), np.arange(*COLS["dn_b"])])


def _fm(v, nchunk):
    return np.ascontiguousarray(np.asarray(v, np.float32).reshape(nchunk, 128).T)


def _lhs_chunks(w, cols):
    K = w.shape[0]
    sub = w[:, cols]
    nc_ = sub.shape[1] // 128
    a = sub.reshape(K // 128, 128, nc_, 128)
    return np.ascontiguousarray(a.transpose(2, 1, 0, 3))


def _rhs_rows(w):
    K, N = w.shape
    return np.ascontiguousarray(w.reshape(K // 128, 128, N).transpose(1, 0, 2))


def host_layout(inp):
    m = {}
    x = np.concatenate([inp["x"][0], inp["ctx"][0]], axis=0)
    m["xin"] = np.ascontiguousarray(x.T.reshape(8, 128, NT))
    m["cvec"] = np.ascontiguousarray(np.stack([_fm(inp["c"][0], 8), _fm(inp["c_ctx"], 8)], axis=2))
    for l in range(2):
        w_in = inp["w_in"][l]
        fbcols = np.concatenate([np.arange(COLS[g][0] + i * 128, COLS[g][0] + (i + 1) * 128) for g, i in FB_CHUNKS])
        m[f"w_inb{l}"] = _lhs_chunks(w_in, fbcols)
        m[f"w_ina{l}"] = _rhs_rows(w_in[:, FA_COLS])
        m[f"w_ada{l}"] = _lhs_chunks(inp["w_ada"][l], np.arange(6 * D))
        m[f"b_ada{l}"] = _fm(inp["b_ada"][l], 48)
        m[f"norm1_{l}"] = _fm(inp["norm1"][l], 8)
        m[f"norm2_{l}"] = _fm(inp["norm2"][l], 8)
        ab = np.concatenate([inp["a_log"][l].reshape(-1), inp["dt_bias"][l].reshape(-1)])
        m[f"abt{l}"] = np.ascontiguousarray(np.broadcast_to(ab[None, :], (128, 16))).astype(np.float32)
        m[f"convw{l}"] = np.ascontiguousarray(inp["conv_w"][l].T.reshape(12, 128, 5).transpose(1, 0, 2))
    cos, sin, rm = rope_tables()
    m["rope_cos"], m["rope_sin"], m["rope_rm"] = cos, sin, rm
    mk = dn_masks()
    m["dnmask0"], m["dnmask1"] = mk[0], mk[1]
    for l in range(2):
        ar = np.arange(D)
        m[f"w_nao{l}"] = _lhs_chunks(inp["w_na_o"][l], ar)
        m[f"w_dno{l}"] = _lhs_chunks(inp["w_dn_o"][l], ar)
        m[f"w_fno{l}"] = _lhs_chunks(inp["w_fn"][l], ar)
        m[f"w_out{l}"] = _lhs_chunks(inp["w_out"][l], ar)
        m[f"w_mlp1_{l}"] = _lhs_chunks(inp["w_mlp1"][l], np.arange(4 * D))
        m[f"w_mlp2_{l}"] = _lhs_chunks(inp["w_mlp2"][l], ar)
        m[f"dnw{l}"] = np.ascontiguousarray(np.broadcast_to(inp["dn_norm"][l][None, :], (128, 128))).astype(np.float32)
        m[f"natab{l}"] = na_table(inp["rpb"][l])
    m["fn_dft"], m["fn_tw"], m["fn_d256"] = fn_tables()
    m["norm_f"] = _fm(inp["norm_f"], 8)
    ident = np.eye(128, dtype=np.float32)
    m["ident"] = ident
    return m


class K:
    def __init__(self, nc, p, shapes):
        self.nc = nc
        self.p = p
        self.inp = {k: nc.dram_tensor(k, list(v), F32, kind="ExternalInput").ap() for k, v in shapes.items()}
        self.scr = {}
        self.sdep = {}
        self.banks = [p.psum(f"bank{i}", [128, 512]) for i in range(8)]
        self.bdep = [Dep(f"bank{i}") for i in range(8)]
        self.cdep = Dep("consts")
        self._ld = 0

    def scratch(self, name, shape, kind="Internal"):
        if name in self.inp and name not in self.scr:
            self.scr[name] = self.inp[name]
            self.sdep[name] = Dep(name)
        if name not in self.scr:
            self.scr[name] = self.nc.dram_tensor(name, list(shape), F32, kind=kind).ap()
            self.sdep[name] = Dep(name)
        return self.scr[name], self.sdep[name]

    def const(self, name, shape, src, eng="sp"):
        t = self.p.sbuf("c_" + name, shape)
        d = Dep("c_" + name)
        self.p.dma(eng, t[:], src, writes=[d], sem="ld_c_" + name)
        return t, d


def stage_mod(k, l):
    p = k.p
    cv, dcv = k.const("cvec", [128, 8, 2], k.inp["cvec"])
    cs = p.sbuf("cs", [128, 8, 2])
    dcs = Dep("cs")
    p.op("act", lambda e: e.activation(out=cs[:], in_=cv[:], func=AF.Silu), reads=[dcv], writes=[dcs])
    mod, dmod = k.modt[l]
    bt, dbt = k.const(f"b_ada{l}", [128, 48], k.inp[f"b_ada{l}"])
    ring = Ring(p, f"wada{l}_", [128, 8, 128], 3)
    wsrc = k.inp[f"w_ada{l}"]
    for j in range(48):
        wt, dw, sem = ring.next()
        p.dma("sp", wt[:], wsrc[j], writes=[dw], sem=sem)
        b = j % 2
        ps = k.banks[b]
        for kk in range(8):
            p.op("pe", lambda e, wt=wt, kk=kk, ps=ps: e.matmul(ps[:, 0:2], lhsT=wt[:, kk, :], rhs=cs[:, kk, :],
                                                              start=(kk == 0), stop=(kk == 7)),
                 reads=[dw, dcs], writes=[k.bdep[b]])
        p.op("dve", lambda e, ps=ps, j=j: e.tensor_scalar(out=mod[:, j, :], in0=ps[:, 0:2], scalar1=bt[:, j:j + 1],
                                                        scalar2=None, op0=ALU.add),
             reads=[k.bdep[b], dbt], writes=[dmod])
    return mod, dmod


def norm_mod(k, xt, dx, N, gam, dgam, mod, dmod, sh_j, sc_j, col, out, dout, tmp, dtmp, ones_t, dones, bank):
    p = k.p
    ps = k.banks[bank]
    p.op("act", lambda e: e.activation(out=tmp[:, :, 0:N], in_=xt[:, :, 0:N], func=AF.Square), reads=[dx], writes=[dtmp])
    for kk in range(8):
        p.op("pe", lambda e, kk=kk: e.matmul(ps[:, 0:N], lhsT=ones_t[:], rhs=tmp[:, kk, 0:N], start=(kk == 0), stop=(kk == 7)),
             reads=[dtmp, dones], writes=[k.bdep[bank]])
    rstd = k.rstd
    p.op("dve", lambda e: e.tensor_scalar(out=rstd[:, 0:N], in0=ps[:, 0:N], scalar1=EPS, scalar2=None, op0=ALU.add),
         reads=[k.bdep[bank]], writes=[k.drstd])
    p.op("act", lambda e: e.activation(out=rstd[:, 0:N], in_=rstd[:, 0:N], func=AF.Sqrt), reads=[k.drstd], writes=[k.drstd])
    p.op("dve", lambda e: e.reciprocal(out=rstd[:, 0:N], in_=rstd[:, 0:N]), reads=[k.drstd], writes=[k.drstd])
    for kk in range(8):
        eng = "dve" if kk % 2 == 0 else "pool"
        p.op(eng, lambda e, kk=kk: e.tensor_tensor(out=tmp[:, kk, 0:N], in0=xt[:, kk, 0:N], in1=rstd[:, 0:N], op=ALU.mult),
             reads=[dx, k.drstd], writes=[dtmp])
    gm = k.gm
    for kk in range(8):
        p.op("act", lambda e, kk=kk: e.activation(out=out[:, kk, 0:N], in_=tmp[:, kk, 0:N], func=AF.Identity,
                                                  bias=mod[:, sh_j * 8 + kk, col:col + 1], scale=gm[:, kk, col:col + 1]),
             reads=[dtmp, k.dgm, dmod], writes=[dout])


def make_gm(k, gam, dgam, mod, dmod, sc_j):
    p = k.p
    gm = k.gm
    for col in range(2):
        p.op("dve", lambda e, col=col: e.scalar_tensor_tensor(out=gm[:, :, col], in0=mod[:, sc_j * 8:sc_j * 8 + 8, col], scalar=1.0,
                                                              in1=gam[:, :], op0=ALU.add, op1=ALU.mult),
             reads=[dmod, dgam], writes=[k.dgm])


def super_tiles():
    st = [(i * 256, 256, 0) for i in range(64)]
    st.append((T, 256, 1))
    return st


def stage_proj(k, l, xname, mod, dmod, tiles=None):
    p = k.p
    xsrc, dxs = k.scratch(xname, [8, 128, NT])
    QT, dQT = k.scratch(f"QT{l}", [8, 64, NT])
    KT, dKT = k.scratch(f"KT{l}", [8, 64, NT])
    V, dV = k.scratch(f"V{l}", [NT, 512])
    DQ, dDQ = k.scratch(f"DQ{l}", [4, 128, NT])
    DK, dDK = k.scratch(f"DK{l}", [4, 128, NT])
    DV, dDV = k.scratch(f"DV{l}", [4, 128, NT])
    Z, dZ = k.scratch(f"Z{l}", [NT, 512])
    GB, dGB = k.scratch(f"GB{l}", [NT, 16])
    U, dU = k.scratch(f"U{l}", [4, 128, NT])
    GATE, dGATE = k.scratch(f"GATE{l}", [24, 128, NT])
    gam, dgam = k.const(f"norm1_{l}", [128, 8], k.inp[f"norm1_{l}"])
    abt, dabt = k.const(f"abt{l}", [128, 16], k.inp[f"abt{l}"])
    nexp = p.sbuf(f"nexp{l}", [128, 8])
    dnexp = Dep("nexp")
    p.op("act", lambda e: e.activation(out=nexp[:], in_=abt[:, 0:8], func=AF.Exp), reads=[dabt], writes=[dnexp])
    p.op("dve", lambda e: e.tensor_scalar(out=nexp[:], in0=nexp[:], scalar1=-1.0, scalar2=None, op0=ALU.mult), reads=[dnexp], writes=[dnexp])
    make_gm(k, gam, dgam, mod, dmod, 1)
    wa, dwa = k.const(f"w_ina{l}", [128, 8, 1040], k.inp[f"w_ina{l}"], eng="act")
    xr = Ring(p, f"px{l}_", [128, 8, 256], 2)
    hr = Ring(p, f"ph{l}_", [128, 8, 256], 2)
    tr = Ring(p, f"pt{l}_", [128, 8, 256], 1)
    wr = Ring(p, f"pw{l}_", [128, 8, 128], 3)
    orr = Ring(p, f"po{l}_", [128, 512], 4)
    gr = Ring(p, f"pg{l}_", [128, 48], 2)
    wsrc = k.inp[f"w_inb{l}"]
    fb_dst = {"na_q": None, "na_k": None, "dn_q": (DQ, dDQ), "dn_k": (DK, dDK), "dn_v": (DV, dDV), "fn_u": (U, dU), "gate": (GATE, dGATE)}
    nbank = 0
    for (t0, N, col) in (tiles or super_tiles()):
        xt, dx, sem = xr.next()
        p.dma("sp", xt[:, :, 0:N], xsrc[:, :, t0:t0 + N].rearrange("k p t -> p k t"), reads=[dxs], writes=[dx], sem=sem)
        ht, dh, _ = hr.next()
        tmp, dtmp, _ = tr.next()
        norm_mod(k, xt, dx, N, gam, dgam, mod, dmod, 0, 1, col, ht, dh, tmp, dtmp, k.ones_t, k.dones, 7)
        for c, (g, gi) in enumerate(FB_CHUNKS):
            wt, dw, sem = wr.next()
            p.dma("sp" if c % 2 == 0 else "act", wt[:], wsrc[c], writes=[dw], sem=sem)
            b = nbank % 4
            nbank += 1
            ps = k.banks[b]
            for kk in range(8):
                p.op("pe", lambda e, wt=wt, kk=kk, ps=ps, ht=ht: e.matmul(ps[:, 0:N], lhsT=wt[:, kk, :], rhs=ht[:, kk, 0:N],
                                                                         start=(kk == 0), stop=(kk == 7)),
                     reads=[dw, dh], writes=[k.bdep[b]])
            ot, do, _ = orr.next()
            if g == "gate":
                p.op("act", lambda e, ot=ot, ps=ps: e.activation(out=ot[:, 0:N], in_=ps[:, 0:N], func=AF.Sigmoid),
                     reads=[k.bdep[b]], writes=[do])
            elif g == "na_q":
                p.op("act", lambda e, ot=ot, ps=ps: e.activation(out=ot[:, 0:N], in_=ps[:, 0:N], func=AF.Copy, scale=0.125),
                     reads=[k.bdep[b]], writes=[do])
            else:
                p.op("dve", lambda e, ot=ot, ps=ps: e.tensor_copy(out=ot[:, 0:N], in_=ps[:, 0:N]), reads=[k.bdep[b]], writes=[do])
            if g in ("na_q", "na_k"):
                dst, dd = (QT, dQT) if g == "na_q" else (KT, dKT)
                p.store("sp", dst[2 * gi:2 * gi + 2, :, t0:t0 + N].rearrange("h d t -> (h d) t"), ot[:, 0:N], [do], dd)
            else:
                dst, dd = fb_dst[g]
                p.store("sp", dst[gi, :, t0:t0 + N], ot[:, 0:N], [do], dd)
        for tb in range(N // 128):
            tok = t0 + tb * 128
            for gi, (c0, cn, dst, dd) in enumerate(((0, 512, V, dV), (512, 512, Z, dZ), (1024, 16, None, None))):
                b = nbank % 4
                nbank += 1
                ps = k.banks[b]
                for kk in range(8):
                    p.op("pe", lambda e, kk=kk, ps=ps, ht=ht, tb=tb, c0=c0, cn=cn: e.matmul(
                        ps[:, 0:cn], lhsT=ht[:, kk, tb * 128:(tb + 1) * 128], rhs=wa[:, kk, c0:c0 + cn], start=(kk == 0), stop=(kk == 7)),
                        reads=[dwa, dh], writes=[k.bdep[b]])
                if dst is not None:
                    ot, do, _ = orr.next()
                    p.op("act" if gi == 0 else "dve", (lambda e, ot=ot, ps=ps: e.activation(out=ot[:], in_=ps[:], func=AF.Copy)) if gi == 0 else
                         (lambda e, ot=ot, ps=ps: e.tensor_copy(out=ot[:], in_=ps[:])), reads=[k.bdep[b]], writes=[do])
                    p.store("sp", dst[tok:tok + 128, :], ot[:], [do], dd)
                else:
                    gt, dg, _ = gr.next()
                    p.op("dve", lambda e, gt=gt, ps=ps: e.tensor_tensor(out=gt[:, 0:8], in0=ps[:, 0:8], in1=abt[:, 8:16], op=ALU.add),
                         reads=[k.bdep[b], dabt], writes=[dg])
                    p.op("act", lambda e, gt=gt: e.activation(out=gt[:, 0:8], in_=gt[:, 0:8], func=AF.Exp), reads=[dg], writes=[dg])
                    p.op("act", lambda e, gt=gt: e.activation(out=gt[:, 0:8], in_=gt[:, 0:8], func=AF.Ln, bias=1.0), reads=[dg], writes=[dg])
                    p.op("dve", lambda e, gt=gt: e.tensor_tensor(out=gt[:, 0:8], in0=gt[:, 0:8], in1=nexp[:], op=ALU.mult),
                         reads=[dg, dnexp], writes=[dg])
                    p.op("act", lambda e, gt=gt, ps=ps: e.activation(out=gt[:, 8:16], in_=ps[:, 8:16], func=AF.Sigmoid),
                         reads=[k.bdep[b]], writes=[dg])
                    p.store("sp", GB[tok:tok + 128, :], gt[:, 0:16], [dg], dGB)


def common(k):
    p = k.p
    k.ident, k.dident = k.const("ident", [128, 128], k.inp["ident"])
    k.ones_t = p.sbuf("ones_t", [128, 128])
    k.dones = Dep("ones")
    p.op("pool", lambda e: e.memset(k.ones_t[:], 1.0 / D), writes=[k.dones])
    k.one1 = p.sbuf("one1", [128, 128])
    k.done1 = Dep("one1")
    p.op("pool", lambda e: e.memset(k.one1[:], 1.0), writes=[k.done1])
    k.rstd = p.sbuf("rstd", [128, 512])
    k.drstd = Dep("rstd")
    k.gm = p.sbuf("gm", [128, 8, 2])
    k.dgm = Dep("gm")
    k.modt = [(p.sbuf(f"mod{l}", [128, 48, 2]), Dep(f"mod{l}")) for l in range(2)]


def rope_tables():
    half = 64
    inv = (1.0 / (10000.0 ** (np.arange(0, half, 2, dtype=np.float32) / np.float32(half)))).astype(np.float32)
    t = np.arange(T)
    pos = np.stack([t // GW, t % GW], axis=-1).astype(np.float32)
    ang = pos[:, :, None] * inv[None, None, :]
    ang = np.concatenate([ang, ang], axis=-1).reshape(T, 128)
    cos = np.cos(ang).astype(np.float32).T
    sin = np.sin(ang).astype(np.float32).T
    rm = np.zeros((128, 128), np.float32)
    for o in (0, 64):
        for i in range(64):
            if i < 32:
                rm[o + i + 32, o + i] = -1.0
            else:
                rm[o + i - 32, o + i] = 1.0
    return np.ascontiguousarray(cos), np.ascontiguousarray(sin), rm


def dn_masks():
    i = np.arange(128)
    out = {}
    for d in range(2):
        le = (i[:, None] <= i[None, :]) if d == 0 else (i[:, None] >= i[None, :])
        gt = (i[:, None] > i[None, :]) if d == 0 else (i[:, None] < i[None, :])
        blk = lambda b: (i[:, None] // b) == (i[None, :] // b)
        ms = [le, gt, gt, ~le | (i[:, None] == i[None, :]), le, blk(16)]
        for b in (16, 32, 64):
            up = blk(2 * b) & ((i[:, None] // b) < (i[None, :] // b))
            mN = up if d == 0 else up.T
            ms += [mN, mN.T]
        out[d] = np.ascontiguousarray(np.stack(ms, axis=1).astype(np.float32))
    return out


def stage_dn_pre(k, l, tiles=None):
    p = k.p
    src = [k.scratch(f"D{x}{l}", [4, 128, NT]) for x in "QKV"]
    dst = [k.scratch(f"D{x}2_{l}", [4, 128, NT]) for x in "QKV"]
    cw, dcw = k.const(f"convw{l}", [128, 12, 5], k.inp[f"convw{l}"])
    k.rm, k.drm = k.const("rope_rm", [128, 128], k.inp["rope_rm"])
    ur = Ring(p, f"du{l}_", [128, 260], 3)
    ar = Ring(p, f"da{l}_", [128, 256], 3)
    yr = Ring(p, f"dy{l}_", [128, 256], 3)
    sr = Ring(p, f"ds{l}_", [128, 256], 2)
    rr = Ring(p, f"dr{l}_", [128, 256], 2)
    cr = Ring(p, f"dc{l}_", [128, 2, 256], 2)
    outr = Ring(p, f"do{l}_", [128, 256], 3)
    cosd, sind = k.inp["rope_cos"], k.inp["rope_sin"]
    nb = 0
    for (t0, N, col) in (tiles or super_tiles()):
        seg0, seg1 = (0, T) if col == 0 else (T, T + CT)
        ct, dct = None, None
        if col == 0:
            ct, dct, sem = cr.next()
            p.dma("sp", ct[:, 0, :], cosd[:, t0:t0 + N], writes=[dct], sem=sem)
            p.dma("sp", ct[:, 1, :], sind[:, t0:t0 + N], writes=[dct], sem=sem)
        for h in range(4):
            for xi in range(3):
                ut, du, sem = ur.next()
                lo, hi = max(t0 - 2, seg0), min(t0 + N + 2, seg1)
                if lo != t0 - 2 or hi != t0 + N + 2:
                    p.op("pool", lambda e, ut=ut: e.memset(ut[:], 0.0), writes=[du])
                p.dma("sp" if xi != 1 else "act", ut[:, lo - (t0 - 2):hi - (t0 - 2)], src[xi][0][h, :, lo:hi], reads=[src[xi][1]], writes=[du], sem=sem)
                at, da, _ = ar.next()
                eng = "dve"
                j = xi * 4 + h
                p.op(eng, lambda e, at=at, ut=ut, j=j: e.tensor_scalar(out=at[:, 0:N], in0=ut[:, 0:N], scalar1=cw[:, j, 0:1], scalar2=None, op0=ALU.mult),
                     reads=[du, dcw], writes=[da])
                for tap in range(1, 5):
                    p.op(eng, lambda e, at=at, ut=ut, j=j, tap=tap: e.scalar_tensor_tensor(out=at[:, 0:N], in0=ut[:, tap:tap + N], scalar=cw[:, j, tap:tap + 1],
                                                                                         in1=at[:, 0:N], op0=ALU.mult, op1=ALU.add),
                         reads=[du, dcw, da], writes=[da])
                yt, dy, _ = yr.next()
                p.op("act", lambda e, yt=yt, at=at: e.activation(out=yt[:, 0:N], in_=at[:, 0:N], func=AF.Silu), reads=[da], writes=[dy])
                if xi == 2:
                    p.store("sp", dst[2][0][h, :, t0:t0 + N], yt[:, 0:N], [dy], dst[2][1])
                    continue
                st, dsq, _ = sr.next()
                p.op("pool", lambda e, st=st, yt=yt: e.tensor_tensor(out=st[:, 0:N], in0=yt[:, 0:N], in1=yt[:, 0:N], op=ALU.mult), reads=[dy], writes=[dsq])
                b = 4 + (nb % 2)
                nb += 1
                ps = k.banks[b]
                p.op("pe", lambda e, ps=ps, st=st: e.matmul(ps[:, 0:N], lhsT=k.one1[:], rhs=st[:, 0:N], start=True, stop=True),
                     reads=[dsq, k.done1], writes=[k.bdep[b]])
                p.op("dve", lambda e, ps=ps, st=st: e.tensor_scalar(out=st[:, 0:N], in0=ps[:, 0:N], scalar1=EPS, scalar2=None, op0=ALU.add),
                     reads=[k.bdep[b]], writes=[dsq])
                p.op("act", lambda e, st=st: e.activation(out=st[:, 0:N], in_=st[:, 0:N], func=AF.Sqrt), reads=[dsq], writes=[dsq])
                p.op("dve", lambda e, st=st: e.reciprocal(out=st[:, 0:N], in_=st[:, 0:N]), reads=[dsq], writes=[dsq])
                ot, do, _ = outr.next()
                if col == 1:
                    p.op("dve", lambda e, ot=ot, yt=yt, st=st: e.tensor_tensor(out=ot[:, 0:N], in0=yt[:, 0:N], in1=st[:, 0:N], op=ALU.mult),
                         reads=[dy, dsq], writes=[do])
                else:
                    p.op("dve", lambda e, yt=yt, st=st: e.tensor_tensor(out=yt[:, 0:N], in0=yt[:, 0:N], in1=st[:, 0:N], op=ALU.mult),
                         reads=[dy, dsq], writes=[dy])
                    b2 = 6 + (nb % 2)
                    ps2 = k.banks[b2]
                    p.op("pe", lambda e, ps2=ps2, yt=yt: e.matmul(ps2[:, 0:N], lhsT=k.rm[:], rhs=yt[:, 0:N], start=True, stop=True),
                         reads=[dy, k.drm], writes=[k.bdep[b2]])
                    rt, dr, _ = rr.next()
                    p.op("dve", lambda e, rt=rt, ps2=ps2, ct=ct: e.tensor_tensor(out=rt[:, 0:N], in0=ps2[:, 0:N], in1=ct[:, 1, 0:N], op=ALU.mult),
                         reads=[k.bdep[b2], dct], writes=[dr])
                    p.op("pool", lambda e, ot=ot, yt=yt, ct=ct: e.tensor_tensor(out=ot[:, 0:N], in0=yt[:, 0:N], in1=ct[:, 0, 0:N], op=ALU.mult),
                         reads=[dy, dct], writes=[do])
                    p.op("pool", lambda e, ot=ot, rt=rt: e.tensor_tensor(out=ot[:, 0:N], in0=ot[:, 0:N], in1=rt[:, 0:N], op=ALU.add),
                         reads=[dr, do], writes=[do])
                p.store("sp", dst[xi][0][h, :, t0:t0 + N], ot[:, 0:N], [do], dst[xi][1])


class PS4:
    def __init__(self, k, si):
        bs = [2 * si, 2 * si + 1]
        self.t = [k.banks[b][:, 0:128] for b in bs]
        self.d = [k.bdep[b] for b in bs]
        self.i = 0

    def next(self):
        i = self.i
        self.i = (i + 1) % 2
        return self.t[i], self.d[i]


def dn_scan_gen(k, l, si, h, d, blocks, want_o):
    p = k.p
    SC = 128 ** -0.5
    QT, dQ = k.scratch(f"DQ2_{l}", [4, 128, NT])
    KT, dK = k.scratch(f"DK2_{l}", [4, 128, NT])
    VT, dV = k.scratch(f"DV2_{l}", [4, 128, NT])
    GB, dGB = k.scratch(f"GB{l}", [NT, 16])
    OD, dOD = k.scratch(f"OD{l}", [2, 4, NT, 128])
    mk, dmk = k.dnm[d]
    U, SL, MLs, MLD, MLDT = (mk[:, i, :] for i in range(5))
    BD16 = mk[:, 5, :]
    MRG = [(mk[:, 6 + 2 * j, :], mk[:, 7 + 2 * j, :]) for j in range(3)]
    B = k.dnbuf[si]
    ps = PS4(k, si)
    S = [B["S0"], B["S1"]]
    dS = [B["d_S0"], B["d_S1"]]
    p.op("pool", lambda e: e.memset(S[0][:], 0.0), writes=[dS[0]])
    cur = 0
    ident = k.ident

    def T_(name):
        return B[name], B["d_" + name]
    for bi, tok in enumerate(blocks):
        par = bi % 2
        qT, dq = T_(f"qT{par}"); kT, dk_ = T_(f"kT{par}"); vT, dv = T_(f"vT{par}"); gb, dgb = T_(f"gb{par}")
        lsem = f"ld_blk{par}_{si}"
        p.dma("sp", qT[:], QT[h, :, tok:tok + 128], reads=[dQ], writes=[dq], sem=lsem)
        p.dma("act", kT[:], KT[h, :, tok:tok + 128], reads=[dK], writes=[dk_], sem=lsem)
        p.dma("sp", vT[:], VT[h, :, tok:tok + 128], reads=[dV], writes=[dv], sem=lsem)
        p.dma("act", gb[:], GB[tok:tok + 128, :], reads=[dGB], writes=[dgb], sem=lsem)
        for dd_ in (dq, dk_, dv, dgb):
            dd_.w = dgb.w
        g = gb[:, d * 4 + h:d * 4 + h + 1]
        beta = gb[:, 8 + d * 4 + h:8 + d * 4 + h + 1]
        sc, dsc = T_("sc")
        ktok, dkt = T_("ktok"); vb, dvb = T_("vb"); SLg, dSLg = T_("SLg"); Dm, dDm = T_("Dm")
        Dstr, dDstr = T_("Dstr"); Dlow, dDlow = T_("Dlow"); DlowT, dDlowT = T_("DlowT")
        Na, dNa = T_("Na"); Nb, dNb = T_("Nb"); NTa, dNTa = T_("NTa"); NTb, dNTb = T_("NTb")
        Xa, dXa = T_("Xa"); Xb, dXb = T_("Xb")
        kbg, dkbg = T_("kbg"); ktail, dktail = T_("ktail"); u_sb, du = T_("u"); wT, dwT = T_("wT"); qkT, dqkT = T_("qkT")
        vnew, dvn = T_("vnew"); ob, dob = T_("ob"); o_sb, do = T_(f"o{par}")
        p1, dp1 = ps.next()
        p.op("pe", lambda e, p1=p1, kT=kT: e.transpose(out=p1, in_=kT[:], identity=ident[:]), reads=[dk_, k.dident], writes=[dp1])
        p.op("act", lambda e, p1=p1, ktok=ktok: e.activation(out=ktok[:], in_=p1, func=AF.Copy), reads=[dp1], writes=[dkt])
        p2, dp2 = ps.next()
        p.op("pe", lambda e, p2=p2, vT=vT: e.transpose(out=p2, in_=vT[:], identity=ident[:]), reads=[dv, k.dident], writes=[dp2])
        p.op("dve", lambda e, p2=p2, vb=vb, beta=beta: e.tensor_scalar(out=vb[:], in0=p2, scalar1=beta, scalar2=None, op0=ALU.mult),
             reads=[dp2, dgb], writes=[dvb])
        p.op("pool", lambda e, SLg=SLg, g=g: e.tensor_scalar(out=SLg[:], in0=SL, scalar1=g, scalar2=None, op0=ALU.mult),
             reads=[dmk, dgb], writes=[dSLg])
        p.op("pool", lambda e, sc=sc, beta=beta: e.tensor_scalar(out=sc[:, 0:1], in0=beta, scalar1=-1.0, scalar2=None, op0=ALU.mult),
             reads=[dgb], writes=[dsc])
        yield
        p3, dp3 = ps.next()
        p.op("pe", lambda e, p3=p3, SLg=SLg: e.matmul(p3, lhsT=U, rhs=SLg[:], start=True, stop=True), reads=[dSLg, dmk], writes=[dp3])
        p4, dp4 = ps.next()
        p.op("pe", lambda e, p4=p4, g=g: e.matmul(p4[:, 0:1], lhsT=U, rhs=g, start=True, stop=True), reads=[dgb, dmk], writes=[dp4])
        p.op("pe", lambda e, p4=p4, g=g: e.matmul(p4[:, 1:2], lhsT=k.one1[:], rhs=g, start=True, stop=True), reads=[dgb, k.done1], writes=[dp4])
        p.op("act", lambda e, p3=p3, Dm=Dm: e.activation(out=Dm[:], in_=p3, func=AF.Exp), reads=[dp3], writes=[dDm])
        p.op("dve", lambda e, p4=p4, sc=sc: e.tensor_copy(out=sc[:, 1:3], in_=p4[:, 0:2]), reads=[dp4], writes=[dsc])
        yield
        p.op("pool", lambda e, Dstr=Dstr, Dm=Dm: e.tensor_tensor(out=Dstr[:], in0=Dm[:], in1=MLs, op=ALU.mult), reads=[dDm, dmk], writes=[dDstr])
        p.op("pool", lambda e, Dlow=Dlow, Dm=Dm: e.tensor_tensor(out=Dlow[:], in0=Dm[:], in1=MLD, op=ALU.mult), reads=[dDm, dmk], writes=[dDlow])
        yield
        p.op("act", lambda e, sc=sc: e.activation(out=sc[:, 3:4], in_=sc[:, 1:2], func=AF.Exp), reads=[dsc], writes=[dsc])
        p.op("dve", lambda e, sc=sc: e.tensor_tensor(out=sc[:, 4:5], in0=sc[:, 2:3], in1=sc[:, 1:2], op=ALU.subtract), reads=[dsc], writes=[dsc])
        p.op("act", lambda e, sc=sc: e.activation(out=sc[:, 4:5], in_=sc[:, 4:5], func=AF.Exp), reads=[dsc], writes=[dsc])
        p.op("act", lambda e, sc=sc: e.activation(out=sc[:, 5:6], in_=sc[:, 2:3], func=AF.Exp), reads=[dsc], writes=[dsc])
        yield
        p.op("dve", lambda e, sc=sc, beta=beta: e.tensor_tensor(out=sc[:, 6:7], in0=sc[:, 3:4], in1=beta, op=ALU.mult), reads=[dsc, dgb], writes=[dsc])
        p.op("dve", lambda e, sc=sc: e.tensor_scalar(out=sc[:, 7:8], in0=sc[:, 3:4], scalar1=SC, scalar2=None, op0=ALU.mult), reads=[dsc], writes=[dsc])
        yield
        p5, dp5 = ps.next()
        kT2, dkT2 = T_("kT2")
        p.op("pool", lambda e, kT2=kT2, kT=kT: e.tensor_copy(out=kT2[:], in_=kT[:]), reads=[dk_], writes=[dkT2])
        p.op("pe", lambda e, p5=p5, kT=kT, kT2=kT2: e.matmul(p5, lhsT=kT[:], rhs=kT2[:], start=True, stop=True), reads=[dk_, dkT2], writes=[dp5])
        yield
        p.op("dve", lambda e, p5=p5, NTa=NTa, sc=sc: e.tensor_scalar(out=NTa[:], in0=p5, scalar1=sc[:, 0:1], scalar2=None, op0=ALU.mult),
             reads=[dp5, dsc], writes=[dNTa])
        p.op("pool", lambda e, NTa=NTa, Dstr=Dstr: e.tensor_tensor(out=NTa[:], in0=NTa[:], in1=Dstr[:], op=ALU.mult),
             reads=[dDstr, dNTa], writes=[dNTa])
        yield
        N0, dN0 = T_("N0"); NT0 = NTa; dNT0 = dNTa
        p6, dp6 = ps.next()
        p.op("pe", lambda e, p6=p6: e.transpose(out=p6, in_=NT0[:], identity=ident[:]), reads=[dNT0, k.dident], writes=[dp6])
        p.op("act", lambda e, p6=p6: e.activation(out=N0[:], in_=p6, func=AF.Copy), reads=[dp6], writes=[dN0])
        yield
        Xc, dXc = T_("Xa"); XTc, dXTc = T_("XTa"); Xn_, dXn_ = T_("Xb"); XTn_, dXTn_ = T_("XTb")
        Np, dNp = T_("Na"); NTp, dNTp = T_("NTb"); Nn, dNn = T_("Nb"); NTn, dNTn = T_("NTc")
        p.op("pool", lambda e, Np=Np, N0=N0: e.tensor_tensor(out=Np[:], in0=N0[:], in1=BD16, op=ALU.mult), reads=[dN0, dmk], writes=[dNp])
        p.op("pool", lambda e, NTp=NTp, NT0=NT0: e.tensor_tensor(out=NTp[:], in0=NT0[:], in1=BD16, op=ALU.mult), reads=[dNT0, dmk], writes=[dNTp])
        p.op("dve", lambda e, Xc=Xc, Np=Np: e.tensor_tensor(out=Xc[:], in0=Np[:], in1=ident[:], op=ALU.add), reads=[dNp, k.dident], writes=[dXc])
        p.op("dve", lambda e, XTc=XTc, NTp=NTp: e.tensor_tensor(out=XTc[:], in0=NTp[:], in1=ident[:], op=ALU.add), reads=[dNTp, k.dident], writes=[dXTc])
        yield
        for lvl in range(3):
            pa, dpa = ps.next()
            p.op("pe", lambda e, pa=pa, Np=Np, NTp=NTp: e.matmul(pa, lhsT=NTp[:], rhs=Np[:], start=True, stop=True), reads=[dNp, dNTp], writes=[dpa])
            pb, dpb = ps.next()
            p.op("pe", lambda e, pb=pb, Np=Np, NTp=NTp: e.matmul(pb, lhsT=Np[:], rhs=NTp[:], start=True, stop=True), reads=[dNp, dNTp], writes=[dpb])
            p.op("act", lambda e, pa=pa, Nn=Nn: e.activation(out=Nn[:], in_=pa, func=AF.Copy), reads=[dpa], writes=[dNn])
            p.op("dve", lambda e, pb=pb, NTn=NTn: e.tensor_copy(out=NTn[:], in_=pb), reads=[dpb], writes=[dNTn])
            yield
            pc, dpc = ps.next()
            p.op("pe", lambda e, pc=pc, NTn=NTn, Xc=Xc: e.matmul(pc, lhsT=NTn[:], rhs=Xc[:], start=True, stop=True), reads=[dNTn, dXc], writes=[dpc])
            pd, dpd = ps.next()
            p.op("pe", lambda e, pd=pd, Nn=Nn, XTc=XTc: e.matmul(pd, lhsT=Nn[:], rhs=XTc[:], start=True, stop=True), reads=[dNn, dXTc], writes=[dpd])
            p.op("dve", lambda e, pc=pc, Xc=Xc, Xn_=Xn_: e.tensor_tensor(out=Xn_[:], in0=pc, in1=Xc[:], op=ALU.add), reads=[dpc, dXc], writes=[dXn_])
            p.op("dve", lambda e, pd=pd, XTc=XTc, XTn_=XTn_: e.tensor_tensor(out=XTn_[:], in0=pd, in1=XTc[:], op=ALU.add), reads=[dpd, dXTc], writes=[dXTn_])
            (Xc, dXc, Xn_, dXn_) = (Xn_, dXn_, Xc, dXc)
            (XTc, dXTc, XTn_, dXTn_) = (XTn_, dXTn_, XTc, dXTc)
            (Np, dNp, Nn, dNn) = (Nn, dNn, Np, dNp)
            (NTp, dNTp, NTn, dNTn) = (NTn, dNTn, NTp, dNTp)
            yield
        Nm, dNm = T_("Nm"); NTm, dNTm = T_("NTm"); Y, dY = T_("Y"); Y2, dY2 = T_("Y2")
        for j in range(3):
            Mj, MjT = MRG[j]
            last = (j == 2)
            p.op("pool", lambda e, Mj=Mj: e.tensor_tensor(out=NTm[:], in0=NT0[:], in1=MjT, op=ALU.mult) if False else e.tensor_tensor(out=NTm[:], in0=NT0[:], in1=MjT, op=ALU.mult),
                 reads=[dNT0, dmk], writes=[dNTm]) if False else None
            p.op("pool", lambda e, MjT=MjT: e.tensor_tensor(out=NTm[:], in0=NT0[:], in1=MjT, op=ALU.mult), reads=[dNT0, dmk], writes=[dNTm])
            if not last:
                p.op("pool", lambda e, Mj=Mj: e.tensor_tensor(out=Nm[:], in0=N0[:], in1=Mj, op=ALU.mult), reads=[dN0, dmk], writes=[dNm])
            py, dpy = ps.next()
            p.op("pe", lambda e, py=py, Xc=Xc: e.matmul(py, lhsT=NTm[:], rhs=Xc[:], start=True, stop=True), reads=[dNTm, dXc], writes=[dpy])
            p.op("act", lambda e, py=py: e.activation(out=Y[:], in_=py, func=AF.Copy), reads=[dpy], writes=[dY])
            if not last:
                py2, dpy2 = ps.next()
                p.op("pe", lambda e, py2=py2, XTc=XTc: e.matmul(py2, lhsT=Nm[:], rhs=XTc[:], start=True, stop=True), reads=[dNm, dXTc], writes=[dpy2])
                p.op("dve", lambda e, py2=py2: e.tensor_copy(out=Y2[:], in_=py2), reads=[dpy2], writes=[dY2])
            yield
            pz, dpz = ps.next()
            p.op("pe", lambda e, pz=pz, XTc=XTc: e.matmul(pz, lhsT=XTc[:], rhs=Y[:], start=True, stop=True), reads=[dXTc, dY], writes=[dpz])
            if not last:
                pz2, dpz2 = ps.next()
                p.op("pe", lambda e, pz2=pz2, Xc=Xc: e.matmul(pz2, lhsT=Xc[:], rhs=Y2[:], start=True, stop=True), reads=[dXc, dY2], writes=[dpz2])
            p.op("dve", lambda e, pz=pz, Xc=Xc, Xn_=Xn_: e.tensor_tensor(out=Xn_[:], in0=pz, in1=Xc[:], op=ALU.add), reads=[dpz, dXc], writes=[dXn_])
            if not last:
                p.op("dve", lambda e, pz2=pz2, XTc=XTc, XTn_=XTn_: e.tensor_tensor(out=XTn_[:], in0=pz2, in1=XTc[:], op=ALU.add), reads=[dpz2, dXTc], writes=[dXTn_])
                (XTc, dXTc, XTn_, dXTn_) = (XTn_, dXTn_, XTc, dXTc)
            (Xc, dXc, Xn_, dXn_) = (Xn_, dXn_, Xc, dXc)
            yield
        Xs = [(Xc, dXc)]
        X, dX = Xs[0]
        p.op("pool", lambda e, kbg=kbg, ktok=ktok, sc=sc: e.tensor_scalar(out=kbg[:], in0=ktok[:], scalar1=sc[:, 6:7], scalar2=None, op0=ALU.mult),
             reads=[dkt, dsc], writes=[dkbg])
        p.op("pool", lambda e, ktail=ktail, ktok=ktok, sc=sc: e.tensor_scalar(out=ktail[:], in0=ktok[:], scalar1=sc[:, 4:5], scalar2=None, op0=ALU.mult),
             reads=[dkt, dsc], writes=[dktail])
        pu, dpu = ps.next()
        p.op("pe", lambda e, pu=pu, X=X, vb=vb: e.matmul(pu, lhsT=X[:], rhs=vb[:], start=True, stop=True), reads=[dX, dvb], writes=[dpu])
        p.op("act", lambda e, pu=pu, u_sb=u_sb: e.activation(out=u_sb[:], in_=pu, func=AF.Copy), reads=[dpu], writes=[du])
        pw, dpw = ps.next()
        p.op("pe", lambda e, pw=pw, X=X, kbg=kbg: e.matmul(pw, lhsT=kbg[:], rhs=X[:], start=True, stop=True), reads=[dX, dkbg], writes=[dpw])
        p.op("dve", lambda e, pw=pw, wT=wT: e.tensor_copy(out=wT[:], in_=pw), reads=[dpw], writes=[dwT])
        yield
        if want_o:
            pt, dpt = ps.next()
            p.op("pe", lambda e, pt=pt, Dlow=Dlow: e.transpose(out=pt, in_=Dlow[:], identity=ident[:]), reads=[dDlow, k.dident], writes=[dpt])
            p.op("act", lambda e, pt=pt, DlowT=DlowT: e.activation(out=DlowT[:], in_=pt, func=AF.Copy), reads=[dpt], writes=[dDlowT])
            pq, dpq = ps.next()
            p.op("pe", lambda e, pq=pq, kT=kT, qT=qT: e.matmul(pq, lhsT=kT[:], rhs=qT[:], start=True, stop=True), reads=[dk_, dq], writes=[dpq])
            p.op("dve", lambda e, pq=pq, qkT=qkT, DlowT=DlowT: e.scalar_tensor_tensor(out=qkT[:], in0=pq, scalar=SC, in1=DlowT[:], op0=ALU.mult, op1=ALU.mult),
                 reads=[dpq, dDlowT], writes=[dqkT])
            yield
        Sc, dSc = S[cur], dS[cur]
        Sn, dSn = S[1 - cur], dS[1 - cur]
        pv, dpv = ps.next()
        p.op("pe", lambda e, pv=pv, wT=wT, Sc=Sc: e.matmul(pv, lhsT=wT[:], rhs=Sc[:], start=True, stop=True), reads=[dwT, dSc], writes=[dpv])
        po, dpo = ps.next()
        if want_o:
            p.op("pe", lambda e, po=po, qT=qT, Sc=Sc: e.matmul(po, lhsT=qT[:], rhs=Sc[:], start=True, stop=True), reads=[dq, dSc], writes=[dpo])
        p.op("dve", lambda e, pv=pv, vnew=vnew, u_sb=u_sb: e.tensor_tensor(out=vnew[:], in0=u_sb[:], in1=pv, op=ALU.subtract), reads=[dpv, du], writes=[dvn])
        if want_o:
            p.op("dve", lambda e, po=po, o_sb=o_sb, sc=sc: e.tensor_scalar(out=o_sb[:], in0=po, scalar1=sc[:, 7:8], scalar2=None, op0=ALU.mult),
                 reads=[dpo, dsc], writes=[do])
        yield
        pn, dpn = ps.next()
        p.op("pe", lambda e, pn=pn, ktail=ktail, vnew=vnew: e.matmul(pn, lhsT=ktail[:], rhs=vnew[:], start=True, stop=True), reads=[dktail, dvn], writes=[dpn])
        pb2, dpb2 = ps.next()
        if want_o:
            p.op("pe", lambda e, pb2=pb2, qkT=qkT, vnew=vnew: e.matmul(pb2, lhsT=qkT[:], rhs=vnew[:], start=True, stop=True), reads=[dqkT, dvn], writes=[dpb2])
        p.op("dve", lambda e, pn=pn, Sn=Sn, Sc=Sc, sc=sc: e.scalar_tensor_tensor(out=Sn[:], in0=Sc[:], scalar=sc[:, 5:6], in1=pn, op0=ALU.mult, op1=ALU.add),
             reads=[dpn, dSc, dsc], writes=[dSn])
        cur = 1 - cur
        if want_o:
            p.op("dve", lambda e, pb2=pb2, o_sb=o_sb: e.tensor_tensor(out=o_sb[:], in0=o_sb[:], in1=pb2, op=ALU.add), reads=[dpb2, do], writes=[do])
            p.store("sp", OD[d, h, tok:tok + 128, :], o_sb[:], [do], dOD)
        yield


def dn_blocks(d, nlat=128, with_ctx=True):
    cb = [T, T + 128]
    lb = [i * 128 for i in range(nlat)]
    if d == 1:
        cb = cb[::-1]
        lb = [i * 128 for i in range(128)][::-1][:nlat]
    return (cb if with_ctx else []) + lb


def stage_dn_scan(k, l, nlat=128, scans=None):
    p = k.p
    if True:
        k.dnm = [k.const(f"dnmask{d}", [128, 12, 128], k.inp[f"dnmask{d}"]) for d in range(2)]
        names = ["ktok", "vb", "SLg", "Dm", "Dstr", "Dlow", "DlowT", "Na", "Nb", "NTa", "NTb", "Xa", "Xb", "kbg", "ktail", "u", "wT", "qkT",
                 "vnew", "ob", "o0", "o1", "kT2", "N0", "XTa", "XTb", "NTc", "Nm", "NTm", "Y", "Y2", "qT0", "qT1", "kT0", "kT1", "vT0", "vT1", "S0", "S1"]
        k.dnbuf = []
        for si in range(4):
            B = {}
            for n in names:
                B[n] = p.sbuf(f"dn{si}_{n}", [128, 128])
                B["d_" + n] = Dep(f"dn{si}_{n}")
            for n in ("gb0", "gb1"):
                B[n] = p.sbuf(f"dn{si}_{n}", [128, 16]); B["d_" + n] = Dep(f"dn{si}_{n}")
            B["sc"] = p.sbuf(f"dn{si}_sc", [128, 8]); B["d_sc"] = Dep(f"dn{si}_sc")
            k.dnbuf.append(B)
    import os
    maxsteps = int(os.environ.get("DN_STEPS", "1000000000"))
    nstep = 0
    allsc = list(scans or [(h, d) for h in range(4) for d in range(2)])
    GRP = int(os.environ.get("DN_GROUP", "1"))
    for g0 in range(0, len(allsc), GRP):
        gens = []
        for si, (h, d) in enumerate(allsc[g0:g0 + GRP]):
            gens.append(dn_scan_gen(k, l, si, h, d, dn_blocks(d, nlat), True))
        while gens and nstep < maxsteps:
            for g in list(gens):
                nstep += 1
                try:
                    next(g)
                except StopIteration:
                    gens.remove(g)


def stage_dn_post(k, l, toks=None):
    p = k.p
    OD, dOD = k.scratch(f"OD{l}", [2, 4, NT, 128])
    Z, dZ = k.scratch(f"Z{l}", [NT, 512])
    ODT, dODT = k.scratch(f"ODT{l}", [4, 128, NT])
    dnw, ddnw = k.const("dnw", [128, 128], k.inp[f"dnw{l}"])
    fr = Ring(p, "qf", [128, 4, 128], 2); br = Ring(p, "qb", [128, 4, 128], 2); zr = Ring(p, "qz", [128, 512], 2)
    sr = Ring(p, "qs", [128, 4, 128], 2); yr = Ring(p, "qy", [128, 4, 128], 2); tr_ = Ring(p, "qt", [128, 4, 128], 2)
    cr = Ring(p, "qc", [128, 8], 2)
    nb = 0
    for tok in (toks if toks is not None else range(0, NT, 128)):
        ft, df, sem = fr.next()
        p.dma("sp", ft[:], OD[0, :, tok:tok + 128, :].rearrange("h t d -> t h d"), reads=[dOD], writes=[df], sem=sem)
        bt, db, sem = br.next()
        p.dma("act", bt[:], OD[1, :, tok:tok + 128, :].rearrange("h t d -> t h d"), reads=[dOD], writes=[db], sem=sem)
        zt, dz, sem = zr.next()
        p.dma("sp", zt[:], Z[tok:tok + 128, :], reads=[dZ], writes=[dz], sem=sem)
        st, ds, _ = sr.next(); yt, dy, _ = yr.next(); ct, dc, _ = cr.next()
        p.op("dve", lambda e, st=st, ft=ft, bt=bt: e.tensor_tensor(out=st[:], in0=ft[:], in1=bt[:], op=ALU.add), reads=[df, db], writes=[ds])
        p.op("pool", lambda e, st=st, yt=yt: e.tensor_tensor(out=yt[:], in0=st[:], in1=st[:], op=ALU.mult), reads=[ds], writes=[dy])
        p.op("dve", lambda e, ct=ct, yt=yt: e.reduce_sum(out=ct[:, 0:4], in_=yt[:], axis=AX.X), reads=[dy], writes=[dc])
        p.op("dve", lambda e, ct=ct: e.tensor_scalar(out=ct[:, 0:4], in0=ct[:, 0:4], scalar1=1.0 / 128, scalar2=EPS, op0=ALU.mult, op1=ALU.add), reads=[dc], writes=[dc])
        p.op("act", lambda e, ct=ct: e.activation(out=ct[:, 0:4], in_=ct[:, 0:4], func=AF.Sqrt), reads=[dc], writes=[dc])
        p.op("dve", lambda e, ct=ct: e.reciprocal(out=ct[:, 0:4], in_=ct[:, 0:4]), reads=[dc], writes=[dc])
        p.op("act", lambda e, zt=zt: e.activation(out=zt[:], in_=zt[:], func=AF.Silu), reads=[dz], writes=[dz])
        for h in range(4):
            p.op("dve", lambda e, yt=yt, st=st, ct=ct, h=h: e.scalar_tensor_tensor(out=yt[:, h, :], in0=st[:, h, :], scalar=ct[:, h:h + 1], in1=dnw[:],
                                                                               op0=ALU.mult, op1=ALU.mult), reads=[ds, dc, ddnw, dy], writes=[dy])
        p.op("pool", lambda e, yt=yt, zt=zt: e.tensor_tensor(out=yt[:].rearrange("p h d -> p (h d)"), in0=yt[:].rearrange("p h d -> p (h d)"), in1=zt[:], op=ALU.mult),
             reads=[dz, dy], writes=[dy])
        b = nb % 2
        nb += 1
        ps = k.banks[b]
        for h in range(4):
            p.op("pe", lambda e, ps=ps, yt=yt, h=h: e.transpose(out=ps[:, h * 128:(h + 1) * 128], in_=yt[:, h, :], identity=k.ident[:]),
                 reads=[dy, k.dident], writes=[k.bdep[b]])
        tt, dt, _ = tr_.next()
        p.op("act", lambda e, tt=tt, ps=ps: e.activation(out=tt[:].rearrange("p h d -> p (h d)"), in_=ps[:, 0:512], func=AF.Copy), reads=[k.bdep[b]], writes=[dt])
        p.store("sp", ODT[:, :, tok:tok + 128].rearrange("h p t -> p h t"), tt[:], [dt], dODT)


def stage_mlp(k, l, xname, mod, dmod, tiles=None):
    p = k.p
    xsrc, dxs = k.scratch(xname, [8, 128, NT])
    XO, dXO = k.scratch(f"XO{l}", [8, 128, NT])
    ONT, dONT = k.scratch(f"ONT{l}", [4, 128, NT])
    ODT, dODT = k.scratch(f"ODT{l}", [4, 128, NT])
    OF, dOF = k.scratch(f"OF{l}", [NT, 512])
    GATE, dGATE = k.scratch(f"GATE{l}", [24, 128, NT])
    gam, dgam = k.const("norm2", [128, 8], k.inp[f"norm2_{l}"])
    make_gm(k, gam, dgam, mod, dmod, 4)
    xr = Ring(p, "mx", [128, 8, 256], 1); br_ = [Ring(p, f"mo{b}", [128, 4, 256], 1) for b in range(3)]
    gr = Ring(p, "mg", [128, 24, 256], 1); yr = Ring(p, "my", [128, 8, 256], 1); mr = Ring(p, "mm", [128, 8, 256], 1)
    tr_ = Ring(p, "mt", [128, 8, 256], 1); hr = Ring(p, "mh", [128, 8, 256], 1); ar = Ring(p, "ma", [128, 32, 256], 1)
    tmr = Ring(p, "mtm", [128, 256], 3); ofr = Ring(p, "mof", [128, 2, 512], 1); outr = Ring(p, "mout", [128, 256], 3)
    wmr = Ring(p, "wm", [128, 4, 128], 3); wor = Ring(p, "wo", [128, 8, 128], 2); w1r = Ring(p, "w1", [128, 8, 128], 3); w2r = Ring(p, "w2", [128, 32, 128], 2)
    wm_src = [k.inp[f"w_nao{l}"], k.inp[f"w_dno{l}"], k.inp[f"w_fno{l}"]]
    nb = 0
    nq = 0

    def wq():
        nonlocal nq
        nq += 1
        return "sp" if nq % 2 else "act"
    for (t0, N, col) in (tiles or super_tiles()):
        xt, dx, sem = xr.next()
        p.dma("sp", xt[:, :, 0:N], xsrc[:, :, t0:t0 + N].rearrange("k p t -> p k t"), reads=[dxs], writes=[dx], sem=sem)
        obs = []
        for b, (src, dsrc) in enumerate(((ONT, dONT), (ODT, dODT))):
            ot, do, sem = br_[b].next()
            p.dma("act", ot[:, :, 0:N], src[:, :, t0:t0 + N].rearrange("k p t -> p k t"), reads=[dsrc], writes=[do], sem=sem)
            obs.append((ot, do))
        oft, dof, sem = ofr.next()
        p.dma("sp", oft[:], OF[t0:t0 + N, :].rearrange("(a p) c -> p a c", p=128), reads=[dOF], writes=[dof], sem=sem)
        ot, do, _ = br_[2].next()
        for a in range(2):
            b_ = 6 + a
            for g in range(4):
                p.op("pe", lambda e, a=a, g=g, b_=b_, oft=oft: e.transpose(out=k.banks[b_][:, g * 128:(g + 1) * 128], in_=oft[:, a, g * 128:(g + 1) * 128], identity=k.ident[:]),
                     reads=[dof, k.dident], writes=[k.bdep[b_]])
            p.op("act", lambda e, a=a, b_=b_, ot=ot: e.activation(out=ot[:, :, a * 128:(a + 1) * 128], in_=k.banks[b_][:, 0:512].rearrange("p (g t) -> p g t", g=4), func=AF.Copy),
                 reads=[k.bdep[b_]], writes=[do])
        obs.append((ot, do))
        gt, dg, sem = gr.next()
        for q4 in range(4):
            p.dma(wq(), gt[:, q4 * 6:(q4 + 1) * 6, 0:N], GATE[q4 * 6:(q4 + 1) * 6, :, t0:t0 + N].rearrange("k p t -> p k t"), reads=[dGATE], writes=[dg], sem=sem)
        yt, dy, _ = yr.next()
        for c in range(8):
            for b in range(3):
                wt, dw, sem = wmr.next()
                p.dma(wq(), wt[:], wm_src[b][c], writes=[dw], sem=sem)
                bk = nb % 4
                nb += 1
                ps = k.banks[bk]
                ot, do = obs[b]
                for kk in range(4):
                    p.op("pe", lambda e, ps=ps, wt=wt, ot=ot, kk=kk: e.matmul(ps[:, 0:N], lhsT=wt[:, kk, :], rhs=ot[:, kk, 0:N], start=(kk == 0), stop=(kk == 3)),
                         reads=[dw, do], writes=[k.bdep[bk]])
                if b == 0:
                    p.op("dve", lambda e, ps=ps, yt=yt, gt=gt, c=c: e.tensor_tensor(out=yt[:, c, 0:N], in0=ps[:, 0:N], in1=gt[:, c, 0:N], op=ALU.mult),
                         reads=[k.bdep[bk], dg], writes=[dy])
                else:
                    tm, dtm, _ = tmr.next()
                    p.op("dve", lambda e, ps=ps, tm=tm, gt=gt, c=c, b=b: e.tensor_tensor(out=tm[:, 0:N], in0=ps[:, 0:N], in1=gt[:, b * 8 + c, 0:N], op=ALU.mult),
                         reads=[k.bdep[bk], dg], writes=[dtm])
                    p.op("pool", lambda e, tm=tm, yt=yt, c=c: e.tensor_tensor(out=yt[:, c, 0:N], in0=yt[:, c, 0:N], in1=tm[:, 0:N], op=ALU.add),
                         reads=[dtm, dy], writes=[dy])
        mt, dm, _ = mr.next()
        for c in range(8):
            wt, dw, sem = wor.next()
            p.dma(wq(), wt[:], k.inp[f"w_out{l}"][c], writes=[dw], sem=sem)
            bk = nb % 4
            nb += 1
            ps = k.banks[bk]
            for kk in range(8):
                p.op("pe", lambda e, ps=ps, wt=wt, yt=yt, kk=kk: e.matmul(ps[:, 0:N], lhsT=wt[:, kk, :], rhs=yt[:, kk, 0:N], start=(kk == 0), stop=(kk == 7)),
                     reads=[dw, dy], writes=[k.bdep[bk]])
            p.op("dve", lambda e, ps=ps, mt=mt, xt=xt, c=c, col=col: e.scalar_tensor_tensor(out=mt[:, c, 0:N], in0=ps[:, 0:N], scalar=mod[:, 16 + c, col:col + 1], in1=xt[:, c, 0:N],
                                                                                        op0=ALU.mult, op1=ALU.add), reads=[k.bdep[bk], dx, dmod], writes=[dm])
        ht, dh, _ = hr.next(); tmp, dtmp, _ = tr_.next()
        norm_mod(k, mt, dm, N, gam, dgam, mod, dmod, 3, 4, col, ht, dh, tmp, dtmp, k.ones_t, k.dones, 7)
        at, da, _ = ar.next()
        for j in range(32):
            wt, dw, sem = w1r.next()
            p.dma(wq(), wt[:], k.inp[f"w_mlp1_{l}"][j], writes=[dw], sem=sem)
            bk = nb % 4
            nb += 1
            ps = k.banks[bk]
            for kk in range(8):
                p.op("pe", lambda e, ps=ps, wt=wt, ht=ht, kk=kk: e.matmul(ps[:, 0:N], lhsT=wt[:, kk, :], rhs=ht[:, kk, 0:N], start=(kk == 0), stop=(kk == 7)),
                     reads=[dw, dh], writes=[k.bdep[bk]])
            p.op("act", lambda e, ps=ps, at=at, j=j: e.activation(out=at[:, j, 0:N], in_=ps[:, 0:N], func=AF.Relu), reads=[k.bdep[bk]], writes=[da])
            p.op("pool", lambda e, at=at, j=j: e.tensor_tensor(out=at[:, j, 0:N], in0=at[:, j, 0:N], in1=at[:, j, 0:N], op=ALU.mult), reads=[da], writes=[da])
        for c in range(8):
            wt, dw, sem = w2r.next()
            p.dma(wq(), wt[:], k.inp[f"w_mlp2_{l}"][c], writes=[dw], sem=sem)
            bk = nb % 4
            nb += 1
            ps = k.banks[bk]
            for j in range(32):
                p.op("pe", lambda e, ps=ps, wt=wt, at=at, j=j: e.matmul(ps[:, 0:N], lhsT=wt[:, j, :], rhs=at[:, j, 0:N], start=(j == 0), stop=(j == 31)),
                     reads=[dw, da], writes=[k.bdep[bk]])
            ot, do, _ = outr.next()
            p.op("dve", lambda e, ps=ps, ot=ot, mt=mt, c=c, col=col: e.scalar_tensor_tensor(out=ot[:, 0:N], in0=ps[:, 0:N], scalar=mod[:, 40 + c, col:col + 1], in1=mt[:, c, 0:N],
                                                                                        op0=ALU.mult, op1=ALU.add), reads=[k.bdep[bk], dm, dmod], writes=[do])
            p.store("sp", XO[c, :, t0:t0 + N], ot[:, 0:N], [do], dXO)


def stage_final(k, xname, tiles=None):
    p = k.p
    xsrc, dxs = k.scratch(xname, [8, 128, NT])
    y = k.nc.dram_tensor("y", [T, D], F32, kind="ExternalOutput").ap()
    k.ydep = Dep("y")
    gam, dgam = k.const("normf", [128, 8], k.inp["norm_f"])
    xr = Ring(p, "fx", [128, 8, 256], 2); tr_ = Ring(p, "ft", [128, 8, 256], 2); outr = Ring(p, "fo", [128, 1024], 2)
    nb = 0
    for (t0, N, col) in (tiles or super_tiles()[:-1]):
        xt, dx, sem = xr.next()
        p.dma("sp", xt[:, :, 0:N], xsrc[:, :, t0:t0 + N].rearrange("k p t -> p k t"), reads=[dxs], writes=[dx], sem=sem)
        tmp, dtmp, _ = tr_.next()
        p.op("act", lambda e, tmp=tmp, xt=xt: e.activation(out=tmp[:], in_=xt[:], func=AF.Square), reads=[dx], writes=[dtmp])
        ps = k.banks[7]
        for kk in range(8):
            p.op("pe", lambda e, kk=kk, tmp=tmp: e.matmul(ps[:, 0:N], lhsT=k.ones_t[:], rhs=tmp[:, kk, 0:N], start=(kk == 0), stop=(kk == 7)),
                 reads=[dtmp, k.dones], writes=[k.bdep[7]])
        rstd = k.rstd
        p.op("dve", lambda e: e.tensor_scalar(out=rstd[:, 0:N], in0=ps[:, 0:N], scalar1=EPS, scalar2=None, op0=ALU.add), reads=[k.bdep[7]], writes=[k.drstd])
        p.op("act", lambda e: e.activation(out=rstd[:, 0:N], in_=rstd[:, 0:N], func=AF.Sqrt), reads=[k.drstd], writes=[k.drstd])
        p.op("dve", lambda e: e.reciprocal(out=rstd[:, 0:N], in_=rstd[:, 0:N]), reads=[k.drstd], writes=[k.drstd])
        for kk in range(8):
            p.op("dve", lambda e, kk=kk, tmp=tmp, xt=xt: e.scalar_tensor_tensor(out=tmp[:, kk, 0:N], in0=xt[:, kk, 0:N], scalar=gam[:, kk:kk + 1], in1=rstd[:, 0:N],
                                                                               op0=ALU.mult, op1=ALU.mult), reads=[dx, k.drstd, dgam, dtmp], writes=[dtmp])
        for tb in range(N // 128):
            ot, do, _ = outr.next()
            for half in range(2):
                b = (nb % 3) * 2
                nb += 1
                b = b if False else (nb % 6)
                ps2 = k.banks[b]
                for q in range(4):
                    kk = half * 4 + q
                    p.op("pe", lambda e, ps2=ps2, tmp=tmp, kk=kk, q=q, tb=tb: e.transpose(out=ps2[:, q * 128:(q + 1) * 128], in_=tmp[:, kk, tb * 128:(tb + 1) * 128], identity=k.ident[:]),
                         reads=[dtmp, k.dident], writes=[k.bdep[b]])
                if half == 0:
                    p.op("act", lambda e, ps2=ps2, ot=ot: e.activation(out=ot[:, 0:512], in_=ps2[:, 0:512], func=AF.Copy), reads=[k.bdep[b]], writes=[do])
                else:
                    p.op("dve", lambda e, ps2=ps2, ot=ot: e.tensor_copy(out=ot[:, 512:1024], in_=ps2[:, 0:512]), reads=[k.bdep[b]], writes=[do])
            tok = t0 + tb * 128
            p.store("sp", y[tok:tok + 128, :], ot[:], [do], k.ydep)


def na_table(rpb):
    w = np.arange(64)
    cs = np.clip(w - 8, 0, 48)
    j = np.arange(64)
    inwin = (j[:, None] >= cs[None, :]) & (j[:, None] < cs[None, :] + 16)
    idx = np.clip(j[:, None] - w[None, :] + 15, 0, 30)
    tab = rpb[:, :, idx]
    tab = np.where(inwin[None, None], tab, np.float32(-30000.0))
    return np.ascontiguousarray(tab.transpose(2, 0, 1, 3)).astype(np.float32)


def stage_na(k, l, rows=None, do_ctx=True):
    p = k.p
    QT, dQT = k.scratch(f"QT{l}", [8, 64, NT])
    KT, dKT = k.scratch(f"KT{l}", [8, 64, NT])
    V, dV = k.scratch(f"V{l}", [NT, 512])
    ONT, dONT = k.scratch(f"ONT{l}", [4, 128, NT])
    eb, deb = k.const("natab", [64, 8 * 15 * 64], k.inp[f"natab{l}"].rearrange("j h o w -> j (h o w)"))
    for h in range(8):
        p.op("act", lambda e, h=h: e.activation(out=eb[:, h * 960:(h + 1) * 960], in_=eb[:, h * 960:(h + 1) * 960], func=AF.Exp), reads=[deb], writes=[deb])
    kc = p.sbuf("na_kc", [64, 8, 256]); dkc = Dep("na_kc")
    p.dma("sp", kc[:], KT[:, :, T:T + CT].rearrange("h d t -> d h t"), reads=[dKT], writes=[dkc], sem="ld_kc")
    vc = p.sbuf("na_vc", [128, 2, 8, 65]); dvc = Dep("na_vc")
    p.op("pool", lambda e: e.memset(vc[:], 1.0), writes=[dvc])
    for c in range(2):
        p.dma("act", vc[:, c, :, 0:64], V[T + c * 128:T + (c + 1) * 128, :].rearrange("t (h d) -> t h d", d=64), reads=[dV], writes=[dvc], sem="ld_vc")
    NS = 10
    kring = p.sbuf("na_kr", [64, NS, 8, 64]); vring = p.sbuf("na_vr", [64, NS, 8, 65])
    dkr = [Dep(f"na_kr{i}") for i in range(NS)]; dvr = [Dep(f"na_vr{i}") for i in range(NS)]
    p.op("pool", lambda e: e.memset(vring[:], 1.0), writes=dvr)
    qr = Ring(p, "na_q", [64, 8, 256], 2)
    er = Ring(p, "na_e", [64, 512], 3); ecr = Ring(p, "na_ec", [128, 128], 3)
    orr = Ring(p, "na_o", [64, 512], 2); otr = Ring(p, "na_ot", [128, 4, 64], 2); rcr = Ring(p, "na_rc", [64, 8], 2)
    loaded = -1
    it = 0
    qt = None
    rows = list(rows if rows is not None else range(256))
    for r in rows:
        rs = min(max(r - 4, 0), 248)
        while loaded < rs + 7:
            loaded += 1
            if loaded < rs:
                continue
            sl = loaded % NS
            p.dma("sp", kring[:, sl, :, :], KT[:, :, loaded * 64:(loaded + 1) * 64].rearrange("h d t -> d h t"), reads=[dKT], writes=[dkr[sl]], sem=f"ld_kr{sl}")
            p.dma("act", vring[:, sl, :, 0:64], V[loaded * 64:(loaded + 1) * 64, :].rearrange("t (h d) -> t h d", d=64), reads=[dV], writes=[dvr[sl]], sem=f"ld_vr{sl}")
        if qt is None or r % 4 == 0 or r == rows[0]:
            qt, dq, sem = qr.next()
            q0 = (r // 4) * 4
            p.dma("sp", qt[:], QT[:, :, q0 * 64:(q0 + 4) * 64].rearrange("h d t -> d h t"), reads=[dQT], writes=[dq], sem=sem)
        rr = r % 4
        o0 = rs - r + 7
        ot, do, _ = orr.next()
        rc, drc, _ = rcr.next()
        for h in range(8):
            bA, bB, bC = (it % 2), 2 + (it % 2), 4 + (it % 2)
            it += 1
            A, Bk, C = k.banks[bA], k.banks[bB], k.banks[bC]
            qv = qt[:, h, rr * 64:(rr + 1) * 64]
            for i in range(8):
                sl = (rs + i) % NS
                p.op("pe", lambda e, A=A, i=i, sl=sl, h=h, qv=qv: e.matmul(A[0:64, i * 64:(i + 1) * 64], lhsT=kring[:, sl, h, :], rhs=qv, start=True, stop=True),
                     reads=[dkr[sl], dq], writes=[k.bdep[bA]])
            for c in range(2):
                p.op("pe", lambda e, Bk=Bk, c=c, h=h, qv=qv: e.matmul(Bk[:, c * 64:(c + 1) * 64], lhsT=kc[:, h, c * 128:(c + 1) * 128], rhs=qv, start=True, stop=True),
                     reads=[dkc, dq], writes=[k.bdep[bB]])
            et, de, _ = er.next(); ect, dec, _ = ecr.next()
            p.op("act", lambda e, et=et, A=A: e.activation(out=et[:], in_=A[0:64, :], func=AF.Exp), reads=[k.bdep[bA]], writes=[de])
            p.op("act", lambda e, ect=ect, Bk=Bk: e.activation(out=ect[:], in_=Bk[:, 0:128], func=AF.Exp), reads=[k.bdep[bB]], writes=[dec])
            e0 = h * 960 + o0 * 64
            p.op("pool", lambda e, et=et, e0=e0: e.tensor_tensor(out=et[:], in0=et[:], in1=eb[:, e0:e0 + 512], op=ALU.mult), reads=[de, deb], writes=[de])
            for i in range(8):
                sl = (rs + i) % NS
                p.op("pe", lambda e, C=C, et=et, i=i, sl=sl, h=h: e.matmul(C[0:64, 0:65], lhsT=et[:, i * 64:(i + 1) * 64], rhs=vring[:, sl, h, :], start=(i == 0), stop=False),
                     reads=[de, dvr[sl]], writes=[k.bdep[bC]])
            for c in range(2):
                p.op("pe", lambda e, C=C, ect=ect, c=c, h=h: e.matmul(C[0:64, 0:65], lhsT=ect[:, c * 64:(c + 1) * 64], rhs=vc[:, c, h, :], start=False, stop=(c == 1)),
                     reads=[dec, dvc], writes=[k.bdep[bC]])
            p.op("dve", lambda e, C=C, rc=rc, h=h: e.reciprocal(out=rc[:, h:h + 1], in_=C[0:64, 64:65]), reads=[k.bdep[bC]], writes=[drc])
            p.op("dve", lambda e, C=C, rc=rc, ot=ot, h=h: e.tensor_scalar(out=ot[:, h * 64:(h + 1) * 64], in0=C[0:64, 0:64], scalar1=rc[:, h:h + 1], scalar2=None, op0=ALU.mult),
                 reads=[k.bdep[bC], drc], writes=[do])
        bT = 6 + (r % 2)
        for c in range(4):
            p.op("pe", lambda e, bT=bT, ot=ot, c=c: e.transpose(out=k.banks[bT][:, c * 64:(c + 1) * 64], in_=ot[:, c * 128:(c + 1) * 128], identity=k.ident[0:64, 0:64]),
                 reads=[do, k.dident], writes=[k.bdep[bT]])
        tt, dt, _ = otr.next()
        p.op("act", lambda e, bT=bT, tt=tt: e.activation(out=tt[:].rearrange("p c t -> p (c t)"), in_=k.banks[bT][:, 0:256], func=AF.Copy), reads=[k.bdep[bT]], writes=[dt])
        p.store("sp", ONT[:, :, r * 64:(r + 1) * 64].rearrange("c p t -> p c t"), tt[:], [dt], dONT)
    if not do_ctx:
        return
    qc = p.sbuf("na_qc", [64, 8, 256]); dqc = Dep("na_qc")
    p.dma("sp", qc[:], QT[:, :, T:T + CT].rearrange("h d t -> d h t"), reads=[dQT], writes=[dqc], sem="ld_qc")
    ocr = Ring(p, "na_oc", [128, 512], 2); otc = Ring(p, "na_otc", [128, 4, 128], 2); rcc = Ring(p, "na_rcc", [128, 8], 2)
    e2r = Ring(p, "na_e2", [128, 2, 128], 3)
    for t in range(2):
        ot, do, _ = ocr.next(); rc, drc, _ = rcc.next()
        for h in range(8):
            bA, bC = (it % 2), 4 + (it % 2)
            it += 1
            A, C = k.banks[bA], k.banks[bC]
            for c in range(2):
                p.op("pe", lambda e, A=A, c=c, h=h, t=t: e.matmul(A[:, c * 128:(c + 1) * 128], lhsT=kc[:, h, c * 128:(c + 1) * 128], rhs=qc[:, h, t * 128:(t + 1) * 128], start=True, stop=True),
                     reads=[dkc, dqc], writes=[k.bdep[bA]])
            et, de, _ = e2r.next()
            p.op("act", lambda e, et=et, A=A: e.activation(out=et[:].rearrange("p c t -> p (c t)"), in_=A[:, 0:256], func=AF.Exp), reads=[k.bdep[bA]], writes=[de])
            for c in range(2):
                p.op("pe", lambda e, C=C, et=et, c=c, h=h: e.matmul(C[:, 0:65], lhsT=et[:, c, :], rhs=vc[:, c, h, :], start=(c == 0), stop=(c == 1)),
                     reads=[de, dvc], writes=[k.bdep[bC]])
            p.op("dve", lambda e, C=C, rc=rc, h=h: e.reciprocal(out=rc[:, h:h + 1], in_=C[:, 64:65]), reads=[k.bdep[bC]], writes=[drc])
            p.op("dve", lambda e, C=C, rc=rc, ot=ot, h=h: e.tensor_scalar(out=ot[:, h * 64:(h + 1) * 64], in0=C[:, 0:64], scalar1=rc[:, h:h + 1], scalar2=None, op0=ALU.mult),
                 reads=[k.bdep[bC], drc], writes=[do])
        bT = 6 + t
        for c in range(4):
            p.op("pe", lambda e, bT=bT, ot=ot, c=c: e.transpose(out=k.banks[bT][:, c * 128:(c + 1) * 128], in_=ot[:, c * 128:(c + 1) * 128], identity=k.ident[:]),
                 reads=[do, k.dident], writes=[k.bdep[bT]])
        tt, dt, _ = otc.next()
        p.op("act", lambda e, bT=bT, tt=tt: e.activation(out=tt[:].rearrange("p c t -> p (c t)"), in_=k.banks[bT][:, 0:512], func=AF.Copy), reads=[k.bdep[bT]], writes=[dt])
        p.store("sp", ONT[:, :, T + t * 128:T + (t + 1) * 128].rearrange("c p t -> p c t"), tt[:], [dt], dONT)


def fn_tables():
    n = np.arange(128, dtype=np.float64)
    a = 2 * np.pi * np.outer(n, n) / 128.0
    c128, s128 = np.cos(a), np.sin(a)
    tw = 2 * np.pi * np.outer(n, n) / float(T)
    m = np.arange(256, dtype=np.float64)
    a2 = 2 * np.pi * np.outer(m, m) / 256.0
    c256, s256 = np.cos(a2), np.sin(a2)
    f = lambda x: np.ascontiguousarray(x.astype(np.float32))
    dft = np.stack([c128, s128, -c128, -s128], axis=1)
    twt = np.stack([np.cos(tw), np.sin(tw), -np.sin(tw)], axis=1)
    d256 = np.stack([c256, -s256], axis=0).reshape(2, 2, 128, 256).transpose(2, 0, 1, 3)
    return f(dft), f(twt), f(d256)


def stage_fn(k, l, do_ctx=True, t2s=None, t1s=None):
    p = k.p
    U, dU = k.scratch(f"U{l}", [4, 128, NT])
    PQ, dPQ = k.scratch(f"PQ{l}", [NT, 1024])
    YD, dYD = k.scratch(f"YD{l}", [128, 128, 1024])
    OF, dOF = k.scratch(f"OF{l}", [NT, 512])
    dft, ddft = k.const("dft", [128, 4, 128], k.inp["fn_dft"])
    twt, dtw = k.const("twt", [128, 3, 128], k.inp["fn_tw"])
    C, S, NC_, NS_ = (dft[:, i, :] for i in range(4))
    ur = Ring(p, "fu", [128, 4, 128], 2); pqr = Ring(p, "fpq", [128, 1024], 2)
    nb = 0
    for tok in range(0, NT if do_ctx else T, 128):
        ut, du, sem = ur.next()
        p.dma("sp", ut[:], U[:, :, tok:tok + 128].rearrange("g c t -> c g t"), reads=[dU], writes=[du], sem=sem)
        b0 = (nb % 2) * 2
        nb += 1
        for g in range(4):
            p.op("pe", lambda e, ut=ut, g=g, b0=b0: e.matmul(k.banks[b0][:, g * 128:(g + 1) * 128], lhsT=ut[:, g, :], rhs=C, start=True, stop=True),
                 reads=[du, ddft], writes=[k.bdep[b0]])
            p.op("pe", lambda e, ut=ut, g=g, b0=b0: e.matmul(k.banks[b0 + 1][:, g * 128:(g + 1) * 128], lhsT=ut[:, g, :], rhs=S, start=True, stop=True),
                 reads=[du, ddft], writes=[k.bdep[b0 + 1]])
        pt, dp, _ = pqr.next()
        p.op("act", lambda e, pt=pt, b0=b0: e.activation(out=pt[:, 0:512], in_=k.banks[b0][:, 0:512], func=AF.Copy), reads=[k.bdep[b0]], writes=[dp])
        p.op("dve", lambda e, pt=pt, b0=b0: e.tensor_copy(out=pt[:, 512:1024], in_=k.banks[b0 + 1][:, 0:512]), reads=[k.bdep[b0 + 1]], writes=[dp])
        p.store("sp", PQ[tok:tok + 128, :], pt[:], [dp], dPQ)
    xr = Ring(p, "fx", [128, 1024], 2); yr = Ring(p, "fy", [128, 1024], 2); tr_ = Ring(p, "ftm", [128, 2, 512], 2)
    PQv = PQ[0:T, :].rearrange("(t1 t2) c -> t2 t1 c", t2=128)
    for t2 in (t2s if t2s is not None else range(128)):
        xt, dx, sem = xr.next()
        for q4 in range(4):
            p.dma("sp" if q4 % 2 == 0 else "act", xt[:, q4 * 256:(q4 + 1) * 256], PQv[t2, :, q4 * 256:(q4 + 1) * 256], reads=[dPQ], writes=[dx], sem=sem)
        b0 = (nb % 2) * 2
        nb += 1
        Yr, Yi = k.banks[b0], k.banks[b0 + 1]
        p.op("pe", lambda e, Yr=Yr, xt=xt: e.matmul(Yr[:, 0:512], lhsT=C, rhs=xt[:, 0:512], start=True, stop=False), reads=[dx, ddft], writes=[k.bdep[b0]])
        p.op("pe", lambda e, Yr=Yr, xt=xt: e.matmul(Yr[:, 0:512], lhsT=NS_, rhs=xt[:, 512:1024], start=False, stop=True), reads=[dx, ddft], writes=[k.bdep[b0]])
        p.op("pe", lambda e, Yi=Yi, xt=xt: e.matmul(Yi[:, 0:512], lhsT=NC_, rhs=xt[:, 512:1024], start=True, stop=False), reads=[dx, ddft], writes=[k.bdep[b0 + 1]])
        p.op("pe", lambda e, Yi=Yi, xt=xt: e.matmul(Yi[:, 0:512], lhsT=NS_, rhs=xt[:, 0:512], start=False, stop=True), reads=[dx, ddft], writes=[k.bdep[b0 + 1]])
        tm, dtm, _ = tr_.next(); yt, dy, _ = yr.next()
        cf, sf, nsf = twt[:, 0, t2:t2 + 1], twt[:, 1, t2:t2 + 1], twt[:, 2, t2:t2 + 1]
        p.op("dve", lambda e, tm=tm, Yr=Yr, cf=cf: e.tensor_scalar(out=tm[:, 0, :], in0=Yr[:, 0:512], scalar1=cf, scalar2=None, op0=ALU.mult), reads=[k.bdep[b0], dtw], writes=[dtm])
        p.op("dve", lambda e, tm=tm, Yi=Yi, cf=cf: e.tensor_scalar(out=tm[:, 1, :], in0=Yi[:, 0:512], scalar1=cf, scalar2=None, op0=ALU.mult), reads=[k.bdep[b0 + 1], dtw], writes=[dtm])
        p.op("dve", lambda e, tm=tm, yt=yt, Yi=Yi, sf=sf: e.scalar_tensor_tensor(out=yt[:, 0:512], in0=Yi[:, 0:512], scalar=sf, in1=tm[:, 0, :], op0=ALU.mult, op1=ALU.add),
             reads=[k.bdep[b0 + 1], dtw, dtm], writes=[dy])
        p.op("dve", lambda e, tm=tm, yt=yt, Yr=Yr, nsf=nsf: e.scalar_tensor_tensor(out=yt[:, 512:1024], in0=Yr[:, 0:512], scalar=nsf, in1=tm[:, 1, :], op0=ALU.mult, op1=ALU.add),
             reads=[k.bdep[b0], dtw, dtm], writes=[dy])
        for q4 in range(4):
            p.store("sp", YD[:, t2, q4 * 256:(q4 + 1) * 256], yt[:, q4 * 256:(q4 + 1) * 256], [dy], dYD)
    sc_lat = 1.0 / math.sqrt(T * 128.0)
    zr = Ring(p, "fz", [128, 1024], 2); orr = Ring(p, "fo", [128, 512], 2)
    OFv = OF[0:T, :].rearrange("(b a) c -> a b c", a=128)
    for t1 in (t1s if t1s is not None else range(128)):
        zt, dz, sem = zr.next()
        p.dma("sp", zt[:], YD[t1], reads=[dYD], writes=[dz], sem=sem)
        b = 4 + (nb % 2)
        nb += 1
        ps = k.banks[b]
        p.op("pe", lambda e, ps=ps, zt=zt: e.matmul(ps[:, 0:512], lhsT=C, rhs=zt[:, 0:512], start=True, stop=False), reads=[dz, ddft], writes=[k.bdep[b]])
        p.op("pe", lambda e, ps=ps, zt=zt: e.matmul(ps[:, 0:512], lhsT=S, rhs=zt[:, 512:1024], start=False, stop=True), reads=[dz, ddft], writes=[k.bdep[b]])
        ot, do, _ = orr.next()
        p.op("act", lambda e, ps=ps, ot=ot: e.activation(out=ot[:], in_=ps[:, 0:512], func=AF.Copy, scale=sc_lat), reads=[k.bdep[b]], writes=[do])
        for q2 in range(2):
            p.store("sp", OFv[t1, :, q2 * 256:(q2 + 1) * 256], ot[:, q2 * 256:(q2 + 1) * 256], [do], dOF)
    if not do_ctx:
        return
    d256, dd256 = k.const("d256", [128, 2, 2, 256], k.inp["fn_d256"])
    sc_ctx = 1.0 / math.sqrt(256.0 * 128.0)
    cx = p.sbuf("fcx", [128, 2, 1024]); dcx = Dep("fcx")
    p.dma("sp", cx[:], PQ[T:T + 256, :].rearrange("(a p) c -> p a c", p=128), reads=[dPQ], writes=[dcx], sem="ld_fcx")
    for tt_ in range(2):
        b = 4 + (nb % 2)
        nb += 1
        ps = k.banks[b]
        n_ = 0
        for cs_ in range(2):
            for kt in range(2):
                p.op("pe", lambda e, ps=ps, cs_=cs_, kt=kt, tt_=tt_, n_=n_: e.matmul(ps[:, 0:512], lhsT=d256[:, cs_, kt, tt_ * 128:(tt_ + 1) * 128],
                                                                                 rhs=cx[:, kt, cs_ * 512:(cs_ + 1) * 512], start=(n_ == 0), stop=(n_ == 3)),
                     reads=[dcx, dd256], writes=[k.bdep[b]])
                n_ += 1
        ot, do, _ = orr.next()
        p.op("act", lambda e, ps=ps, ot=ot: e.activation(out=ot[:], in_=ps[:, 0:512], func=AF.Copy, scale=sc_ctx), reads=[k.bdep[b]], writes=[do])
        p.store("sp", OF[T + tt_ * 128:T + (tt_ + 1) * 128, :], ot[:], [do], dOF)


def build_program(shapes, debug=None):
    nc = bass.Bass("TRN2", target_bir_lowering=False)
    p = Prog(nc)
    k = K(nc, p, shapes)
    common(k)
    if debug is not None:
        fin = debug(k)
        p.build(final_deps=fin)
        return nc
    for l in range(2):
        xname = "xin" if l == 0 else "XO0"
        mod, dmod = k.modt[l]
        lat = super_tiles()[:-1]
        for fn in (lambda: stage_mod(k, l),
                   lambda: stage_proj(k, l, xname, mod, dmod),
                   lambda: stage_dn_pre(k, l),
                   lambda: stage_dn_scan(k, l),
                   lambda: stage_dn_post(k, l, toks=None if l == 0 else range(0, T, 128)),
                   lambda: stage_na(k, l, do_ctx=(l == 0)),
                   lambda: stage_fn(k, l, do_ctx=(l == 0)),
                   lambda: stage_mlp(k, l, xname, mod, dmod, tiles=None if l == 0 else lat)):
            p.scope_begin()
            fn()
            p.scope_end()
    p.scope_begin()
    stage_final(k, "XO1")
    p.scope_end()
    p.build(final_deps=[k.ydep])
    return nc


def kernel(**inputs):
    inp = {k_: np.asarray(v, np.float32) for k_, v in inputs.items()}
    m = host_layout(inp)
    shapes = {k_: v.shape for k_, v in m.items()}
    nc = build_program(shapes)
    res = run_bass_kernel_spmd(nc, [m], core_ids=[0])
    y = np.asarray(res.results[0]["y"], np.float32)
    return y.reshape(1, T, D)
```

```python
import math
import numpy as np
import concourse.bass as bass
import concourse.mybir as mybir
from concourse.bass_utils import run_bass_kernel_spmd

F32 = mybir.dt.float32
AF = mybir.ActivationFunctionType
ALU = mybir.AluOpType
AX = mybir.AxisListType

D = 1024
T = 16384
CT = 256
NT = T + CT
GW = 64
EPS = 1e-6
NTAB = 15


class Dep:
    __slots__ = ("name", "w", "r")

    def __init__(self, name=""):
        self.name = name
        self.w = None
        self.r = []


class Prog:
    ENGS = ("pe", "act", "dve", "pool", "sp")

    def __init__(self, nc, self_sync=True):
        self.nc = nc
        self.q = {e: [] for e in self.ENGS}
        self.cnt = {}
        self.known = {e: {} for e in self.ENGS}
        self.sems = {}
        self.self_sync = self_sync
        self._uid = 0
        self._stack = []
        self._semstack = []
        self._smap = {}
        self._scope_mark = 0
        self.n_inst = 0
        for e in self.ENGS:
            self._mksem("E_" + e)

    def _mksem(self, key):
        cm = self.nc.semaphore(key)
        h = cm.__enter__()
        self._semstack.append(cm)
        self.sems[key] = h
        self.cnt[key] = 0
        return key

    _mksem_global = _mksem

    def sbuf(self, name, shape, dt=F32):
        self._uid += 1
        cm = self.nc.sbuf_tensor(f"{name}_u{self._uid}", list(shape), dt)
        t = cm.__enter__()
        self._stack.append(cm)
        return t

    def psum(self, name, shape, dt=F32):
        cm = self.nc.psum_tensor(name, list(shape), dt)
        t = cm.__enter__()
        self._stack.append(cm)
        return t

    def close(self):
        while self._stack:
            self._stack.pop().__exit__(None, None, None)
        while self._semstack:
            self._semstack.pop().__exit__(None, None, None)

    def _waits(self, eng, reads, writes):
        need = {}

        def add(ev):
            if ev is None:
                return
            if isinstance(ev, dict):
                for k, v in ev.items():
                    if need.get(k, 0) < v:
                        need[k] = v
                return
            k, v = ev
            if need.get(k, 0) < v:
                need[k] = v
        for d in reads:
            add(d.w)
        for d in writes:
            add(d.w)
            for ev in d.r:
                add(ev)
        out = []
        own = "E_" + eng
        for k, v in need.items():
            if k == own and (eng == "pe" or not self.self_sync):
                continue
            if self.known[eng].get(k, 0) >= v:
                continue
            self.known[eng][k] = v
            out.append((k, v))
        return out

    def _emit(self, eng, fn, reads, writes, semkey, inc, track_w=True):
        waits = self._waits(eng, reads, writes if track_w else [])
        self.cnt[semkey] += inc
        ev = (semkey, self.cnt[semkey])
        for d in writes:
            d.w = ev
            d.r = []
        for d in reads:
            if d not in writes:
                d.r.append(ev)
                if len(d.r) > 64:
                    d.r = d.r[-64:]
        self.q[eng].append((waits, fn, semkey, inc))
        self.n_inst += 1 + len(waits)

    def scope_begin(self):
        self.barrier()
        self._scope_mark = len(self._stack)
        self._smap = {}

    def scope_end(self):
        self.barrier()
        while len(self._stack) > self._scope_mark:
            self._stack.pop().__exit__(None, None, None)
        self._smap = {}

    def barrier(self):
        for eng in self.ENGS:
            waits = []
            for key, v in self.cnt.items():
                if v == 0 or key == "E_" + eng:
                    continue
                if self.known[eng].get(key, 0) >= v:
                    continue
                self.known[eng][key] = v
                waits.append((key, v))
            if waits:
                self.q[eng].append((waits, None, None, 0))
                self.n_inst += len(waits)

    def dsem(self, key):
        m = self._smap
        if key not in m:
            phys = f"dma{len(m)}"
            if phys not in self.sems:
                self._mksem_global(phys)
            m[key] = phys
        return m[key]

    def op(self, eng, fn, reads=(), writes=()):
        self._emit(eng, fn, list(reads), list(writes), "E_" + eng, 1)

    def dma(self, eng, out, in_, reads=(), writes=(), sem=None, track_w=True, **kw):
        sem = self.dsem(sem)
        self._emit(eng, lambda e: e.dma_start(out=out, in_=in_, **kw), list(reads), list(writes), sem, 16,
                   track_w=track_w)

    def store(self, eng, out, in_, reads, ddep, **kw):
        sem = "st_" + reads[0].name
        self.dma(eng, out, in_, reads=reads, writes=[], sem=sem, **kw)
        sem = self.dsem(sem)
        if not isinstance(ddep.w, dict):
            ddep.w = {}
        ddep.w[sem] = self.cnt[sem]

    def build(self, final_deps=()):
        nc = self.nc
        fw = self._waits("sp", list(final_deps), [])
        sems = self.sems
        q = self.q

        def replay(eh, items, extra=()):
            for waits, fn, semkey, inc in items:
                for k, v in waits:
                    eh.wait_ge(sems[k], v)
                if fn is not None:
                    fn(eh).then_inc(sems[semkey], inc)
            for k, v in extra:
                eh.wait_ge(sems[k], v)

        with nc.Block() as block:
            @block.tensor
            def _(e):
                replay(e, q["pe"])

            @block.scalar
            def _(e):
                replay(e, q["act"])

            @block.vector
            def _(e):
                replay(e, q["dve"])

            @block.gpsimd
            def _(e):
                replay(e, q["pool"])

            @block.sync
            def _(e):
                replay(e, q["sp"], fw)
        self.close()


class Ring:
    def __init__(self, p, name, shape, n=2):
        self.t = [p.sbuf(f"{name}{i}", shape) for i in range(n)]
        self.d = [Dep(f"{name}{i}") for i in range(n)]
        self.i = 0
        self.n = n
        self.name = name

    def next(self):
        i = self.i
        self.i = (i + 1) % self.n
        return self.t[i], self.d[i], f"ld_{self.name}{i}"


_o = 0
COLS = {}
for _n, _s in (("na_k", 512), ("na_v", 512), ("dn_k", 512), ("dn_v", 512), ("dn_a", 8), ("dn_b", 8),
               ("na_q", 512), ("dn_q", 512), ("dn_z", 512), ("fn_u", 512), ("gate", 3072)):
    COLS[_n] = (_o, _o + _s)
    _o += _s
IN_W = _o
FB_GROUPS = (("na_q", 4), ("na_k", 4), ("dn_q", 4), ("dn_k", 4), ("dn_v", 4), ("fn_u", 4), ("gate", 24))
FB_CHUNKS = [(g, i) for g, n in FB_GROUPS for i in range(n)]
NFB = len(FB_CHUNKS)
FA_COLS = np.concatenate([np.arange(*COLS["na_v"]), np.arange(*COLS["dn_z"]),
                          np.arange(*COLS["dn_a"]), np.arange(*COLS["dn_b"])])


def _fm(v, nchunk):
    return np.ascontiguousarray(np.asarray(v, np.float32).reshape(nchunk, 128).T)


def _lhs_chunks(w, cols):
    K = w.shape[0]
    sub = w[:, cols]
    nc_ = sub.shape[1] // 128
    a = sub.reshape(K // 128, 128, nc_, 128)
    return np.ascontiguousarray(a.transpose(2, 1, 0, 3))


def _rhs_rows(w):
    K, N = w.shape
    return np.ascontiguousarray(w.reshape(K // 128, 128, N).transpose(1, 0, 2))


def host_layout(inp):
    m = {}
    x = np.concatenate([inp["x"][0], inp["ctx"][0]], axis=0)
    m["xin"] = np.ascontiguousarray(x.T.reshape(8, 128, NT))
    m["cvec"] = np.ascontiguousarray(np.stack([_fm(inp["c"][0], 8), _fm(inp["c_ctx"], 8)], axis=2))
    for l in range(2):
        w_in = inp["w_in"][l]
        fbcols = np.concatenate([np.arange(COLS[g][0] + i * 128, COLS[g][0] + (i + 1) * 128) for g, i in FB_CHUNKS])
        m[f"w_inb{l}"] = _lhs_chunks(w_in, fbcols)
        m[f"w_ina{l}"] = _rhs_rows(w_in[:, FA_COLS])
        m[f"w_ada{l}"] = _lhs_chunks(inp["w_ada"][l], np.arange(6 * D))
        m[f"b_ada{l}"] = _fm(inp["b_ada"][l], 48)
        m[f"norm1_{l}"] = _fm(inp["norm1"][l], 8)
        m[f"norm2_{l}"] = _fm(inp["norm2"][l], 8)
        ab = np.concatenate([inp["a_log"][l].reshape(-1), inp["dt_bias"][l].reshape(-1)])
        m[f"abt{l}"] = np.ascontiguousarray(np.broadcast_to(ab[None, :], (128, 16))).astype(np.float32)
        m[f"convw{l}"] = np.ascontiguousarray(inp["conv_w"][l].T.reshape(12, 128, 5).transpose(1, 0, 2))
    cos, sin, rm = rope_tables()
    m["rope_cos"], m["rope_sin"], m["rope_rm"] = cos, sin, rm
    mk = dn_masks()
    m["dnmask0"], m["dnmask1"] = mk[0], mk[1]
    for l in range(2):
        ar = np.arange(D)
        m[f"w_nao{l}"] = _lhs_chunks(inp["w_na_o"][l], ar)
        m[f"w_dno{l}"] = _lhs_chunks(inp["w_dn_o"][l], ar)
        m[f"w_fno{l}"] = _lhs_chunks(inp["w_fn"][l], ar)
        m[f"w_out{l}"] = _lhs_chunks(inp["w_out"][l], ar)
        m[f"w_mlp1_{l}"] = _lhs_chunks(inp["w_mlp1"][l], np.arange(4 * D))
        m[f"w_mlp2_{l}"] = _lhs_chunks(inp["w_mlp2"][l], ar)
        m[f"dnw{l}"] = np.ascontiguousarray(np.broadcast_to(inp["dn_norm"][l][None, :], (128, 128))).astype(np.float32)
        m[f"natab{l}"] = na_table(inp["rpb"][l])
    m["fn_dft"], m["fn_tw"], m["fn_d256"] = fn_tables()
    m["norm_f"] = _fm(inp["norm_f"], 8)
    ident = np.eye(128, dtype=np.float32)
    m["ident"] = ident
    return m


class K:
    def __init__(self, nc, p, shapes):
        self.nc = nc
        self.p = p
        self.inp = {k: nc.dram_tensor(k, list(v), F32, kind="ExternalInput").ap() for k, v in shapes.items()}
        self.scr = {}
        self.sdep = {}
        self.banks = [p.psum(f"bank{i}", [128, 512]) for i in range(8)]
        self.bdep = [Dep(f"bank{i}") for i in range(8)]
        self.cdep = Dep("consts")
        self._ld = 0

    def scratch(self, name, shape, kind="Internal"):
        if name in self.inp and name not in self.scr:
            self.scr[name] = self.inp[name]
            self.sdep[name] = Dep(name)
        if name not in self.scr:
            self.scr[name] = self.nc.dram_tensor(name, list(shape), F32, kind=kind).ap()
            self.sdep[name] = Dep(name)
        return self.scr[name], self.sdep[name]

    def const(self, name, shape, src, eng="sp"):
        t = self.p.sbuf("c_" + name, shape)
        d = Dep("c_" + name)
        self.p.dma(eng, t[:], src, writes=[d], sem="ld_c_" + name)
        return t, d


def stage_mod(k, l):
    p = k.p
    cv, dcv = k.const("cvec", [128, 8, 2], k.inp["cvec"])
    cs = p.sbuf("cs", [128, 8, 2])
    dcs = Dep("cs")
    p.op("act", lambda e: e.activation(out=cs[:], in_=cv[:], func=AF.Silu), reads=[dcv], writes=[dcs])
    mod, dmod = k.modt[l]
    bt, dbt = k.const(f"b_ada{l}", [128, 48], k.inp[f"b_ada{l}"])
    ring = Ring(p, f"wada{l}_", [128, 8, 128], 3)
    wsrc = k.inp[f"w_ada{l}"]
    for j in range(48):
        wt, dw, sem = ring.next()
        p.dma("sp", wt[:], wsrc[j], writes=[dw], sem=sem)
        b = j % 2
        ps = k.banks[b]
        for kk in range(8):
            p.op("pe", lambda e, wt=wt, kk=kk, ps=ps: e.matmul(ps[:, 0:2], lhsT=wt[:, kk, :], rhs=cs[:, kk, :],
                                                              start=(kk == 0), stop=(kk == 7)),
                 reads=[dw, dcs], writes=[k.bdep[b]])
        p.op("dve", lambda e, ps=ps, j=j: e.tensor_scalar(out=mod[:, j, :], in0=ps[:, 0:2], scalar1=bt[:, j:j + 1],
                                                        scalar2=None, op0=ALU.add),
             reads=[k.bdep[b], dbt], writes=[dmod])
    return mod, dmod


def norm_mod(k, xt, dx, N, gam, dgam, mod, dmod, sh_j, sc_j, col, out, dout, tmp, dtmp, ones_t, dones, bank):
    p = k.p
    ps = k.banks[bank]
    p.op("act", lambda e: e.activation(out=tmp[:, :, 0:N], in_=xt[:, :, 0:N], func=AF.Square), reads=[dx], writes=[dtmp])
    for kk in range(8):
        p.op("pe", lambda e, kk=kk: e.matmul(ps[:, 0:N], lhsT=ones_t[:], rhs=tmp[:, kk, 0:N], start=(kk == 0), stop=(kk == 7)),
             reads=[dtmp, dones], writes=[k.bdep[bank]])
    rstd = k.rstd
    p.op("dve", lambda e: e.tensor_scalar(out=rstd[:, 0:N], in0=ps[:, 0:N], scalar1=EPS, scalar2=None, op0=ALU.add),
         reads=[k.bdep[bank]], writes=[k.drstd])
    p.op("act", lambda e: e.activation(out=rstd[:, 0:N], in_=rstd[:, 0:N], func=AF.Sqrt), reads=[k.drstd], writes=[k.drstd])
    p.op("dve", lambda e: e.reciprocal(out=rstd[:, 0:N], in_=rstd[:, 0:N]), reads=[k.drstd], writes=[k.drstd])
    for kk in range(8):
        eng = "dve" if kk % 2 == 0 else "pool"
        p.op(eng, lambda e, kk=kk: e.tensor_tensor(out=tmp[:, kk, 0:N], in0=xt[:, kk, 0:N], in1=rstd[:, 0:N], op=ALU.mult),
             reads=[dx, k.drstd], writes=[dtmp])
    gm = k.gm
    for kk in range(8):
        p.op("act", lambda e, kk=kk: e.activation(out=out[:, kk, 0:N], in_=tmp[:, kk, 0:N], func=AF.Identity,
                                                  bias=mod[:, sh_j * 8 + kk, col:col + 1], scale=gm[:, kk, col:col + 1]),
             reads=[dtmp, k.dgm, dmod], writes=[dout])


def make_gm(k, gam, dgam, mod, dmod, sc_j):
    p = k.p
    gm = k.gm
    for col in range(2):
        p.op("dve", lambda e, col=col: e.scalar_tensor_tensor(out=gm[:, :, col], in0=mod[:, sc_j * 8:sc_j * 8 + 8, col], scalar=1.0,
                                                              in1=gam[:, :], op0=ALU.add, op1=ALU.mult),
             reads=[dmod, dgam], writes=[k.dgm])


def super_tiles():
    st = [(i * 256, 256, 0) for i in range(64)]
    st.append((T, 256, 1))
    return st


def stage_proj(k, l, xname, mod, dmod, tiles=None):
    p = k.p
    xsrc, dxs = k.scratch(xname, [8, 128, NT])
    QT, dQT = k.scratch(f"QT{l}", [8, 64, NT])
    KT, dKT = k.scratch(f"KT{l}", [8, 64, NT])
    V, dV = k.scratch(f"V{l}", [NT, 512])
    DQ, dDQ = k.scratch(f"DQ{l}", [4, 128, NT])
    DK, dDK = k.scratch(f"DK{l}", [4, 128, NT])
    DV, dDV = k.scratch(f"DV{l}", [4, 128, NT])
    Z, dZ = k.scratch(f"Z{l}", [NT, 512])
    GB, dGB = k.scratch(f"GB{l}", [NT, 16])
    U, dU = k.scratch(f"U{l}", [4, 128, NT])
    GATE, dGATE = k.scratch(f"GATE{l}", [24, 128, NT])
    gam, dgam = k.const(f"norm1_{l}", [128, 8], k.inp[f"norm1_{l}"])
    abt, dabt = k.const(f"abt{l}", [128, 16], k.inp[f"abt{l}"])
    nexp = p.sbuf(f"nexp{l}", [128, 8])
    dnexp = Dep("nexp")
    p.op("act", lambda e: e.activation(out=nexp[:], in_=abt[:, 0:8], func=AF.Exp), reads=[dabt], writes=[dnexp])
    p.op("dve", lambda e: e.tensor_scalar(out=nexp[:], in0=nexp[:], scalar1=-1.0, scalar2=None, op0=ALU.mult), reads=[dnexp], writes=[dnexp])
    make_gm(k, gam, dgam, mod, dmod, 1)
    wa, dwa = k.const(f"w_ina{l}", [128, 8, 1040], k.inp[f"w_ina{l}"], eng="act")
    xr = Ring(p, f"px{l}_", [128, 8, 256], 2)
    hr = Ring(p, f"ph{l}_", [128, 8, 256], 2)
    tr = Ring(p, f"pt{l}_", [128, 8, 256], 1)
    wr = Ring(p, f"pw{l}_", [128, 8, 128], 3)
    orr = Ring(p, f"po{l}_", [128, 512], 4)
    gr = Ring(p, f"pg{l}_", [128, 48], 2)
    wsrc = k.inp[f"w_inb{l}"]
    fb_dst = {"na_q": None, "na_k": None, "dn_q": (DQ, dDQ), "dn_k": (DK, dDK), "dn_v": (DV, dDV), "fn_u": (U, dU), "gate": (GATE, dGATE)}
    nbank = 0
    for (t0, N, col) in (tiles or super_tiles()):
        xt, dx, sem = xr.next()
        p.dma("sp", xt[:, :, 0:N], xsrc[:, :, t0:t0 + N].rearrange("k p t -> p k t"), reads=[dxs], writes=[dx], sem=sem)
        ht, dh, _ = hr.next()
        tmp, dtmp, _ = tr.next()
        norm_mod(k, xt, dx, N, gam, dgam, mod, dmod, 0, 1, col, ht, dh, tmp, dtmp, k.ones_t, k.dones, 7)
        for c, (g, gi) in enumerate(FB_CHUNKS):
            wt, dw, sem = wr.next()
            p.dma("sp" if c % 2 == 0 else "act", wt[:], wsrc[c], writes=[dw], sem=sem)
            b = nbank % 4
            nbank += 1
            ps = k.banks[b]
            for kk in range(8):
                p.op("pe", lambda e, wt=wt, kk=kk, ps=ps, ht=ht: e.matmul(ps[:, 0:N], lhsT=wt[:, kk, :], rhs=ht[:, kk, 0:N],
                                                                         start=(kk == 0), stop=(kk == 7)),
                     reads=[dw, dh], writes=[k.bdep[b]])
            ot, do, _ = orr.next()
            if g == "gate":
                p.op("act", lambda e, ot=ot, ps=ps: e.activation(out=ot[:, 0:N], in_=ps[:, 0:N], func=AF.Sigmoid),
                     reads=[k.bdep[b]], writes=[do])
            elif g == "na_q":
                p.op("act", lambda e, ot=ot, ps=ps: e.activation(out=ot[:, 0:N], in_=ps[:, 0:N], func=AF.Copy, scale=0.125),
                     reads=[k.bdep[b]], writes=[do])
            else:
                p.op("dve", lambda e, ot=ot, ps=ps: e.tensor_copy(out=ot[:, 0:N], in_=ps[:, 0:N]), reads=[k.bdep[b]], writes=[do])
            if g in ("na_q", "na_k"):
                dst, dd = (QT, dQT) if g == "na_q" else (KT, dKT)
                p.store("sp", dst[2 * gi:2 * gi + 2, :, t0:t0 + N].rearrange("h d t -> (h d) t"), ot[:, 0:N], [do], dd)
            else:
                dst, dd = fb_dst[g]
                p.store("sp", dst[gi, :, t0:t0 + N], ot[:, 0:N], [do], dd)
        for tb in range(N // 128):
            tok = t0 + tb * 128
            for gi, (c0, cn, dst, dd) in enumerate(((0, 512, V, dV), (512, 512, Z, dZ), (1024, 16, None, None))):
                b = nbank % 4
                nbank += 1
                ps = k.banks[b]
                for kk in range(8):
                    p.op("pe", lambda e, kk=kk, ps=ps, ht=ht, tb=tb, c0=c0, cn=cn: e.matmul(
                        ps[:, 0:cn], lhsT=ht[:, kk, tb * 128:(tb + 1) * 128], rhs=wa[:, kk, c0:c0 + cn], start=(kk == 0), stop=(kk == 7)),
                        reads=[dwa, dh], writes=[k.bdep[b]])
                if dst is not None:
                    ot, do, _ = orr.next()
                    p.op("act" if gi == 0 else "dve", (lambda e, ot=ot, ps=ps: e.activation(out=ot[:], in_=ps[:], func=AF.Copy)) if gi == 0 else
                         (lambda e, ot=ot, ps=ps: e.tensor_copy(out=ot[:], in_=ps[:])), reads=[k.bdep[b]], writes=[do])
                    p.store("sp", dst[tok:tok + 128, :], ot[:], [do], dd)
                else:
                    gt, dg, _ = gr.next()
                    p.op("dve", lambda e, gt=gt, ps=ps: e.tensor_tensor(out=gt[:, 0:8], in0=ps[:, 0:8], in1=abt[:, 8:16], op=ALU.add),
                         reads=[k.bdep[b], dabt], writes=[dg])
                    p.op("act", lambda e, gt=gt: e.activation(out=gt[:, 0:8], in_=gt[:, 0:8], func=AF.Exp), reads=[dg], writes=[dg])
                    p.op("act", lambda e, gt=gt: e.activation(out=gt[:, 0:8], in_=gt[:, 0:8], func=AF.Ln, bias=1.0), reads=[dg], writes=[dg])
                    p.op("dve", lambda e, gt=gt: e.tensor_tensor(out=gt[:, 0:8], in0=gt[:, 0:8], in1=nexp[:], op=ALU.mult),
                         reads=[dg, dnexp], writes=[dg])
                    p.op("act", lambda e, gt=gt, ps=ps: e.activation(out=gt[:, 8:16], in_=ps[:, 8:16], func=AF.Sigmoid),
                         reads=[k.bdep[b]], writes=[dg])
                    p.store("sp", GB[tok:tok + 128, :], gt[:, 0:16], [dg], dGB)


def common(k):
    p = k.p
    k.ident, k.dident = k.const("ident", [128, 128], k.inp["ident"])
    k.ones_t = p.sbuf("ones_t", [128, 128])
    k.dones = Dep("ones")
    p.op("pool", lambda e: e.memset(k.ones_t[:], 1.0 / D), writes=[k.dones])
    k.one1 = p.sbuf("one1", [128, 128])
    k.done1 = Dep("one1")
    p.op("pool", lambda e: e.memset(k.one1[:], 1.0), writes=[k.done1])
    k.rstd = p.sbuf("rstd", [128, 512])
    k.drstd = Dep("rstd")
    k.gm = p.sbuf("gm", [128, 8, 2])
    k.dgm = Dep("gm")
    k.modt = [(p.sbuf(f"mod{l}", [128, 48, 2]), Dep(f"mod{l}")) for l in range(2)]


def rope_tables():
    half = 64
    inv = (1.0 / (10000.0 ** (np.arange(0, half, 2, dtype=np.float32) / np.float32(half)))).astype(np.float32)
    t = np.arange(T)
    pos = np.stack([t // GW, t % GW], axis=-1).astype(np.float32)
    ang = pos[:, :, None] * inv[None, None, :]
    ang = np.concatenate([ang, ang], axis=-1).reshape(T, 128)
    cos = np.cos(ang).astype(np.float32).T
    sin = np.sin(ang).astype(np.float32).T
    rm = np.zeros((128, 128), np.float32)
    for o in (0, 64):
        for i in range(64):
            if i < 32:
                rm[o + i + 32, o + i] = -1.0
            else:
                rm[o + i - 32, o + i] = 1.0
    return np.ascontiguousarray(cos), np.ascontiguousarray(sin), rm


def dn_masks():
    i = np.arange(128)
    out = {}
    for d in range(2):
        le = (i[:, None] <= i[None, :]) if d == 0 else (i[:, None] >= i[None, :])
        gt = (i[:, None] > i[None, :]) if d == 0 else (i[:, None] < i[None, :])
        blk = lambda b: (i[:, None] // b) == (i[None, :] // b)
        ms = [le, gt, gt, ~le | (i[:, None] == i[None, :]), le, blk(16)]
        for b in (16, 32, 64):
            up = blk(2 * b) & ((i[:, None] // b) < (i[None, :] // b))
            mN = up if d == 0 else up.T
            ms += [mN, mN.T]
        out[d] = np.ascontiguousarray(np.stack(ms, axis=1).astype(np.float32))
    return out


def stage_dn_pre(k, l, tiles=None):
    p = k.p
    src = [k.scratch(f"D{x}{l}", [4, 128, NT]) for x in "QKV"]
    dst = [k.scratch(f"D{x}2_{l}", [4, 128, NT]) for x in "QKV"]
    cw, dcw = k.const(f"convw{l}", [128, 12, 5], k.inp[f"convw{l}"])
    k.rm, k.drm = k.const("rope_rm", [128, 128], k.inp["rope_rm"])
    ur = Ring(p, f"du{l}_", [128, 260], 3)
    ar = Ring(p, f"da{l}_", [128, 256], 3)
    yr = Ring(p, f"dy{l}_", [128, 256], 3)
    sr = Ring(p, f"ds{l}_", [128, 256], 2)
    rr = Ring(p, f"dr{l}_", [128, 256], 2)
    cr = Ring(p, f"dc{l}_", [128, 2, 256], 2)
    outr = Ring(p, f"do{l}_", [128, 256], 3)
    cosd, sind = k.inp["rope_cos"], k.inp["rope_sin"]
    nb = 0
    for (t0, N, col) in (tiles or super_tiles()):
        seg0, seg1 = (0, T) if col == 0 else (T, T + CT)
        ct, dct = None, None
        if col == 0:
            ct, dct, sem = cr.next()
            p.dma("sp", ct[:, 0, :], cosd[:, t0:t0 + N], writes=[dct], sem=sem)
            p.dma("sp", ct[:, 1, :], sind[:, t0:t0 + N], writes=[dct], sem=sem)
        for h in range(4):
            for xi in range(3):
                ut, du, sem = ur.next()
                lo, hi = max(t0 - 2, seg0), min(t0 + N + 2, seg1)
                if lo != t0 - 2 or hi != t0 + N + 2:
                    p.op("pool", lambda e, ut=ut: e.memset(ut[:], 0.0), writes=[du])
                p.dma("sp" if xi != 1 else "act", ut[:, lo - (t0 - 2):hi - (t0 - 2)], src[xi][0][h, :, lo:hi], reads=[src[xi][1]], writes=[du], sem=sem)
                at, da, _ = ar.next()
                eng = "dve"
                j = xi * 4 + h
                p.op(eng, lambda e, at=at, ut=ut, j=j: e.tensor_scalar(out=at[:, 0:N], in0=ut[:, 0:N], scalar1=cw[:, j, 0:1], scalar2=None, op0=ALU.mult),
                     reads=[du, dcw], writes=[da])
                for tap in range(1, 5):
                    p.op(eng, lambda e, at=at, ut=ut, j=j, tap=tap: e.scalar_tensor_tensor(out=at[:, 0:N], in0=ut[:, tap:tap + N], scalar=cw[:, j, tap:tap + 1],
                                                                                         in1=at[:, 0:N], op0=ALU.mult, op1=ALU.add),
                         reads=[du, dcw, da], writes=[da])
                yt, dy, _ = yr.next()
                p.op("act", lambda e, yt=yt, at=at: e.activation(out=yt[:, 0:N], in_=at[:, 0:N], func=AF.Silu), reads=[da], writes=[dy])
                if xi == 2:
                    p.store("sp", dst[2][0][h, :, t0:t0 + N], yt[:, 0:N], [dy], dst[2][1])
                    continue
                st, dsq, _ = sr.next()
                p.op("pool", lambda e, st=st, yt=yt: e.tensor_tensor(out=st[:, 0:N], in0=yt[:, 0:N], in1=yt[:, 0:N], op=ALU.mult), reads=[dy], writes=[dsq])
                b = 4 + (nb % 2)
                nb += 1
                ps = k.banks[b]
                p.op("pe", lambda e, ps=ps, st=st: e.matmul(ps[:, 0:N], lhsT=k.one1[:], rhs=st[:, 0:N], start=True, stop=True),
                     reads=[dsq, k.done1], writes=[k.bdep[b]])
                p.op("dve", lambda e, ps=ps, st=st: e.tensor_scalar(out=st[:, 0:N], in0=ps[:, 0:N], scalar1=EPS, scalar2=None, op0=ALU.add),
                     reads=[k.bdep[b]], writes=[dsq])
                p.op("act", lambda e, st=st: e.activation(out=st[:, 0:N], in_=st[:, 0:N], func=AF.Sqrt), reads=[dsq], writes=[dsq])
                p.op("dve", lambda e, st=st: e.reciprocal(out=st[:, 0:N], in_=st[:, 0:N]), reads=[dsq], writes=[dsq])
                ot, do, _ = outr.next()
                if col == 1:
                    p.op("dve", lambda e, ot=ot, yt=yt, st=st: e.tensor_tensor(out=ot[:, 0:N], in0=yt[:, 0:N], in1=st[:, 0:N], op=ALU.mult),
                         reads=[dy, dsq], writes=[do])
                else:
                    p.op("dve", lambda e, yt=yt, st=st: e.tensor_tensor(out=yt[:, 0:N], in0=yt[:, 0:N], in1=st[:, 0:N], op=ALU.mult),
                         reads=[dy, dsq], writes=[dy])
                    b2 = 6 + (nb % 2)
                    ps2 = k.banks[b2]
                    p.op("pe", lambda e, ps2=ps2, yt=yt: e.matmul(ps2[:, 0:N], lhsT=k.rm[:], rhs=yt[:, 0:N], start=True, stop=True),
                         reads=[dy, k.drm], writes=[k.bdep[b2]])
                    rt, dr, _ = rr.next()
                    p.op("dve", lambda e, rt=rt, ps2=ps2, ct=ct: e.tensor_tensor(out=rt[:, 0:N], in0=ps2[:, 0:N], in1=ct[:, 1, 0:N], op=ALU.mult),
                         reads=[k.bdep[b2], dct], writes=[dr])
                    p.op("pool", lambda e, ot=ot, yt=yt, ct=ct: e.tensor_tensor(out=ot[:, 0:N], in0=yt[:, 0:N], in1=ct[:, 0, 0:N], op=ALU.mult),
                         reads=[dy, dct], writes=[do])
                    p.op("pool", lambda e, ot=ot, rt=rt: e.tensor_tensor(out=ot[:, 0:N], in0=ot[:, 0:N], in1=rt[:, 0:N], op=ALU.add),
                         reads=[dr, do], writes=[do])
                p.store("sp", dst[xi][0][h, :, t0:t0 + N], ot[:, 0:N], [do], dst[xi][1])


class PS4:
    def __init__(self, k, si):
        bs = [2 * si, 2 * si + 1]
        self.t = [k.banks[b][:, 0:128] for b in bs]
        self.d = [k.bdep[b] for b in bs]
        self.i = 0

    def next(self):
        i = self.i
        self.i = (i + 1) % 2
        return self.t[i], self.d[i]


def dn_scan_gen(k, l, si, h, d, blocks, want_o):
    p = k.p
    SC = 128 ** -0.5
    QT, dQ = k.scratch(f"DQ2_{l}", [4, 128, NT])
    KT, dK = k.scratch(f"DK2_{l}", [4, 128, NT])
    VT, dV = k.scratch(f"DV2_{l}", [4, 128, NT])
    GB, dGB = k.scratch(f"GB{l}", [NT, 16])
    OD, dOD = k.scratch(f"OD{l}", [2, 4, NT, 128])
    mk, dmk = k.dnm[d]
    U, SL, MLs, MLD, MLDT = (mk[:, i, :] for i in range(5))
    BD16 = mk[:, 5, :]
    MRG = [(mk[:, 6 + 2 * j, :], mk[:, 7 + 2 * j, :]) for j in range(3)]
    B = k.dnbuf[si]
    ps = PS4(k, si)
    S = [B["S0"], B["S1"]]
    dS = [B["d_S0"], B["d_S1"]]
    p.op("pool", lambda e: e.memset(S[0][:], 0.0), writes=[dS[0]])
    cur = 0
    ident = k.ident

    def T_(name):
        return B[name], B["d_" + name]
    for bi, tok in enumerate(blocks):
        par = bi % 2
        qT, dq = T_(f"qT{par}"); kT, dk_ = T_(f"kT{par}"); vT, dv = T_(f"vT{par}"); gb, dgb = T_(f"gb{par}")
        lsem = f"ld_blk{par}_{si}"
        p.dma("sp", qT[:], QT[h, :, tok:tok + 128], reads=[dQ], writes=[dq], sem=lsem)
        p.dma("act", kT[:], KT[h, :, tok:tok + 128], reads=[dK], writes=[dk_], sem=lsem)
        p.dma("sp", vT[:], VT[h, :, tok:tok + 128], reads=[dV], writes=[dv], sem=lsem)
        p.dma("act", gb[:], GB[tok:tok + 128, :], reads=[dGB], writes=[dgb], sem=lsem)
        for dd_ in (dq, dk_, dv, dgb):
            dd_.w = dgb.w
        g = gb[:, d * 4 + h:d * 4 + h + 1]
        beta = gb[:, 8 + d * 4 + h:8 + d * 4 + h + 1]
        sc, dsc = T_("sc")
        ktok, dkt = T_("ktok"); vb, dvb = T_("vb"); SLg, dSLg = T_("SLg"); Dm, dDm = T_("Dm")
        Dstr, dDstr = T_("Dstr"); Dlow, dDlow = T_("Dlow"); DlowT, dDlowT = T_("DlowT")
        Na, dNa = T_("Na"); Nb, dNb = T_("Nb"); NTa, dNTa = T_("NTa"); NTb, dNTb = T_("NTb")
        Xa, dXa = T_("Xa"); Xb, dXb = T_("Xb")
        kbg, dkbg = T_("kbg"); ktail, dktail = T_("ktail"); u_sb, du = T_("u"); wT, dwT = T_("wT"); qkT, dqkT = T_("qkT")
        vnew, dvn = T_("vnew"); ob, dob = T_("ob"); o_sb, do = T_(f"o{par}")
        p1, dp1 = ps.next()
        p.op("pe", lambda e, p1=p1, kT=kT: e.transpose(out=p1, in_=kT[:], identity=ident[:]), reads=[dk_, k.dident], writes=[dp1])
        p.op("act", lambda e, p1=p1, ktok=ktok: e.activation(out=ktok[:], in_=p1, func=AF.Copy), reads=[dp1], writes=[dkt])
        p2, dp2 = ps.next()
        p.op("pe", lambda e, p2=p2, vT=vT: e.transpose(out=p2, in_=vT[:], identity=ident[:]), reads=[dv, k.dident], writes=[dp2])
        p.op("dve", lambda e, p2=p2, vb=vb, beta=beta: e.tensor_scalar(out=vb[:], in0=p2, scalar1=beta, scalar2=None, op0=ALU.mult),
             reads=[dp2, dgb], writes=[dvb])
        p.op("pool", lambda e, SLg=SLg, g=g: e.tensor_scalar(out=SLg[:], in0=SL, scalar1=g, scalar2=None, op0=ALU.mult),
             reads=[dmk, dgb], writes=[dSLg])
        p.op("pool", lambda e, sc=sc, beta=beta: e.tensor_scalar(out=sc[:, 0:1], in0=beta, scalar1=-1.0, scalar2=None, op0=ALU.mult),
             reads=[dgb], writes=[dsc])
        yield
        p3, dp3 = ps.next()
        p.op("pe", lambda e, p3=p3, SLg=SLg: e.matmul(p3, lhsT=U, rhs=SLg[:], start=True, stop=True), reads=[dSLg, dmk], writes=[dp3])
        p4, dp4 = ps.next()
        p.op("pe", lambda e, p4=p4, g=g: e.matmul(p4[:, 0:1], lhsT=U, rhs=g, start=True, stop=True), reads=[dgb, dmk], writes=[dp4])
        p.op("pe", lambda e, p4=p4, g=g: e.matmul(p4[:, 1:2], lhsT=k.one1[:], rhs=g, start=True, stop=True), reads=[dgb, k.done1], writes=[dp4])
        p.op("act", lambda e, p3=p3, Dm=Dm: e.activation(out=Dm[:], in_=p3, func=AF.Exp), reads=[dp3], writes=[dDm])
        p.op("dve", lambda e, p4=p4, sc=sc: e.tensor_copy(out=sc[:, 1:3], in_=p4[:, 0:2]), reads=[dp4], writes=[dsc])
        yield
        p.op("pool", lambda e, Dstr=Dstr, Dm=Dm: e.tensor_tensor(out=Dstr[:], in0=Dm[:], in1=MLs, op=ALU.mult), reads=[dDm, dmk], writes=[dDstr])
        p.op("pool", lambda e, Dlow=Dlow, Dm=Dm: e.tensor_tensor(out=Dlow[:], in0=Dm[:], in1=MLD, op=ALU.mult), reads=[dDm, dmk], writes=[dDlow])
        yield
        p.op("act", lambda e, sc=sc: e.activation(out=sc[:, 3:4], in_=sc[:, 1:2], func=AF.Exp), reads=[dsc], writes=[dsc])
        p.op("dve", lambda e, sc=sc: e.tensor_tensor(out=sc[:, 4:5], in0=sc[:, 2:3], in1=sc[:, 1:2], op=ALU.subtract), reads=[dsc], writes=[dsc])
        p.op("act", lambda e, sc=sc: e.activation(out=sc[:, 4:5], in_=sc[:, 4:5], func=AF.Exp), reads=[dsc], writes=[dsc])
        p.op("act", lambda e, sc=sc: e.activation(out=sc[:, 5:6], in_=sc[:, 2:3], func=AF.Exp), reads=[dsc], writes=[dsc])
        yield
        p.op("dve", lambda e, sc=sc, beta=beta: e.tensor_tensor(out=sc[:, 6:7], in0=sc[:, 3:4], in1=beta, op=ALU.mult), reads=[dsc, dgb], writes=[dsc])
        p.op("dve", lambda e, sc=sc: e.tensor_scalar(out=sc[:, 7:8], in0=sc[:, 3:4], scalar1=SC, scalar2=None, op0=ALU.mult), reads=[dsc], writes=[dsc])
        yield
        p5, dp5 = ps.next()
        kT2, dkT2 = T_("kT2")
        p.op("pool", lambda e, kT2=kT2, kT=kT: e.tensor_copy(out=kT2[:], in_=kT[:]), reads=[dk_], writes=[dkT2])
        p.op("pe", lambda e, p5=p5, kT=kT, kT2=kT2: e.matmul(p5, lhsT=kT[:], rhs=kT2[:], start=True, stop=True), reads=[dk_, dkT2], writes=[dp5])
        yield
        p.op("dve", lambda e, p5=p5, NTa=NTa, sc=sc: e.tensor_scalar(out=NTa[:], in0=p5, scalar1=sc[:, 0:1], scalar2=None, op0=ALU.mult),
             reads=[dp5, dsc], writes=[dNTa])
        p.op("pool", lambda e, NTa=NTa, Dstr=Dstr: e.tensor_tensor(out=NTa[:], in0=NTa[:], in1=Dstr[:], op=ALU.mult),
             reads=[dDstr, dNTa], writes=[dNTa])
        yield
        N0, dN0 = T_("N0"); NT0 = NTa; dNT0 = dNTa
        p6, dp6 = ps.next()
        p.op("pe", lambda e, p6=p6: e.transpose(out=p6, in_=NT0[:], identity=ident[:]), reads=[dNT0, k.dident], writes=[dp6])
        p.op("act", lambda e, p6=p6: e.activation(out=N0[:], in_=p6, func=AF.Copy), reads=[dp6], writes=[dN0])
        yield
        Xc, dXc = T_("Xa"); XTc, dXTc = T_("XTa"); Xn_, dXn_ = T_("Xb"); XTn_, dXTn_ = T_("XTb")
        Np, dNp = T_("Na"); NTp, dNTp = T_("NTb"); Nn, dNn = T_("Nb"); NTn, dNTn = T_("NTc")
        p.op("pool", lambda e, Np=Np, N0=N0: e.tensor_tensor(out=Np[:], in0=N0[:], in1=BD16, op=ALU.mult), reads=[dN0, dmk], writes=[dNp])
        p.op("pool", lambda e, NTp=NTp, NT0=NT0: e.tensor_tensor(out=NTp[:], in0=NT0[:], in1=BD16, op=ALU.mult), reads=[dNT0, dmk], writes=[dNTp])
        p.op("dve", lambda e, Xc=Xc, Np=Np: e.tensor_tensor(out=Xc[:], in0=Np[:], in1=ident[:], op=ALU.add), reads=[dNp, k.dident], writes=[dXc])
        p.op("dve", lambda e, XTc=XTc, NTp=NTp: e.tensor_tensor(out=XTc[:], in0=NTp[:], in1=ident[:], op=ALU.add), reads=[dNTp, k.dident], writes=[dXTc])
        yield
        for lvl in range(3):
            pa, dpa = ps.next()
            p.op("pe", lambda e, pa=pa, Np=Np, NTp=NTp: e.matmul(pa, lhsT=NTp[:], rhs=Np[:], start=True, stop=True), reads=[dNp, dNTp], writes=[dpa])
            pb, dpb = ps.next()
            p.op("pe", lambda e, pb=pb, Np=Np, NTp=NTp: e.matmul(pb, lhsT=Np[:], rhs=NTp[:], start=True, stop=True), reads=[dNp, dNTp], writes=[dpb])
            p.op("act", lambda e, pa=pa, Nn=Nn: e.activation(out=Nn[:], in_=pa, func=AF.Copy), reads=[dpa], writes=[dNn])
            p.op("dve", lambda e, pb=pb, NTn=NTn: e.tensor_copy(out=NTn[:], in_=pb), reads=[dpb], writes=[dNTn])
            yield
            pc, dpc = ps.next()
            p.op("pe", lambda e, pc=pc, NTn=NTn, Xc=Xc: e.matmul(pc, lhsT=NTn[:], rhs=Xc[:], start=True, stop=True), reads=[dNTn, dXc], writes=[dpc])
            pd, dpd = ps.next()
            p.op("pe", lambda e, pd=pd, Nn=Nn, XTc=XTc: e.matmul(pd, lhsT=Nn[:], rhs=XTc[:], start=True, stop=True), reads=[dNn, dXTc], writes=[dpd])
            p.op("dve", lambda e, pc=pc, Xc=Xc, Xn_=Xn_: e.tensor_tensor(out=Xn_[:], in0=pc, in1=Xc[:], op=ALU.add), reads=[dpc, dXc], writes=[dXn_])
            p.op("dve", lambda e, pd=pd, XTc=XTc, XTn_=XTn_: e.tensor_tensor(out=XTn_[:], in0=pd, in1=XTc[:], op=ALU.add), reads=[dpd, dXTc], writes=[dXTn_])
            (Xc, dXc, Xn_, dXn_) = (Xn_, dXn_, Xc, dXc)
            (XTc, dXTc, XTn_, dXTn_) = (XTn_, dXTn_, XTc, dXTc)
            (Np, dNp, Nn, dNn) = (Nn, dNn, Np, dNp)
            (NTp, dNTp, NTn, dNTn) = (NTn, dNTn, NTp, dNTp)
            yield
        Nm, dNm = T_("Nm"); NTm, dNTm = T_("NTm"); Y, dY = T_("Y"); Y2, dY2 = T_("Y2")
        for j in range(3):
            Mj, MjT = MRG[j]
            last = (j == 2)
            p.op("pool", lambda e, Mj=Mj: e.tensor_tensor(out=NTm[:], in0=NT0[:], in1=MjT, op=ALU.mult) if False else e.tensor_tensor(out=NTm[:], in0=NT0[:], in1=MjT, op=ALU.mult),
                 reads=[dNT0, dmk], writes=[dNTm]) if False else None
            p.op("pool", lambda e, MjT=MjT: e.tensor_tensor(out=NTm[:], in0=NT0[:], in1=MjT, op=ALU.mult), reads=[dNT0, dmk], writes=[dNTm])
            if not last:
                p.op("pool", lambda e, Mj=Mj: e.tensor_tensor(out=Nm[:], in0=N0[:], in1=Mj, op=ALU.mult), reads=[dN0, dmk], writes=[dNm])
            py, dpy = ps.next()
            p.op("pe", lambda e, py=py, Xc=Xc: e.matmul(py, lhsT=NTm[:], rhs=Xc[:], start=True, stop=True), reads=[dNTm, dXc], writes=[dpy])
            p.op("act", lambda e, py=py: e.activation(out=Y[:], in_=py, func=AF.Copy), reads=[dpy], writes=[dY])
            if not last:
                py2, dpy2 = ps.next()
                p.op("pe", lambda e, py2=py2, XTc=XTc: e.matmul(py2, lhsT=Nm[:], rhs=XTc[:], start=True, stop=True), reads=[dNm, dXTc], writes=[dpy2])
                p.op("dve", lambda e, py2=py2: e.tensor_copy(out=Y2[:], in_=py2), reads=[dpy2], writes=[dY2])
            yield
            pz, dpz = ps.next()
            p.op("pe", lambda e, pz=pz, XTc=XTc: e.matmul(pz, lhsT=XTc[:], rhs=Y[:], start=True, stop=True), reads=[dXTc, dY], writes=[dpz])
            if not last:
                pz2, dpz2 = ps.next()
                p.op("pe", lambda e, pz2=pz2, Xc=Xc: e.matmul(pz2, lhsT=Xc[:], rhs=Y2[:], start=True, stop=True), reads=[dXc, dY2], writes=[dpz2])
            p.op("dve", lambda e, pz=pz, Xc=Xc, Xn_=Xn_: e.tensor_tensor(out=Xn_[:], in0=pz, in1=Xc[:], op=ALU.add), reads=[dpz, dXc], writes=[dXn_])
            if not last:
                p.op("dve", lambda e, pz2=pz2, XTc=XTc, XTn_=XTn_: e.tensor_tensor(out=XTn_[:], in0=pz2, in1=XTc[:], op=ALU.add), reads=[dpz2, dXTc], writes=[dXTn_])
                (XTc, dXTc, XTn_, dXTn_) = (XTn_, dXTn_, XTc, dXTc)
            (Xc, dXc, Xn_, dXn_) = (Xn_, dXn_, Xc, dXc)
            yield
        Xs = [(Xc, dXc)]
        X, dX = Xs[0]
        p.op("pool", lambda e, kbg=kbg, ktok=ktok, sc=sc: e.tensor_scalar(out=kbg[:], in0=ktok[:], scalar1=sc[:, 6:7], scalar2=None, op0=ALU.mult),
             reads=[dkt, dsc], writes=[dkbg])
        p.op("pool", lambda e, ktail=ktail, ktok=ktok, sc=sc: e.tensor_scalar(out=ktail[:], in0=ktok[:], scalar1=sc[:, 4:5], scalar2=None, op0=ALU.mult),
             reads=[dkt, dsc], writes=[dktail])
        pu, dpu = ps.next()
        p.op("pe", lambda e, pu=pu, X=X, vb=vb: e.matmul(pu, lhsT=X[:], rhs=vb[:], start=True, stop=True), reads=[dX, dvb], writes=[dpu])
        p.op("act", lambda e, pu=pu, u_sb=u_sb: e.activation(out=u_sb[:], in_=pu, func=AF.Copy), reads=[dpu], writes=[du])
        pw, dpw = ps.next()
        p.op("pe", lambda e, pw=pw, X=X, kbg=kbg: e.matmul(pw, lhsT=kbg[:], rhs=X[:], start=True, stop=True), reads=[dX, dkbg], writes=[dpw])
        p.op("dve", lambda e, pw=pw, wT=wT: e.tensor_copy(out=wT[:], in_=pw), reads=[dpw], writes=[dwT])
        yield
        if want_o:
            pt, dpt = ps.next()
            p.op("pe", lambda e, pt=pt, Dlow=Dlow: e.transpose(out=pt, in_=Dlow[:], identity=ident[:]), reads=[dDlow, k.dident], writes=[dpt])
            p.op("act", lambda e, pt=pt, DlowT=DlowT: e.activation(out=DlowT[:], in_=pt, func=AF.Copy), reads=[dpt], writes=[dDlowT])
            pq, dpq = ps.next()
            p.op("pe", lambda e, pq=pq, kT=kT, qT=qT: e.matmul(pq, lhsT=kT[:], rhs=qT[:], start=True, stop=True), reads=[dk_, dq], writes=[dpq])
            p.op("dve", lambda e, pq=pq, qkT=qkT, DlowT=DlowT: e.scalar_tensor_tensor(out=qkT[:], in0=pq, scalar=SC, in1=DlowT[:], op0=ALU.mult, op1=ALU.mult),
                 reads=[dpq, dDlowT], writes=[dqkT])
            yield
        Sc, dSc = S[cur], dS[cur]
        Sn, dSn = S[1 - cur], dS[1 - cur]
        pv, dpv = ps.next()
        p.op("pe", lambda e, pv=pv, wT=wT, Sc=Sc: e.matmul(pv, lhsT=wT[:], rhs=Sc[:], start=True, stop=True), reads=[dwT, dSc], writes=[dpv])
        po, dpo = ps.next()
        if want_o:
            p.op("pe", lambda e, po=po, qT=qT, Sc=Sc: e.matmul(po, lhsT=qT[:], rhs=Sc[:], start=True, stop=True), reads=[dq, dSc], writes=[dpo])
        p.op("dve", lambda e, pv=pv, vnew=vnew, u_sb=u_sb: e.tensor_tensor(out=vnew[:], in0=u_sb[:], in1=pv, op=ALU.subtract), reads=[dpv, du], writes=[dvn])
        if want_o:
            p.op("dve", lambda e, po=po, o_sb=o_sb, sc=sc: e.tensor_scalar(out=o_sb[:], in0=po, scalar1=sc[:, 7:8], scalar2=None, op0=ALU.mult),
                 reads=[dpo, dsc], writes=[do])
        yield
        pn, dpn = ps.next()
        p.op("pe", lambda e, pn=pn, ktail=ktail, vnew=vnew: e.matmul(pn, lhsT=ktail[:], rhs=vnew[:], start=True, stop=True), reads=[dktail, dvn], writes=[dpn])
        pb2, dpb2 = ps.next()
        if want_o:
            p.op("pe", lambda e, pb2=pb2, qkT=qkT, vnew=vnew: e.matmul(pb2, lhsT=qkT[:], rhs=vnew[:], start=True, stop=True), reads=[dqkT, dvn], writes=[dpb2])
        p.op("dve", lambda e, pn=pn, Sn=Sn, Sc=Sc, sc=sc: e.scalar_tensor_tensor(out=Sn[:], in0=Sc[:], scalar=sc[:, 5:6], in1=pn, op0=ALU.mult, op1=ALU.add),
             reads=[dpn, dSc, dsc], writes=[dSn])
        cur = 1 - cur
        if want_o:
            p.op("dve", lambda e, pb2=pb2, o_sb=o_sb: e.tensor_tensor(out=o_sb[:], in0=o_sb[:], in1=pb2, op=ALU.add), reads=[dpb2, do], writes=[do])
            p.store("sp", OD[d, h, tok:tok + 128, :], o_sb[:], [do], dOD)
        yield


def dn_blocks(d, nlat=128, with_ctx=True):
    cb = [T, T + 128]
    lb = [i * 128 for i in range(nlat)]
    if d == 1:
        cb = cb[::-1]
        lb = [i * 128 for i in range(128)][::-1][:nlat]
    return (cb if with_ctx else []) + lb


def stage_dn_scan(k, l, nlat=128, scans=None):
    p = k.p
    if True:
        k.dnm = [k.const(f"dnmask{d}", [128, 12, 128], k.inp[f"dnmask{d}"]) for d in range(2)]
        names = ["ktok", "vb", "SLg", "Dm", "Dstr", "Dlow", "DlowT", "Na", "Nb", "NTa", "NTb", "Xa", "Xb", "kbg", "ktail", "u", "wT", "qkT",
                 "vnew", "ob", "o0", "o1", "kT2", "N0", "XTa", "XTb", "NTc", "Nm", "NTm", "Y", "Y2", "qT0", "qT1", "kT0", "kT1", "vT0", "vT1", "S0", "S1"]
        k.dnbuf = []
        for si in range(4):
            B = {}
            for n in names:
                B[n] = p.sbuf(f"dn{si}_{n}", [128, 128])
                B["d_" + n] = Dep(f"dn{si}_{n}")
            for n in ("gb0", "gb1"):
                B[n] = p.sbuf(f"dn{si}_{n}", [128, 16]); B["d_" + n] = Dep(f"dn{si}_{n}")
            B["sc"] = p.sbuf(f"dn{si}_sc", [128, 8]); B["d_sc"] = Dep(f"dn{si}_sc")
            k.dnbuf.append(B)
    import os
    maxsteps = int(os.environ.get("DN_STEPS", "1000000000"))
    nstep = 0
    allsc = list(scans or [(h, d) for h in range(4) for d in range(2)])
    GRP = 4
    for g0 in range(0, len(allsc), GRP):
        gens = []
        for si, (h, d) in enumerate(allsc[g0:g0 + GRP]):
            gens.append(dn_scan_gen(k, l, si, h, d, dn_blocks(d, nlat), True))
        while gens and nstep < maxsteps:
            for g in list(gens):
                nstep += 1
                try:
                    next(g)
                except StopIteration:
                    gens.remove(g)


def stage_dn_post(k, l, toks=None):
    p = k.p
    OD, dOD = k.scratch(f"OD{l}", [2, 4, NT, 128])
    Z, dZ = k.scratch(f"Z{l}", [NT, 512])
    ODT, dODT = k.scratch(f"ODT{l}", [4, 128, NT])
    dnw, ddnw = k.const("dnw", [128, 128], k.inp[f"dnw{l}"])
    fr = Ring(p, "qf", [128, 4, 128], 2); br = Ring(p, "qb", [128, 4, 128], 2); zr = Ring(p, "qz", [128, 512], 2)
    sr = Ring(p, "qs", [128, 4, 128], 2); yr = Ring(p, "qy", [128, 4, 128], 2); tr_ = Ring(p, "qt", [128, 4, 128], 2)
    cr = Ring(p, "qc", [128, 8], 2)
    nb = 0
    for tok in (toks if toks is not None else range(0, NT, 128)):
        ft, df, sem = fr.next()
        p.dma("sp", ft[:], OD[0, :, tok:tok + 128, :].rearrange("h t d -> t h d"), reads=[dOD], writes=[df], sem=sem)
        bt, db, sem = br.next()
        p.dma("act", bt[:], OD[1, :, tok:tok + 128, :].rearrange("h t d -> t h d"), reads=[dOD], writes=[db], sem=sem)
        zt, dz, sem = zr.next()
        p.dma("sp", zt[:], Z[tok:tok + 128, :], reads=[dZ], writes=[dz], sem=sem)
        st, ds, _ = sr.next(); yt, dy, _ = yr.next(); ct, dc, _ = cr.next()
        p.op("dve", lambda e, st=st, ft=ft, bt=bt: e.tensor_tensor(out=st[:], in0=ft[:], in1=bt[:], op=ALU.add), reads=[df, db], writes=[ds])
        p.op("pool", lambda e, st=st, yt=yt: e.tensor_tensor(out=yt[:], in0=st[:], in1=st[:], op=ALU.mult), reads=[ds], writes=[dy])
        p.op("dve", lambda e, ct=ct, yt=yt: e.reduce_sum(out=ct[:, 0:4], in_=yt[:], axis=AX.X), reads=[dy], writes=[dc])
        p.op("dve", lambda e, ct=ct: e.tensor_scalar(out=ct[:, 0:4], in0=ct[:, 0:4], scalar1=1.0 / 128, scalar2=EPS, op0=ALU.mult, op1=ALU.add), reads=[dc], writes=[dc])
        p.op("act", lambda e, ct=ct: e.activation(out=ct[:, 0:4], in_=ct[:, 0:4], func=AF.Sqrt), reads=[dc], writes=[dc])
        p.op("dve", lambda e, ct=ct: e.reciprocal(out=ct[:, 0:4], in_=ct[:, 0:4]), reads=[dc], writes=[dc])
        p.op("act", lambda e, zt=zt: e.activation(out=zt[:], in_=zt[:], func=AF.Silu), reads=[dz], writes=[dz])
        for h in range(4):
            p.op("dve", lambda e, yt=yt, st=st, ct=ct, h=h: e.scalar_tensor_tensor(out=yt[:, h, :], in0=st[:, h, :], scalar=ct[:, h:h + 1], in1=dnw[:],
                                                                               op0=ALU.mult, op1=ALU.mult), reads=[ds, dc, ddnw, dy], writes=[dy])
        p.op("pool", lambda e, yt=yt, zt=zt: e.tensor_tensor(out=yt[:].rearrange("p h d -> p (h d)"), in0=yt[:].rearrange("p h d -> p (h d)"), in1=zt[:], op=ALU.mult),
             reads=[dz, dy], writes=[dy])
        b = nb % 2
        nb += 1
        ps = k.banks[b]
        for h in range(4):
            p.op("pe", lambda e, ps=ps, yt=yt, h=h: e.transpose(out=ps[:, h * 128:(h + 1) * 128], in_=yt[:, h, :], identity=k.ident[:]),
                 reads=[dy, k.dident], writes=[k.bdep[b]])
        tt, dt, _ = tr_.next()
        p.op("act", lambda e, tt=tt, ps=ps: e.activation(out=tt[:].rearrange("p h d -> p (h d)"), in_=ps[:, 0:512], func=AF.Copy), reads=[k.bdep[b]], writes=[dt])
        p.store("sp", ODT[:, :, tok:tok + 128].rearrange("h p t -> p h t"), tt[:], [dt], dODT)


def stage_mlp(k, l, xname, mod, dmod, tiles=None):
    p = k.p
    xsrc, dxs = k.scratch(xname, [8, 128, NT])
    XO, dXO = k.scratch(f"XO{l}", [8, 128, NT])
    ONT, dONT = k.scratch(f"ONT{l}", [4, 128, NT])
    ODT, dODT = k.scratch(f"ODT{l}", [4, 128, NT])
    OF, dOF = k.scratch(f"OF{l}", [NT, 512])
    GATE, dGATE = k.scratch(f"GATE{l}", [24, 128, NT])
    gam, dgam = k.const("norm2", [128, 8], k.inp[f"norm2_{l}"])
    make_gm(k, gam, dgam, mod, dmod, 4)
    xr = Ring(p, "mx", [128, 8, 256], 1); br_ = [Ring(p, f"mo{b}", [128, 4, 256], 1) for b in range(3)]
    gr = Ring(p, "mg", [128, 24, 256], 1); yr = Ring(p, "my", [128, 8, 256], 1); mr = Ring(p, "mm", [128, 8, 256], 1)
    tr_ = Ring(p, "mt", [128, 8, 256], 1); hr = Ring(p, "mh", [128, 8, 256], 1); ar = Ring(p, "ma", [128, 32, 256], 1)
    tmr = Ring(p, "mtm", [128, 256], 3); ofr = Ring(p, "mof", [128, 2, 512], 1); outr = Ring(p, "mout", [128, 256], 3)
    wmr = Ring(p, "wm", [128, 4, 128], 3); wor = Ring(p, "wo", [128, 8, 128], 2); w1r = Ring(p, "w1", [128, 8, 128], 3); w2r = Ring(p, "w2", [128, 32, 128], 2)
    wm_src = [k.inp[f"w_nao{l}"], k.inp[f"w_dno{l}"], k.inp[f"w_fno{l}"]]
    nb = 0
    nq = 0

    def wq():
        nonlocal nq
        nq += 1
        return "sp" if nq % 2 else "act"
    for (t0, N, col) in (tiles or super_tiles()):
        xt, dx, sem = xr.next()
        p.dma("sp", xt[:, :, 0:N], xsrc[:, :, t0:t0 + N].rearrange("k p t -> p k t"), reads=[dxs], writes=[dx], sem=sem)
        obs = []
        for b, (src, dsrc) in enumerate(((ONT, dONT), (ODT, dODT))):
            ot, do, sem = br_[b].next()
            p.dma("act", ot[:, :, 0:N], src[:, :, t0:t0 + N].rearrange("k p t -> p k t"), reads=[dsrc], writes=[do], sem=sem)
            obs.append((ot, do))
        oft, dof, sem = ofr.next()
        p.dma("sp", oft[:], OF[t0:t0 + N, :].rearrange("(a p) c -> p a c", p=128), reads=[dOF], writes=[dof], sem=sem)
        ot, do, _ = br_[2].next()
        for a in range(2):
            b_ = 6 + a
            for g in range(4):
                p.op("pe", lambda e, a=a, g=g, b_=b_, oft=oft: e.transpose(out=k.banks[b_][:, g * 128:(g + 1) * 128], in_=oft[:, a, g * 128:(g + 1) * 128], identity=k.ident[:]),
                     reads=[dof, k.dident], writes=[k.bdep[b_]])
            p.op("act", lambda e, a=a, b_=b_, ot=ot: e.activation(out=ot[:, :, a * 128:(a + 1) * 128], in_=k.banks[b_][:, 0:512].rearrange("p (g t) -> p g t", g=4), func=AF.Copy),
                 reads=[k.bdep[b_]], writes=[do])
        obs.append((ot, do))
        gt, dg, sem = gr.next()
        for q4 in range(4):
            p.dma(wq(), gt[:, q4 * 6:(q4 + 1) * 6, 0:N], GATE[q4 * 6:(q4 + 1) * 6, :, t0:t0 + N].rearrange("k p t -> p k t"), reads=[dGATE], writes=[dg], sem=sem)
        yt, dy, _ = yr.next()
        for c in range(8):
            for b in range(3):
                wt, dw, sem = wmr.next()
                p.dma(wq(), wt[:], wm_src[b][c], writes=[dw], sem=sem)
                bk = nb % 4
                nb += 1
                ps = k.banks[bk]
                ot, do = obs[b]
                for kk in range(4):
                    p.op("pe", lambda e, ps=ps, wt=wt, ot=ot, kk=kk: e.matmul(ps[:, 0:N], lhsT=wt[:, kk, :], rhs=ot[:, kk, 0:N], start=(kk == 0), stop=(kk == 3)),
                         reads=[dw, do], writes=[k.bdep[bk]])
                if b == 0:
                    p.op("dve", lambda e, ps=ps, yt=yt, gt=gt, c=c: e.tensor_tensor(out=yt[:, c, 0:N], in0=ps[:, 0:N], in1=gt[:, c, 0:N], op=ALU.mult),
                         reads=[k.bdep[bk], dg], writes=[dy])
                else:
                    tm, dtm, _ = tmr.next()
                    p.op("dve", lambda e, ps=ps, tm=tm, gt=gt, c=c, b=b: e.tensor_tensor(out=tm[:, 0:N], in0=ps[:, 0:N], in1=gt[:, b * 8 + c, 0:N], op=ALU.mult),
                         reads=[k.bdep[bk], dg], writes=[dtm])
                    p.op("pool", lambda e, tm=tm, yt=yt, c=c: e.tensor_tensor(out=yt[:, c, 0:N], in0=yt[:, c, 0:N], in1=tm[:, 0:N], op=ALU.add),
                         reads=[dtm, dy], writes=[dy])
        mt, dm, _ = mr.next()
        for c in range(8):
            wt, dw, sem = wor.next()
            p.dma(wq(), wt[:], k.inp[f"w_out{l}"][c], writes=[dw], sem=sem)
            bk = nb % 4
            nb += 1
            ps = k.banks[bk]
            for kk in range(8):
                p.op("pe", lambda e, ps=ps, wt=wt, yt=yt, kk=kk: e.matmul(ps[:, 0:N], lhsT=wt[:, kk, :], rhs=yt[:, kk, 0:N], start=(kk == 0), stop=(kk == 7)),
                     reads=[dw, dy], writes=[k.bdep[bk]])
            p.op("dve", lambda e, ps=ps, mt=mt, xt=xt, c=c, col=col: e.scalar_tensor_tensor(out=mt[:, c, 0:N], in0=ps[:, 0:N], scalar=mod[:, 16 + c, col:col + 1], in1=xt[:, c, 0:N],
                                                                                        op0=ALU.mult, op1=ALU.add), reads=[k.bdep[bk], dx, dmod], writes=[dm])
        ht, dh, _ = hr.next(); tmp, dtmp, _ = tr_.next()
        norm_mod(k, mt, dm, N, gam, dgam, mod, dmod, 3, 4, col, ht, dh, tmp, dtmp, k.ones_t, k.dones, 7)
        at, da, _ = ar.next()
        for j in range(32):
            wt, dw, sem = w1r.next()
            p.dma(wq(), wt[:], k.inp[f"w_mlp1_{l}"][j], writes=[dw], sem=sem)
            bk = nb % 4
            nb += 1
            ps = k.banks[bk]
            for kk in range(8):
                p.op("pe", lambda e, ps=ps, wt=wt, ht=ht, kk=kk: e.matmul(ps[:, 0:N], lhsT=wt[:, kk, :], rhs=ht[:, kk, 0:N], start=(kk == 0), stop=(kk == 7)),
                     reads=[dw, dh], writes=[k.bdep[bk]])
            p.op("act", lambda e, ps=ps, at=at, j=j: e.activation(out=at[:, j, 0:N], in_=ps[:, 0:N], func=AF.Relu), reads=[k.bdep[bk]], writes=[da])
            p.op("pool", lambda e, at=at, j=j: e.tensor_tensor(out=at[:, j, 0:N], in0=at[:, j, 0:N], in1=at[:, j, 0:N], op=ALU.mult), reads=[da], writes=[da])
        for c in range(8):
            wt, dw, sem = w2r.next()
            p.dma(wq(), wt[:], k.inp[f"w_mlp2_{l}"][c], writes=[dw], sem=sem)
            bk = nb % 4
            nb += 1
            ps = k.banks[bk]
            for j in range(32):
                p.op("pe", lambda e, ps=ps, wt=wt, at=at, j=j: e.matmul(ps[:, 0:N], lhsT=wt[:, j, :], rhs=at[:, j, 0:N], start=(j == 0), stop=(j == 31)),
                     reads=[dw, da], writes=[k.bdep[bk]])
            ot, do, _ = outr.next()
            p.op("dve", lambda e, ps=ps, ot=ot, mt=mt, c=c, col=col: e.scalar_tensor_tensor(out=ot[:, 0:N], in0=ps[:, 0:N], scalar=mod[:, 40 + c, col:col + 1], in1=mt[:, c, 0:N],
                                                                                        op0=ALU.mult, op1=ALU.add), reads=[k.bdep[bk], dm, dmod], writes=[do])
            p.store("sp", XO[c, :, t0:t0 + N], ot[:, 0:N], [do], dXO)


def stage_final(k, xname, tiles=None):
    p = k.p
    xsrc, dxs = k.scratch(xname, [8, 128, NT])
    y = k.nc.dram_tensor("y", [T, D], F32, kind="ExternalOutput").ap()
    k.ydep = Dep("y")
    gam, dgam = k.const("normf", [128, 8], k.inp["norm_f"])
    xr = Ring(p, "fx", [128, 8, 256], 2); tr_ = Ring(p, "ft", [128, 8, 256], 2); outr = Ring(p, "fo", [128, 1024], 2)
    nb = 0
    for (t0, N, col) in (tiles or super_tiles()[:-1]):
        xt, dx, sem = xr.next()
        p.dma("sp", xt[:, :, 0:N], xsrc[:, :, t0:t0 + N].rearrange("k p t -> p k t"), reads=[dxs], writes=[dx], sem=sem)
        tmp, dtmp, _ = tr_.next()
        p.op("act", lambda e, tmp=tmp, xt=xt: e.activation(out=tmp[:], in_=xt[:], func=AF.Square), reads=[dx], writes=[dtmp])
        ps = k.banks[7]
        for kk in range(8):
            p.op("pe", lambda e, kk=kk, tmp=tmp: e.matmul(ps[:, 0:N], lhsT=k.ones_t[:], rhs=tmp[:, kk, 0:N], start=(kk == 0), stop=(kk == 7)),
                 reads=[dtmp, k.dones], writes=[k.bdep[7]])
        rstd = k.rstd
        p.op("dve", lambda e: e.tensor_scalar(out=rstd[:, 0:N], in0=ps[:, 0:N], scalar1=EPS, scalar2=None, op0=ALU.add), reads=[k.bdep[7]], writes=[k.drstd])
        p.op("act", lambda e: e.activation(out=rstd[:, 0:N], in_=rstd[:, 0:N], func=AF.Sqrt), reads=[k.drstd], writes=[k.drstd])
        p.op("dve", lambda e: e.reciprocal(out=rstd[:, 0:N], in_=rstd[:, 0:N]), reads=[k.drstd], writes=[k.drstd])
        for kk in range(8):
            p.op("dve", lambda e, kk=kk, tmp=tmp, xt=xt: e.scalar_tensor_tensor(out=tmp[:, kk, 0:N], in0=xt[:, kk, 0:N], scalar=gam[:, kk:kk + 1], in1=rstd[:, 0:N],
                                                                               op0=ALU.mult, op1=ALU.mult), reads=[dx, k.drstd, dgam, dtmp], writes=[dtmp])
        for tb in range(N // 128):
            ot, do, _ = outr.next()
            for half in range(2):
                b = (nb % 3) * 2
                nb += 1
                b = b if False else (nb % 6)
                ps2 = k.banks[b]
                for q in range(4):
                    kk = half * 4 + q
                    p.op("pe", lambda e, ps2=ps2, tmp=tmp, kk=kk, q=q, tb=tb: e.transpose(out=ps2[:, q * 128:(q + 1) * 128], in_=tmp[:, kk, tb * 128:(tb + 1) * 128], identity=k.ident[:]),
                         reads=[dtmp, k.dident], writes=[k.bdep[b]])
                if half == 0:
                    p.op("act", lambda e, ps2=ps2, ot=ot: e.activation(out=ot[:, 0:512], in_=ps2[:, 0:512], func=AF.Copy), reads=[k.bdep[b]], writes=[do])
                else:
                    p.op("dve", lambda e, ps2=ps2, ot=ot: e.tensor_copy(out=ot[:, 512:1024], in_=ps2[:, 0:512]), reads=[k.bdep[b]], writes=[do])
            tok = t0 + tb * 128
            p.store("sp", y[tok:tok + 128, :], ot[:], [do], k.ydep)


def na_table(rpb):
    w = np.arange(64)
    cs = np.clip(w - 8, 0, 48)
    j = np.arange(64)
    inwin = (j[:, None] >= cs[None, :]) & (j[:, None] < cs[None, :] + 16)
    idx = np.clip(j[:, None] - w[None, :] + 15, 0, 30)
    tab = rpb[:, :, idx]
    tab = np.where(inwin[None, None], tab, np.float32(-30000.0))
    return np.ascontiguousarray(tab.transpose(2, 0, 1, 3)).astype(np.float32)


def stage_na(k, l, rows=None, do_ctx=True):
    p = k.p
    QT, dQT = k.scratch(f"QT{l}", [8, 64, NT])
    KT, dKT = k.scratch(f"KT{l}", [8, 64, NT])
    V, dV = k.scratch(f"V{l}", [NT, 512])
    ONT, dONT = k.scratch(f"ONT{l}", [4, 128, NT])
    eb, deb = k.const("natab", [64, 8 * 15 * 64], k.inp[f"natab{l}"].rearrange("j h o w -> j (h o w)"))
    for h in range(8):
        p.op("act", lambda e, h=h: e.activation(out=eb[:, h * 960:(h + 1) * 960], in_=eb[:, h * 960:(h + 1) * 960], func=AF.Exp), reads=[deb], writes=[deb])
    kc = p.sbuf("na_kc", [64, 8, 256]); dkc = Dep("na_kc")
    p.dma("sp", kc[:], KT[:, :, T:T + CT].rearrange("h d t -> d h t"), reads=[dKT], writes=[dkc], sem="ld_kc")
    vc = p.sbuf("na_vc", [128, 2, 8, 65]); dvc = Dep("na_vc")
    p.op("pool", lambda e: e.memset(vc[:], 1.0), writes=[dvc])
    for c in range(2):
        p.dma("act", vc[:, c, :, 0:64], V[T + c * 128:T + (c + 1) * 128, :].rearrange("t (h d) -> t h d", d=64), reads=[dV], writes=[dvc], sem="ld_vc")
    NS = 10
    kring = p.sbuf("na_kr", [64, NS, 8, 64]); vring = p.sbuf("na_vr", [64, NS, 8, 65])
    dkr = [Dep(f"na_kr{i}") for i in range(NS)]; dvr = [Dep(f"na_vr{i}") for i in range(NS)]
    p.op("pool", lambda e: e.memset(vring[:], 1.0), writes=dvr)
    qr = Ring(p, "na_q", [64, 8, 256], 2)
    er = Ring(p, "na_e", [64, 512], 3); ecr = Ring(p, "na_ec", [128, 128], 3)
    orr = Ring(p, "na_o", [64, 512], 2); otr = Ring(p, "na_ot", [128, 4, 64], 2); rcr = Ring(p, "na_rc", [64, 8], 2)
    loaded = -1
    it = 0
    qt = None
    rows = list(rows if rows is not None else range(256))
    for r in rows:
        rs = min(max(r - 4, 0), 248)
        while loaded < rs + 7:
            loaded += 1
            if loaded < rs:
                continue
            sl = loaded % NS
            p.dma("sp", kring[:, sl, :, :], KT[:, :, loaded * 64:(loaded + 1) * 64].rearrange("h d t -> d h t"), reads=[dKT], writes=[dkr[sl]], sem=f"ld_kr{sl}")
            p.dma("act", vring[:, sl, :, 0:64], V[loaded * 64:(loaded + 1) * 64, :].rearrange("t (h d) -> t h d", d=64), reads=[dV], writes=[dvr[sl]], sem=f"ld_vr{sl}")
        if qt is None or r % 4 == 0 or r == rows[0]:
            qt, dq, sem = qr.next()
            q0 = (r // 4) * 4
            p.dma("sp", qt[:], QT[:, :, q0 * 64:(q0 + 4) * 64].rearrange("h d t -> d h t"), reads=[dQT], writes=[dq], sem=sem)
        rr = r % 4
        o0 = rs - r + 7
        ot, do, _ = orr.next()
        rc, drc, _ = rcr.next()
        for h in range(8):
            bA, bB, bC = (it % 2), 2 + (it % 2), 4 + (it % 2)
            it += 1
            A, Bk, C = k.banks[bA], k.banks[bB], k.banks[bC]
            qv = qt[:, h, rr * 64:(rr + 1) * 64]
            for i in range(8):
                sl = (rs + i) % NS
                p.op("pe", lambda e, A=A, i=i, sl=sl, h=h, qv=qv: e.matmul(A[0:64, i * 64:(i + 1) * 64], lhsT=kring[:, sl, h, :], rhs=qv, start=True, stop=True),
                     reads=[dkr[sl], dq], writes=[k.bdep[bA]])
            for c in range(2):
                p.op("pe", lambda e, Bk=Bk, c=c, h=h, qv=qv: e.matmul(Bk[:, c * 64:(c + 1) * 64], lhsT=kc[:, h, c * 128:(c + 1) * 128], rhs=qv, start=True, stop=True),
                     reads=[dkc, dq], writes=[k.bdep[bB]])
            et, de, _ = er.next(); ect, dec, _ = ecr.next()
            p.op("act", lambda e, et=et, A=A: e.activation(out=et[:], in_=A[0:64, :], func=AF.Exp), reads=[k.bdep[bA]], writes=[de])
            p.op("act", lambda e, ect=ect, Bk=Bk: e.activation(out=ect[:], in_=Bk[:, 0:128], func=AF.Exp), reads=[k.bdep[bB]], writes=[dec])
            e0 = h * 960 + o0 * 64
            p.op("pool", lambda e, et=et, e0=e0: e.tensor_tensor(out=et[:], in0=et[:], in1=eb[:, e0:e0 + 512], op=ALU.mult), reads=[de, deb], writes=[de])
            for i in range(8):
                sl = (rs + i) % NS
                p.op("pe", lambda e, C=C, et=et, i=i, sl=sl, h=h: e.matmul(C[0:64, 0:65], lhsT=et[:, i * 64:(i + 1) * 64], rhs=vring[:, sl, h, :], start=(i == 0), stop=False),
                     reads=[de, dvr[sl]], writes=[k.bdep[bC]])
            for c in range(2):
                p.op("pe", lambda e, C=C, ect=ect, c=c, h=h: e.matmul(C[0:64, 0:65], lhsT=ect[:, c * 64:(c + 1) * 64], rhs=vc[:, c, h, :], start=False, stop=(c == 1)),
                     reads=[dec, dvc], writes=[k.bdep[bC]])
            p.op("dve", lambda e, C=C, rc=rc, h=h: e.reciprocal(out=rc[:, h:h + 1], in_=C[0:64, 64:65]), reads=[k.bdep[bC]], writes=[drc])
            p.op("dve", lambda e, C=C, rc=rc, ot=ot, h=h: e.tensor_scalar(out=ot[:, h * 64:(h + 1) * 64], in0=C[0:64, 0:64], scalar1=rc[:, h:h + 1], scalar2=None, op0=ALU.mult),
                 reads=[k.bdep[bC], drc], writes=[do])
        bT = 6 + (r % 2)
        for c in range(4):
            p.op("pe", lambda e, bT=bT, ot=ot, c=c: e.transpose(out=k.banks[bT][:, c * 64:(c + 1) * 64], in_=ot[:, c * 128:(c + 1) * 128], identity=k.ident[0:64, 0:64]),
                 reads=[do, k.dident], writes=[k.bdep[bT]])
        tt, dt, _ = otr.next()
        p.op("act", lambda e, bT=bT, tt=tt: e.activation(out=tt[:].rearrange("p c t -> p (c t)"), in_=k.banks[bT][:, 0:256], func=AF.Copy), reads=[k.bdep[bT]], writes=[dt])
        p.store("sp", ONT[:, :, r * 64:(r + 1) * 64].rearrange("c p t -> p c t"), tt[:], [dt], dONT)
    if not do_ctx:
        return
    qc = p.sbuf("na_qc", [64, 8, 256]); dqc = Dep("na_qc")
    p.dma("sp", qc[:], QT[:, :, T:T + CT].rearrange("h d t -> d h t"), reads=[dQT], writes=[dqc], sem="ld_qc")
    ocr = Ring(p, "na_oc", [128, 512], 2); otc = Ring(p, "na_otc", [128, 4, 128], 2); rcc = Ring(p, "na_rcc", [128, 8], 2)
    e2r = Ring(p, "na_e2", [128, 2, 128], 3)
    for t in range(2):
        ot, do, _ = ocr.next(); rc, drc, _ = rcc.next()
        for h in range(8):
            bA, bC = (it % 2), 4 + (it % 2)
            it += 1
            A, C = k.banks[bA], k.banks[bC]
            for c in range(2):
                p.op("pe", lambda e, A=A, c=c, h=h, t=t: e.matmul(A[:, c * 128:(c + 1) * 128], lhsT=kc[:, h, c * 128:(c + 1) * 128], rhs=qc[:, h, t * 128:(t + 1) * 128], start=True, stop=True),
                     reads=[dkc, dqc], writes=[k.bdep[bA]])
            et, de, _ = e2r.next()
            p.op("act", lambda e, et=et, A=A: e.activation(out=et[:].rearrange("p c t -> p (c t)"), in_=A[:, 0:256], func=AF.Exp), reads=[k.bdep[bA]], writes=[de])
            for c in range(2):
                p.op("pe", lambda e, C=C, et=et, c=c, h=h: e.matmul(C[:, 0:65], lhsT=et[:, c, :], rhs=vc[:, c, h, :], start=(c == 0), stop=(c == 1)),
                     reads=[de, dvc], writes=[k.bdep[bC]])
            p.op("dve", lambda e, C=C, rc=rc, h=h: e.reciprocal(out=rc[:, h:h + 1], in_=C[:, 64:65]), reads=[k.bdep[bC]], writes=[drc])
            p.op("dve", lambda e, C=C, rc=rc, ot=ot, h=h: e.tensor_scalar(out=ot[:, h * 64:(h + 1) * 64], in0=C[:, 0:64], scalar1=rc[:, h:h + 1], scalar2=None, op0=ALU.mult),
                 reads=[k.bdep[bC], drc], writes=[do])
        bT = 6 + t
        for c in range(4):
            p.op("pe", lambda e, bT=bT, ot=ot, c=c: e.transpose(out=k.banks[bT][:, c * 128:(c + 1) * 128], in_=ot[:, c * 128:(c + 1) * 128], identity=k.ident[:]),
                 reads=[do, k.dident], writes=[k.bdep[bT]])
        tt, dt, _ = otc.next()
        p.op("act", lambda e, bT=bT, tt=tt: e.activation(out=tt[:].rearrange("p c t -> p (c t)"), in_=k.banks[bT][:, 0:512], func=AF.Copy), reads=[k.bdep[bT]], writes=[dt])
        p.store("sp", ONT[:, :, T + t * 128:T + (t + 1) * 128].rearrange("c p t -> p c t"), tt[:], [dt], dONT)


def fn_tables():
    n = np.arange(128, dtype=np.float64)
    a = 2 * np.pi * np.outer(n, n) / 128.0
    c128, s128 = np.cos(a), np.sin(a)
    tw = 2 * np.pi * np.outer(n, n) / float(T)
    m = np.arange(256, dtype=np.float64)
    a2 = 2 * np.pi * np.outer(m, m) / 256.0
    c256, s256 = np.cos(a2), np.sin(a2)
    f = lambda x: np.ascontiguousarray(x.astype(np.float32))
    dft = np.stack([c128, s128, -c128, -s128], axis=1)
    twt = np.stack([np.cos(tw), np.sin(tw), -np.sin(tw)], axis=1)
    d256 = np.stack([c256, -s256], axis=0).reshape(2, 2, 128, 256).transpose(2, 0, 1, 3)
    return f(dft), f(twt), f(d256)


def stage_fn(k, l, do_ctx=True, t2s=None, t1s=None):
    p = k.p
    U, dU = k.scratch(f"U{l}", [4, 128, NT])
    PQ, dPQ = k.scratch(f"PQ{l}", [NT, 1024])
    YD, dYD = k.scratch(f"YD{l}", [128, 128, 1024])
    OF, dOF = k.scratch(f"OF{l}", [NT, 512])
    dft, ddft = k.const("dft", [128, 4, 128], k.inp["fn_dft"])
    twt, dtw = k.const("twt", [128, 3, 128], k.inp["fn_tw"])
    C, S, NC_, NS_ = (dft[:, i, :] for i in range(4))
    ur = Ring(p, "fu", [128, 4, 128], 2); pqr = Ring(p, "fpq", [128, 1024], 2)
    nb = 0
    for tok in range(0, NT if do_ctx else T, 128):
        ut, du, sem = ur.next()
        p.dma("sp", ut[:], U[:, :, tok:tok + 128].rearrange("g c t -> c g t"), reads=[dU], writes=[du], sem=sem)
        b0 = (nb % 2) * 2
        nb += 1
        for g in range(4):
            p.op("pe", lambda e, ut=ut, g=g, b0=b0: e.matmul(k.banks[b0][:, g * 128:(g + 1) * 128], lhsT=ut[:, g, :], rhs=C, start=True, stop=True),
                 reads=[du, ddft], writes=[k.bdep[b0]])
            p.op("pe", lambda e, ut=ut, g=g, b0=b0: e.matmul(k.banks[b0 + 1][:, g * 128:(g + 1) * 128], lhsT=ut[:, g, :], rhs=S, start=True, stop=True),
                 reads=[du, ddft], writes=[k.bdep[b0 + 1]])
        pt, dp, _ = pqr.next()
        p.op("act", lambda e, pt=pt, b0=b0: e.activation(out=pt[:, 0:512], in_=k.banks[b0][:, 0:512], func=AF.Copy), reads=[k.bdep[b0]], writes=[dp])
        p.op("dve", lambda e, pt=pt, b0=b0: e.tensor_copy(out=pt[:, 512:1024], in_=k.banks[b0 + 1][:, 0:512]), reads=[k.bdep[b0 + 1]], writes=[dp])
        p.store("sp", PQ[tok:tok + 128, :], pt[:], [dp], dPQ)
    xr = Ring(p, "fx", [128, 1024], 2); yr = Ring(p, "fy", [128, 1024], 2); tr_ = Ring(p, "ftm", [128, 2, 512], 2)
    PQv = PQ[0:T, :].rearrange("(t1 t2) c -> t2 t1 c", t2=128)
    for t2 in (t2s if t2s is not None else range(128)):
        xt, dx, sem = xr.next()
        for q4 in range(4):
            p.dma("sp" if q4 % 2 == 0 else "act", xt[:, q4 * 256:(q4 + 1) * 256], PQv[t2, :, q4 * 256:(q4 + 1) * 256], reads=[dPQ], writes=[dx], sem=sem)
        b0 = (nb % 2) * 2
        nb += 1
        Yr, Yi = k.banks[b0], k.banks[b0 + 1]
        p.op("pe", lambda e, Yr=Yr, xt=xt: e.matmul(Yr[:, 0:512], lhsT=C, rhs=xt[:, 0:512], start=True, stop=False), reads=[dx, ddft], writes=[k.bdep[b0]])
        p.op("pe", lambda e, Yr=Yr, xt=xt: e.matmul(Yr[:, 0:512], lhsT=NS_, rhs=xt[:, 512:1024], start=False, stop=True), reads=[dx, ddft], writes=[k.bdep[b0]])
        p.op("pe", lambda e, Yi=Yi, xt=xt: e.matmul(Yi[:, 0:512], lhsT=NC_, rhs=xt[:, 512:1024], start=True, stop=False), reads=[dx, ddft], writes=[k.bdep[b0 + 1]])
        p.op("pe", lambda e, Yi=Yi, xt=xt: e.matmul(Yi[:, 0:512], lhsT=NS_, rhs=xt[:, 0:512], start=False, stop=True), reads=[dx, ddft], writes=[k.bdep[b0 + 1]])
        tm, dtm, _ = tr_.next(); yt, dy, _ = yr.next()
        cf, sf, nsf = twt[:, 0, t2:t2 + 1], twt[:, 1, t2:t2 + 1], twt[:, 2, t2:t2 + 1]
        p.op("dve", lambda e, tm=tm, Yr=Yr, cf=cf: e.tensor_scalar(out=tm[:, 0, :], in0=Yr[:, 0:512], scalar1=cf, scalar2=None, op0=ALU.mult), reads=[k.bdep[b0], dtw], writes=[dtm])
        p.op("dve", lambda e, tm=tm, Yi=Yi, cf=cf: e.tensor_scalar(out=tm[:, 1, :], in0=Yi[:, 0:512], scalar1=cf, scalar2=None, op0=ALU.mult), reads=[k.bdep[b0 + 1], dtw], writes=[dtm])
        p.op("dve", lambda e, tm=tm, yt=yt, Yi=Yi, sf=sf: e.scalar_tensor_tensor(out=yt[:, 0:512], in0=Yi[:, 0:512], scalar=sf, in1=tm[:, 0, :], op0=ALU.mult, op1=ALU.add),
             reads=[k.bdep[b0 + 1], dtw, dtm], writes=[dy])
        p.op("dve", lambda e, tm=tm, yt=yt, Yr=Yr, nsf=nsf: e.scalar_tensor_tensor(out=yt[:, 512:1024], in0=Yr[:, 0:512], scalar=nsf, in1=tm[:, 1, :], op0=ALU.mult, op1=ALU.add),
             reads=[k.bdep[b0], dtw, dtm], writes=[dy])
        for q4 in range(4):
            p.store("sp", YD[:, t2, q4 * 256:(q4 + 1) * 256], yt[:, q4 * 256:(q4 + 1) * 256], [dy], dYD)
    sc_lat = 1.0 / math.sqrt(T * 128.0)
    zr = Ring(p, "fz", [128, 1024], 2); orr = Ring(p, "fo", [128, 512], 2)
    OFv = OF[0:T, :].rearrange("(b a) c -> a b c", a=128)
    for t1 in (t1s if t1s is not None else range(128)):
        zt, dz, sem = zr.next()
        p.dma("sp", zt[:], YD[t1], reads=[dYD], writes=[dz], sem=sem)
        b = 4 + (nb % 2)
        nb += 1
        ps = k.banks[b]
        p.op("pe", lambda e, ps=ps, zt=zt: e.matmul(ps[:, 0:512], lhsT=C, rhs=zt[:, 0:512], start=True, stop=False), reads=[dz, ddft], writes=[k.bdep[b]])
        p.op("pe", lambda e, ps=ps, zt=zt: e.matmul(ps[:, 0:512], lhsT=S, rhs=zt[:, 512:1024], start=False, stop=True), reads=[dz, ddft], writes=[k.bdep[b]])
        ot, do, _ = orr.next()
        p.op("act", lambda e, ps=ps, ot=ot: e.activation(out=ot[:], in_=ps[:, 0:512], func=AF.Copy, scale=sc_lat), reads=[k.bdep[b]], writes=[do])
        for q2 in range(2):
            p.store("sp", OFv[t1, :, q2 * 256:(q2 + 1) * 256], ot[:, q2 * 256:(q2 + 1) * 256], [do], dOF)
    if not do_ctx:
        return
    d256, dd256 = k.const("d256", [128, 2, 2, 256], k.inp["fn_d256"])
    sc_ctx = 1.0 / math.sqrt(256.0 * 128.0)
    cx = p.sbuf("fcx", [128, 2, 1024]); dcx = Dep("fcx")
    p.dma("sp", cx[:], PQ[T:T + 256, :].rearrange("(a p) c -> p a c", p=128), reads=[dPQ], writes=[dcx], sem="ld_fcx")
    for tt_ in range(2):
        b = 4 + (nb % 2)
        nb += 1
        ps = k.banks[b]
        n_ = 0
        for cs_ in range(2):
            for kt in range(2):
                p.op("pe", lambda e, ps=ps, cs_=cs_, kt=kt, tt_=tt_, n_=n_: e.matmul(ps[:, 0:512], lhsT=d256[:, cs_, kt, tt_ * 128:(tt_ + 1) * 128],
                                                                                 rhs=cx[:, kt, cs_ * 512:(cs_ + 1) * 512], start=(n_ == 0), stop=(n_ == 3)),
                     reads=[dcx, dd256], writes=[k.bdep[b]])
                n_ += 1
        ot, do, _ = orr.next()
        p.op("act", lambda e, ps=ps, ot=ot: e.activation(out=ot[:], in_=ps[:, 0:512], func=AF.Copy, scale=sc_ctx), reads=[k.bdep[b]], writes=[do])
        p.store("sp", OF[T + tt_ * 128:T + (tt_ + 1) * 128, :], ot[:], [do], dOF)


def build_program(shapes, debug=None):
    nc = bass.Bass("TRN2", target_bir_lowering=False)
    p = Prog(nc)
    k = K(nc, p, shapes)
    common(k)
    if debug is not None:
        fin = debug(k)
        p.build(final_deps=fin)
        return nc
    for l in range(2):
        xname = "xin" if l == 0 else "XO0"
        mod, dmod = k.modt[l]
        lat = super_tiles()[:-1]
        for fn in (lambda: stage_mod(k, l),
                   lambda: stage_proj(k, l, xname, mod, dmod),
                   lambda: stage_dn_pre(k, l),
                   lambda: stage_dn_scan(k, l),
                   lambda: stage_dn_post(k, l, toks=None if l == 0 else range(0, T, 128)),
                   lambda: stage_na(k, l, do_ctx=(l == 0)),
                   lambda: stage_fn(k, l, do_ctx=(l == 0)),
                   lambda: stage_mlp(k, l, xname, mod, dmod, tiles=None if l == 0 else lat)):
            p.scope_begin()
            fn()
            p.scope_end()
    p.scope_begin()
    stage_final(k, "XO1")
    p.scope_end()
    p.build(final_deps=[k.ydep])
    return nc


def kernel(**inputs):
    inp = {k_: np.asarray(v, np.float32) for k_, v in inputs.items()}
    m = host_layout(inp)
    shapes = {k_: v.shape for k_, v in m.items()}
    nc = build_program(shapes)
    res = run_bass_kernel_spmd(nc, [m], core_ids=[0])
    y = np.asarray(res.results[0]["y"], np.float32)
    return y.reshape(1, T, D)
```

```python
import math
import numpy as np
import concourse.bass as bass
import concourse.mybir as mybir
from concourse.bass_utils import run_bass_kernel_spmd

F32 = mybir.dt.float32
AF = mybir.ActivationFunctionType
ALU = mybir.AluOpType
AX = mybir.AxisListType

D = 1024
T = 16384
CT = 256
NT = T + CT
GW = 64
EPS = 1e-6
NTAB = 15


class Dep:
    __slots__ = ("name", "w", "r")

    def __init__(self, name=""):
        self.name = name
        self.w = None
        self.r = []


class Prog:
    ENGS = ("pe", "act", "dve", "pool", "sp")

    def __init__(self, nc, self_sync=True):
        self.nc = nc
        self.q = {e: [] for e in self.ENGS}
        self.cnt = {}
        self.known = {e: {} for e in self.ENGS}
        self.sems = {}
        self.self_sync = self_sync
        self._uid = 0
        self._stack = []
        self._semstack = []
        self._smap = {}
        self._scope_mark = 0
        self.n_inst = 0
        for e in self.ENGS:
            self._mksem("E_" + e)

    def _mksem(self, key):
        cm = self.nc.semaphore(key)
        h = cm.__enter__()
        self._semstack.append(cm)
        self.sems[key] = h
        self.cnt[key] = 0
        return key

    _mksem_global = _mksem

    def sbuf(self, name, shape, dt=F32):
        self._uid += 1
        cm = self.nc.sbuf_tensor(f"{name}_u{self._uid}", list(shape), dt)
        t = cm.__enter__()
        self._stack.append(cm)
        return t

    def psum(self, name, shape, dt=F32):
        cm = self.nc.psum_tensor(name, list(shape), dt)
        t = cm.__enter__()
        self._stack.append(cm)
        return t

    def close(self):
        while self._stack:
            self._stack.pop().__exit__(None, None, None)
        while self._semstack:
            self._semstack.pop().__exit__(None, None, None)

    def _waits(self, eng, reads, writes):
        need = {}

        def add(ev):
            if ev is None:
                return
            if isinstance(ev, dict):
                for k, v in ev.items():
                    if need.get(k, 0) < v:
                        need[k] = v
                return
            k, v = ev
            if need.get(k, 0) < v:
                need[k] = v
        for d in reads:
            add(d.w)
        for d in writes:
            add(d.w)
            for ev in d.r:
                add(ev)
        out = []
        own = "E_" + eng
        for k, v in need.items():
            if k == own and (eng == "pe" or not self.self_sync):
                continue
            if self.known[eng].get(k, 0) >= v:
                continue
            self.known[eng][k] = v
            out.append((k, v))
        return out

    def _emit(self, eng, fn, reads, writes, semkey, inc, track_w=True):
        waits = self._waits(eng, reads, writes if track_w else [])
        self.cnt[semkey] += inc
        ev = (semkey, self.cnt[semkey])
        for d in writes:
            d.w = ev
            d.r = []
        for d in reads:
            if d not in writes:
                d.r.append(ev)
                if len(d.r) > 64:
                    d.r = d.r[-64:]
        self.q[eng].append((waits, fn, semkey, inc))
        self.n_inst += 1 + len(waits)

    def scope_begin(self):
        self.barrier()
        self._scope_mark = len(self._stack)
        self._smap = {}

    def scope_end(self):
        self.barrier()
        while len(self._stack) > self._scope_mark:
            self._stack.pop().__exit__(None, None, None)
        self._smap = {}

    def barrier(self):
        for eng in self.ENGS:
            waits = []
            for key, v in self.cnt.items():
                if v == 0 or key == "E_" + eng:
                    continue
                if self.known[eng].get(key, 0) >= v:
                    continue
                self.known[eng][key] = v
                waits.append((key, v))
            if waits:
                self.q[eng].append((waits, None, None, 0))
                self.n_inst += len(waits)

    def dsem(self, key):
        m = self._smap
        if key not in m:
            phys = f"dma{len(m)}"
            if phys not in self.sems:
                self._mksem_global(phys)
            m[key] = phys
        return m[key]

    def op(self, eng, fn, reads=(), writes=()):
        self._emit(eng, fn, list(reads), list(writes), "E_" + eng, 1)

    def dma(self, eng, out, in_, reads=(), writes=(), sem=None, track_w=True, **kw):
        sem = self.dsem(sem)
        self._emit(eng, lambda e: e.dma_start(out=out, in_=in_, **kw), list(reads), list(writes), sem, 16,
                   track_w=track_w)

    def store(self, eng, out, in_, reads, ddep, **kw):
        sem = "st_" + reads[0].name
        self.dma(eng, out, in_, reads=reads, writes=[], sem=sem, **kw)
        sem = self.dsem(sem)
        if not isinstance(ddep.w, dict):
            ddep.w = {}
        ddep.w[sem] = self.cnt[sem]

    def build(self, final_deps=()):
        nc = self.nc
        fw = self._waits("sp", list(final_deps), [])
        sems = self.sems
        q = self.q

        def replay(eh, items, extra=()):
            for waits, fn, semkey, inc in items:
                for k, v in waits:
                    eh.wait_ge(sems[k], v)
                if fn is not None:
                    fn(eh).then_inc(sems[semkey], inc)
            for k, v in extra:
                eh.wait_ge(sems[k], v)

        with nc.Block() as block:
            @block.tensor
            def _(e):
                replay(e, q["pe"])

            @block.scalar
            def _(e):
                replay(e, q["act"])

            @block.vector
            def _(e):
                replay(e, q["dve"])

            @block.gpsimd
            def _(e):
                replay(e, q["pool"])

            @block.sync
            def _(e):
                replay(e, q["sp"], fw)
        self.close()


class Ring:
    def __init__(self, p, name, shape, n=2):
        self.t = [p.sbuf(f"{name}{i}", shape) for i in range(n)]
        self.d = [Dep(f"{name}{i}") for i in range(n)]
        self.i = 0
        self.n = n
        self.name = name

    def next(self):
        i = self.i
        self.i = (i + 1) % self.n
        return self.t[i], self.d[i], f"ld_{self.name}{i}"


_o = 0
COLS = {}
for _n, _s in (("na_k", 512), ("na_v", 512), ("dn_k", 512), ("dn_v", 512), ("dn_a", 8), ("dn_b", 8),
               ("na_q", 512), ("dn_q", 512), ("dn_z", 512), ("fn_u", 512), ("gate", 3072)):
    COLS[_n] = (_o, _o + _s)
    _o += _s
IN_W = _o
FB_GROUPS = (("na_q", 4), ("na_k", 4), ("dn_q", 4), ("dn_k", 4), ("dn_v", 4), ("fn_u", 4), ("gate", 24))
FB_CHUNKS = [(g, i) for g, n in FB_GROUPS for i in range(n)]
NFB = len(FB_CHUNKS)
FA_COLS = np.concatenate([np.arange(*COLS["na_v"]), np.arange(*COLS["dn_z"]),
                          np.arange(*COLS["dn_a"]), np.arange(*COLS["dn_b"])])


def _fm(v, nchunk):
    return np.ascontiguousarray(np.asarray(v, np.float32).reshape(nchunk, 128).T)


def _lhs_chunks(w, cols):
    K = w.shape[0]
    sub = w[:, cols]
    nc_ = sub.shape[1] // 128
    a = sub.reshape(K // 128, 128, nc_, 128)
    return np.ascontiguousarray(a.transpose(2, 1, 0, 3))


def _rhs_rows(w):
    K, N = w.shape
    return np.ascontiguousarray(w.reshape(K // 128, 128, N).transpose(1, 0, 2))


def host_layout(inp):
    m = {}
    x = np.concatenate([inp["x"][0], inp["ctx"][0]], axis=0)
    m["xin"] = np.ascontiguousarray(x.T.reshape(8, 128, NT))
    m["cvec"] = np.ascontiguousarray(np.stack([_fm(inp["c"][0], 8), _fm(inp["c_ctx"], 8)], axis=2))
    for l in range(2):
        w_in = inp["w_in"][l]
        fbcols = np.concatenate([np.arange(COLS[g][0] + i * 128, COLS[g][0] + (i + 1) * 128) for g, i in FB_CHUNKS])
        m[f"w_inb{l}"] = _lhs_chunks(w_in, fbcols)
        m[f"w_ina{l}"] = _rhs_rows(w_in[:, FA_COLS])
        m[f"w_ada{l}"] = _lhs_chunks(inp["w_ada"][l], np.arange(6 * D))
        m[f"b_ada{l}"] = _fm(inp["b_ada"][l], 48)
        m[f"norm1_{l}"] = _fm(inp["norm1"][l], 8)
        m[f"norm2_{l}"] = _fm(inp["norm2"][l], 8)
        ab = np.concatenate([inp["a_log"][l].reshape(-1), inp["dt_bias"][l].reshape(-1)])
        m[f"abt{l}"] = np.ascontiguousarray(np.broadcast_to(ab[None, :], (128, 16))).astype(np.float32)
        m[f"convw{l}"] = np.ascontiguousarray(inp["conv_w"][l].T.reshape(12, 128, 5).transpose(1, 0, 2))
    cos, sin, rm = rope_tables()
    m["rope_cos"], m["rope_sin"], m["rope_rm"] = cos, sin, rm
    mk = dn_masks()
    m["dnmask0"], m["dnmask1"] = mk[0], mk[1]
    for l in range(2):
        ar = np.arange(D)
        m[f"w_nao{l}"] = _lhs_chunks(inp["w_na_o"][l], ar)
        m[f"w_dno{l}"] = _lhs_chunks(inp["w_dn_o"][l], ar)
        m[f"w_fno{l}"] = _lhs_chunks(inp["w_fn"][l], ar)
        m[f"w_out{l}"] = _lhs_chunks(inp["w_out"][l], ar)
        m[f"w_mlp1_{l}"] = _lhs_chunks(inp["w_mlp1"][l], np.arange(4 * D))
        m[f"w_mlp2_{l}"] = _lhs_chunks(inp["w_mlp2"][l], ar)
        m[f"dnw{l}"] = np.ascontiguousarray(np.broadcast_to(inp["dn_norm"][l][None, :], (128, 128))).astype(np.float32)
        m[f"natab{l}"] = na_table(inp["rpb"][l])
    m["fn_dft"], m["fn_tw"], m["fn_d256"] = fn_tables()
    m["norm_f"] = _fm(inp["norm_f"], 8)
    ident = np.eye(128, dtype=np.float32)
    m["ident"] = ident
    return m


class K:
    def __init__(self, nc, p, shapes):
        self.nc = nc
        self.p = p
        self.inp = {k: nc.dram_tensor(k, list(v), F32, kind="ExternalInput").ap() for k, v in shapes.items()}
        self.scr = {}
        self.sdep = {}
        self.banks = [p.psum(f"bank{i}", [128, 512]) for i in range(8)]
        self.bdep = [Dep(f"bank{i}") for i in range(8)]
        self.cdep = Dep("consts")
        self._ld = 0

    def scratch(self, name, shape, kind="Internal"):
        if name in self.inp and name not in self.scr:
            self.scr[name] = self.inp[name]
            self.sdep[name] = Dep(name)
        if name not in self.scr:
            self.scr[name] = self.nc.dram_tensor(name, list(shape), F32, kind=kind).ap()
            self.sdep[name] = Dep(name)
        return self.scr[name], self.sdep[name]

    def const(self, name, shape, src, eng="sp"):
        t = self.p.sbuf("c_" + name, shape)
        d = Dep("c_" + name)
        self.p.dma(eng, t[:], src, writes=[d], sem="ld_c_" + name)
        return t, d


def stage_mod(k, l):
    p = k.p
    cv, dcv = k.const("cvec", [128, 8, 2], k.inp["cvec"])
    cs = p.sbuf("cs", [128, 8, 2])
    dcs = Dep("cs")
    p.op("act", lambda e: e.activation(out=cs[:], in_=cv[:], func=AF.Silu), reads=[dcv], writes=[dcs])
    mod, dmod = k.modt[l]
    bt, dbt = k.const(f"b_ada{l}", [128, 48], k.inp[f"b_ada{l}"])
    ring = Ring(p, f"wada{l}_", [128, 8, 128], 3)
    wsrc = k.inp[f"w_ada{l}"]
    for j in range(48):
        wt, dw, sem = ring.next()
        p.dma("sp", wt[:], wsrc[j], writes=[dw], sem=sem)
        b = j % 2
        ps = k.banks[b]
        for kk in range(8):
            p.op("pe", lambda e, wt=wt, kk=kk, ps=ps: e.matmul(ps[:, 0:2], lhsT=wt[:, kk, :], rhs=cs[:, kk, :],
                                                              start=(kk == 0), stop=(kk == 7)),
                 reads=[dw, dcs], writes=[k.bdep[b]])
        p.op("dve", lambda e, ps=ps, j=j: e.tensor_scalar(out=mod[:, j, :], in0=ps[:, 0:2], scalar1=bt[:, j:j + 1],
                                                        scalar2=None, op0=ALU.add),
             reads=[k.bdep[b], dbt], writes=[dmod])
    return mod, dmod


def norm_mod(k, xt, dx, N, gam, dgam, mod, dmod, sh_j, sc_j, col, out, dout, tmp, dtmp, ones_t, dones, bank):
    p = k.p
    ps = k.banks[bank]
    p.op("act", lambda e: e.activation(out=tmp[:, :, 0:N], in_=xt[:, :, 0:N], func=AF.Square), reads=[dx], writes=[dtmp])
    for kk in range(8):
        p.op("pe", lambda e, kk=kk: e.matmul(ps[:, 0:N], lhsT=ones_t[:], rhs=tmp[:, kk, 0:N], start=(kk == 0), stop=(kk == 7)),
             reads=[dtmp, dones], writes=[k.bdep[bank]])
    rstd = k.rstd
    p.op("dve", lambda e: e.tensor_scalar(out=rstd[:, 0:N], in0=ps[:, 0:N], scalar1=EPS, scalar2=None, op0=ALU.add),
         reads=[k.bdep[bank]], writes=[k.drstd])
    p.op("act", lambda e: e.activation(out=rstd[:, 0:N], in_=rstd[:, 0:N], func=AF.Sqrt), reads=[k.drstd], writes=[k.drstd])
    p.op("dve", lambda e: e.reciprocal(out=rstd[:, 0:N], in_=rstd[:, 0:N]), reads=[k.drstd], writes=[k.drstd])
    for kk in range(8):
        eng = "dve" if kk % 2 == 0 else "pool"
        p.op(eng, lambda e, kk=kk: e.tensor_tensor(out=tmp[:, kk, 0:N], in0=xt[:, kk, 0:N], in1=rstd[:, 0:N], op=ALU.mult),
             reads=[dx, k.drstd], writes=[dtmp])
    gm = k.gm
    for kk in range(8):
        p.op("act", lambda e, kk=kk: e.activation(out=out[:, kk, 0:N], in_=tmp[:, kk, 0:N], func=AF.Identity,
                                                  bias=mod[:, sh_j * 8 + kk, col:col + 1], scale=gm[:, kk, col:col + 1]),
             reads=[dtmp, k.dgm, dmod], writes=[dout])


def make_gm(k, gam, dgam, mod, dmod, sc_j):
    p = k.p
    gm = k.gm
    for col in range(2):
        p.op("dve", lambda e, col=col: e.scalar_tensor_tensor(out=gm[:, :, col], in0=mod[:, sc_j * 8:sc_j * 8 + 8, col], scalar=1.0,
                                                              in1=gam[:, :], op0=ALU.add, op1=ALU.mult),
             reads=[dmod, dgam], writes=[k.dgm])


def super_tiles():
    st = [(i * 256, 256, 0) for i in range(64)]
    st.append((T, 256, 1))
    return st


def stage_proj(k, l, xname, mod, dmod, tiles=None):
    p = k.p
    xsrc, dxs = k.scratch(xname, [8, 128, NT])
    QT, dQT = k.scratch(f"QT{l}", [8, 64, NT])
    KT, dKT = k.scratch(f"KT{l}", [8, 64, NT])
    V, dV = k.scratch(f"V{l}", [NT, 512])
    DQ, dDQ = k.scratch(f"DQ{l}", [4, 128, NT])
    DK, dDK = k.scratch(f"DK{l}", [4, 128, NT])
    DV, dDV = k.scratch(f"DV{l}", [4, 128, NT])
    Z, dZ = k.scratch(f"Z{l}", [NT, 512])
    GB, dGB = k.scratch(f"GB{l}", [NT, 16])
    U, dU = k.scratch(f"U{l}", [4, 128, NT])
    GATE, dGATE = k.scratch(f"GATE{l}", [24, 128, NT])
    gam, dgam = k.const(f"norm1_{l}", [128, 8], k.inp[f"norm1_{l}"])
    abt, dabt = k.const(f"abt{l}", [128, 16], k.inp[f"abt{l}"])
    nexp = p.sbuf(f"nexp{l}", [128, 8])
    dnexp = Dep("nexp")
    p.op("act", lambda e: e.activation(out=nexp[:], in_=abt[:, 0:8], func=AF.Exp), reads=[dabt], writes=[dnexp])
    p.op("dve", lambda e: e.tensor_scalar(out=nexp[:], in0=nexp[:], scalar1=-1.0, scalar2=None, op0=ALU.mult), reads=[dnexp], writes=[dnexp])
    make_gm(k, gam, dgam, mod, dmod, 1)
    wa, dwa = k.const(f"w_ina{l}", [128, 8, 1040], k.inp[f"w_ina{l}"], eng="act")
    xr = Ring(p, f"px{l}_", [128, 8, 256], 2)
    hr = Ring(p, f"ph{l}_", [128, 8, 256], 2)
    tr = Ring(p, f"pt{l}_", [128, 8, 256], 1)
    wr = Ring(p, f"pw{l}_", [128, 8, 128], 3)
    orr = Ring(p, f"po{l}_", [128, 512], 4)
    gr = Ring(p, f"pg{l}_", [128, 48], 2)
    wsrc = k.inp[f"w_inb{l}"]
    fb_dst = {"na_q": None, "na_k": None, "dn_q": (DQ, dDQ), "dn_k": (DK, dDK), "dn_v": (DV, dDV), "fn_u": (U, dU), "gate": (GATE, dGATE)}
    nbank = 0
    tl = list(tiles or super_tiles())
    pairs = [tl[i:i + 2] for i in range(0, len(tl), 2)]
    for pair in pairs:
      hts = []
      for (t0, N, col) in pair:
        xt, dx, sem = xr.next()
        p.dma("sp", xt[:, :, 0:N], xsrc[:, :, t0:t0 + N].rearrange("k p t -> p k t"), reads=[dxs], writes=[dx], sem=sem)
        ht, dh, _ = hr.next()
        tmp, dtmp, _ = tr.next()
        norm_mod(k, xt, dx, N, gam, dgam, mod, dmod, 0, 1, col, ht, dh, tmp, dtmp, k.ones_t, k.dones, 7)
        hts.append((ht, dh))
      for c, (g, gi) in enumerate(FB_CHUNKS):
        wt, dw, sem = wr.next()
        p.dma("sp" if c % 2 == 0 else "act", wt[:], wsrc[c], writes=[dw], sem=sem)
        for (t0, N, col), (ht, dh) in zip(pair, hts):
            b = nbank % 4
            nbank += 1
            ps = k.banks[b]
            for kk in range(8):
                p.op("pe", lambda e, wt=wt, kk=kk, ps=ps, ht=ht, N=N: e.matmul(ps[:, 0:N], lhsT=wt[:, kk, :], rhs=ht[:, kk, 0:N],
                                                                         start=(kk == 0), stop=(kk == 7)),
                     reads=[dw, dh], writes=[k.bdep[b]])
            ot, do, _ = orr.next()
            if g == "gate":
                p.op("act", lambda e, ot=ot, ps=ps, N=N: e.activation(out=ot[:, 0:N], in_=ps[:, 0:N], func=AF.Sigmoid),
                     reads=[k.bdep[b]], writes=[do])
            elif g == "na_q":
                p.op("act", lambda e, ot=ot, ps=ps, N=N: e.activation(out=ot[:, 0:N], in_=ps[:, 0:N], func=AF.Copy, scale=0.125),
                     reads=[k.bdep[b]], writes=[do])
            else:
                p.op("dve", lambda e, ot=ot, ps=ps, N=N: e.tensor_copy(out=ot[:, 0:N], in_=ps[:, 0:N]), reads=[k.bdep[b]], writes=[do])
            if g in ("na_q", "na_k"):
                dst, dd = (QT, dQT) if g == "na_q" else (KT, dKT)
                p.store("sp", dst[2 * gi:2 * gi + 2, :, t0:t0 + N].rearrange("h d t -> (h d) t"), ot[:, 0:N], [do], dd)
            else:
                dst, dd = fb_dst[g]
                p.store("sp", dst[gi, :, t0:t0 + N], ot[:, 0:N], [do], dd)
      for (t0, N, col), (ht, dh) in zip(pair, hts):
        for tb in range(N // 128):
            tok = t0 + tb * 128
            for gi, (c0, cn, dst, dd) in enumerate(((0, 512, V, dV), (512, 512, Z, dZ), (1024, 16, None, None))):
                b = nbank % 4
                nbank += 1
                ps = k.banks[b]
                for kk in range(8):
                    p.op("pe", lambda e, kk=kk, ps=ps, ht=ht, tb=tb, c0=c0, cn=cn: e.matmul(
                        ps[:, 0:cn], lhsT=ht[:, kk, tb * 128:(tb + 1) * 128], rhs=wa[:, kk, c0:c0 + cn], start=(kk == 0), stop=(kk == 7)),
                        reads=[dwa, dh], writes=[k.bdep[b]])
                if dst is not None:
                    ot, do, _ = orr.next()
                    p.op("act" if gi == 0 else "dve", (lambda e, ot=ot, ps=ps: e.activation(out=ot[:], in_=ps[:], func=AF.Copy)) if gi == 0 else
                         (lambda e, ot=ot, ps=ps: e.tensor_copy(out=ot[:], in_=ps[:])), reads=[k.bdep[b]], writes=[do])
                    p.store("sp", dst[tok:tok + 128, :], ot[:], [do], dd)
                else:
                    gt, dg, _ = gr.next()
                    p.op("dve", lambda e, gt=gt, ps=ps: e.tensor_tensor(out=gt[:, 0:8], in0=ps[:, 0:8], in1=abt[:, 8:16], op=ALU.add),
                         reads=[k.bdep[b], dabt], writes=[dg])
                    p.op("act", lambda e, gt=gt: e.activation(out=gt[:, 0:8], in_=gt[:, 0:8], func=AF.Exp), reads=[dg], writes=[dg])
                    p.op("act", lambda e, gt=gt: e.activation(out=gt[:, 0:8], in_=gt[:, 0:8], func=AF.Ln, bias=1.0), reads=[dg], writes=[dg])
                    p.op("dve", lambda e, gt=gt: e.tensor_tensor(out=gt[:, 0:8], in0=gt[:, 0:8], in1=nexp[:], op=ALU.mult),
                         reads=[dg, dnexp], writes=[dg])
                    p.op("act", lambda e, gt=gt, ps=ps: e.activation(out=gt[:, 8:16], in_=ps[:, 8:16], func=AF.Sigmoid),
                         reads=[k.bdep[b]], writes=[dg])
                    p.store("sp", GB[tok:tok + 128, :], gt[:, 0:16], [dg], dGB)


def common(k):
    p = k.p
    k.ident, k.dident = k.const("ident", [128, 128], k.inp["ident"])
    k.ones_t = p.sbuf("ones_t", [128, 128])
    k.dones = Dep("ones")
    p.op("pool", lambda e: e.memset(k.ones_t[:], 1.0 / D), writes=[k.dones])
    k.one1 = p.sbuf("one1", [128, 128])
    k.done1 = Dep("one1")
    p.op("pool", lambda e: e.memset(k.one1[:], 1.0), writes=[k.done1])
    k.rstd = p.sbuf("rstd", [128, 512])
    k.drstd = Dep("rstd")
    k.gm = p.sbuf("gm", [128, 8, 2])
    k.dgm = Dep("gm")
    k.modt = [(p.sbuf(f"mod{l}", [128, 48, 2]), Dep(f"mod{l}")) for l in range(2)]


def rope_tables():
    half = 64
    inv = (1.0 / (10000.0 ** (np.arange(0, half, 2, dtype=np.float32) / np.float32(half)))).astype(np.float32)
    t = np.arange(T)
    pos = np.stack([t // GW, t % GW], axis=-1).astype(np.float32)
    ang = pos[:, :, None] * inv[None, None, :]
    ang = np.concatenate([ang, ang], axis=-1).reshape(T, 128)
    cos = np.cos(ang).astype(np.float32).T
    sin = np.sin(ang).astype(np.float32).T
    rm = np.zeros((128, 128), np.float32)
    for o in (0, 64):
        for i in range(64):
            if i < 32:
                rm[o + i + 32, o + i] = -1.0
            else:
                rm[o + i - 32, o + i] = 1.0
    return np.ascontiguousarray(cos), np.ascontiguousarray(sin), rm


def dn_masks():
    i = np.arange(128)
    out = {}
    for d in range(2):
        le = (i[:, None] <= i[None, :]) if d == 0 else (i[:, None] >= i[None, :])
        gt = (i[:, None] > i[None, :]) if d == 0 else (i[:, None] < i[None, :])
        blk = lambda b: (i[:, None] // b) == (i[None, :] // b)
        ms = [le, gt, gt, ~le | (i[:, None] == i[None, :]), le, blk(16)]
        for b in (16, 32, 64):
            up = blk(2 * b) & ((i[:, None] // b) < (i[None, :] // b))
            mN = up if d == 0 else up.T
            ms += [mN, mN.T]
        out[d] = np.ascontiguousarray(np.stack(ms, axis=1).astype(np.float32))
    return out


def stage_dn_pre(k, l, tiles=None):
    p = k.p
    src = [k.scratch(f"D{x}{l}", [4, 128, NT]) for x in "QKV"]
    dst = [k.scratch(f"D{x}2_{l}", [4, 128, NT]) for x in "QKV"]
    cw, dcw = k.const(f"convw{l}", [128, 12, 5], k.inp[f"convw{l}"])
    k.rm, k.drm = k.const("rope_rm", [128, 128], k.inp["rope_rm"])
    ur = Ring(p, f"du{l}_", [128, 260], 3)
    ar = Ring(p, f"da{l}_", [128, 256], 3)
    yr = Ring(p, f"dy{l}_", [128, 256], 3)
    sr = Ring(p, f"ds{l}_", [128, 256], 2)
    rr = Ring(p, f"dr{l}_", [128, 256], 2)
    cr = Ring(p, f"dc{l}_", [128, 2, 256], 2)
    outr = Ring(p, f"do{l}_", [128, 256], 3)
    cosd, sind = k.inp["rope_cos"], k.inp["rope_sin"]
    nb = 0
    for (t0, N, col) in (tiles or super_tiles()):
        seg0, seg1 = (0, T) if col == 0 else (T, T + CT)
        ct, dct = None, None
        if col == 0:
            ct, dct, sem = cr.next()
            p.dma("sp", ct[:, 0, :], cosd[:, t0:t0 + N], writes=[dct], sem=sem)
            p.dma("sp", ct[:, 1, :], sind[:, t0:t0 + N], writes=[dct], sem=sem)
        for h in range(4):
            for xi in range(3):
                ut, du, sem = ur.next()
                lo, hi = max(t0 - 2, seg0), min(t0 + N + 2, seg1)
                if lo != t0 - 2 or hi != t0 + N + 2:
                    p.op("pool", lambda e, ut=ut: e.memset(ut[:], 0.0), writes=[du])
                p.dma("sp" if xi != 1 else "act", ut[:, lo - (t0 - 2):hi - (t0 - 2)], src[xi][0][h, :, lo:hi], reads=[src[xi][1]], writes=[du], sem=sem)
                at, da, _ = ar.next()
                eng = "dve"
                j = xi * 4 + h
                p.op(eng, lambda e, at=at, ut=ut, j=j: e.tensor_scalar(out=at[:, 0:N], in0=ut[:, 0:N], scalar1=cw[:, j, 0:1], scalar2=None, op0=ALU.mult),
                     reads=[du, dcw], writes=[da])
                for tap in range(1, 5):
                    p.op(eng, lambda e, at=at, ut=ut, j=j, tap=tap: e.scalar_tensor_tensor(out=at[:, 0:N], in0=ut[:, tap:tap + N], scalar=cw[:, j, tap:tap + 1],
                                                                                         in1=at[:, 0:N], op0=ALU.mult, op1=ALU.add),
                         reads=[du, dcw, da], writes=[da])
                yt, dy, _ = yr.next()
                p.op("act", lambda e, yt=yt, at=at: e.activation(out=yt[:, 0:N], in_=at[:, 0:N], func=AF.Silu), reads=[da], writes=[dy])
                if xi == 2:
                    p.store("sp", dst[2][0][h, :, t0:t0 + N], yt[:, 0:N], [dy], dst[2][1])
                    continue
                st, dsq, _ = sr.next()
                p.op("pool", lambda e, st=st, yt=yt: e.tensor_tensor(out=st[:, 0:N], in0=yt[:, 0:N], in1=yt[:, 0:N], op=ALU.mult), reads=[dy], writes=[dsq])
                b = 4 + (nb % 2)
                nb += 1
                ps = k.banks[b]
                p.op("pe", lambda e, ps=ps, st=st: e.matmul(ps[:, 0:N], lhsT=k.one1[:], rhs=st[:, 0:N], start=True, stop=True),
                     reads=[dsq, k.done1], writes=[k.bdep[b]])
                p.op("dve", lambda e, ps=ps, st=st: e.tensor_scalar(out=st[:, 0:N], in0=ps[:, 0:N], scalar1=EPS, scalar2=None, op0=ALU.add),
                     reads=[k.bdep[b]], writes=[dsq])
                p.op("act", lambda e, st=st: e.activation(out=st[:, 0:N], in_=st[:, 0:N], func=AF.Sqrt), reads=[dsq], writes=[dsq])
                p.op("dve", lambda e, st=st: e.reciprocal(out=st[:, 0:N], in_=st[:, 0:N]), reads=[dsq], writes=[dsq])
                ot, do, _ = outr.next()
                if col == 1:
                    p.op("dve", lambda e, ot=ot, yt=yt, st=st: e.tensor_tensor(out=ot[:, 0:N], in0=yt[:, 0:N], in1=st[:, 0:N], op=ALU.mult),
                         reads=[dy, dsq], writes=[do])
                else:
                    p.op("dve", lambda e, yt=yt, st=st: e.tensor_tensor(out=yt[:, 0:N], in0=yt[:, 0:N], in1=st[:, 0:N], op=ALU.mult),
                         reads=[dy, dsq], writes=[dy])
                    b2 = 6 + (nb % 2)
                    ps2 = k.banks[b2]
                    p.op("pe", lambda e, ps2=ps2, yt=yt: e.matmul(ps2[:, 0:N], lhsT=k.rm[:], rhs=yt[:, 0:N], start=True, stop=True),
                         reads=[dy, k.drm], writes=[k.bdep[b2]])
                    rt, dr, _ = rr.next()
                    p.op("dve", lambda e, rt=rt, ps2=ps2, ct=ct: e.tensor_tensor(out=rt[:, 0:N], in0=ps2[:, 0:N], in1=ct[:, 1, 0:N], op=ALU.mult),
                         reads=[k.bdep[b2], dct], writes=[dr])
                    p.op("pool", lambda e, ot=ot, yt=yt, ct=ct: e.tensor_tensor(out=ot[:, 0:N], in0=yt[:, 0:N], in1=ct[:, 0, 0:N], op=ALU.mult),
                         reads=[dy, dct], writes=[do])
                    p.op("pool", lambda e, ot=ot, rt=rt: e.tensor_tensor(out=ot[:, 0:N], in0=ot[:, 0:N], in1=rt[:, 0:N], op=ALU.add),
                         reads=[dr, do], writes=[do])
                p.store("sp", dst[xi][0][h, :, t0:t0 + N], ot[:, 0:N], [do], dst[xi][1])


class PS4:
    def __init__(self, k, si):
        bs = [2 * si, 2 * si + 1]
        self.t = [k.banks[b][:, 0:128] for b in bs]
        self.d = [k.bdep[b] for b in bs]
        self.i = 0

    def next(self):
        i = self.i
        self.i = (i + 1) % 2
        return self.t[i], self.d[i]


def dn_scan_gen(k, l, si, h, d, blocks, want_o):
    p = k.p
    SC = 128 ** -0.5
    QT, dQ = k.scratch(f"DQ2_{l}", [4, 128, NT])
    KT, dK = k.scratch(f"DK2_{l}", [4, 128, NT])
    VT, dV = k.scratch(f"DV2_{l}", [4, 128, NT])
    GB, dGB = k.scratch(f"GB{l}", [NT, 16])
    OD, dOD = k.scratch(f"OD{l}", [2, 4, NT, 128])
    mk, dmk = k.dnm[d]
    U, SL, MLs, MLD, MLDT = (mk[:, i, :] for i in range(5))
    BD16 = mk[:, 5, :]
    MRG = [(mk[:, 6 + 2 * j, :], mk[:, 7 + 2 * j, :]) for j in range(3)]
    B = k.dnbuf[si]
    ps = PS4(k, si)
    S = [B["S0"], B["S1"]]
    dS = [B["d_S0"], B["d_S1"]]
    p.op("pool", lambda e: e.memset(S[0][:], 0.0), writes=[dS[0]])
    cur = 0
    ident = k.ident

    def T_(name):
        return B[name], B["d_" + name]
    for bi, tok in enumerate(blocks):
        par = bi % 2
        qT, dq = T_(f"qT{par}"); kT, dk_ = T_(f"kT{par}"); vT, dv = T_(f"vT{par}"); gb, dgb = T_(f"gb{par}")
        lsem = f"ld_blk{par}_{si}"
        p.dma("sp", qT[:], QT[h, :, tok:tok + 128], reads=[dQ], writes=[dq], sem=lsem)
        p.dma("act", kT[:], KT[h, :, tok:tok + 128], reads=[dK], writes=[dk_], sem=lsem)
        p.dma("sp", vT[:], VT[h, :, tok:tok + 128], reads=[dV], writes=[dv], sem=lsem)
        p.dma("act", gb[:], GB[tok:tok + 128, :], reads=[dGB], writes=[dgb], sem=lsem)
        for dd_ in (dq, dk_, dv, dgb):
            dd_.w = dgb.w
        g = gb[:, d * 4 + h:d * 4 + h + 1]
        beta = gb[:, 8 + d * 4 + h:8 + d * 4 + h + 1]
        sc, dsc = T_("sc")
        ktok, dkt = T_("ktok"); vb, dvb = T_("vb"); SLg, dSLg = T_("SLg"); Dm, dDm = T_("Dm")
        Dstr, dDstr = T_("Dstr"); Dlow, dDlow = T_("Dlow"); DlowT, dDlowT = T_("DlowT")
        Na, dNa = T_("Na"); Nb, dNb = T_("Nb"); NTa, dNTa = T_("NTa"); NTb, dNTb = T_("NTb")
        Xa, dXa = T_("Xa"); Xb, dXb = T_("Xb")
        kbg, dkbg = T_("kbg"); ktail, dktail = T_("ktail"); u_sb, du = T_("u"); wT, dwT = T_("wT"); qkT, dqkT = T_("qkT")
        vnew, dvn = T_("vnew"); ob, dob = T_("ob"); o_sb, do = T_(f"o{par}")
        p1, dp1 = ps.next()
        p.op("pe", lambda e, p1=p1, kT=kT: e.transpose(out=p1, in_=kT[:], identity=ident[:]), reads=[dk_, k.dident], writes=[dp1])
        p.op("act", lambda e, p1=p1, ktok=ktok: e.activation(out=ktok[:], in_=p1, func=AF.Copy), reads=[dp1], writes=[dkt])
        p2, dp2 = ps.next()
        p.op("pe", lambda e, p2=p2, vT=vT: e.transpose(out=p2, in_=vT[:], identity=ident[:]), reads=[dv, k.dident], writes=[dp2])
        p.op("dve", lambda e, p2=p2, vb=vb, beta=beta: e.tensor_scalar(out=vb[:], in0=p2, scalar1=beta, scalar2=None, op0=ALU.mult),
             reads=[dp2, dgb], writes=[dvb])
        p.op("pool", lambda e, SLg=SLg, g=g: e.tensor_scalar(out=SLg[:], in0=SL, scalar1=g, scalar2=None, op0=ALU.mult),
             reads=[dmk, dgb], writes=[dSLg])
        p.op("pool", lambda e, sc=sc, beta=beta: e.tensor_scalar(out=sc[:, 0:1], in0=beta, scalar1=-1.0, scalar2=None, op0=ALU.mult),
             reads=[dgb], writes=[dsc])
        yield
        p3, dp3 = ps.next()
        p.op("pe", lambda e, p3=p3, SLg=SLg: e.matmul(p3, lhsT=U, rhs=SLg[:], start=True, stop=True), reads=[dSLg, dmk], writes=[dp3])
        p4, dp4 = ps.next()
        p.op("pe", lambda e, p4=p4, g=g: e.matmul(p4[:, 0:1], lhsT=U, rhs=g, start=True, stop=True), reads=[dgb, dmk], writes=[dp4])
        p.op("pe", lambda e, p4=p4, g=g: e.matmul(p4[:, 1:2], lhsT=k.one1[:], rhs=g, start=True, stop=True), reads=[dgb, k.done1], writes=[dp4])
        p.op("act", lambda e, p3=p3, Dm=Dm: e.activation(out=Dm[:], in_=p3, func=AF.Exp), reads=[dp3], writes=[dDm])
        p.op("dve", lambda e, p4=p4, sc=sc: e.tensor_copy(out=sc[:, 1:3], in_=p4[:, 0:2]), reads=[dp4], writes=[dsc])
        yield
        p.op("pool", lambda e, Dstr=Dstr, Dm=Dm: e.tensor_tensor(out=Dstr[:], in0=Dm[:], in1=MLs, op=ALU.mult), reads=[dDm, dmk], writes=[dDstr])
        p.op("pool", lambda e, Dlow=Dlow, Dm=Dm: e.tensor_tensor(out=Dlow[:], in0=Dm[:], in1=MLD, op=ALU.mult), reads=[dDm, dmk], writes=[dDlow])
        yield
        p.op("act", lambda e, sc=sc: e.activation(out=sc[:, 3:4], in_=sc[:, 1:2], func=AF.Exp), reads=[dsc], writes=[dsc])
        p.op("dve", lambda e, sc=sc: e.tensor_tensor(out=sc[:, 4:5], in0=sc[:, 2:3], in1=sc[:, 1:2], op=ALU.subtract), reads=[dsc], writes=[dsc])
        p.op("act", lambda e, sc=sc: e.activation(out=sc[:, 4:5], in_=sc[:, 4:5], func=AF.Exp), reads=[dsc], writes=[dsc])
        p.op("act", lambda e, sc=sc: e.activation(out=sc[:, 5:6], in_=sc[:, 2:3], func=AF.Exp), reads=[dsc], writes=[dsc])
        yield
        p.op("dve", lambda e, sc=sc, beta=beta: e.tensor_tensor(out=sc[:, 6:7], in0=sc[:, 3:4], in1=beta, op=ALU.mult), reads=[dsc, dgb], writes=[dsc])
        p.op("dve", lambda e, sc=sc: e.tensor_scalar(out=sc[:, 7:8], in0=sc[:, 3:4], scalar1=SC, scalar2=None, op0=ALU.mult), reads=[dsc], writes=[dsc])
        yield
        p5, dp5 = ps.next()
        kT2, dkT2 = T_("kT2")
        p.op("pool", lambda e, kT2=kT2, kT=kT: e.tensor_copy(out=kT2[:], in_=kT[:]), reads=[dk_], writes=[dkT2])
        p.op("pe", lambda e, p5=p5, kT=kT, kT2=kT2: e.matmul(p5, lhsT=kT[:], rhs=kT2[:], start=True, stop=True), reads=[dk_, dkT2], writes=[dp5])
        yield
        p.op("dve", lambda e, p5=p5, NTa=NTa, sc=sc: e.tensor_scalar(out=NTa[:], in0=p5, scalar1=sc[:, 0:1], scalar2=None, op0=ALU.mult),
             reads=[dp5, dsc], writes=[dNTa])
        p.op("pool", lambda e, NTa=NTa, Dstr=Dstr: e.tensor_tensor(out=NTa[:], in0=NTa[:], in1=Dstr[:], op=ALU.mult),
             reads=[dDstr, dNTa], writes=[dNTa])
        yield
        N0, dN0 = T_("N0"); NT0 = NTa; dNT0 = dNTa
        p6, dp6 = ps.next()
        p.op("pe", lambda e, p6=p6: e.transpose(out=p6, in_=NT0[:], identity=ident[:]), reads=[dNT0, k.dident], writes=[dp6])
        p.op("act", lambda e, p6=p6: e.activation(out=N0[:], in_=p6, func=AF.Copy), reads=[dp6], writes=[dN0])
        yield
        Xc, dXc = T_("Xa"); XTc, dXTc = T_("XTa"); Xn_, dXn_ = T_("Xb"); XTn_, dXTn_ = T_("XTb")
        Np, dNp = T_("Na"); NTp, dNTp = T_("NTb"); Nn, dNn = T_("Nb"); NTn, dNTn = T_("NTc")
        p.op("pool", lambda e, Np=Np, N0=N0: e.tensor_tensor(out=Np[:], in0=N0[:], in1=BD16, op=ALU.mult), reads=[dN0, dmk], writes=[dNp])
        p.op("pool", lambda e, NTp=NTp, NT0=NT0: e.tensor_tensor(out=NTp[:], in0=NT0[:], in1=BD16, op=ALU.mult), reads=[dNT0, dmk], writes=[dNTp])
        p.op("dve", lambda e, Xc=Xc, Np=Np: e.tensor_tensor(out=Xc[:], in0=Np[:], in1=ident[:], op=ALU.add), reads=[dNp, k.dident], writes=[dXc])
        p.op("dve", lambda e, XTc=XTc, NTp=NTp: e.tensor_tensor(out=XTc[:], in0=NTp[:], in1=ident[:], op=ALU.add), reads=[dNTp, k.dident], writes=[dXTc])
        yield
        for lvl in range(3):
            pa, dpa = ps.next()
            p.op("pe", lambda e, pa=pa, Np=Np, NTp=NTp: e.matmul(pa, lhsT=NTp[:], rhs=Np[:], start=True, stop=True), reads=[dNp, dNTp], writes=[dpa])
            pb, dpb = ps.next()
            p.op("pe", lambda e, pb=pb, Np=Np, NTp=NTp: e.matmul(pb, lhsT=Np[:], rhs=NTp[:], start=True, stop=True), reads=[dNp, dNTp], writes=[dpb])
            p.op("act", lambda e, pa=pa, Nn=Nn: e.activation(out=Nn[:], in_=pa, func=AF.Copy), reads=[dpa], writes=[dNn])
            p.op("dve", lambda e, pb=pb, NTn=NTn: e.tensor_copy(out=NTn[:], in_=pb), reads=[dpb], writes=[dNTn])
            yield
            pc, dpc = ps.next()
            p.op("pe", lambda e, pc=pc, NTn=NTn, Xc=Xc: e.matmul(pc, lhsT=NTn[:], rhs=Xc[:], start=True, stop=True), reads=[dNTn, dXc], writes=[dpc])
            pd, dpd = ps.next()
            p.op("pe", lambda e, pd=pd, Nn=Nn, XTc=XTc: e.matmul(pd, lhsT=Nn[:], rhs=XTc[:], start=True, stop=True), reads=[dNn, dXTc], writes=[dpd])
            p.op("dve", lambda e, pc=pc, Xc=Xc, Xn_=Xn_: e.tensor_tensor(out=Xn_[:], in0=pc, in1=Xc[:], op=ALU.add), reads=[dpc, dXc], writes=[dXn_])
            p.op("dve", lambda e, pd=pd, XTc=XTc, XTn_=XTn_: e.tensor_tensor(out=XTn_[:], in0=pd, in1=XTc[:], op=ALU.add), reads=[dpd, dXTc], writes=[dXTn_])
            (Xc, dXc, Xn_, dXn_) = (Xn_, dXn_, Xc, dXc)
            (XTc, dXTc, XTn_, dXTn_) = (XTn_, dXTn_, XTc, dXTc)
            (Np, dNp, Nn, dNn) = (Nn, dNn, Np, dNp)
            (NTp, dNTp, NTn, dNTn) = (NTn, dNTn, NTp, dNTp)
            yield
        Nm, dNm = T_("Nm"); NTm, dNTm = T_("NTm"); Y, dY = T_("Y"); Y2, dY2 = T_("Y2")
        for j in range(3):
            Mj, MjT = MRG[j]
            last = (j == 2)
            p.op("pool", lambda e, Mj=Mj: e.tensor_tensor(out=NTm[:], in0=NT0[:], in1=MjT, op=ALU.mult) if False else e.tensor_tensor(out=NTm[:], in0=NT0[:], in1=MjT, op=ALU.mult),
                 reads=[dNT0, dmk], writes=[dNTm]) if False else None
            p.op("pool", lambda e, MjT=MjT: e.tensor_tensor(out=NTm[:], in0=NT0[:], in1=MjT, op=ALU.mult), reads=[dNT0, dmk], writes=[dNTm])
            if not last:
                p.op("pool", lambda e, Mj=Mj: e.tensor_tensor(out=Nm[:], in0=N0[:], in1=Mj, op=ALU.mult), reads=[dN0, dmk], writes=[dNm])
            py, dpy = ps.next()
            p.op("pe", lambda e, py=py, Xc=Xc: e.matmul(py, lhsT=NTm[:], rhs=Xc[:], start=True, stop=True), reads=[dNTm, dXc], writes=[dpy])
            p.op("act", lambda e, py=py: e.activation(out=Y[:], in_=py, func=AF.Copy), reads=[dpy], writes=[dY])
            if not last:
                py2, dpy2 = ps.next()
                p.op("pe", lambda e, py2=py2, XTc=XTc: e.matmul(py2, lhsT=Nm[:], rhs=XTc[:], start=True, stop=True), reads=[dNm, dXTc], writes=[dpy2])
                p.op("dve", lambda e, py2=py2: e.tensor_copy(out=Y2[:], in_=py2), reads=[dpy2], writes=[dY2])
            yield
            pz, dpz = ps.next()
            p.op("pe", lambda e, pz=pz, XTc=XTc: e.matmul(pz, lhsT=XTc[:], rhs=Y[:], start=True, stop=True), reads=[dXTc, dY], writes=[dpz])
            if not last:
                pz2, dpz2 = ps.next()
                p.op("pe", lambda e, pz2=pz2, Xc=Xc: e.matmul(pz2, lhsT=Xc[:], rhs=Y2[:], start=True, stop=True), reads=[dXc, dY2], writes=[dpz2])
            p.op("dve", lambda e, pz=pz, Xc=Xc, Xn_=Xn_: e.tensor_tensor(out=Xn_[:], in0=pz, in1=Xc[:], op=ALU.add), reads=[dpz, dXc], writes=[dXn_])
            if not last:
                p.op("dve", lambda e, pz2=pz2, XTc=XTc, XTn_=XTn_: e.tensor_tensor(out=XTn_[:], in0=pz2, in1=XTc[:], op=ALU.add), reads=[dpz2, dXTc], writes=[dXTn_])
                (XTc, dXTc, XTn_, dXTn_) = (XTn_, dXTn_, XTc, dXTc)
            (Xc, dXc, Xn_, dXn_) = (Xn_, dXn_, Xc, dXc)
            yield
        Xs = [(Xc, dXc)]
        X, dX = Xs[0]
        p.op("pool", lambda e, kbg=kbg, ktok=ktok, sc=sc: e.tensor_scalar(out=kbg[:], in0=ktok[:], scalar1=sc[:, 6:7], scalar2=None, op0=ALU.mult),
             reads=[dkt, dsc], writes=[dkbg])
        p.op("pool", lambda e, ktail=ktail, ktok=ktok, sc=sc: e.tensor_scalar(out=ktail[:], in0=ktok[:], scalar1=sc[:, 4:5], scalar2=None, op0=ALU.mult),
             reads=[dkt, dsc], writes=[dktail])
        pu, dpu = ps.next()
        p.op("pe", lambda e, pu=pu, X=X, vb=vb: e.matmul(pu, lhsT=X[:], rhs=vb[:], start=True, stop=True), reads=[dX, dvb], writes=[dpu])
        p.op("act", lambda e, pu=pu, u_sb=u_sb: e.activation(out=u_sb[:], in_=pu, func=AF.Copy), reads=[dpu], writes=[du])
        pw, dpw = ps.next()
        p.op("pe", lambda e, pw=pw, X=X, kbg=kbg: e.matmul(pw, lhsT=kbg[:], rhs=X[:], start=True, stop=True), reads=[dX, dkbg], writes=[dpw])
        p.op("dve", lambda e, pw=pw, wT=wT: e.tensor_copy(out=wT[:], in_=pw), reads=[dpw], writes=[dwT])
        yield
        if want_o:
            pt, dpt = ps.next()
            p.op("pe", lambda e, pt=pt, Dlow=Dlow: e.transpose(out=pt, in_=Dlow[:], identity=ident[:]), reads=[dDlow, k.dident], writes=[dpt])
            p.op("act", lambda e, pt=pt, DlowT=DlowT: e.activation(out=DlowT[:], in_=pt, func=AF.Copy), reads=[dpt], writes=[dDlowT])
            pq, dpq = ps.next()
            p.op("pe", lambda e, pq=pq, kT=kT, qT=qT: e.matmul(pq, lhsT=kT[:], rhs=qT[:], start=True, stop=True), reads=[dk_, dq], writes=[dpq])
            p.op("dve", lambda e, pq=pq, qkT=qkT, DlowT=DlowT: e.scalar_tensor_tensor(out=qkT[:], in0=pq, scalar=SC, in1=DlowT[:], op0=ALU.mult, op1=ALU.mult),
                 reads=[dpq, dDlowT], writes=[dqkT])
            yield
        Sc, dSc = S[cur], dS[cur]
        Sn, dSn = S[1 - cur], dS[1 - cur]
        pv, dpv = ps.next()
        p.op("pe", lambda e, pv=pv, wT=wT, Sc=Sc: e.matmul(pv, lhsT=wT[:], rhs=Sc[:], start=True, stop=True), reads=[dwT, dSc], writes=[dpv])
        po, dpo = ps.next()
        if want_o:
            p.op("pe", lambda e, po=po, qT=qT, Sc=Sc: e.matmul(po, lhsT=qT[:], rhs=Sc[:], start=True, stop=True), reads=[dq, dSc], writes=[dpo])
        p.op("dve", lambda e, pv=pv, vnew=vnew, u_sb=u_sb: e.tensor_tensor(out=vnew[:], in0=u_sb[:], in1=pv, op=ALU.subtract), reads=[dpv, du], writes=[dvn])
        if want_o:
            p.op("dve", lambda e, po=po, o_sb=o_sb, sc=sc: e.tensor_scalar(out=o_sb[:], in0=po, scalar1=sc[:, 7:8], scalar2=None, op0=ALU.mult),
                 reads=[dpo, dsc], writes=[do])
        yield
        pn, dpn = ps.next()
        p.op("pe", lambda e, pn=pn, ktail=ktail, vnew=vnew: e.matmul(pn, lhsT=ktail[:], rhs=vnew[:], start=True, stop=True), reads=[dktail, dvn], writes=[dpn])
        pb2, dpb2 = ps.next()
        if want_o:
            p.op("pe", lambda e, pb2=pb2, qkT=qkT, vnew=vnew: e.matmul(pb2, lhsT=qkT[:], rhs=vnew[:], start=True, stop=True), reads=[dqkT, dvn], writes=[dpb2])
        p.op("dve", lambda e, pn=pn, Sn=Sn, Sc=Sc, sc=sc: e.scalar_tensor_tensor(out=Sn[:], in0=Sc[:], scalar=sc[:, 5:6], in1=pn, op0=ALU.mult, op1=ALU.add),
             reads=[dpn, dSc, dsc], writes=[dSn])
        cur = 1 - cur
        if want_o:
            p.op("dve", lambda e, pb2=pb2, o_sb=o_sb: e.tensor_tensor(out=o_sb[:], in0=o_sb[:], in1=pb2, op=ALU.add), reads=[dpb2, do], writes=[do])
            p.store("sp", OD[d, h, tok:tok + 128, :], o_sb[:], [do], dOD)
        yield


def dn_blocks(d, nlat=128, with_ctx=True):
    cb = [T, T + 128]
    lb = [i * 128 for i in range(nlat)]
    if d == 1:
        cb = cb[::-1]
        lb = [i * 128 for i in range(128)][::-1][:nlat]
    return (cb if with_ctx else []) + lb


def stage_dn_scan(k, l, nlat=128, scans=None):
    p = k.p
    if True:
        k.dnm = [k.const(f"dnmask{d}", [128, 12, 128], k.inp[f"dnmask{d}"]) for d in range(2)]
        names = ["ktok", "vb", "SLg", "Dm", "Dstr", "Dlow", "DlowT", "Na", "Nb", "NTa", "NTb", "Xa", "Xb", "kbg", "ktail", "u", "wT", "qkT",
                 "vnew", "ob", "o0", "o1", "kT2", "N0", "XTa", "XTb", "NTc", "Nm", "NTm", "Y", "Y2", "qT0", "qT1", "kT0", "kT1", "vT0", "vT1", "S0", "S1"]
        k.dnbuf = []
        for si in range(4):
            B = {}
            for n in names:
                B[n] = p.sbuf(f"dn{si}_{n}", [128, 128])
                B["d_" + n] = Dep(f"dn{si}_{n}")
            for n in ("gb0", "gb1"):
                B[n] = p.sbuf(f"dn{si}_{n}", [128, 16]); B["d_" + n] = Dep(f"dn{si}_{n}")
            B["sc"] = p.sbuf(f"dn{si}_sc", [128, 8]); B["d_sc"] = Dep(f"dn{si}_sc")
            k.dnbuf.append(B)
    import os
    maxsteps = int(os.environ.get("DN_STEPS", "1000000000"))
    nstep = 0
    allsc = list(scans or [(h, d) for h in range(4) for d in range(2)])
    GRP = 4
    for g0 in range(0, len(allsc), GRP):
        gens = []
        for si, (h, d) in enumerate(allsc[g0:g0 + GRP]):
            gens.append(dn_scan_gen(k, l, si, h, d, dn_blocks(d, nlat), True))
        while gens and nstep < maxsteps:
            for g in list(gens):
                nstep += 1
                try:
                    next(g)
                except StopIteration:
                    gens.remove(g)


def stage_dn_post(k, l, toks=None):
    p = k.p
    OD, dOD = k.scratch(f"OD{l}", [2, 4, NT, 128])
    Z, dZ = k.scratch(f"Z{l}", [NT, 512])
    ODT, dODT = k.scratch(f"ODT{l}", [4, 128, NT])
    dnw, ddnw = k.const("dnw", [128, 128], k.inp[f"dnw{l}"])
    fr = Ring(p, "qf", [128, 4, 128], 2); br = Ring(p, "qb", [128, 4, 128], 2); zr = Ring(p, "qz", [128, 512], 2)
    sr = Ring(p, "qs", [128, 4, 128], 2); yr = Ring(p, "qy", [128, 4, 128], 2); tr_ = Ring(p, "qt", [128, 4, 128], 2)
    cr = Ring(p, "qc", [128, 8], 2)
    nb = 0
    for tok in (toks if toks is not None else range(0, NT, 128)):
        ft, df, sem = fr.next()
        p.dma("sp", ft[:], OD[0, :, tok:tok + 128, :].rearrange("h t d -> t h d"), reads=[dOD], writes=[df], sem=sem)
        bt, db, sem = br.next()
        p.dma("act", bt[:], OD[1, :, tok:tok + 128, :].rearrange("h t d -> t h d"), reads=[dOD], writes=[db], sem=sem)
        zt, dz, sem = zr.next()
        p.dma("sp", zt[:], Z[tok:tok + 128, :], reads=[dZ], writes=[dz], sem=sem)
        st, ds, _ = sr.next(); yt, dy, _ = yr.next(); ct, dc, _ = cr.next()
        p.op("dve", lambda e, st=st, ft=ft, bt=bt: e.tensor_tensor(out=st[:], in0=ft[:], in1=bt[:], op=ALU.add), reads=[df, db], writes=[ds])
        p.op("pool", lambda e, st=st, yt=yt: e.tensor_tensor(out=yt[:], in0=st[:], in1=st[:], op=ALU.mult), reads=[ds], writes=[dy])
        p.op("dve", lambda e, ct=ct, yt=yt: e.reduce_sum(out=ct[:, 0:4], in_=yt[:], axis=AX.X), reads=[dy], writes=[dc])
        p.op("dve", lambda e, ct=ct: e.tensor_scalar(out=ct[:, 0:4], in0=ct[:, 0:4], scalar1=1.0 / 128, scalar2=EPS, op0=ALU.mult, op1=ALU.add), reads=[dc], writes=[dc])
        p.op("act", lambda e, ct=ct: e.activation(out=ct[:, 0:4], in_=ct[:, 0:4], func=AF.Sqrt), reads=[dc], writes=[dc])
        p.op("dve", lambda e, ct=ct: e.reciprocal(out=ct[:, 0:4], in_=ct[:, 0:4]), reads=[dc], writes=[dc])
        p.op("act", lambda e, zt=zt: e.activation(out=zt[:], in_=zt[:], func=AF.Silu), reads=[dz], writes=[dz])
        for h in range(4):
            p.op("dve", lambda e, yt=yt, st=st, ct=ct, h=h: e.scalar_tensor_tensor(out=yt[:, h, :], in0=st[:, h, :], scalar=ct[:, h:h + 1], in1=dnw[:],
                                                                               op0=ALU.mult, op1=ALU.mult), reads=[ds, dc, ddnw, dy], writes=[dy])
        p.op("pool", lambda e, yt=yt, zt=zt: e.tensor_tensor(out=yt[:].rearrange("p h d -> p (h d)"), in0=yt[:].rearrange("p h d -> p (h d)"), in1=zt[:], op=ALU.mult),
             reads=[dz, dy], writes=[dy])
        b = nb % 2
        nb += 1
        ps = k.banks[b]
        for h in range(4):
            p.op("pe", lambda e, ps=ps, yt=yt, h=h: e.transpose(out=ps[:, h * 128:(h + 1) * 128], in_=yt[:, h, :], identity=k.ident[:]),
                 reads=[dy, k.dident], writes=[k.bdep[b]])
        tt, dt, _ = tr_.next()
        p.op("act", lambda e, tt=tt, ps=ps: e.activation(out=tt[:].rearrange("p h d -> p (h d)"), in_=ps[:, 0:512], func=AF.Copy), reads=[k.bdep[b]], writes=[dt])
        p.store("sp", ODT[:, :, tok:tok + 128].rearrange("h p t -> p h t"), tt[:], [dt], dODT)


def stage_mlp(k, l, xname, mod, dmod, tiles=None):
    p = k.p
    xsrc, dxs = k.scratch(xname, [8, 128, NT])
    XO, dXO = k.scratch(f"XO{l}", [8, 128, NT])
    ONT, dONT = k.scratch(f"ONT{l}", [4, 128, NT])
    ODT, dODT = k.scratch(f"ODT{l}", [4, 128, NT])
    OF, dOF = k.scratch(f"OF{l}", [NT, 512])
    GATE, dGATE = k.scratch(f"GATE{l}", [24, 128, NT])
    gam, dgam = k.const("norm2", [128, 8], k.inp[f"norm2_{l}"])
    make_gm(k, gam, dgam, mod, dmod, 4)
    xr = Ring(p, "mx", [128, 8, 256], 1); br_ = [Ring(p, f"mo{b}", [128, 4, 256], 1) for b in range(3)]
    gr = Ring(p, "mg", [128, 24, 256], 1); yr = Ring(p, "my", [128, 8, 256], 1); mr = Ring(p, "mm", [128, 8, 256], 1)
    tr_ = Ring(p, "mt", [128, 8, 256], 1); hr = Ring(p, "mh", [128, 8, 256], 1); ar = Ring(p, "ma", [128, 32, 256], 1)
    tmr = Ring(p, "mtm", [128, 256], 3); ofr = Ring(p, "mof", [128, 2, 512], 1); outr = Ring(p, "mout", [128, 256], 3)
    wmr = Ring(p, "wm", [128, 4, 128], 3); wor = Ring(p, "wo", [128, 8, 128], 2); w1r = Ring(p, "w1", [128, 8, 128], 3); w2r = Ring(p, "w2", [128, 32, 128], 2)
    wm_src = [k.inp[f"w_nao{l}"], k.inp[f"w_dno{l}"], k.inp[f"w_fno{l}"]]
    nb = 0
    nq = 0

    def wq():
        nonlocal nq
        nq += 1
        return "sp" if nq % 2 else "act"
    for (t0, N, col) in (tiles or super_tiles()):
        xt, dx, sem = xr.next()
        p.dma("sp", xt[:, :, 0:N], xsrc[:, :, t0:t0 + N].rearrange("k p t -> p k t"), reads=[dxs], writes=[dx], sem=sem)
        obs = []
        for b, (src, dsrc) in enumerate(((ONT, dONT), (ODT, dODT))):
            ot, do, sem = br_[b].next()
            p.dma("act", ot[:, :, 0:N], src[:, :, t0:t0 + N].rearrange("k p t -> p k t"), reads=[dsrc], writes=[do], sem=sem)
            obs.append((ot, do))
        oft, dof, sem = ofr.next()
        p.dma("sp", oft[:], OF[t0:t0 + N, :].rearrange("(a p) c -> p a c", p=128), reads=[dOF], writes=[dof], sem=sem)
        ot, do, _ = br_[2].next()
        for a in range(2):
            b_ = 6 + a
            for g in range(4):
                p.op("pe", lambda e, a=a, g=g, b_=b_, oft=oft: e.transpose(out=k.banks[b_][:, g * 128:(g + 1) * 128], in_=oft[:, a, g * 128:(g + 1) * 128], identity=k.ident[:]),
                     reads=[dof, k.dident], writes=[k.bdep[b_]])
            p.op("act", lambda e, a=a, b_=b_, ot=ot: e.activation(out=ot[:, :, a * 128:(a + 1) * 128], in_=k.banks[b_][:, 0:512].rearrange("p (g t) -> p g t", g=4), func=AF.Copy),
                 reads=[k.bdep[b_]], writes=[do])
        obs.append((ot, do))
        gt, dg, sem = gr.next()
        for q4 in range(4):
            p.dma(wq(), gt[:, q4 * 6:(q4 + 1) * 6, 0:N], GATE[q4 * 6:(q4 + 1) * 6, :, t0:t0 + N].rearrange("k p t -> p k t"), reads=[dGATE], writes=[dg], sem=sem)
        yt, dy, _ = yr.next()
        for c in range(8):
            for b in range(3):
                wt, dw, sem = wmr.next()
                p.dma(wq(), wt[:], wm_src[b][c], writes=[dw], sem=sem)
                bk = nb % 4
                nb += 1
                ps = k.banks[bk]
                ot, do = obs[b]
                for kk in range(4):
                    p.op("pe", lambda e, ps=ps, wt=wt, ot=ot, kk=kk: e.matmul(ps[:, 0:N], lhsT=wt[:, kk, :], rhs=ot[:, kk, 0:N], start=(kk == 0), stop=(kk == 3)),
                         reads=[dw, do], writes=[k.bdep[bk]])
                if b == 0:
                    p.op("dve", lambda e, ps=ps, yt=yt, gt=gt, c=c: e.tensor_tensor(out=yt[:, c, 0:N], in0=ps[:, 0:N], in1=gt[:, c, 0:N], op=ALU.mult),
                         reads=[k.bdep[bk], dg], writes=[dy])
                else:
                    tm, dtm, _ = tmr.next()
                    p.op("dve", lambda e, ps=ps, tm=tm, gt=gt, c=c, b=b: e.tensor_tensor(out=tm[:, 0:N], in0=ps[:, 0:N], in1=gt[:, b * 8 + c, 0:N], op=ALU.mult),
                         reads=[k.bdep[bk], dg], writes=[dtm])
                    p.op("pool", lambda e, tm=tm, yt=yt, c=c: e.tensor_tensor(out=yt[:, c, 0:N], in0=yt[:, c, 0:N], in1=tm[:, 0:N], op=ALU.add),
                         reads=[dtm, dy], writes=[dy])
        mt, dm, _ = mr.next()
        for c in range(8):
            wt, dw, sem = wor.next()
            p.dma(wq(), wt[:], k.inp[f"w_out{l}"][c], writes=[dw], sem=sem)
            bk = nb % 4
            nb += 1
            ps = k.banks[bk]
            for kk in range(8):
                p.op("pe", lambda e, ps=ps, wt=wt, yt=yt, kk=kk: e.matmul(ps[:, 0:N], lhsT=wt[:, kk, :], rhs=yt[:, kk, 0:N], start=(kk == 0), stop=(kk == 7)),
                     reads=[dw, dy], writes=[k.bdep[bk]])
            p.op("dve", lambda e, ps=ps, mt=mt, xt=xt, c=c, col=col: e.scalar_tensor_tensor(out=mt[:, c, 0:N], in0=ps[:, 0:N], scalar=mod[:, 16 + c, col:col + 1], in1=xt[:, c, 0:N],
                                                                                        op0=ALU.mult, op1=ALU.add), reads=[k.bdep[bk], dx, dmod], writes=[dm])
        ht, dh, _ = hr.next(); tmp, dtmp, _ = tr_.next()
        norm_mod(k, mt, dm, N, gam, dgam, mod, dmod, 3, 4, col, ht, dh, tmp, dtmp, k.ones_t, k.dones, 7)
        at, da, _ = ar.next()
        for j in range(32):
            wt, dw, sem = w1r.next()
            p.dma(wq(), wt[:], k.inp[f"w_mlp1_{l}"][j], writes=[dw], sem=sem)
            bk = nb % 4
            nb += 1
            ps = k.banks[bk]
            for kk in range(8):
                p.op("pe", lambda e, ps=ps, wt=wt, ht=ht, kk=kk: e.matmul(ps[:, 0:N], lhsT=wt[:, kk, :], rhs=ht[:, kk, 0:N], start=(kk == 0), stop=(kk == 7)),
                     reads=[dw, dh], writes=[k.bdep[bk]])
            p.op("act", lambda e, ps=ps, at=at, j=j: e.activation(out=at[:, j, 0:N], in_=ps[:, 0:N], func=AF.Relu), reads=[k.bdep[bk]], writes=[da])
            p.op("pool", lambda e, at=at, j=j: e.tensor_tensor(out=at[:, j, 0:N], in0=at[:, j, 0:N], in1=at[:, j, 0:N], op=ALU.mult), reads=[da], writes=[da])
        for c in range(8):
            wt, dw, sem = w2r.next()
            p.dma(wq(), wt[:], k.inp[f"w_mlp2_{l}"][c], writes=[dw], sem=sem)
            bk = nb % 4
            nb += 1
            ps = k.banks[bk]
            for j in range(32):
                p.op("pe", lambda e, ps=ps, wt=wt, at=at, j=j: e.matmul(ps[:, 0:N], lhsT=wt[:, j, :], rhs=at[:, j, 0:N], start=(j == 0), stop=(j == 31)),
                     reads=[dw, da], writes=[k.bdep[bk]])
            ot, do, _ = outr.next()
            p.op("dve", lambda e, ps=ps, ot=ot, mt=mt, c=c, col=col: e.scalar_tensor_tensor(out=ot[:, 0:N], in0=ps[:, 0:N], scalar=mod[:, 40 + c, col:col + 1], in1=mt[:, c, 0:N],
                                                                                        op0=ALU.mult, op1=ALU.add), reads=[k.bdep[bk], dm, dmod], writes=[do])
            p.store("sp", XO[c, :, t0:t0 + N], ot[:, 0:N], [do], dXO)


def stage_final(k, xname, tiles=None):
    p = k.p
    xsrc, dxs = k.scratch(xname, [8, 128, NT])
    y = k.nc.dram_tensor("y", [T, D], F32, kind="ExternalOutput").ap()
    k.ydep = Dep("y")
    gam, dgam = k.const("normf", [128, 8], k.inp["norm_f"])
    xr = Ring(p, "fx", [128, 8, 256], 2); tr_ = Ring(p, "ft", [128, 8, 256], 2); outr = Ring(p, "fo", [128, 1024], 2)
    nb = 0
    for (t0, N, col) in (tiles or super_tiles()[:-1]):
        xt, dx, sem = xr.next()
        p.dma("sp", xt[:, :, 0:N], xsrc[:, :, t0:t0 + N].rearrange("k p t -> p k t"), reads=[dxs], writes=[dx], sem=sem)
        tmp, dtmp, _ = tr_.next()
        p.op("act", lambda e, tmp=tmp, xt=xt: e.activation(out=tmp[:], in_=xt[:], func=AF.Square), reads=[dx], writes=[dtmp])
        ps = k.banks[7]
        for kk in range(8):
            p.op("pe", lambda e, kk=kk, tmp=tmp: e.matmul(ps[:, 0:N], lhsT=k.ones_t[:], rhs=tmp[:, kk, 0:N], start=(kk == 0), stop=(kk == 7)),
                 reads=[dtmp, k.dones], writes=[k.bdep[7]])
        rstd = k.rstd
        p.op("dve", lambda e: e.tensor_scalar(out=rstd[:, 0:N], in0=ps[:, 0:N], scalar1=EPS, scalar2=None, op0=ALU.add), reads=[k.bdep[7]], writes=[k.drstd])
        p.op("act", lambda e: e.activation(out=rstd[:, 0:N], in_=rstd[:, 0:N], func=AF.Sqrt), reads=[k.drstd], writes=[k.drstd])
        p.op("dve", lambda e: e.reciprocal(out=rstd[:, 0:N], in_=rstd[:, 0:N]), reads=[k.drstd], writes=[k.drstd])
        for kk in range(8):
            p.op("dve", lambda e, kk=kk, tmp=tmp, xt=xt: e.scalar_tensor_tensor(out=tmp[:, kk, 0:N], in0=xt[:, kk, 0:N], scalar=gam[:, kk:kk + 1], in1=rstd[:, 0:N],
                                                                               op0=ALU.mult, op1=ALU.mult), reads=[dx, k.drstd, dgam, dtmp], writes=[dtmp])
        for tb in range(N // 128):
            ot, do, _ = outr.next()
            for half in range(2):
                b = (nb % 3) * 2
                nb += 1
                b = b if False else (nb % 6)
                ps2 = k.banks[b]
                for q in range(4):
                    kk = half * 4 + q
                    p.op("pe", lambda e, ps2=ps2, tmp=tmp, kk=kk, q=q, tb=tb: e.transpose(out=ps2[:, q * 128:(q + 1) * 128], in_=tmp[:, kk, tb * 128:(tb + 1) * 128], identity=k.ident[:]),
                         reads=[dtmp, k.dident], writes=[k.bdep[b]])
                if half == 0:
                    p.op("act", lambda e, ps2=ps2, ot=ot: e.activation(out=ot[:, 0:512], in_=ps2[:, 0:512], func=AF.Copy), reads=[k.bdep[b]], writes=[do])
                else:
                    p.op("dve", lambda e, ps2=ps2, ot=ot: e.tensor_copy(out=ot[:, 512:1024], in_=ps2[:, 0:512]), reads=[k.bdep[b]], writes=[do])
            tok = t0 + tb * 128
            p.store("sp", y[tok:tok + 128, :], ot[:], [do], k.ydep)


def na_table(rpb):
    w = np.arange(64)
    cs = np.clip(w - 8, 0, 48)
    j = np.arange(64)
    inwin = (j[:, None] >= cs[None, :]) & (j[:, None] < cs[None, :] + 16)
    idx = np.clip(j[:, None] - w[None, :] + 15, 0, 30)
    tab = rpb[:, :, idx]
    tab = np.where(inwin[None, None], tab, np.float32(-30000.0))
    return np.ascontiguousarray(tab.transpose(2, 0, 1, 3)).astype(np.float32)


def stage_na(k, l, rows=None, do_ctx=True):
    p = k.p
    QT, dQT = k.scratch(f"QT{l}", [8, 64, NT])
    KT, dKT = k.scratch(f"KT{l}", [8, 64, NT])
    V, dV = k.scratch(f"V{l}", [NT, 512])
    ONT, dONT = k.scratch(f"ONT{l}", [4, 128, NT])
    eb, deb = k.const("natab", [64, 8 * 15 * 64], k.inp[f"natab{l}"].rearrange("j h o w -> j (h o w)"))
    for h in range(8):
        p.op("act", lambda e, h=h: e.activation(out=eb[:, h * 960:(h + 1) * 960], in_=eb[:, h * 960:(h + 1) * 960], func=AF.Exp), reads=[deb], writes=[deb])
    kc = p.sbuf("na_kc", [64, 8, 256]); dkc = Dep("na_kc")
    p.dma("sp", kc[:], KT[:, :, T:T + CT].rearrange("h d t -> d h t"), reads=[dKT], writes=[dkc], sem="ld_kc")
    vc = p.sbuf("na_vc", [128, 2, 8, 65]); dvc = Dep("na_vc")
    p.op("pool", lambda e: e.memset(vc[:], 1.0), writes=[dvc])
    for c in range(2):
        p.dma("act", vc[:, c, :, 0:64], V[T + c * 128:T + (c + 1) * 128, :].rearrange("t (h d) -> t h d", d=64), reads=[dV], writes=[dvc], sem="ld_vc")
    NS = 10
    kring = p.sbuf("na_kr", [64, NS, 8, 64]); vring = p.sbuf("na_vr", [64, NS, 8, 65])
    dkr = [Dep(f"na_kr{i}") for i in range(NS)]; dvr = [Dep(f"na_vr{i}") for i in range(NS)]
    p.op("pool", lambda e: e.memset(vring[:], 1.0), writes=dvr)
    qr = Ring(p, "na_q", [64, 8, 256], 2)
    er = Ring(p, "na_e", [64, 512], 3); ecr = Ring(p, "na_ec", [128, 128], 3)
    orr = Ring(p, "na_o", [64, 512], 2); otr = Ring(p, "na_ot", [128, 4, 64], 2); rcr = Ring(p, "na_rc", [64, 8], 2)
    loaded = -1
    it = 0
    qt = None
    rows = list(rows if rows is not None else range(256))
    for r in rows:
        rs = min(max(r - 4, 0), 248)
        while loaded < rs + 7:
            loaded += 1
            if loaded < rs:
                continue
            sl = loaded % NS
            p.dma("sp", kring[:, sl, :, :], KT[:, :, loaded * 64:(loaded + 1) * 64].rearrange("h d t -> d h t"), reads=[dKT], writes=[dkr[sl]], sem=f"ld_kr{sl}")
            p.dma("act", vring[:, sl, :, 0:64], V[loaded * 64:(loaded + 1) * 64, :].rearrange("t (h d) -> t h d", d=64), reads=[dV], writes=[dvr[sl]], sem=f"ld_vr{sl}")
        if qt is None or r % 4 == 0 or r == rows[0]:
            qt, dq, sem = qr.next()
            q0 = (r // 4) * 4
            p.dma("sp", qt[:], QT[:, :, q0 * 64:(q0 + 4) * 64].rearrange("h d t -> d h t"), reads=[dQT], writes=[dq], sem=sem)
        rr = r % 4
        o0 = rs - r + 7
        ot, do, _ = orr.next()
        rc, drc, _ = rcr.next()
        for h in range(8):
            bA, bB, bC = (it % 2), 2 + (it % 2), 4 + (it % 2)
            it += 1
            A, Bk, C = k.banks[bA], k.banks[bB], k.banks[bC]
            qv = qt[:, h, rr * 64:(rr + 1) * 64]
            for i in range(8):
                sl = (rs + i) % NS
                p.op("pe", lambda e, A=A, i=i, sl=sl, h=h, qv=qv: e.matmul(A[0:64, i * 64:(i + 1) * 64], lhsT=kring[:, sl, h, :], rhs=qv, start=True, stop=True),
                     reads=[dkr[sl], dq], writes=[k.bdep[bA]])
            for c in range(2):
                p.op("pe", lambda e, Bk=Bk, c=c, h=h, qv=qv: e.matmul(Bk[:, c * 64:(c + 1) * 64], lhsT=kc[:, h, c * 128:(c + 1) * 128], rhs=qv, start=True, stop=True),
                     reads=[dkc, dq], writes=[k.bdep[bB]])
            et, de, _ = er.next(); ect, dec, _ = ecr.next()
            p.op("act", lambda e, et=et, A=A: e.activation(out=et[:], in_=A[0:64, :], func=AF.Exp), reads=[k.bdep[bA]], writes=[de])
            p.op("act", lambda e, ect=ect, Bk=Bk: e.activation(out=ect[:], in_=Bk[:, 0:128], func=AF.Exp), reads=[k.bdep[bB]], writes=[dec])
            e0 = h * 960 + o0 * 64
            p.op("pool", lambda e, et=et, e0=e0: e.tensor_tensor(out=et[:], in0=et[:], in1=eb[:, e0:e0 + 512], op=ALU.mult), reads=[de, deb], writes=[de])
            for i in range(8):
                sl = (rs + i) % NS
                p.op("pe", lambda e, C=C, et=et, i=i, sl=sl, h=h: e.matmul(C[0:64, 0:65], lhsT=et[:, i * 64:(i + 1) * 64], rhs=vring[:, sl, h, :], start=(i == 0), stop=False),
                     reads=[de, dvr[sl]], writes=[k.bdep[bC]])
            for c in range(2):
                p.op("pe", lambda e, C=C, ect=ect, c=c, h=h: e.matmul(C[0:64, 0:65], lhsT=ect[:, c * 64:(c + 1) * 64], rhs=vc[:, c, h, :], start=False, stop=(c == 1)),
                     reads=[dec, dvc], writes=[k.bdep[bC]])
            p.op("dve", lambda e, C=C, rc=rc, h=h: e.reciprocal(out=rc[:, h:h + 1], in_=C[0:64, 64:65]), reads=[k.bdep[bC]], writes=[drc])
            p.op("dve", lambda e, C=C, rc=rc, ot=ot, h=h: e.tensor_scalar(out=ot[:, h * 64:(h + 1) * 64], in0=C[0:64, 0:64], scalar1=rc[:, h:h + 1], scalar2=None, op0=ALU.mult),
                 reads=[k.bdep[bC], drc], writes=[do])
        bT = 6 + (r % 2)
        for c in range(4):
            p.op("pe", lambda e, bT=bT, ot=ot, c=c: e.transpose(out=k.banks[bT][:, c * 64:(c + 1) * 64], in_=ot[:, c * 128:(c + 1) * 128], identity=k.ident[0:64, 0:64]),
                 reads=[do, k.dident], writes=[k.bdep[bT]])
        tt, dt, _ = otr.next()
        p.op("act", lambda e, bT=bT, tt=tt: e.activation(out=tt[:].rearrange("p c t -> p (c t)"), in_=k.banks[bT][:, 0:256], func=AF.Copy), reads=[k.bdep[bT]], writes=[dt])
        p.store("sp", ONT[:, :, r * 64:(r + 1) * 64].rearrange("c p t -> p c t"), tt[:], [dt], dONT)
    if not do_ctx:
        return
    qc = p.sbuf("na_qc", [64, 8, 256]); dqc = Dep("na_qc")
    p.dma("sp", qc[:], QT[:, :, T:T + CT].rearrange("h d t -> d h t"), reads=[dQT], writes=[dqc], sem="ld_qc")
    ocr = Ring(p, "na_oc", [128, 512], 2); otc = Ring(p, "na_otc", [128, 4, 128], 2); rcc = Ring(p, "na_rcc", [128, 8], 2)
    e2r = Ring(p, "na_e2", [128, 2, 128], 3)
    for t in range(2):
        ot, do, _ = ocr.next(); rc, drc, _ = rcc.next()
        for h in range(8):
            bA, bC = (it % 2), 4 + (it % 2)
            it += 1
            A, C = k.banks[bA], k.banks[bC]
            for c in range(2):
                p.op("pe", lambda e, A=A, c=c, h=h, t=t: e.matmul(A[:, c * 128:(c + 1) * 128], lhsT=kc[:, h, c * 128:(c + 1) * 128], rhs=qc[:, h, t * 128:(t + 1) * 128], start=True, stop=True),
                     reads=[dkc, dqc], writes=[k.bdep[bA]])
            et, de, _ = e2r.next()
            p.op("act", lambda e, et=et, A=A: e.activation(out=et[:].rearrange("p c t -> p (c t)"), in_=A[:, 0:256], func=AF.Exp), reads=[k.bdep[bA]], writes=[de])
            for c in range(2):
                p.op("pe", lambda e, C=C, et=et, c=c, h=h: e.matmul(C[:, 0:65], lhsT=et[:, c, :], rhs=vc[:, c, h, :], start=(c == 0), stop=(c == 1)),
                     reads=[de, dvc], writes=[k.bdep[bC]])
            p.op("dve", lambda e, C=C, rc=rc, h=h: e.reciprocal(out=rc[:, h:h + 1], in_=C[:, 64:65]), reads=[k.bdep[bC]], writes=[drc])
            p.op("dve", lambda e, C=C, rc=rc, ot=ot, h=h: e.tensor_scalar(out=ot[:, h * 64:(h + 1) * 64], in0=C[:, 0:64], scalar1=rc[:, h:h + 1], scalar2=None, op0=ALU.mult),
                 reads=[k.bdep[bC], drc], writes=[do])
        bT = 6 + t
        for c in range(4):
            p.op("pe", lambda e, bT=bT, ot=ot, c=c: e.transpose(out=k.banks[bT][:, c * 128:(c + 1) * 128], in_=ot[:, c * 128:(c + 1) * 128], identity=k.ident[:]),
                 reads=[do, k.dident], writes=[k.bdep[bT]])
        tt, dt, _ = otc.next()
        p.op("act", lambda e, bT=bT, tt=tt: e.activation(out=tt[:].rearrange("p c t -> p (c t)"), in_=k.banks[bT][:, 0:512], func=AF.Copy), reads=[k.bdep[bT]], writes=[dt])
        p.store("sp", ONT[:, :, T + t * 128:T + (t + 1) * 128].rearrange("c p t -> p c t"), tt[:], [dt], dONT)


def fn_tables():
    n = np.arange(128, dtype=np.float64)
    a = 2 * np.pi * np.outer(n, n) / 128.0
    c128, s128 = np.cos(a), np.sin(a)
    tw = 2 * np.pi * np.outer(n, n) / float(T)
    m = np.arange(256, dtype=np.float64)
    a2 = 2 * np.pi * np.outer(m, m) / 256.0
    c256, s256 = np.cos(a2), np.sin(a2)
    f = lambda x: np.ascontiguousarray(x.astype(np.float32))
    dft = np.stack([c128, s128, -c128, -s128], axis=1)
    twt = np.stack([np.cos(tw), np.sin(tw), -np.sin(tw)], axis=1)
    d256 = np.stack([c256, -s256], axis=0).reshape(2, 2, 128, 256).transpose(2, 0, 1, 3)
    return f(dft), f(twt), f(d256)


def stage_fn(k, l, do_ctx=True, t2s=None, t1s=None):
    p = k.p
    U, dU = k.scratch(f"U{l}", [4, 128, NT])
    PQ, dPQ = k.scratch(f"PQ{l}", [NT, 1024])
    YD, dYD = k.scratch(f"YD{l}", [128, 128, 1024])
    OF, dOF = k.scratch(f"OF{l}", [NT, 512])
    dft, ddft = k.const("dft", [128, 4, 128], k.inp["fn_dft"])
    twt, dtw = k.const("twt", [128, 3, 128], k.inp["fn_tw"])
    C, S, NC_, NS_ = (dft[:, i, :] for i in range(4))
    ur = Ring(p, "fu", [128, 4, 128], 2); pqr = Ring(p, "fpq", [128, 1024], 2)
    nb = 0
    for tok in range(0, NT if do_ctx else T, 128):
        ut, du, sem = ur.next()
        p.dma("sp", ut[:], U[:, :, tok:tok + 128].rearrange("g c t -> c g t"), reads=[dU], writes=[du], sem=sem)
        b0 = (nb % 2) * 2
        nb += 1
        for g in range(4):
            p.op("pe", lambda e, ut=ut, g=g, b0=b0: e.matmul(k.banks[b0][:, g * 128:(g + 1) * 128], lhsT=ut[:, g, :], rhs=C, start=True, stop=True),
                 reads=[du, ddft], writes=[k.bdep[b0]])
            p.op("pe", lambda e, ut=ut, g=g, b0=b0: e.matmul(k.banks[b0 + 1][:, g * 128:(g + 1) * 128], lhsT=ut[:, g, :], rhs=S, start=True, stop=True),
                 reads=[du, ddft], writes=[k.bdep[b0 + 1]])
        pt, dp, _ = pqr.next()
        p.op("act", lambda e, pt=pt, b0=b0: e.activation(out=pt[:, 0:512], in_=k.banks[b0][:, 0:512], func=AF.Copy), reads=[k.bdep[b0]], writes=[dp])
        p.op("dve", lambda e, pt=pt, b0=b0: e.tensor_copy(out=pt[:, 512:1024], in_=k.banks[b0 + 1][:, 0:512]), reads=[k.bdep[b0 + 1]], writes=[dp])
        p.store("sp", PQ[tok:tok + 128, :], pt[:], [dp], dPQ)
    xr = Ring(p, "fx", [128, 1024], 2); yr = Ring(p, "fy", [128, 1024], 2); tr_ = Ring(p, "ftm", [128, 2, 512], 2)
    PQv = PQ[0:T, :].rearrange("(t1 t2) c -> t2 t1 c", t2=128)
    for t2 in (t2s if t2s is not None else range(128)):
        xt, dx, sem = xr.next()
        for q4 in range(4):
            p.dma("sp" if q4 % 2 == 0 else "act", xt[:, q4 * 256:(q4 + 1) * 256], PQv[t2, :, q4 * 256:(q4 + 1) * 256], reads=[dPQ], writes=[dx], sem=sem)
        b0 = (nb % 2) * 2
        nb += 1
        Yr, Yi = k.banks[b0], k.banks[b0 + 1]
        p.op("pe", lambda e, Yr=Yr, xt=xt: e.matmul(Yr[:, 0:512], lhsT=C, rhs=xt[:, 0:512], start=True, stop=False), reads=[dx, ddft], writes=[k.bdep[b0]])
        p.op("pe", lambda e, Yr=Yr, xt=xt: e.matmul(Yr[:, 0:512], lhsT=NS_, rhs=xt[:, 512:1024], start=False, stop=True), reads=[dx, ddft], writes=[k.bdep[b0]])
        p.op("pe", lambda e, Yi=Yi, xt=xt: e.matmul(Yi[:, 0:512], lhsT=NC_, rhs=xt[:, 512:1024], start=True, stop=False), reads=[dx, ddft], writes=[k.bdep[b0 + 1]])
        p.op("pe", lambda e, Yi=Yi, xt=xt: e.matmul(Yi[:, 0:512], lhsT=NS_, rhs=xt[:, 0:512], start=False, stop=True), reads=[dx, ddft], writes=[k.bdep[b0 + 1]])
        tm, dtm, _ = tr_.next(); yt, dy, _ = yr.next()
        cf, sf, nsf = twt[:, 0, t2:t2 + 1], twt[:, 1, t2:t2 + 1], twt[:, 2, t2:t2 + 1]
        p.op("dve", lambda e, tm=tm, Yr=Yr, cf=cf: e.tensor_scalar(out=tm[:, 0, :], in0=Yr[:, 0:512], scalar1=cf, scalar2=None, op0=ALU.mult), reads=[k.bdep[b0], dtw], writes=[dtm])
        p.op("dve", lambda e, tm=tm, Yi=Yi, cf=cf: e.tensor_scalar(out=tm[:, 1, :], in0=Yi[:, 0:512], scalar1=cf, scalar2=None, op0=ALU.mult), reads=[k.bdep[b0 + 1], dtw], writes=[dtm])
        p.op("dve", lambda e, tm=tm, yt=yt, Yi=Yi, sf=sf: e.scalar_tensor_tensor(out=yt[:, 0:512], in0=Yi[:, 0:512], scalar=sf, in1=tm[:, 0, :], op0=ALU.mult, op1=ALU.add),
             reads=[k.bdep[b0 + 1], dtw, dtm], writes=[dy])
        p.op("dve", lambda e, tm=tm, yt=yt, Yr=Yr, nsf=nsf: e.scalar_tensor_tensor(out=yt[:, 512:1024], in0=Yr[:, 0:512], scalar=nsf, in1=tm[:, 1, :], op0=ALU.mult, op1=ALU.add),
             reads=[k.bdep[b0], dtw, dtm], writes=[dy])
        for q4 in range(4):
            p.store("sp", YD[:, t2, q4 * 256:(q4 + 1) * 256], yt[:, q4 * 256:(q4 + 1) * 256], [dy], dYD)
    sc_lat = 1.0 / math.sqrt(T * 128.0)
    zr = Ring(p, "fz", [128, 1024], 2); orr = Ring(p, "fo", [128, 512], 2)
    OFv = OF[0:T, :].rearrange("(b a) c -> a b c", a=128)
    for t1 in (t1s if t1s is not None else range(128)):
        zt, dz, sem = zr.next()
        p.dma("sp", zt[:], YD[t1], reads=[dYD], writes=[dz], sem=sem)
        b = 4 + (nb % 2)
        nb += 1
        ps = k.banks[b]
        p.op("pe", lambda e, ps=ps, zt=zt: e.matmul(ps[:, 0:512], lhsT=C, rhs=zt[:, 0:512], start=True, stop=False), reads=[dz, ddft], writes=[k.bdep[b]])
        p.op("pe", lambda e, ps=ps, zt=zt: e.matmul(ps[:, 0:512], lhsT=S, rhs=zt[:, 512:1024], start=False, stop=True), reads=[dz, ddft], writes=[k.bdep[b]])
        ot, do, _ = orr.next()
        p.op("act", lambda e, ps=ps, ot=ot: e.activation(out=ot[:], in_=ps[:, 0:512], func=AF.Copy, scale=sc_lat), reads=[k.bdep[b]], writes=[do])
        for q2 in range(2):
            p.store("sp", OFv[t1, :, q2 * 256:(q2 + 1) * 256], ot[:, q2 * 256:(q2 + 1) * 256], [do], dOF)
    if not do_ctx:
        return
    d256, dd256 = k.const("d256", [128, 2, 2, 256], k.inp["fn_d256"])
    sc_ctx = 1.0 / math.sqrt(256.0 * 128.0)
    cx = p.sbuf("fcx", [128, 2, 1024]); dcx = Dep("fcx")
    p.dma("sp", cx[:], PQ[T:T + 256, :].rearrange("(a p) c -> p a c", p=128), reads=[dPQ], writes=[dcx], sem="ld_fcx")
    for tt_ in range(2):
        b = 4 + (nb % 2)
        nb += 1
        ps = k.banks[b]
        n_ = 0
        for cs_ in range(2):
            for kt in range(2):
                p.op("pe", lambda e, ps=ps, cs_=cs_, kt=kt, tt_=tt_, n_=n_: e.matmul(ps[:, 0:512], lhsT=d256[:, cs_, kt, tt_ * 128:(tt_ + 1) * 128],
                                                                                 rhs=cx[:, kt, cs_ * 512:(cs_ + 1) * 512], start=(n_ == 0), stop=(n_ == 3)),
                     reads=[dcx, dd256], writes=[k.bdep[b]])
                n_ += 1
        ot, do, _ = orr.next()
        p.op("act", lambda e, ps=ps, ot=ot: e.activation(out=ot[:], in_=ps[:, 0:512], func=AF.Copy, scale=sc_ctx), reads=[k.bdep[b]], writes=[do])
        p.store("sp", OF[T + tt_ * 128:T + (tt_ + 1) * 128, :], ot[:], [do], dOF)


def build_program(shapes, debug=None):
    nc = bass.Bass("TRN2", target_bir_lowering=False)
    p = Prog(nc)
    k = K(nc, p, shapes)
    common(k)
    if debug is not None:
        fin = debug(k)
        p.build(final_deps=fin)
        return nc
    for l in range(2):
        xname = "xin" if l == 0 else "XO0"
        mod, dmod = k.modt[l]
        lat = super_tiles()[:-1]
        for fn in (lambda: stage_mod(k, l),
                   lambda: stage_proj(k, l, xname, mod, dmod),
                   lambda: stage_dn_pre(k, l),
                   lambda: stage_dn_scan(k, l),
                   lambda: stage_dn_post(k, l, toks=None if l == 0 else range(0, T, 128)),
                   lambda: stage_na(k, l, do_ctx=(l == 0)),
                   lambda: stage_fn(k, l, do_ctx=(l == 0)),
                   lambda: stage_mlp(k, l, xname, mod, dmod, tiles=None if l == 0 else lat)):
            p.scope_begin()
            fn()
            p.scope_end()
    p.scope_begin()
    stage_final(k, "XO1")
    p.scope_end()
    p.build(final_deps=[k.ydep])
    return nc


def kernel(**inputs):
    inp = {k_: np.asarray(v, np.float32) for k_, v in inputs.items()}
    m = host_layout(inp)
    shapes = {k_: v.shape for k_, v in m.items()}
    nc = build_program(shapes)
    res = run_bass_kernel_spmd(nc, [m], core_ids=[0])
    y = np.asarray(res.results[0]["y"], np.float32)
    return y.reshape(1, T, D)
```

```python
import math
import numpy as np
import concourse.bass as bass
import concourse.mybir as mybir
from concourse.bass_utils import run_bass_kernel_spmd

F32 = mybir.dt.float32
AF = mybir.ActivationFunctionType
ALU = mybir.AluOpType
AX = mybir.AxisListType

D = 1024
T = 16384
CT = 256
NT = T + CT
GW = 64
EPS = 1e-6
NTAB = 15


class Dep:
    __slots__ = ("name", "w", "r")

    def __init__(self, name=""):
        self.name = name
        self.w = None
        self.r = []


class Prog:
    ENGS = ("pe", "act", "dve", "pool", "sp")

    def __init__(self, nc, self_sync=True):
        self.nc = nc
        self.q = {e: [] for e in self.ENGS}
        self.cnt = {}
        self.known = {e: {} for e in self.ENGS}
        self.sems = {}
        self.self_sync = self_sync
        self._uid = 0
        self._stack = []
        self._semstack = []
        self._smap = {}
        self._scope_mark = 0
        self.n_inst = 0
        for e in self.ENGS:
            self._mksem("E_" + e)

    def _mksem(self, key):
        cm = self.nc.semaphore(key)
        h = cm.__enter__()
        self._semstack.append(cm)
        self.sems[key] = h
        self.cnt[key] = 0
        return key

    _mksem_global = _mksem

    def sbuf(self, name, shape, dt=F32):
        self._uid += 1
        cm = self.nc.sbuf_tensor(f"{name}_u{self._uid}", list(shape), dt)
        t = cm.__enter__()
        self._stack.append(cm)
        return t

    def psum(self, name, shape, dt=F32):
        cm = self.nc.psum_tensor(name, list(shape), dt)
        t = cm.__enter__()
        self._stack.append(cm)
        return t

    def close(self):
        while self._stack:
            self._stack.pop().__exit__(None, None, None)
        while self._semstack:
            self._semstack.pop().__exit__(None, None, None)

    def _waits(self, eng, reads, writes):
        need = {}

        def add(ev):
            if ev is None:
                return
            if isinstance(ev, dict):
                for k, v in ev.items():
                    if need.get(k, 0) < v:
                        need[k] = v
                return
            k, v = ev
            if need.get(k, 0) < v:
                need[k] = v
        for d in reads:
            add(d.w)
        for d in writes:
            add(d.w)
            for ev in d.r:
                add(ev)
        out = []
        own = "E_" + eng
        for k, v in need.items():
            if k == own and (eng == "pe" or not self.self_sync):
                continue
            if self.known[eng].get(k, 0) >= v:
                continue
            self.known[eng][k] = v
            out.append((k, v))
        return out

    def _emit(self, eng, fn, reads, writes, semkey, inc, track_w=True):
        waits = self._waits(eng, reads, writes if track_w else [])
        self.cnt[semkey] += inc
        ev = (semkey, self.cnt[semkey])
        for d in writes:
            d.w = ev
            d.r = []
        for d in reads:
            if d not in writes:
                d.r.append(ev)
                if len(d.r) > 64:
                    d.r = d.r[-64:]
        self.q[eng].append((waits, fn, semkey, inc))
        self.n_inst += 1 + len(waits)

    def scope_begin(self):
        self.barrier()
        self._scope_mark = len(self._stack)
        self._smap = {}

    def scope_end(self):
        self.barrier()
        while len(self._stack) > self._scope_mark:
            self._stack.pop().__exit__(None, None, None)
        self._smap = {}

    def barrier(self):
        for eng in self.ENGS:
            waits = []
            for key, v in self.cnt.items():
                if v == 0 or key == "E_" + eng:
                    continue
                if self.known[eng].get(key, 0) >= v:
                    continue
                self.known[eng][key] = v
                waits.append((key, v))
            if waits:
                self.q[eng].append((waits, None, None, 0))
                self.n_inst += len(waits)

    def dsem(self, key):
        m = self._smap
        if key not in m:
            phys = f"dma{len(m)}"
            if phys not in self.sems:
                self._mksem_global(phys)
            m[key] = phys
        return m[key]

    def op(self, eng, fn, reads=(), writes=()):
        self._emit(eng, fn, list(reads), list(writes), "E_" + eng, 1)

    def dma(self, eng, out, in_, reads=(), writes=(), sem=None, track_w=True, **kw):
        sem = self.dsem(sem)
        self._emit(eng, lambda e: e.dma_start(out=out, in_=in_, **kw), list(reads), list(writes), sem, 16,
                   track_w=track_w)

    def store(self, eng, out, in_, reads, ddep, **kw):
        sem = "st_" + reads[0].name
        self.dma(eng, out, in_, reads=reads, writes=[], sem=sem, **kw)
        sem = self.dsem(sem)
        if not isinstance(ddep.w, dict):
            ddep.w = {}
        ddep.w[sem] = self.cnt[sem]

    def build(self, final_deps=()):
        nc = self.nc
        fw = self._waits("sp", list(final_deps), [])
        sems = self.sems
        q = self.q

        def replay(eh, items, extra=()):
            for waits, fn, semkey, inc in items:
                for k, v in waits:
                    eh.wait_ge(sems[k], v)
                if fn is not None:
                    fn(eh).then_inc(sems[semkey], inc)
            for k, v in extra:
                eh.wait_ge(sems[k], v)

        with nc.Block() as block:
            @block.tensor
            def _(e):
                replay(e, q["pe"])

            @block.scalar
            def _(e):
                replay(e, q["act"])

            @block.vector
            def _(e):
                replay(e, q["dve"])

            @block.gpsimd
            def _(e):
                replay(e, q["pool"])

            @block.sync
            def _(e):
                replay(e, q["sp"], fw)
        self.close()


class Ring:
    def __init__(self, p, name, shape, n=2):
        self.t = [p.sbuf(f"{name}{i}", shape) for i in range(n)]
        self.d = [Dep(f"{name}{i}") for i in range(n)]
        self.i = 0
        self.n = n
        self.name = name

    def next(self):
        i = self.i
        self.i = (i + 1) % self.n
        return self.t[i], self.d[i], f"ld_{self.name}{i}"


_o = 0
COLS = {}
for _n, _s in (("na_k", 512), ("na_v", 512), ("dn_k", 512), ("dn_v", 512), ("dn_a", 8), ("dn_b", 8),
               ("na_q", 512), ("dn_q", 512), ("dn_z", 512), ("fn_u", 512), ("gate", 3072)):
    COLS[_n] = (_o, _o + _s)
    _o += _s
IN_W = _o
FB_GROUPS = (("na_q", 4), ("na_k", 4), ("dn_q", 4), ("dn_k", 4), ("dn_v", 4), ("fn_u", 4), ("gate", 24))
FB_CHUNKS = [(g, i) for g, n in FB_GROUPS for i in range(n)]
NFB = len(FB_CHUNKS)
FA_COLS = np.concatenate([np.arange(*COLS["na_v"]), np.arange(*COLS["dn_z"]),
                          np.arange(*COLS["dn_a"]), np.arange(*COLS["dn_b"])])


def _fm(v, nchunk):
    return np.ascontiguousarray(np.asarray(v, np.float32).reshape(nchunk, 128).T)


def _lhs_chunks(w, cols):
    K = w.shape[0]
    sub = w[:, cols]
    nc_ = sub.shape[1] // 128
    a = sub.reshape(K // 128, 128, nc_, 128)
    return np.ascontiguousarray(a.transpose(2, 1, 0, 3))


def _rhs_rows(w):
    K, N = w.shape
    return np.ascontiguousarray(w.reshape(K // 128, 128, N).transpose(1, 0, 2))


def host_layout(inp):
    m = {}
    x = np.concatenate([inp["x"][0], inp["ctx"][0]], axis=0)
    m["xin"] = np.ascontiguousarray(x.T.reshape(8, 128, NT))
    m["cvec"] = np.ascontiguousarray(np.stack([_fm(inp["c"][0], 8), _fm(inp["c_ctx"], 8)], axis=2))
    for l in range(2):
        w_in = inp["w_in"][l]
        fbcols = np.concatenate([np.arange(COLS[g][0] + i * 128, COLS[g][0] + (i + 1) * 128) for g, i in FB_CHUNKS])
        m[f"w_inb{l}"] = _lhs_chunks(w_in, fbcols)
        m[f"w_ina{l}"] = _rhs_rows(w_in[:, FA_COLS])
        m[f"w_ada{l}"] = _lhs_chunks(inp["w_ada"][l], np.arange(6 * D))
        m[f"b_ada{l}"] = _fm(inp["b_ada"][l], 48)
        m[f"norm1_{l}"] = _fm(inp["norm1"][l], 8)
        m[f"norm2_{l}"] = _fm(inp["norm2"][l], 8)
        ab = np.concatenate([inp["a_log"][l].reshape(-1), inp["dt_bias"][l].reshape(-1)])
        m[f"abt{l}"] = np.ascontiguousarray(np.broadcast_to(ab[None, :], (128, 16))).astype(np.float32)
        m[f"convw{l}"] = np.ascontiguousarray(inp["conv_w"][l].T.reshape(12, 128, 5).transpose(1, 0, 2))
    cos, sin, rm = rope_tables()
    m["rope_cos"], m["rope_sin"], m["rope_rm"] = cos, sin, rm
    mk = dn_masks()
    m["dnmask0"], m["dnmask1"] = mk[0], mk[1]
    for l in range(2):
        ar = np.arange(D)
        m[f"w_nao{l}"] = _lhs_chunks(inp["w_na_o"][l], ar)
        m[f"w_dno{l}"] = _lhs_chunks(inp["w_dn_o"][l], ar)
        m[f"w_fno{l}"] = _lhs_chunks(inp["w_fn"][l], ar)
        m[f"w_out{l}"] = _lhs_chunks(inp["w_out"][l], ar)
        m[f"w_mlp1_{l}"] = _lhs_chunks(inp["w_mlp1"][l], np.arange(4 * D))
        m[f"w_mlp2_{l}"] = _lhs_chunks(inp["w_mlp2"][l], ar)
        m[f"dnw{l}"] = np.ascontiguousarray(np.broadcast_to(inp["dn_norm"][l][None, :], (128, 128))).astype(np.float32)
        m[f"natab{l}"] = na_table(inp["rpb"][l])
    m["fn_dft"], m["fn_tw"], m["fn_d256"] = fn_tables()
    m["norm_f"] = _fm(inp["norm_f"], 8)
    ident = np.eye(128, dtype=np.float32)
    m["ident"] = ident
    return m


class K:
    def __init__(self, nc, p, shapes):
        self.nc = nc
        self.p = p
        self.inp = {k: nc.dram_tensor(k, list(v), F32, kind="ExternalInput").ap() for k, v in shapes.items()}
        self.scr = {}
        self.sdep = {}
        self.banks = [p.psum(f"bank{i}", [128, 512]) for i in range(8)]
        self.bdep = [Dep(f"bank{i}") for i in range(8)]
        self.cdep = Dep("consts")
        self._ld = 0

    def scratch(self, name, shape, kind="Internal"):
        if name in self.inp and name not in self.scr:
            self.scr[name] = self.inp[name]
            self.sdep[name] = Dep(name)
        if name not in self.scr:
            self.scr[name] = self.nc.dram_tensor(name, list(shape), F32, kind=kind).ap()
            self.sdep[name] = Dep(name)
        return self.scr[name], self.sdep[name]

    def const(self, name, shape, src, eng="sp"):
        t = self.p.sbuf("c_" + name, shape)
        d = Dep("c_" + name)
        self.p.dma(eng, t[:], src, writes=[d], sem="ld_c_" + name)
        return t, d


def stage_mod(k, l):
    p = k.p
    cv, dcv = k.const("cvec", [128, 8, 2], k.inp["cvec"])
    cs = p.sbuf("cs", [128, 8, 2])
    dcs = Dep("cs")
    p.op("act", lambda e: e.activation(out=cs[:], in_=cv[:], func=AF.Silu), reads=[dcv], writes=[dcs])
    mod, dmod = k.modt[l]
    bt, dbt = k.const(f"b_ada{l}", [128, 48], k.inp[f"b_ada{l}"])
    ring = Ring(p, f"wada{l}_", [128, 8, 128], 3)
    wsrc = k.inp[f"w_ada{l}"]
    for j in range(48):
        wt, dw, sem = ring.next()
        p.dma("sp", wt[:], wsrc[j], writes=[dw], sem=sem)
        b = j % 2
        ps = k.banks[b]
        for kk in range(8):
            p.op("pe", lambda e, wt=wt, kk=kk, ps=ps: e.matmul(ps[:, 0:2], lhsT=wt[:, kk, :], rhs=cs[:, kk, :],
                                                              start=(kk == 0), stop=(kk == 7)),
                 reads=[dw, dcs], writes=[k.bdep[b]])
        p.op("dve", lambda e, ps=ps, j=j: e.tensor_scalar(out=mod[:, j, :], in0=ps[:, 0:2], scalar1=bt[:, j:j + 1],
                                                        scalar2=None, op0=ALU.add),
             reads=[k.bdep[b], dbt], writes=[dmod])
    return mod, dmod


def norm_mod(k, xt, dx, N, gam, dgam, mod, dmod, sh_j, sc_j, col, out, dout, tmp, dtmp, ones_t, dones, bank):
    p = k.p
    ps = k.banks[bank]
    p.op("act", lambda e: e.activation(out=tmp[:, :, 0:N], in_=xt[:, :, 0:N], func=AF.Square), reads=[dx], writes=[dtmp])
    for kk in range(8):
        p.op("pe", lambda e, kk=kk: e.matmul(ps[:, 0:N], lhsT=ones_t[:], rhs=tmp[:, kk, 0:N], start=(kk == 0), stop=(kk == 7)),
             reads=[dtmp, dones], writes=[k.bdep[bank]])
    rstd = k.rstd
    p.op("dve", lambda e: e.tensor_scalar(out=rstd[:, 0:N], in0=ps[:, 0:N], scalar1=EPS, scalar2=None, op0=ALU.add),
         reads=[k.bdep[bank]], writes=[k.drstd])
    p.op("act", lambda e: e.activation(out=rstd[:, 0:N], in_=rstd[:, 0:N], func=AF.Sqrt), reads=[k.drstd], writes=[k.drstd])
    p.op("dve", lambda e: e.reciprocal(out=rstd[:, 0:N], in_=rstd[:, 0:N]), reads=[k.drstd], writes=[k.drstd])
    for kk in range(8):
        eng = "dve" if kk % 2 == 0 else "pool"
        p.op(eng, lambda e, kk=kk: e.tensor_tensor(out=tmp[:, kk, 0:N], in0=xt[:, kk, 0:N], in1=rstd[:, 0:N], op=ALU.mult),
             reads=[dx, k.drstd], writes=[dtmp])
    gm = k.gm
    for kk in range(8):
        p.op("act", lambda e, kk=kk: e.activation(out=out[:, kk, 0:N], in_=tmp[:, kk, 0:N], func=AF.Identity,
                                                  bias=mod[:, sh_j * 8 + kk, col:col + 1], scale=gm[:, kk, col:col + 1]),
             reads=[dtmp, k.dgm, dmod], writes=[dout])


def make_gm(k, gam, dgam, mod, dmod, sc_j):
    p = k.p
    gm = k.gm
    for col in range(2):
        p.op("dve", lambda e, col=col: e.scalar_tensor_tensor(out=gm[:, :, col], in0=mod[:, sc_j * 8:sc_j * 8 + 8, col], scalar=1.0,
                                                              in1=gam[:, :], op0=ALU.add, op1=ALU.mult),
             reads=[dmod, dgam], writes=[k.dgm])


def super_tiles():
    st = [(i * 256, 256, 0) for i in range(64)]
    st.append((T, 256, 1))
    return st


def stage_proj(k, l, xname, mod, dmod, tiles=None):
    p = k.p
    xsrc, dxs = k.scratch(xname, [8, 128, NT])
    QT, dQT = k.scratch(f"QT{l}", [8, 64, NT])
    KT, dKT = k.scratch(f"KT{l}", [8, 64, NT])
    V, dV = k.scratch(f"V{l}", [NT, 512])
    DQ, dDQ = k.scratch(f"DQ{l}", [4, 128, NT])
    DK, dDK = k.scratch(f"DK{l}", [4, 128, NT])
    DV, dDV = k.scratch(f"DV{l}", [4, 128, NT])
    Z, dZ = k.scratch(f"Z{l}", [NT, 512])
    GB, dGB = k.scratch(f"GB{l}", [NT, 16])
    U, dU = k.scratch(f"U{l}", [4, 128, NT])
    GATE, dGATE = k.scratch(f"GATE{l}", [24, 128, NT])
    gam, dgam = k.const(f"norm1_{l}", [128, 8], k.inp[f"norm1_{l}"])
    abt, dabt = k.const(f"abt{l}", [128, 16], k.inp[f"abt{l}"])
    nexp = p.sbuf(f"nexp{l}", [128, 8])
    dnexp = Dep("nexp")
    p.op("act", lambda e: e.activation(out=nexp[:], in_=abt[:, 0:8], func=AF.Exp), reads=[dabt], writes=[dnexp])
    p.op("dve", lambda e: e.tensor_scalar(out=nexp[:], in0=nexp[:], scalar1=-1.0, scalar2=None, op0=ALU.mult), reads=[dnexp], writes=[dnexp])
    make_gm(k, gam, dgam, mod, dmod, 1)
    wa, dwa = k.const(f"w_ina{l}", [128, 8, 1040], k.inp[f"w_ina{l}"], eng="act")
    xr = Ring(p, f"px{l}_", [128, 8, 256], 2)
    hr = Ring(p, f"ph{l}_", [128, 8, 256], 4)
    tr = Ring(p, f"pt{l}_", [128, 8, 256], 1)
    wr = Ring(p, f"pw{l}_", [128, 8, 128], 3)
    orr = Ring(p, f"po{l}_", [128, 512], 4)
    gr = Ring(p, f"pg{l}_", [128, 48], 2)
    wsrc = k.inp[f"w_inb{l}"]
    fb_dst = {"na_q": None, "na_k": None, "dn_q": (DQ, dDQ), "dn_k": (DK, dDK), "dn_v": (DV, dDV), "fn_u": (U, dU), "gate": (GATE, dGATE)}
    nbank = 0
    tl = list(tiles or super_tiles())
    pairs = [tl[i:i + 4] for i in range(0, len(tl), 4)]
    for pair in pairs:
      hts = []
      for (t0, N, col) in pair:
        xt, dx, sem = xr.next()
        p.dma("sp", xt[:, :, 0:N], xsrc[:, :, t0:t0 + N].rearrange("k p t -> p k t"), reads=[dxs], writes=[dx], sem=sem)
        ht, dh, _ = hr.next()
        tmp, dtmp, _ = tr.next()
        norm_mod(k, xt, dx, N, gam, dgam, mod, dmod, 0, 1, col, ht, dh, tmp, dtmp, k.ones_t, k.dones, 7)
        hts.append((ht, dh))
      for c, (g, gi) in enumerate(FB_CHUNKS):
        wt, dw, sem = wr.next()
        p.dma("sp" if c % 2 == 0 else "act", wt[:], wsrc[c], writes=[dw], sem=sem)
        for (t0, N, col), (ht, dh) in zip(pair, hts):
            b = nbank % 4
            nbank += 1
            ps = k.banks[b]
            for kk in range(8):
                p.op("pe", lambda e, wt=wt, kk=kk, ps=ps, ht=ht, N=N: e.matmul(ps[:, 0:N], lhsT=wt[:, kk, :], rhs=ht[:, kk, 0:N],
                                                                         start=(kk == 0), stop=(kk == 7)),
                     reads=[dw, dh], writes=[k.bdep[b]])
            ot, do, _ = orr.next()
            if g == "gate":
                p.op("act", lambda e, ot=ot, ps=ps, N=N: e.activation(out=ot[:, 0:N], in_=ps[:, 0:N], func=AF.Sigmoid),
                     reads=[k.bdep[b]], writes=[do])
            elif g == "na_q":
                p.op("act", lambda e, ot=ot, ps=ps, N=N: e.activation(out=ot[:, 0:N], in_=ps[:, 0:N], func=AF.Copy, scale=0.125),
                     reads=[k.bdep[b]], writes=[do])
            else:
                p.op("dve", lambda e, ot=ot, ps=ps, N=N: e.tensor_copy(out=ot[:, 0:N], in_=ps[:, 0:N]), reads=[k.bdep[b]], writes=[do])
            if g in ("na_q", "na_k"):
                dst, dd = (QT, dQT) if g == "na_q" else (KT, dKT)
                p.store("sp", dst[2 * gi:2 * gi + 2, :, t0:t0 + N].rearrange("h d t -> (h d) t"), ot[:, 0:N], [do], dd)
            else:
                dst, dd = fb_dst[g]
                p.store("sp", dst[gi, :, t0:t0 + N], ot[:, 0:N], [do], dd)
      for (t0, N, col), (ht, dh) in zip(pair, hts):
        for tb in range(N // 128):
            tok = t0 + tb * 128
            for gi, (c0, cn, dst, dd) in enumerate(((0, 512, V, dV), (512, 512, Z, dZ), (1024, 16, None, None))):
                b = nbank % 4
                nbank += 1
                ps = k.banks[b]
                for kk in range(8):
                    p.op("pe", lambda e, kk=kk, ps=ps, ht=ht, tb=tb, c0=c0, cn=cn: e.matmul(
                        ps[:, 0:cn], lhsT=ht[:, kk, tb * 128:(tb + 1) * 128], rhs=wa[:, kk, c0:c0 + cn], start=(kk == 0), stop=(kk == 7)),
                        reads=[dwa, dh], writes=[k.bdep[b]])
                if dst is not None:
                    ot, do, _ = orr.next()
                    p.op("act" if gi == 0 else "dve", (lambda e, ot=ot, ps=ps: e.activation(out=ot[:], in_=ps[:], func=AF.Copy)) if gi == 0 else
                         (lambda e, ot=ot, ps=ps: e.tensor_copy(out=ot[:], in_=ps[:])), reads=[k.bdep[b]], writes=[do])
                    p.store("sp", dst[tok:tok + 128, :], ot[:], [do], dd)
                else:
                    gt, dg, _ = gr.next()
                    p.op("dve", lambda e, gt=gt, ps=ps: e.tensor_tensor(out=gt[:, 0:8], in0=ps[:, 0:8], in1=abt[:, 8:16], op=ALU.add),
                         reads=[k.bdep[b], dabt], writes=[dg])
                    p.op("act", lambda e, gt=gt: e.activation(out=gt[:, 0:8], in_=gt[:, 0:8], func=AF.Exp), reads=[dg], writes=[dg])
                    p.op("act", lambda e, gt=gt: e.activation(out=gt[:, 0:8], in_=gt[:, 0:8], func=AF.Ln, bias=1.0), reads=[dg], writes=[dg])
                    p.op("dve", lambda e, gt=gt: e.tensor_tensor(out=gt[:, 0:8], in0=gt[:, 0:8], in1=nexp[:], op=ALU.mult),
                         reads=[dg, dnexp], writes=[dg])
                    p.op("act", lambda e, gt=gt, ps=ps: e.activation(out=gt[:, 8:16], in_=ps[:, 8:16], func=AF.Sigmoid),
                         reads=[k.bdep[b]], writes=[dg])
                    p.store("sp", GB[tok:tok + 128, :], gt[:, 0:16], [dg], dGB)


def common(k):
    p = k.p
    k.ident, k.dident = k.const("ident", [128, 128], k.inp["ident"])
    k.ones_t = p.sbuf("ones_t", [128, 128])
    k.dones = Dep("ones")
    p.op("pool", lambda e: e.memset(k.ones_t[:], 1.0 / D), writes=[k.dones])
    k.one1 = p.sbuf("one1", [128, 128])
    k.done1 = Dep("one1")
    p.op("pool", lambda e: e.memset(k.one1[:], 1.0), writes=[k.done1])
    k.rstd = p.sbuf("rstd", [128, 512])
    k.drstd = Dep("rstd")
    k.gm = p.sbuf("gm", [128, 8, 2])
    k.dgm = Dep("gm")
    k.modt = [(p.sbuf(f"mod{l}", [128, 48, 2]), Dep(f"mod{l}")) for l in range(2)]


def rope_tables():
    half = 64
    inv = (1.0 / (10000.0 ** (np.arange(0, half, 2, dtype=np.float32) / np.float32(half)))).astype(np.float32)
    t = np.arange(T)
    pos = np.stack([t // GW, t % GW], axis=-1).astype(np.float32)
    ang = pos[:, :, None] * inv[None, None, :]
    ang = np.concatenate([ang, ang], axis=-1).reshape(T, 128)
    cos = np.cos(ang).astype(np.float32).T
    sin = np.sin(ang).astype(np.float32).T
    rm = np.zeros((128, 128), np.float32)
    for o in (0, 64):
        for i in range(64):
            if i < 32:
                rm[o + i + 32, o + i] = -1.0
            else:
                rm[o + i - 32, o + i] = 1.0
    return np.ascontiguousarray(cos), np.ascontiguousarray(sin), rm


def dn_masks():
    i = np.arange(128)
    out = {}
    for d in range(2):
        le = (i[:, None] <= i[None, :]) if d == 0 else (i[:, None] >= i[None, :])
        gt = (i[:, None] > i[None, :]) if d == 0 else (i[:, None] < i[None, :])
        blk = lambda b: (i[:, None] // b) == (i[None, :] // b)
        ms = [le, gt, gt, ~le | (i[:, None] == i[None, :]), le, blk(16)]
        for b in (16, 32, 64):
            up = blk(2 * b) & ((i[:, None] // b) < (i[None, :] // b))
            mN = up if d == 0 else up.T
            ms += [mN, mN.T]
        out[d] = np.ascontiguousarray(np.stack(ms, axis=1).astype(np.float32))
    return out


def stage_dn_pre(k, l, tiles=None):
    p = k.p
    src = [k.scratch(f"D{x}{l}", [4, 128, NT]) for x in "QKV"]
    dst = [k.scratch(f"D{x}2_{l}", [4, 128, NT]) for x in "QKV"]
    cw, dcw = k.const(f"convw{l}", [128, 12, 5], k.inp[f"convw{l}"])
    k.rm, k.drm = k.const("rope_rm", [128, 128], k.inp["rope_rm"])
    ur = Ring(p, f"du{l}_", [128, 260], 3)
    ar = Ring(p, f"da{l}_", [128, 256], 3)
    yr = Ring(p, f"dy{l}_", [128, 256], 3)
    sr = Ring(p, f"ds{l}_", [128, 256], 2)
    rr = Ring(p, f"dr{l}_", [128, 256], 2)
    cr = Ring(p, f"dc{l}_", [128, 2, 256], 2)
    outr = Ring(p, f"do{l}_", [128, 256], 3)
    cosd, sind = k.inp["rope_cos"], k.inp["rope_sin"]
    nb = 0
    for (t0, N, col) in (tiles or super_tiles()):
        seg0, seg1 = (0, T) if col == 0 else (T, T + CT)
        ct, dct = None, None
        if col == 0:
            ct, dct, sem = cr.next()
            p.dma("sp", ct[:, 0, :], cosd[:, t0:t0 + N], writes=[dct], sem=sem)
            p.dma("sp", ct[:, 1, :], sind[:, t0:t0 + N], writes=[dct], sem=sem)
        for h in range(4):
            for xi in range(3):
                ut, du, sem = ur.next()
                lo, hi = max(t0 - 2, seg0), min(t0 + N + 2, seg1)
                if lo != t0 - 2 or hi != t0 + N + 2:
                    p.op("pool", lambda e, ut=ut: e.memset(ut[:], 0.0), writes=[du])
                p.dma("sp" if xi != 1 else "act", ut[:, lo - (t0 - 2):hi - (t0 - 2)], src[xi][0][h, :, lo:hi], reads=[src[xi][1]], writes=[du], sem=sem)
                at, da, _ = ar.next()
                eng = "dve"
                j = xi * 4 + h
                p.op(eng, lambda e, at=at, ut=ut, j=j: e.tensor_scalar(out=at[:, 0:N], in0=ut[:, 0:N], scalar1=cw[:, j, 0:1], scalar2=None, op0=ALU.mult),
                     reads=[du, dcw], writes=[da])
                for tap in range(1, 5):
                    p.op(eng, lambda e, at=at, ut=ut, j=j, tap=tap: e.scalar_tensor_tensor(out=at[:, 0:N], in0=ut[:, tap:tap + N], scalar=cw[:, j, tap:tap + 1],
                                                                                         in1=at[:, 0:N], op0=ALU.mult, op1=ALU.add),
                         reads=[du, dcw, da], writes=[da])
                yt, dy, _ = yr.next()
                p.op("act", lambda e, yt=yt, at=at: e.activation(out=yt[:, 0:N], in_=at[:, 0:N], func=AF.Silu), reads=[da], writes=[dy])
                if xi == 2:
                    p.store("sp", dst[2][0][h, :, t0:t0 + N], yt[:, 0:N], [dy], dst[2][1])
                    continue
                st, dsq, _ = sr.next()
                p.op("pool", lambda e, st=st, yt=yt: e.tensor_tensor(out=st[:, 0:N], in0=yt[:, 0:N], in1=yt[:, 0:N], op=ALU.mult), reads=[dy], writes=[dsq])
                b = 4 + (nb % 2)
                nb += 1
                ps = k.banks[b]
                p.op("pe", lambda e, ps=ps, st=st: e.matmul(ps[:, 0:N], lhsT=k.one1[:], rhs=st[:, 0:N], start=True, stop=True),
                     reads=[dsq, k.done1], writes=[k.bdep[b]])
                p.op("dve", lambda e, ps=ps, st=st: e.tensor_scalar(out=st[:, 0:N], in0=ps[:, 0:N], scalar1=EPS, scalar2=None, op0=ALU.add),
                     reads=[k.bdep[b]], writes=[dsq])
                p.op("act", lambda e, st=st: e.activation(out=st[:, 0:N], in_=st[:, 0:N], func=AF.Sqrt), reads=[dsq], writes=[dsq])
                p.op("dve", lambda e, st=st: e.reciprocal(out=st[:, 0:N], in_=st[:, 0:N]), reads=[dsq], writes=[dsq])
                ot, do, _ = outr.next()
                if col == 1:
                    p.op("dve", lambda e, ot=ot, yt=yt, st=st: e.tensor_tensor(out=ot[:, 0:N], in0=yt[:, 0:N], in1=st[:, 0:N], op=ALU.mult),
                         reads=[dy, dsq], writes=[do])
                else:
                    p.op("dve", lambda e, yt=yt, st=st: e.tensor_tensor(out=yt[:, 0:N], in0=yt[:, 0:N], in1=st[:, 0:N], op=ALU.mult),
                         reads=[dy, dsq], writes=[dy])
                    b2 = 6 + (nb % 2)
                    ps2 = k.banks[b2]
                    p.op("pe", lambda e, ps2=ps2, yt=yt: e.matmul(ps2[:, 0:N], lhsT=k.rm[:], rhs=yt[:, 0:N], start=True, stop=True),
                         reads=[dy, k.drm], writes=[k.bdep[b2]])
                    rt, dr, _ = rr.next()
                    p.op("dve", lambda e, rt=rt, ps2=ps2, ct=ct: e.tensor_tensor(out=rt[:, 0:N], in0=ps2[:, 0:N], in1=ct[:, 1, 0:N], op=ALU.mult),
                         reads=[k.bdep[b2], dct], writes=[dr])
                    p.op("pool", lambda e, ot=ot, yt=yt, ct=ct: e.tensor_tensor(out=ot[:, 0:N], in0=yt[:, 0:N], in1=ct[:, 0, 0:N], op=ALU.mult),
                         reads=[dy, dct], writes=[do])
                    p.op("pool", lambda e, ot=ot, rt=rt: e.tensor_tensor(out=ot[:, 0:N], in0=ot[:, 0:N], in1=rt[:, 0:N], op=ALU.add),
                         reads=[dr, do], writes=[do])
                p.store("sp", dst[xi][0][h, :, t0:t0 + N], ot[:, 0:N], [do], dst[xi][1])


class PS4:
    def __init__(self, k, si):
        bs = [2 * si, 2 * si + 1]
        self.t = [k.banks[b][:, 0:128] for b in bs]
        self.d = [k.bdep[b] for b in bs]
        self.i = 0

    def next(self):
        i = self.i
        self.i = (i + 1) % 2
        return self.t[i], self.d[i]


def dn_scan_gen(k, l, si, h, d, blocks, want_o):
    p = k.p
    SC = 128 ** -0.5
    QT, dQ = k.scratch(f"DQ2_{l}", [4, 128, NT])
    KT, dK = k.scratch(f"DK2_{l}", [4, 128, NT])
    VT, dV = k.scratch(f"DV2_{l}", [4, 128, NT])
    GB, dGB = k.scratch(f"GB{l}", [NT, 16])
    OD, dOD = k.scratch(f"OD{l}", [2, 4, NT, 128])
    mk, dmk = k.dnm[d]
    U, SL, MLs, MLD, MLDT = (mk[:, i, :] for i in range(5))
    BD16 = mk[:, 5, :]
    MRG = [(mk[:, 6 + 2 * j, :], mk[:, 7 + 2 * j, :]) for j in range(3)]
    B = k.dnbuf[si]
    ps = PS4(k, si)
    S = [B["S0"], B["S1"]]
    dS = [B["d_S0"], B["d_S1"]]
    p.op("pool", lambda e: e.memset(S[0][:], 0.0), writes=[dS[0]])
    cur = 0
    ident = k.ident

    def T_(name):
        return B[name], B["d_" + name]
    for bi, tok in enumerate(blocks):
        par = bi % 2
        qT, dq = T_(f"qT{par}"); kT, dk_ = T_(f"kT{par}"); vT, dv = T_(f"vT{par}"); gb, dgb = T_(f"gb{par}")
        lsem = f"ld_blk{par}_{si}"
        p.dma("sp", qT[:], QT[h, :, tok:tok + 128], reads=[dQ], writes=[dq], sem=lsem)
        p.dma("act", kT[:], KT[h, :, tok:tok + 128], reads=[dK], writes=[dk_], sem=lsem)
        p.dma("sp", vT[:], VT[h, :, tok:tok + 128], reads=[dV], writes=[dv], sem=lsem)
        p.dma("act", gb[:], GB[tok:tok + 128, :], reads=[dGB], writes=[dgb], sem=lsem)
        for dd_ in (dq, dk_, dv, dgb):
            dd_.w = dgb.w
        g = gb[:, d * 4 + h:d * 4 + h + 1]
        beta = gb[:, 8 + d * 4 + h:8 + d * 4 + h + 1]
        sc, dsc = T_("sc")
        ktok, dkt = T_("ktok"); vb, dvb = T_("vb"); SLg, dSLg = T_("SLg"); Dm, dDm = T_("Dm")
        Dstr, dDstr = T_("Dstr"); Dlow, dDlow = T_("Dlow"); DlowT, dDlowT = T_("DlowT")
        Na, dNa = T_("Na"); Nb, dNb = T_("Nb"); NTa, dNTa = T_("NTa"); NTb, dNTb = T_("NTb")
        Xa, dXa = T_("Xa"); Xb, dXb = T_("Xb")
        kbg, dkbg = T_("kbg"); ktail, dktail = T_("ktail"); u_sb, du = T_("u"); wT, dwT = T_("wT"); qkT, dqkT = T_("qkT")
        vnew, dvn = T_("vnew"); ob, dob = T_("ob"); o_sb, do = T_(f"o{par}")
        p1, dp1 = ps.next()
        p.op("pe", lambda e, p1=p1, kT=kT: e.transpose(out=p1, in_=kT[:], identity=ident[:]), reads=[dk_, k.dident], writes=[dp1])
        p.op("act", lambda e, p1=p1, ktok=ktok: e.activation(out=ktok[:], in_=p1, func=AF.Copy), reads=[dp1], writes=[dkt])
        p2, dp2 = ps.next()
        p.op("pe", lambda e, p2=p2, vT=vT: e.transpose(out=p2, in_=vT[:], identity=ident[:]), reads=[dv, k.dident], writes=[dp2])
        p.op("dve", lambda e, p2=p2, vb=vb, beta=beta: e.tensor_scalar(out=vb[:], in0=p2, scalar1=beta, scalar2=None, op0=ALU.mult),
             reads=[dp2, dgb], writes=[dvb])
        p.op("pool", lambda e, SLg=SLg, g=g: e.tensor_scalar(out=SLg[:], in0=SL, scalar1=g, scalar2=None, op0=ALU.mult),
             reads=[dmk, dgb], writes=[dSLg])
        p.op("pool", lambda e, sc=sc, beta=beta: e.tensor_scalar(out=sc[:, 0:1], in0=beta, scalar1=-1.0, scalar2=None, op0=ALU.mult),
             reads=[dgb], writes=[dsc])
        yield
        p3, dp3 = ps.next()
        p.op("pe", lambda e, p3=p3, SLg=SLg: e.matmul(p3, lhsT=U, rhs=SLg[:], start=True, stop=True), reads=[dSLg, dmk], writes=[dp3])
        p4, dp4 = ps.next()
        p.op("pe", lambda e, p4=p4, g=g: e.matmul(p4[:, 0:1], lhsT=U, rhs=g, start=True, stop=True), reads=[dgb, dmk], writes=[dp4])
        p.op("pe", lambda e, p4=p4, g=g: e.matmul(p4[:, 1:2], lhsT=k.one1[:], rhs=g, start=True, stop=True), reads=[dgb, k.done1], writes=[dp4])
        p.op("act", lambda e, p3=p3, Dm=Dm: e.activation(out=Dm[:], in_=p3, func=AF.Exp), reads=[dp3], writes=[dDm])
        p.op("dve", lambda e, p4=p4, sc=sc: e.tensor_copy(out=sc[:, 1:3], in_=p4[:, 0:2]), reads=[dp4], writes=[dsc])
        yield
        p.op("pool", lambda e, Dstr=Dstr, Dm=Dm: e.tensor_tensor(out=Dstr[:], in0=Dm[:], in1=MLs, op=ALU.mult), reads=[dDm, dmk], writes=[dDstr])
        p.op("pool", lambda e, Dlow=Dlow, Dm=Dm: e.tensor_tensor(out=Dlow[:], in0=Dm[:], in1=MLD, op=ALU.mult), reads=[dDm, dmk], writes=[dDlow])
        yield
        p.op("act", lambda e, sc=sc: e.activation(out=sc[:, 3:4], in_=sc[:, 1:2], func=AF.Exp), reads=[dsc], writes=[dsc])
        p.op("dve", lambda e, sc=sc: e.tensor_tensor(out=sc[:, 4:5], in0=sc[:, 2:3], in1=sc[:, 1:2], op=ALU.subtract), reads=[dsc], writes=[dsc])
        p.op("act", lambda e, sc=sc: e.activation(out=sc[:, 4:5], in_=sc[:, 4:5], func=AF.Exp), reads=[dsc], writes=[dsc])
        p.op("act", lambda e, sc=sc: e.activation(out=sc[:, 5:6], in_=sc[:, 2:3], func=AF.Exp), reads=[dsc], writes=[dsc])
        yield
        p.op("dve", lambda e, sc=sc, beta=beta: e.tensor_tensor(out=sc[:, 6:7], in0=sc[:, 3:4], in1=beta, op=ALU.mult), reads=[dsc, dgb], writes=[dsc])
        p.op("dve", lambda e, sc=sc: e.tensor_scalar(out=sc[:, 7:8], in0=sc[:, 3:4], scalar1=SC, scalar2=None, op0=ALU.mult), reads=[dsc], writes=[dsc])
        yield
        p5, dp5 = ps.next()
        kT2, dkT2 = T_("kT2")
        p.op("pool", lambda e, kT2=kT2, kT=kT: e.tensor_copy(out=kT2[:], in_=kT[:]), reads=[dk_], writes=[dkT2])
        p.op("pe", lambda e, p5=p5, kT=kT, kT2=kT2: e.matmul(p5, lhsT=kT[:], rhs=kT2[:], start=True, stop=True), reads=[dk_, dkT2], writes=[dp5])
        yield
        p.op("dve", lambda e, p5=p5, NTa=NTa, sc=sc: e.tensor_scalar(out=NTa[:], in0=p5, scalar1=sc[:, 0:1], scalar2=None, op0=ALU.mult),
             reads=[dp5, dsc], writes=[dNTa])
        p.op("pool", lambda e, NTa=NTa, Dstr=Dstr: e.tensor_tensor(out=NTa[:], in0=NTa[:], in1=Dstr[:], op=ALU.mult),
             reads=[dDstr, dNTa], writes=[dNTa])
        yield
        N0, dN0 = T_("N0"); NT0 = NTa; dNT0 = dNTa
        p6, dp6 = ps.next()
        p.op("pe", lambda e, p6=p6: e.transpose(out=p6, in_=NT0[:], identity=ident[:]), reads=[dNT0, k.dident], writes=[dp6])
        p.op("act", lambda e, p6=p6: e.activation(out=N0[:], in_=p6, func=AF.Copy), reads=[dp6], writes=[dN0])
        yield
        Xc, dXc = T_("Xa"); XTc, dXTc = T_("XTa"); Xn_, dXn_ = T_("Xb"); XTn_, dXTn_ = T_("XTb")
        Np, dNp = T_("Na"); NTp, dNTp = T_("NTb"); Nn, dNn = T_("Nb"); NTn, dNTn = T_("NTc")
        p.op("pool", lambda e, Np=Np, N0=N0: e.tensor_tensor(out=Np[:], in0=N0[:], in1=BD16, op=ALU.mult), reads=[dN0, dmk], writes=[dNp])
        p.op("pool", lambda e, NTp=NTp, NT0=NT0: e.tensor_tensor(out=NTp[:], in0=NT0[:], in1=BD16, op=ALU.mult), reads=[dNT0, dmk], writes=[dNTp])
        p.op("dve", lambda e, Xc=Xc, Np=Np: e.tensor_tensor(out=Xc[:], in0=Np[:], in1=ident[:], op=ALU.add), reads=[dNp, k.dident], writes=[dXc])
        p.op("dve", lambda e, XTc=XTc, NTp=NTp: e.tensor_tensor(out=XTc[:], in0=NTp[:], in1=ident[:], op=ALU.add), reads=[dNTp, k.dident], writes=[dXTc])
        yield
        for lvl in range(3):
            pa, dpa = ps.next()
            p.op("pe", lambda e, pa=pa, Np=Np, NTp=NTp: e.matmul(pa, lhsT=NTp[:], rhs=Np[:], start=True, stop=True), reads=[dNp, dNTp], writes=[dpa])
            pb, dpb = ps.next()
            p.op("pe", lambda e, pb=pb, Np=Np, NTp=NTp: e.matmul(pb, lhsT=Np[:], rhs=NTp[:], start=True, stop=True), reads=[dNp, dNTp], writes=[dpb])
            p.op("act", lambda e, pa=pa, Nn=Nn: e.activation(out=Nn[:], in_=pa, func=AF.Copy), reads=[dpa], writes=[dNn])
            p.op("dve", lambda e, pb=pb, NTn=NTn: e.tensor_copy(out=NTn[:], in_=pb), reads=[dpb], writes=[dNTn])
            yield
            pc, dpc = ps.next()
            p.op("pe", lambda e, pc=pc, NTn=NTn, Xc=Xc: e.matmul(pc, lhsT=NTn[:], rhs=Xc[:], start=True, stop=True), reads=[dNTn, dXc], writes=[dpc])
            pd, dpd = ps.next()
            p.op("pe", lambda e, pd=pd, Nn=Nn, XTc=XTc: e.matmul(pd, lhsT=Nn[:], rhs=XTc[:], start=True, stop=True), reads=[dNn, dXTc], writes=[dpd])
            p.op("dve", lambda e, pc=pc, Xc=Xc, Xn_=Xn_: e.tensor_tensor(out=Xn_[:], in0=pc, in1=Xc[:], op=ALU.add), reads=[dpc, dXc], writes=[dXn_])
            p.op("dve", lambda e, pd=pd, XTc=XTc, XTn_=XTn_: e.tensor_tensor(out=XTn_[:], in0=pd, in1=XTc[:], op=ALU.add), reads=[dpd, dXTc], writes=[dXTn_])
            (Xc, dXc, Xn_, dXn_) = (Xn_, dXn_, Xc, dXc)
            (XTc, dXTc, XTn_, dXTn_) = (XTn_, dXTn_, XTc, dXTc)
            (Np, dNp, Nn, dNn) = (Nn, dNn, Np, dNp)
            (NTp, dNTp, NTn, dNTn) = (NTn, dNTn, NTp, dNTp)
            yield
        Nm, dNm = T_("Nm"); NTm, dNTm = T_("NTm"); Y, dY = T_("Y"); Y2, dY2 = T_("Y2")
        for j in range(3):
            Mj, MjT = MRG[j]
            last = (j == 2)
            p.op("pool", lambda e, Mj=Mj: e.tensor_tensor(out=NTm[:], in0=NT0[:], in1=MjT, op=ALU.mult) if False else e.tensor_tensor(out=NTm[:], in0=NT0[:], in1=MjT, op=ALU.mult),
                 reads=[dNT0, dmk], writes=[dNTm]) if False else None
            p.op("pool", lambda e, MjT=MjT: e.tensor_tensor(out=NTm[:], in0=NT0[:], in1=MjT, op=ALU.mult), reads=[dNT0, dmk], writes=[dNTm])
            if not last:
                p.op("pool", lambda e, Mj=Mj: e.tensor_tensor(out=Nm[:], in0=N0[:], in1=Mj, op=ALU.mult), reads=[dN0, dmk], writes=[dNm])
            py, dpy = ps.next()
            p.op("pe", lambda e, py=py, Xc=Xc: e.matmul(py, lhsT=NTm[:], rhs=Xc[:], start=True, stop=True), reads=[dNTm, dXc], writes=[dpy])
            p.op("act", lambda e, py=py: e.activation(out=Y[:], in_=py, func=AF.Copy), reads=[dpy], writes=[dY])
            if not last:
                py2, dpy2 = ps.next()
                p.op("pe", lambda e, py2=py2, XTc=XTc: e.matmul(py2, lhsT=Nm[:], rhs=XTc[:], start=True, stop=True), reads=[dNm, dXTc], writes=[dpy2])
                p.op("dve", lambda e, py2=py2: e.tensor_copy(out=Y2[:], in_=py2), reads=[dpy2], writes=[dY2])
            yield
            pz, dpz = ps.next()
            p.op("pe", lambda e, pz=pz, XTc=XTc: e.matmul(pz, lhsT=XTc[:], rhs=Y[:], start=True, stop=True), reads=[dXTc, dY], writes=[dpz])
            if not last:
                pz2, dpz2 = ps.next()
                p.op("pe", lambda e, pz2=pz2, Xc=Xc: e.matmul(pz2, lhsT=Xc[:], rhs=Y2[:], start=True, stop=True), reads=[dXc, dY2], writes=[dpz2])
            p.op("dve", lambda e, pz=pz, Xc=Xc, Xn_=Xn_: e.tensor_tensor(out=Xn_[:], in0=pz, in1=Xc[:], op=ALU.add), reads=[dpz, dXc], writes=[dXn_])
            if not last:
                p.op("dve", lambda e, pz2=pz2, XTc=XTc, XTn_=XTn_: e.tensor_tensor(out=XTn_[:], in0=pz2, in1=XTc[:], op=ALU.add), reads=[dpz2, dXTc], writes=[dXTn_])
                (XTc, dXTc, XTn_, dXTn_) = (XTn_, dXTn_, XTc, dXTc)
            (Xc, dXc, Xn_, dXn_) = (Xn_, dXn_, Xc, dXc)
            yield
        Xs = [(Xc, dXc)]
        X, dX = Xs[0]
        p.op("pool", lambda e, kbg=kbg, ktok=ktok, sc=sc: e.tensor_scalar(out=kbg[:], in0=ktok[:], scalar1=sc[:, 6:7], scalar2=None, op0=ALU.mult),
             reads=[dkt, dsc], writes=[dkbg])
        p.op("pool", lambda e, ktail=ktail, ktok=ktok, sc=sc: e.tensor_scalar(out=ktail[:], in0=ktok[:], scalar1=sc[:, 4:5], scalar2=None, op0=ALU.mult),
             reads=[dkt, dsc], writes=[dktail])
        pu, dpu = ps.next()
        p.op("pe", lambda e, pu=pu, X=X, vb=vb: e.matmul(pu, lhsT=X[:], rhs=vb[:], start=True, stop=True), reads=[dX, dvb], writes=[dpu])
        p.op("act", lambda e, pu=pu, u_sb=u_sb: e.activation(out=u_sb[:], in_=pu, func=AF.Copy), reads=[dpu], writes=[du])
        pw, dpw = ps.next()
        p.op("pe", lambda e, pw=pw, X=X, kbg=kbg: e.matmul(pw, lhsT=kbg[:], rhs=X[:], start=True, stop=True), reads=[dX, dkbg], writes=[dpw])
        p.op("dve", lambda e, pw=pw, wT=wT: e.tensor_copy(out=wT[:], in_=pw), reads=[dpw], writes=[dwT])
        yield
        if want_o:
            pt, dpt = ps.next()
            p.op("pe", lambda e, pt=pt, Dlow=Dlow: e.transpose(out=pt, in_=Dlow[:], identity=ident[:]), reads=[dDlow, k.dident], writes=[dpt])
            p.op("act", lambda e, pt=pt, DlowT=DlowT: e.activation(out=DlowT[:], in_=pt, func=AF.Copy), reads=[dpt], writes=[dDlowT])
            pq, dpq = ps.next()
            p.op("pe", lambda e, pq=pq, kT=kT, qT=qT: e.matmul(pq, lhsT=kT[:], rhs=qT[:], start=True, stop=True), reads=[dk_, dq], writes=[dpq])
            p.op("dve", lambda e, pq=pq, qkT=qkT, DlowT=DlowT: e.scalar_tensor_tensor(out=qkT[:], in0=pq, scalar=SC, in1=DlowT[:], op0=ALU.mult, op1=ALU.mult),
                 reads=[dpq, dDlowT], writes=[dqkT])
            yield
        Sc, dSc = S[cur], dS[cur]
        Sn, dSn = S[1 - cur], dS[1 - cur]
        pv, dpv = ps.next()
        p.op("pe", lambda e, pv=pv, wT=wT, Sc=Sc: e.matmul(pv, lhsT=wT[:], rhs=Sc[:], start=True, stop=True), reads=[dwT, dSc], writes=[dpv])
        po, dpo = ps.next()
        if want_o:
            p.op("pe", lambda e, po=po, qT=qT, Sc=Sc: e.matmul(po, lhsT=qT[:], rhs=Sc[:], start=True, stop=True), reads=[dq, dSc], writes=[dpo])
        p.op("dve", lambda e, pv=pv, vnew=vnew, u_sb=u_sb: e.tensor_tensor(out=vnew[:], in0=u_sb[:], in1=pv, op=ALU.subtract), reads=[dpv, du], writes=[dvn])
        if want_o:
            p.op("dve", lambda e, po=po, o_sb=o_sb, sc=sc: e.tensor_scalar(out=o_sb[:], in0=po, scalar1=sc[:, 7:8], scalar2=None, op0=ALU.mult),
                 reads=[dpo, dsc], writes=[do])
        yield
        pn, dpn = ps.next()
        p.op("pe", lambda e, pn=pn, ktail=ktail, vnew=vnew: e.matmul(pn, lhsT=ktail[:], rhs=vnew[:], start=True, stop=True), reads=[dktail, dvn], writes=[dpn])
        pb2, dpb2 = ps.next()
        if want_o:
            p.op("pe", lambda e, pb2=pb2, qkT=qkT, vnew=vnew: e.matmul(pb2, lhsT=qkT[:], rhs=vnew[:], start=True, stop=True), reads=[dqkT, dvn], writes=[dpb2])
        p.op("dve", lambda e, pn=pn, Sn=Sn, Sc=Sc, sc=sc: e.scalar_tensor_tensor(out=Sn[:], in0=Sc[:], scalar=sc[:, 5:6], in1=pn, op0=ALU.mult, op1=ALU.add),
             reads=[dpn, dSc, dsc], writes=[dSn])
        cur = 1 - cur
        if want_o:
            p.op("dve", lambda e, pb2=pb2, o_sb=o_sb: e.tensor_tensor(out=o_sb[:], in0=o_sb[:], in1=pb2, op=ALU.add), reads=[dpb2, do], writes=[do])
            p.store("sp", OD[d, h, tok:tok + 128, :], o_sb[:], [do], dOD)
        yield


def dn_blocks(d, nlat=128, with_ctx=True):
    cb = [T, T + 128]
    lb = [i * 128 for i in range(nlat)]
    if d == 1:
        cb = cb[::-1]
        lb = [i * 128 for i in range(128)][::-1][:nlat]
    return (cb if with_ctx else []) + lb


def stage_dn_scan(k, l, nlat=128, scans=None):
    p = k.p
    if True:
        k.dnm = [k.const(f"dnmask{d}", [128, 12, 128], k.inp[f"dnmask{d}"]) for d in range(2)]
        names = ["ktok", "vb", "SLg", "Dm", "Dstr", "Dlow", "DlowT", "Na", "Nb", "NTa", "NTb", "Xa", "Xb", "kbg", "ktail", "u", "wT", "qkT",
                 "vnew", "ob", "o0", "o1", "kT2", "N0", "XTa", "XTb", "NTc", "Nm", "NTm", "Y", "Y2", "qT0", "qT1", "kT0", "kT1", "vT0", "vT1", "S0", "S1"]
        k.dnbuf = []
        for si in range(4):
            B = {}
            for n in names:
                B[n] = p.sbuf(f"dn{si}_{n}", [128, 128])
                B["d_" + n] = Dep(f"dn{si}_{n}")
            for n in ("gb0", "gb1"):
                B[n] = p.sbuf(f"dn{si}_{n}", [128, 16]); B["d_" + n] = Dep(f"dn{si}_{n}")
            B["sc"] = p.sbuf(f"dn{si}_sc", [128, 8]); B["d_sc"] = Dep(f"dn{si}_sc")
            k.dnbuf.append(B)
    import os
    maxsteps = int(os.environ.get("DN_STEPS", "1000000000"))
    nstep = 0
    allsc = list(scans or [(h, d) for h in range(4) for d in range(2)])
    GRP = 4
    for g0 in range(0, len(allsc), GRP):
        gens = []
        for si, (h, d) in enumerate(allsc[g0:g0 + GRP]):
            gens.append(dn_scan_gen(k, l, si, h, d, dn_blocks(d, nlat), True))
        while gens and nstep < maxsteps:
            for g in list(gens):
                nstep += 1
                try:
                    next(g)
                except StopIteration:
                    gens.remove(g)


def stage_dn_post(k, l, toks=None):
    p = k.p
    OD, dOD = k.scratch(f"OD{l}", [2, 4, NT, 128])
    Z, dZ = k.scratch(f"Z{l}", [NT, 512])
    ODT, dODT = k.scratch(f"ODT{l}", [4, 128, NT])
    dnw, ddnw = k.const("dnw", [128, 128], k.inp[f"dnw{l}"])
    fr = Ring(p, "qf", [128, 4, 128], 2); br = Ring(p, "qb", [128, 4, 128], 2); zr = Ring(p, "qz", [128, 512], 2)
    sr = Ring(p, "qs", [128, 4, 128], 2); yr = Ring(p, "qy", [128, 4, 128], 2); tr_ = Ring(p, "qt", [128, 4, 128], 2)
    cr = Ring(p, "qc", [128, 8], 2)
    nb = 0
    for tok in (toks if toks is not None else range(0, NT, 128)):
        ft, df, sem = fr.next()
        p.dma("sp", ft[:], OD[0, :, tok:tok + 128, :].rearrange("h t d -> t h d"), reads=[dOD], writes=[df], sem=sem)
        bt, db, sem = br.next()
        p.dma("act", bt[:], OD[1, :, tok:tok + 128, :].rearrange("h t d -> t h d"), reads=[dOD], writes=[db], sem=sem)
        zt, dz, sem = zr.next()
        p.dma("sp", zt[:], Z[tok:tok + 128, :], reads=[dZ], writes=[dz], sem=sem)
        st, ds, _ = sr.next(); yt, dy, _ = yr.next(); ct, dc, _ = cr.next()
        p.op("dve", lambda e, st=st, ft=ft, bt=bt: e.tensor_tensor(out=st[:], in0=ft[:], in1=bt[:], op=ALU.add), reads=[df, db], writes=[ds])
        p.op("pool", lambda e, st=st, yt=yt: e.tensor_tensor(out=yt[:], in0=st[:], in1=st[:], op=ALU.mult), reads=[ds], writes=[dy])
        p.op("dve", lambda e, ct=ct, yt=yt: e.reduce_sum(out=ct[:, 0:4], in_=yt[:], axis=AX.X), reads=[dy], writes=[dc])
        p.op("dve", lambda e, ct=ct: e.tensor_scalar(out=ct[:, 0:4], in0=ct[:, 0:4], scalar1=1.0 / 128, scalar2=EPS, op0=ALU.mult, op1=ALU.add), reads=[dc], writes=[dc])
        p.op("act", lambda e, ct=ct: e.activation(out=ct[:, 0:4], in_=ct[:, 0:4], func=AF.Sqrt), reads=[dc], writes=[dc])
        p.op("dve", lambda e, ct=ct: e.reciprocal(out=ct[:, 0:4], in_=ct[:, 0:4]), reads=[dc], writes=[dc])
        p.op("act", lambda e, zt=zt: e.activation(out=zt[:], in_=zt[:], func=AF.Silu), reads=[dz], writes=[dz])
        for h in range(4):
            p.op("dve", lambda e, yt=yt, st=st, ct=ct, h=h: e.scalar_tensor_tensor(out=yt[:, h, :], in0=st[:, h, :], scalar=ct[:, h:h + 1], in1=dnw[:],
                                                                               op0=ALU.mult, op1=ALU.mult), reads=[ds, dc, ddnw, dy], writes=[dy])
        p.op("pool", lambda e, yt=yt, zt=zt: e.tensor_tensor(out=yt[:].rearrange("p h d -> p (h d)"), in0=yt[:].rearrange("p h d -> p (h d)"), in1=zt[:], op=ALU.mult),
             reads=[dz, dy], writes=[dy])
        b = nb % 2
        nb += 1
        ps = k.banks[b]
        for h in range(4):
            p.op("pe", lambda e, ps=ps, yt=yt, h=h: e.transpose(out=ps[:, h * 128:(h + 1) * 128], in_=yt[:, h, :], identity=k.ident[:]),
                 reads=[dy, k.dident], writes=[k.bdep[b]])
        tt, dt, _ = tr_.next()
        p.op("act", lambda e, tt=tt, ps=ps: e.activation(out=tt[:].rearrange("p h d -> p (h d)"), in_=ps[:, 0:512], func=AF.Copy), reads=[k.bdep[b]], writes=[dt])
        p.store("sp", ODT[:, :, tok:tok + 128].rearrange("h p t -> p h t"), tt[:], [dt], dODT)


def stage_mlp(k, l, xname, mod, dmod, tiles=None):
    p = k.p
    xsrc, dxs = k.scratch(xname, [8, 128, NT])
    XO, dXO = k.scratch(f"XO{l}", [8, 128, NT])
    ONT, dONT = k.scratch(f"ONT{l}", [4, 128, NT])
    ODT, dODT = k.scratch(f"ODT{l}", [4, 128, NT])
    OF, dOF = k.scratch(f"OF{l}", [NT, 512])
    GATE, dGATE = k.scratch(f"GATE{l}", [24, 128, NT])
    gam, dgam = k.const("norm2", [128, 8], k.inp[f"norm2_{l}"])
    make_gm(k, gam, dgam, mod, dmod, 4)
    xr = Ring(p, "mx", [128, 8, 256], 1); br_ = [Ring(p, f"mo{b}", [128, 4, 256], 1) for b in range(3)]
    gr = Ring(p, "mg", [128, 24, 256], 1); yr = Ring(p, "my", [128, 8, 256], 1); mr = Ring(p, "mm", [128, 8, 256], 1)
    tr_ = Ring(p, "mt", [128, 8, 256], 1); hr = Ring(p, "mh", [128, 8, 256], 1); ar = Ring(p, "ma", [128, 32, 256], 1)
    tmr = Ring(p, "mtm", [128, 256], 3); ofr = Ring(p, "mof", [128, 2, 512], 1); outr = Ring(p, "mout", [128, 256], 3)
    wmr = Ring(p, "wm", [128, 4, 128], 3); wor = Ring(p, "wo", [128, 8, 128], 2); w1r = Ring(p, "w1", [128, 8, 128], 3); w2r = Ring(p, "w2", [128, 32, 128], 2)
    wm_src = [k.inp[f"w_nao{l}"], k.inp[f"w_dno{l}"], k.inp[f"w_fno{l}"]]
    nb = 0
    nq = 0

    def wq():
        nonlocal nq
        nq += 1
        return "sp" if nq % 2 else "act"
    for (t0, N, col) in (tiles or super_tiles()):
        xt, dx, sem = xr.next()
        p.dma("sp", xt[:, :, 0:N], xsrc[:, :, t0:t0 + N].rearrange("k p t -> p k t"), reads=[dxs], writes=[dx], sem=sem)
        obs = []
        for b, (src, dsrc) in enumerate(((ONT, dONT), (ODT, dODT))):
            ot, do, sem = br_[b].next()
            p.dma("act", ot[:, :, 0:N], src[:, :, t0:t0 + N].rearrange("k p t -> p k t"), reads=[dsrc], writes=[do], sem=sem)
            obs.append((ot, do))
        oft, dof, sem = ofr.next()
        p.dma("sp", oft[:], OF[t0:t0 + N, :].rearrange("(a p) c -> p a c", p=128), reads=[dOF], writes=[dof], sem=sem)
        ot, do, _ = br_[2].next()
        for a in range(2):
            b_ = 6 + a
            for g in range(4):
                p.op("pe", lambda e, a=a, g=g, b_=b_, oft=oft: e.transpose(out=k.banks[b_][:, g * 128:(g + 1) * 128], in_=oft[:, a, g * 128:(g + 1) * 128], identity=k.ident[:]),
                     reads=[dof, k.dident], writes=[k.bdep[b_]])
            p.op("act", lambda e, a=a, b_=b_, ot=ot: e.activation(out=ot[:, :, a * 128:(a + 1) * 128], in_=k.banks[b_][:, 0:512].rearrange("p (g t) -> p g t", g=4), func=AF.Copy),
                 reads=[k.bdep[b_]], writes=[do])
        obs.append((ot, do))
        gt, dg, sem = gr.next()
        for q4 in range(4):
            p.dma(wq(), gt[:, q4 * 6:(q4 + 1) * 6, 0:N], GATE[q4 * 6:(q4 + 1) * 6, :, t0:t0 + N].rearrange("k p t -> p k t"), reads=[dGATE], writes=[dg], sem=sem)
        yt, dy, _ = yr.next()
        for c in range(8):
            for b in range(3):
                wt, dw, sem = wmr.next()
                p.dma(wq(), wt[:], wm_src[b][c], writes=[dw], sem=sem)
                bk = nb % 4
                nb += 1
                ps = k.banks[bk]
                ot, do = obs[b]
                for kk in range(4):
                    p.op("pe", lambda e, ps=ps, wt=wt, ot=ot, kk=kk: e.matmul(ps[:, 0:N], lhsT=wt[:, kk, :], rhs=ot[:, kk, 0:N], start=(kk == 0), stop=(kk == 3)),
                         reads=[dw, do], writes=[k.bdep[bk]])
                if b == 0:
                    p.op("dve", lambda e, ps=ps, yt=yt, gt=gt, c=c: e.tensor_tensor(out=yt[:, c, 0:N], in0=ps[:, 0:N], in1=gt[:, c, 0:N], op=ALU.mult),
                         reads=[k.bdep[bk], dg], writes=[dy])
                else:
                    tm, dtm, _ = tmr.next()
                    p.op("dve", lambda e, ps=ps, tm=tm, gt=gt, c=c, b=b: e.tensor_tensor(out=tm[:, 0:N], in0=ps[:, 0:N], in1=gt[:, b * 8 + c, 0:N], op=ALU.mult),
                         reads=[k.bdep[bk], dg], writes=[dtm])
                    p.op("pool", lambda e, tm=tm, yt=yt, c=c: e.tensor_tensor(out=yt[:, c, 0:N], in0=yt[:, c, 0:N], in1=tm[:, 0:N], op=ALU.add),
                         reads=[dtm, dy], writes=[dy])
        mt, dm, _ = mr.next()
        for c in range(8):
            wt, dw, sem = wor.next()
            p.dma(wq(), wt[:], k.inp[f"w_out{l}"][c], writes=[dw], sem=sem)
            bk = nb % 4
            nb += 1
            ps = k.banks[bk]
            for kk in range(8):
                p.op("pe", lambda e, ps=ps, wt=wt, yt=yt, kk=kk: e.matmul(ps[:, 0:N], lhsT=wt[:, kk, :], rhs=yt[:, kk, 0:N], start=(kk == 0), stop=(kk == 7)),
                     reads=[dw, dy], writes=[k.bdep[bk]])
            p.op("dve", lambda e, ps=ps, mt=mt, xt=xt, c=c, col=col: e.scalar_tensor_tensor(out=mt[:, c, 0:N], in0=ps[:, 0:N], scalar=mod[:, 16 + c, col:col + 1], in1=xt[:, c, 0:N],
                                                                                        op0=ALU.mult, op1=ALU.add), reads=[k.bdep[bk], dx, dmod], writes=[dm])
        ht, dh, _ = hr.next(); tmp, dtmp, _ = tr_.next()
        norm_mod(k, mt, dm, N, gam, dgam, mod, dmod, 3, 4, col, ht, dh, tmp, dtmp, k.ones_t, k.dones, 7)
        at, da, _ = ar.next()
        for j in range(32):
            wt, dw, sem = w1r.next()
            p.dma(wq(), wt[:], k.inp[f"w_mlp1_{l}"][j], writes=[dw], sem=sem)
            bk = nb % 4
            nb += 1
            ps = k.banks[bk]
            for kk in range(8):
                p.op("pe", lambda e, ps=ps, wt=wt, ht=ht, kk=kk: e.matmul(ps[:, 0:N], lhsT=wt[:, kk, :], rhs=ht[:, kk, 0:N], start=(kk == 0), stop=(kk == 7)),
                     reads=[dw, dh], writes=[k.bdep[bk]])
            p.op("act", lambda e, ps=ps, at=at, j=j: e.activation(out=at[:, j, 0:N], in_=ps[:, 0:N], func=AF.Relu), reads=[k.bdep[bk]], writes=[da])
            p.op("pool", lambda e, at=at, j=j: e.tensor_tensor(out=at[:, j, 0:N], in0=at[:, j, 0:N], in1=at[:, j, 0:N], op=ALU.mult), reads=[da], writes=[da])
        for c in range(8):
            wt, dw, sem = w2r.next()
            p.dma(wq(), wt[:], k.inp[f"w_mlp2_{l}"][c], writes=[dw], sem=sem)
            bk = nb % 4
            nb += 1
            ps = k.banks[bk]
            for j in range(32):
                p.op("pe", lambda e, ps=ps, wt=wt, at=at, j=j: e.matmul(ps[:, 0:N], lhsT=wt[:, j, :], rhs=at[:, j, 0:N], start=(j == 0), stop=(j == 31)),
                     reads=[dw, da], writes=[k.bdep[bk]])
            ot, do, _ = outr.next()
            p.op("dve", lambda e, ps=ps, ot=ot, mt=mt, c=c, col=col: e.scalar_tensor_tensor(out=ot[:, 0:N], in0=ps[:, 0:N], scalar=mod[:, 40 + c, col:col + 1], in1=mt[:, c, 0:N],
                                                                                        op0=ALU.mult, op1=ALU.add), reads=[k.bdep[bk], dm, dmod], writes=[do])
            p.store("sp", XO[c, :, t0:t0 + N], ot[:, 0:N], [do], dXO)


def stage_final(k, xname, tiles=None):
    p = k.p
    xsrc, dxs = k.scratch(xname, [8, 128, NT])
    y = k.nc.dram_tensor("y", [T, D], F32, kind="ExternalOutput").ap()
    k.ydep = Dep("y")
    gam, dgam = k.const("normf", [128, 8], k.inp["norm_f"])
    xr = Ring(p, "fx", [128, 8, 256], 2); tr_ = Ring(p, "ft", [128, 8, 256], 2); outr = Ring(p, "fo", [128, 1024], 2)
    nb = 0
    for (t0, N, col) in (tiles or super_tiles()[:-1]):
        xt, dx, sem = xr.next()
        p.dma("sp", xt[:, :, 0:N], xsrc[:, :, t0:t0 + N].rearrange("k p t -> p k t"), reads=[dxs], writes=[dx], sem=sem)
        tmp, dtmp, _ = tr_.next()
        p.op("act", lambda e, tmp=tmp, xt=xt: e.activation(out=tmp[:], in_=xt[:], func=AF.Square), reads=[dx], writes=[dtmp])
        ps = k.banks[7]
        for kk in range(8):
            p.op("pe", lambda e, kk=kk, tmp=tmp: e.matmul(ps[:, 0:N], lhsT=k.ones_t[:], rhs=tmp[:, kk, 0:N], start=(kk == 0), stop=(kk == 7)),
                 reads=[dtmp, k.dones], writes=[k.bdep[7]])
        rstd = k.rstd
        p.op("dve", lambda e: e.tensor_scalar(out=rstd[:, 0:N], in0=ps[:, 0:N], scalar1=EPS, scalar2=None, op0=ALU.add), reads=[k.bdep[7]], writes=[k.drstd])
        p.op("act", lambda e: e.activation(out=rstd[:, 0:N], in_=rstd[:, 0:N], func=AF.Sqrt), reads=[k.drstd], writes=[k.drstd])
        p.op("dve", lambda e: e.reciprocal(out=rstd[:, 0:N], in_=rstd[:, 0:N]), reads=[k.drstd], writes=[k.drstd])
        for kk in range(8):
            p.op("dve", lambda e, kk=kk, tmp=tmp, xt=xt: e.scalar_tensor_tensor(out=tmp[:, kk, 0:N], in0=xt[:, kk, 0:N], scalar=gam[:, kk:kk + 1], in1=rstd[:, 0:N],
                                                                               op0=ALU.mult, op1=ALU.mult), reads=[dx, k.drstd, dgam, dtmp], writes=[dtmp])
        for tb in range(N // 128):
            ot, do, _ = outr.next()
            for half in range(2):
                b = (nb % 3) * 2
                nb += 1
                b = b if False else (nb % 6)
                ps2 = k.banks[b]
                for q in range(4):
                    kk = half * 4 + q
                    p.op("pe", lambda e, ps2=ps2, tmp=tmp, kk=kk, q=q, tb=tb: e.transpose(out=ps2[:, q * 128:(q + 1) * 128], in_=tmp[:, kk, tb * 128:(tb + 1) * 128], identity=k.ident[:]),
                         reads=[dtmp, k.dident], writes=[k.bdep[b]])
                if half == 0:
                    p.op("act", lambda e, ps2=ps2, ot=ot: e.activation(out=ot[:, 0:512], in_=ps2[:, 0:512], func=AF.Copy), reads=[k.bdep[b]], writes=[do])
                else:
                    p.op("dve", lambda e, ps2=ps2, ot=ot: e.tensor_copy(out=ot[:, 512:1024], in_=ps2[:, 0:512]), reads=[k.bdep[b]], writes=[do])
            tok = t0 + tb * 128
            p.store("sp", y[tok:tok + 128, :], ot[:], [do], k.ydep)


def na_table(rpb):
    w = np.arange(64)
    cs = np.clip(w - 8, 0, 48)
    j = np.arange(64)
    inwin = (j[:, None] >= cs[None, :]) & (j[:, None] < cs[None, :] + 16)
    idx = np.clip(j[:, None] - w[None, :] + 15, 0, 30)
    tab = rpb[:, :, idx]
    tab = np.where(inwin[None, None], tab, np.float32(-30000.0))
    return np.ascontiguousarray(tab.transpose(2, 0, 1, 3)).astype(np.float32)


def stage_na(k, l, rows=None, do_ctx=True):
    p = k.p
    QT, dQT = k.scratch(f"QT{l}", [8, 64, NT])
    KT, dKT = k.scratch(f"KT{l}", [8, 64, NT])
    V, dV = k.scratch(f"V{l}", [NT, 512])
    ONT, dONT = k.scratch(f"ONT{l}", [4, 128, NT])
    eb, deb = k.const("natab", [64, 8 * 15 * 64], k.inp[f"natab{l}"].rearrange("j h o w -> j (h o w)"))
    for h in range(8):
        p.op("act", lambda e, h=h: e.activation(out=eb[:, h * 960:(h + 1) * 960], in_=eb[:, h * 960:(h + 1) * 960], func=AF.Exp), reads=[deb], writes=[deb])
    kc = p.sbuf("na_kc", [64, 8, 256]); dkc = Dep("na_kc")
    p.dma("sp", kc[:], KT[:, :, T:T + CT].rearrange("h d t -> d h t"), reads=[dKT], writes=[dkc], sem="ld_kc")
    vc = p.sbuf("na_vc", [128, 2, 8, 65]); dvc = Dep("na_vc")
    p.op("pool", lambda e: e.memset(vc[:], 1.0), writes=[dvc])
    for c in range(2):
        p.dma("act", vc[:, c, :, 0:64], V[T + c * 128:T + (c + 1) * 128, :].rearrange("t (h d) -> t h d", d=64), reads=[dV], writes=[dvc], sem="ld_vc")
    NS = 10
    kring = p.sbuf("na_kr", [64, NS, 8, 64]); vring = p.sbuf("na_vr", [64, NS, 8, 65])
    dkr = [Dep(f"na_kr{i}") for i in range(NS)]; dvr = [Dep(f"na_vr{i}") for i in range(NS)]
    p.op("pool", lambda e: e.memset(vring[:], 1.0), writes=dvr)
    qr = Ring(p, "na_q", [64, 8, 256], 2)
    er = Ring(p, "na_e", [64, 512], 3); ecr = Ring(p, "na_ec", [128, 128], 3)
    orr = Ring(p, "na_o", [64, 512], 2); otr = Ring(p, "na_ot", [128, 4, 64], 2); rcr = Ring(p, "na_rc", [64, 8], 2)
    loaded = -1
    it = 0
    qt = None
    rows = list(rows if rows is not None else range(256))
    for r in rows:
        rs = min(max(r - 4, 0), 248)
        while loaded < rs + 7:
            loaded += 1
            if loaded < rs:
                continue
            sl = loaded % NS
            p.dma("sp", kring[:, sl, :, :], KT[:, :, loaded * 64:(loaded + 1) * 64].rearrange("h d t -> d h t"), reads=[dKT], writes=[dkr[sl]], sem=f"ld_kr{sl}")
            p.dma("act", vring[:, sl, :, 0:64], V[loaded * 64:(loaded + 1) * 64, :].rearrange("t (h d) -> t h d", d=64), reads=[dV], writes=[dvr[sl]], sem=f"ld_vr{sl}")
        if qt is None or r % 4 == 0 or r == rows[0]:
            qt, dq, sem = qr.next()
            q0 = (r // 4) * 4
            p.dma("sp", qt[:], QT[:, :, q0 * 64:(q0 + 4) * 64].rearrange("h d t -> d h t"), reads=[dQT], writes=[dq], sem=sem)
        rr = r % 4
        o0 = rs - r + 7
        ot, do, _ = orr.next()
        rc, drc, _ = rcr.next()
        for h in range(8):
            bA, bB, bC = (it % 2), 2 + (it % 2), 4 + (it % 2)
            it += 1
            A, Bk, C = k.banks[bA], k.banks[bB], k.banks[bC]
            qv = qt[:, h, rr * 64:(rr + 1) * 64]
            for i in range(8):
                sl = (rs + i) % NS
                p.op("pe", lambda e, A=A, i=i, sl=sl, h=h, qv=qv: e.matmul(A[0:64, i * 64:(i + 1) * 64], lhsT=kring[:, sl, h, :], rhs=qv, start=True, stop=True),
                     reads=[dkr[sl], dq], writes=[k.bdep[bA]])
            for c in range(2):
                p.op("pe", lambda e, Bk=Bk, c=c, h=h, qv=qv: e.matmul(Bk[:, c * 64:(c + 1) * 64], lhsT=kc[:, h, c * 128:(c + 1) * 128], rhs=qv, start=True, stop=True),
                     reads=[dkc, dq], writes=[k.bdep[bB]])
            et, de, _ = er.next(); ect, dec, _ = ecr.next()
            p.op("act", lambda e, et=et, A=A: e.activation(out=et[:], in_=A[0:64, :], func=AF.Exp), reads=[k.bdep[bA]], writes=[de])
            p.op("act", lambda e, ect=ect, Bk=Bk: e.activation(out=ect[:], in_=Bk[:, 0:128], func=AF.Exp), reads=[k.bdep[bB]], writes=[dec])
            e0 = h * 960 + o0 * 64
            p.op("pool", lambda e, et=et, e0=e0: e.tensor_tensor(out=et[:], in0=et[:], in1=eb[:, e0:e0 + 512], op=ALU.mult), reads=[de, deb], writes=[de])
            for i in range(8):
                sl = (rs + i) % NS
                p.op("pe", lambda e, C=C, et=et, i=i, sl=sl, h=h: e.matmul(C[0:64, 0:65], lhsT=et[:, i * 64:(i + 1) * 64], rhs=vring[:, sl, h, :], start=(i == 0), stop=False),
                     reads=[de, dvr[sl]], writes=[k.bdep[bC]])
            for c in range(2):
                p.op("pe", lambda e, C=C, ect=ect, c=c, h=h: e.matmul(C[0:64, 0:65], lhsT=ect[:, c * 64:(c + 1) * 64], rhs=vc[:, c, h, :], start=False, stop=(c == 1)),
                     reads=[dec, dvc], writes=[k.bdep[bC]])
            p.op("dve", lambda e, C=C, rc=rc, h=h: e.reciprocal(out=rc[:, h:h + 1], in_=C[0:64, 64:65]), reads=[k.bdep[bC]], writes=[drc])
            p.op("dve", lambda e, C=C, rc=rc, ot=ot, h=h: e.tensor_scalar(out=ot[:, h * 64:(h + 1) * 64], in0=C[0:64, 0:64], scalar1=rc[:, h:h + 1], scalar2=None, op0=ALU.mult),
                 reads=[k.bdep[bC], drc], writes=[do])
        bT = 6 + (r % 2)
        for c in range(4):
            p.op("pe", lambda e, bT=bT, ot=ot, c=c: e.transpose(out=k.banks[bT][:, c * 64:(c + 1) * 64], in_=ot[:, c * 128:(c + 1) * 128], identity=k.ident[0:64, 0:64]),
                 reads=[do, k.dident], writes=[k.bdep[bT]])
        tt, dt, _ = otr.next()
        p.op("act", lambda e, bT=bT, tt=tt: e.activation(out=tt[:].rearrange("p c t -> p (c t)"), in_=k.banks[bT][:, 0:256], func=AF.Copy), reads=[k.bdep[bT]], writes=[dt])
        p.store("sp", ONT[:, :, r * 64:(r + 1) * 64].rearrange("c p t -> p c t"), tt[:], [dt], dONT)
    if not do_ctx:
        return
    qc = p.sbuf("na_qc", [64, 8, 256]); dqc = Dep("na_qc")
    p.dma("sp", qc[:], QT[:, :, T:T + CT].rearrange("h d t -> d h t"), reads=[dQT], writes=[dqc], sem="ld_qc")
    ocr = Ring(p, "na_oc", [128, 512], 2); otc = Ring(p, "na_otc", [128, 4, 128], 2); rcc = Ring(p, "na_rcc", [128, 8], 2)
    e2r = Ring(p, "na_e2", [128, 2, 128], 3)
    for t in range(2):
        ot, do, _ = ocr.next(); rc, drc, _ = rcc.next()
        for h in range(8):
            bA, bC = (it % 2), 4 + (it % 2)
            it += 1
            A, C = k.banks[bA], k.banks[bC]
            for c in range(2):
                p.op("pe", lambda e, A=A, c=c, h=h, t=t: e.matmul(A[:, c * 128:(c + 1) * 128], lhsT=kc[:, h, c * 128:(c + 1) * 128], rhs=qc[:, h, t * 128:(t + 1) * 128], start=True, stop=True),
                     reads=[dkc, dqc], writes=[k.bdep[bA]])
            et, de, _ = e2r.next()
            p.op("act", lambda e, et=et, A=A: e.activation(out=et[:].rearrange("p c t -> p (c t)"), in_=A[:, 0:256], func=AF.Exp), reads=[k.bdep[bA]], writes=[de])
            for c in range(2):
                p.op("pe", lambda e, C=C, et=et, c=c, h=h: e.matmul(C[:, 0:65], lhsT=et[:, c, :], rhs=vc[:, c, h, :], start=(c == 0), stop=(c == 1)),
                     reads=[de, dvc], writes=[k.bdep[bC]])
            p.op("dve", lambda e, C=C, rc=rc, h=h: e.reciprocal(out=rc[:, h:h + 1], in_=C[:, 64:65]), reads=[k.bdep[bC]], writes=[drc])
            p.op("dve", lambda e, C=C, rc=rc, ot=ot, h=h: e.tensor_scalar(out=ot[:, h * 64:(h + 1) * 64], in0=C[:, 0:64], scalar1=rc[:, h:h + 1], scalar2=None, op0=ALU.mult),
                 reads=[k.bdep[bC], drc], writes=[do])
        bT = 6 + t
        for c in range(4):
            p.op("pe", lambda e, bT=bT, ot=ot, c=c: e.transpose(out=k.banks[bT][:, c * 128:(c + 1) * 128], in_=ot[:, c * 128:(c + 1) * 128], identity=k.ident[:]),
                 reads=[do, k.dident], writes=[k.bdep[bT]])
        tt, dt, _ = otc.next()
        p.op("act", lambda e, bT=bT, tt=tt: e.activation(out=tt[:].rearrange("p c t -> p (c t)"), in_=k.banks[bT][:, 0:512], func=AF.Copy), reads=[k.bdep[bT]], writes=[dt])
        p.store("sp", ONT[:, :, T + t * 128:T + (t + 1) * 128].rearrange("c p t -> p c t"), tt[:], [dt], dONT)


def fn_tables():
    n = np.arange(128, dtype=np.float64)
    a = 2 * np.pi * np.outer(n, n) / 128.0
    c128, s128 = np.cos(a), np.sin(a)
    tw = 2 * np.pi * np.outer(n, n) / float(T)
    m = np.arange(256, dtype=np.float64)
    a2 = 2 * np.pi * np.outer(m, m) / 256.0
    c256, s256 = np.cos(a2), np.sin(a2)
    f = lambda x: np.ascontiguousarray(x.astype(np.float32))
    dft = np.stack([c128, s128, -c128, -s128], axis=1)
    twt = np.stack([np.cos(tw), np.sin(tw), -np.sin(tw)], axis=1)
    d256 = np.stack([c256, -s256], axis=0).reshape(2, 2, 128, 256).transpose(2, 0, 1, 3)
    return f(dft), f(twt), f(d256)


def stage_fn(k, l, do_ctx=True, t2s=None, t1s=None):
    p = k.p
    U, dU = k.scratch(f"U{l}", [4, 128, NT])
    PQ, dPQ = k.scratch(f"PQ{l}", [NT, 1024])
    YD, dYD = k.scratch(f"YD{l}", [128, 128, 1024])
    OF, dOF = k.scratch(f"OF{l}", [NT, 512])
    dft, ddft = k.const("dft", [128, 4, 128], k.inp["fn_dft"])
    twt, dtw = k.const("twt", [128, 3, 128], k.inp["fn_tw"])
    C, S, NC_, NS_ = (dft[:, i, :] for i in range(4))
    ur = Ring(p, "fu", [128, 4, 128], 2); pqr = Ring(p, "fpq", [128, 1024], 2)
    nb = 0
    for tok in range(0, NT if do_ctx else T, 128):
        ut, du, sem = ur.next()
        p.dma("sp", ut[:], U[:, :, tok:tok + 128].rearrange("g c t -> c g t"), reads=[dU], writes=[du], sem=sem)
        b0 = (nb % 2) * 2
        nb += 1
        for g in range(4):
            p.op("pe", lambda e, ut=ut, g=g, b0=b0: e.matmul(k.banks[b0][:, g * 128:(g + 1) * 128], lhsT=ut[:, g, :], rhs=C, start=True, stop=True),
                 reads=[du, ddft], writes=[k.bdep[b0]])
            p.op("pe", lambda e, ut=ut, g=g, b0=b0: e.matmul(k.banks[b0 + 1][:, g * 128:(g + 1) * 128], lhsT=ut[:, g, :], rhs=S, start=True, stop=True),
                 reads=[du, ddft], writes=[k.bdep[b0 + 1]])
        pt, dp, _ = pqr.next()
        p.op("act", lambda e, pt=pt, b0=b0: e.activation(out=pt[:, 0:512], in_=k.banks[b0][:, 0:512], func=AF.Copy), reads=[k.bdep[b0]], writes=[dp])
        p.op("dve", lambda e, pt=pt, b0=b0: e.tensor_copy(out=pt[:, 512:1024], in_=k.banks[b0 + 1][:, 0:512]), reads=[k.bdep[b0 + 1]], writes=[dp])
        p.store("sp", PQ[tok:tok + 128, :], pt[:], [dp], dPQ)
    xr = Ring(p, "fx", [128, 1024], 2); yr = Ring(p, "fy", [128, 1024], 2); tr_ = Ring(p, "ftm", [128, 2, 512], 2)
    PQv = PQ[0:T, :].rearrange("(t1 t2) c -> t2 t1 c", t2=128)
    for t2 in (t2s if t2s is not None else range(128)):
        xt, dx, sem = xr.next()
        for q4 in range(4):
            p.dma("sp" if q4 % 2 == 0 else "act", xt[:, q4 * 256:(q4 + 1) * 256], PQv[t2, :, q4 * 256:(q4 + 1) * 256], reads=[dPQ], writes=[dx], sem=sem)
        b0 = (nb % 2) * 2
        nb += 1
        Yr, Yi = k.banks[b0], k.banks[b0 + 1]
        p.op("pe", lambda e, Yr=Yr, xt=xt: e.matmul(Yr[:, 0:512], lhsT=C, rhs=xt[:, 0:512], start=True, stop=False), reads=[dx, ddft], writes=[k.bdep[b0]])
        p.op("pe", lambda e, Yr=Yr, xt=xt: e.matmul(Yr[:, 0:512], lhsT=NS_, rhs=xt[:, 512:1024], start=False, stop=True), reads=[dx, ddft], writes=[k.bdep[b0]])
        p.op("pe", lambda e, Yi=Yi, xt=xt: e.matmul(Yi[:, 0:512], lhsT=NC_, rhs=xt[:, 512:1024], start=True, stop=False), reads=[dx, ddft], writes=[k.bdep[b0 + 1]])
        p.op("pe", lambda e, Yi=Yi, xt=xt: e.matmul(Yi[:, 0:512], lhsT=NS_, rhs=xt[:, 0:512], start=False, stop=True), reads=[dx, ddft], writes=[k.bdep[b0 + 1]])
        tm, dtm, _ = tr_.next(); yt, dy, _ = yr.next()
        cf, sf, nsf = twt[:, 0, t2:t2 + 1], twt[:, 1, t2:t2 + 1], twt[:, 2, t2:t2 + 1]
        p.op("dve", lambda e, tm=tm, Yr=Yr, cf=cf: e.tensor_scalar(out=tm[:, 0, :], in0=Yr[:, 0:512], scalar1=cf, scalar2=None, op0=ALU.mult), reads=[k.bdep[b0], dtw], writes=[dtm])
        p.op("dve", lambda e, tm=tm, Yi=Yi, cf=cf: e.tensor_scalar(out=tm[:, 1, :], in0=Yi[:, 0:512], scalar1=cf, scalar2=None, op0=ALU.mult), reads=[k.bdep[b0 + 1], dtw], writes=[dtm])
        p.op("dve", lambda e, tm=tm, yt=yt, Yi=Yi, sf=sf: e.scalar_tensor_tensor(out=yt[:, 0:512], in0=Yi[:, 0:512], scalar=sf, in1=tm[:, 0, :], op0=ALU.mult, op1=ALU.add),
             reads=[k.bdep[b0 + 1], dtw, dtm], writes=[dy])
        p.op("dve", lambda e, tm=tm, yt=yt, Yr=Yr, nsf=nsf: e.scalar_tensor_tensor(out=yt[:, 512:1024], in0=Yr[:, 0:512], scalar=nsf, in1=tm[:, 1, :], op0=ALU.mult, op1=ALU.add),
             reads=[k.bdep[b0], dtw, dtm], writes=[dy])
        for q4 in range(4):
            p.store("sp", YD[:, t2, q4 * 256:(q4 + 1) * 256], yt[:, q4 * 256:(q4 + 1) * 256], [dy], dYD)
    sc_lat = 1.0 / math.sqrt(T * 128.0)
    zr = Ring(p, "fz", [128, 1024], 2); orr = Ring(p, "fo", [128, 512], 2)
    OFv = OF[0:T, :].rearrange("(b a) c -> a b c", a=128)
    for t1 in (t1s if t1s is not None else range(128)):
        zt, dz, sem = zr.next()
        p.dma("sp", zt[:], YD[t1], reads=[dYD], writes=[dz], sem=sem)
        b = 4 + (nb % 2)
        nb += 1
        ps = k.banks[b]
        p.op("pe", lambda e, ps=ps, zt=zt: e.matmul(ps[:, 0:512], lhsT=C, rhs=zt[:, 0:512], start=True, stop=False), reads=[dz, ddft], writes=[k.bdep[b]])
        p.op("pe", lambda e, ps=ps, zt=zt: e.matmul(ps[:, 0:512], lhsT=S, rhs=zt[:, 512:1024], start=False, stop=True), reads=[dz, ddft], writes=[k.bdep[b]])
        ot, do, _ = orr.next()
        p.op("act", lambda e, ps=ps, ot=ot: e.activation(out=ot[:], in_=ps[:, 0:512], func=AF.Copy, scale=sc_lat), reads=[k.bdep[b]], writes=[do])
        for q2 in range(2):
            p.store("sp", OFv[t1, :, q2 * 256:(q2 + 1) * 256], ot[:, q2 * 256:(q2 + 1) * 256], [do], dOF)
    if not do_ctx:
        return
    d256, dd256 = k.const("d256", [128, 2, 2, 256], k.inp["fn_d256"])
    sc_ctx = 1.0 / math.sqrt(256.0 * 128.0)
    cx = p.sbuf("fcx", [128, 2, 1024]); dcx = Dep("fcx")
    p.dma("sp", cx[:], PQ[T:T + 256, :].rearrange("(a p) c -> p a c", p=128), reads=[dPQ], writes=[dcx], sem="ld_fcx")
    for tt_ in range(2):
        b = 4 + (nb % 2)
        nb += 1
        ps = k.banks[b]
        n_ = 0
        for cs_ in range(2):
            for kt in range(2):
                p.op("pe", lambda e, ps=ps, cs_=cs_, kt=kt, tt_=tt_, n_=n_: e.matmul(ps[:, 0:512], lhsT=d256[:, cs_, kt, tt_ * 128:(tt_ + 1) * 128],
                                                                                 rhs=cx[:, kt, cs_ * 512:(cs_ + 1) * 512], start=(n_ == 0), stop=(n_ == 3)),
                     reads=[dcx, dd256], writes=[k.bdep[b]])
                n_ += 1
        ot, do, _ = orr.next()
        p.op("act", lambda e, ps=ps, ot=ot: e.activation(out=ot[:], in_=ps[:, 0:512], func=AF.Copy, scale=sc_ctx), reads=[k.bdep[b]], writes=[do])
        p.store("sp", OF[T + tt_ * 128:T + (tt_ + 1) * 128, :], ot[:], [do], dOF)


def build_program(shapes, debug=None):
    nc = bass.Bass("TRN2", target_bir_lowering=False)
    p = Prog(nc)
    k = K(nc, p, shapes)
    common(k)
    if debug is not None:
        fin = debug(k)
        p.build(final_deps=fin)
        return nc
    for l in range(2):
        xname = "xin" if l == 0 else "XO0"
        mod, dmod = k.modt[l]
        lat = super_tiles()[:-1]
        for fn in (lambda: stage_mod(k, l),
                   lambda: stage_proj(k, l, xname, mod, dmod),
                   lambda: stage_dn_pre(k, l),
                   lambda: stage_dn_scan(k, l),
                   lambda: stage_dn_post(k, l, toks=None if l == 0 else range(0, T, 128)),
                   lambda: stage_na(k, l, do_ctx=(l == 0)),
                   lambda: stage_fn(k, l, do_ctx=(l == 0)),
                   lambda: stage_mlp(k, l, xname, mod, dmod, tiles=None if l == 0 else lat)):
            p.scope_begin()
            fn()
            p.scope_end()
    p.scope_begin()
    stage_final(k, "XO1")
    p.scope_end()
    p.build(final_deps=[k.ydep])
    return nc


def kernel(**inputs):
    inp = {k_: np.asarray(v, np.float32) for k_, v in inputs.items()}
    m = host_layout(inp)
    shapes = {k_: v.shape for k_, v in m.items()}
    nc = build_program(shapes)
    res = run_bass_kernel_spmd(nc, [m], core_ids=[0])
    y = np.asarray(res.results[0]["y"], np.float32)
    return y.reshape(1, T, D)
```

```python
import math
import numpy as np
import concourse.bass as bass
import concourse.mybir as mybir
from concourse.bass_utils import run_bass_kernel_spmd

F32 = mybir.dt.float32
AF = mybir.ActivationFunctionType
ALU = mybir.AluOpType
AX = mybir.AxisListType

D = 1024
T = 16384
CT = 256
NT = T + CT
GW = 64
EPS = 1e-6
NTAB = 15


class Dep:
    __slots__ = ("name", "w", "r")

    def __init__(self, name=""):
        self.name = name
        self.w = None
        self.r = []


class Prog:
    ENGS = ("pe", "act", "dve", "pool", "sp")

    def __init__(self, nc, self_sync=True):
        self.nc = nc
        self.q = {e: [] for e in self.ENGS}
        self.cnt = {}
        self.known = {e: {} for e in self.ENGS}
        self.sems = {}
        self.self_sync = self_sync
        self._uid = 0
        self._stack = []
        self._semstack = []
        self._smap = {}
        self._scope_mark = 0
        self.n_inst = 0
        for e in self.ENGS:
            self._mksem("E_" + e)

    def _mksem(self, key):
        cm = self.nc.semaphore(key)
        h = cm.__enter__()
        self._semstack.append(cm)
        self.sems[key] = h
        self.cnt[key] = 0
        return key

    _mksem_global = _mksem

    def sbuf(self, name, shape, dt=F32):
        self._uid += 1
        cm = self.nc.sbuf_tensor(f"{name}_u{self._uid}", list(shape), dt)
        t = cm.__enter__()
        self._stack.append(cm)
        return t

    def psum(self, name, shape, dt=F32):
        cm = self.nc.psum_tensor(name, list(shape), dt)
        t = cm.__enter__()
        self._stack.append(cm)
        return t

    def close(self):
        while self._stack:
            self._stack.pop().__exit__(None, None, None)
        while self._semstack:
            self._semstack.pop().__exit__(None, None, None)

    def _waits(self, eng, reads, writes):
        need = {}

        def add(ev):
            if ev is None:
                return
            if isinstance(ev, dict):
                for k, v in ev.items():
                    if need.get(k, 0) < v:
                        need[k] = v
                return
            k, v = ev
            if need.get(k, 0) < v:
                need[k] = v
        for d in reads:
            add(d.w)
        for d in writes:
            add(d.w)
            for ev in d.r:
                add(ev)
        out = []
        own = "E_" + eng
        for k, v in need.items():
            if k == own and (eng == "pe" or not self.self_sync):
                continue
            if self.known[eng].get(k, 0) >= v:
                continue
            self.known[eng][k] = v
            out.append((k, v))
        return out

    def _emit(self, eng, fn, reads, writes, semkey, inc, track_w=True):
        waits = self._waits(eng, reads, writes if track_w else [])
        self.cnt[semkey] += inc
        ev = (semkey, self.cnt[semkey])
        for d in writes:
            d.w = ev
            d.r = []
        for d in reads:
            if d not in writes:
                d.r.append(ev)
                if len(d.r) > 64:
                    d.r = d.r[-64:]
        self.q[eng].append((waits, fn, semkey, inc))
        self.n_inst += 1 + len(waits)

    def scope_begin(self):
        self.barrier()
        self._scope_mark = len(self._stack)
        self._smap = {}

    def scope_end(self):
        self.barrier()
        while len(self._stack) > self._scope_mark:
            self._stack.pop().__exit__(None, None, None)
        self._smap = {}

    def barrier(self):
        for eng in self.ENGS:
            waits = []
            for key, v in self.cnt.items():
                if v == 0 or key == "E_" + eng:
                    continue
                if self.known[eng].get(key, 0) >= v:
                    continue
                self.known[eng][key] = v
                waits.append((key, v))
            if waits:
                self.q[eng].append((waits, None, None, 0))
                self.n_inst += len(waits)

    def dsem(self, key):
        m = self._smap
        if key not in m:
            phys = f"dma{len(m)}"
            if phys not in self.sems:
                self._mksem_global(phys)
            m[key] = phys
        return m[key]

    def op(self, eng, fn, reads=(), writes=()):
        self._emit(eng, fn, list(reads), list(writes), "E_" + eng, 1)

    def dma(self, eng, out, in_, reads=(), writes=(), sem=None, track_w=True, **kw):
        sem = self.dsem(sem)
        self._emit(eng, lambda e: e.dma_start(out=out, in_=in_, **kw), list(reads), list(writes), sem, 16,
                   track_w=track_w)

    def store(self, eng, out, in_, reads, ddep, **kw):
        sem = "st_" + reads[0].name
        self.dma(eng, out, in_, reads=reads, writes=[], sem=sem, **kw)
        sem = self.dsem(sem)
        if not isinstance(ddep.w, dict):
            ddep.w = {}
        ddep.w[sem] = self.cnt[sem]

    def build(self, final_deps=()):
        nc = self.nc
        fw = self._waits("sp", list(final_deps), [])
        sems = self.sems
        q = self.q

        def replay(eh, items, extra=()):
            for waits, fn, semkey, inc in items:
                for k, v in waits:
                    eh.wait_ge(sems[k], v)
                if fn is not None:
                    fn(eh).then_inc(sems[semkey], inc)
            for k, v in extra:
                eh.wait_ge(sems[k], v)

        with nc.Block() as block:
            @block.tensor
            def _(e):
                replay(e, q["pe"])

            @block.scalar
            def _(e):
                replay(e, q["act"])

            @block.vector
            def _(e):
                replay(e, q["dve"])

            @block.gpsimd
            def _(e):
                replay(e, q["pool"])

            @block.sync
            def _(e):
                replay(e, q["sp"], fw)
        self.close()


class Ring:
    def __init__(self, p, name, shape, n=2):
        self.t = [p.sbuf(f"{name}{i}", shape) for i in range(n)]
        self.d = [Dep(f"{name}{i}") for i in range(n)]
        self.i = 0
        self.n = n
        self.name = name

    def next(self):
        i = self.i
        self.i = (i + 1) % self.n
        return self.t[i], self.d[i], f"ld_{self.name}{i}"


_o = 0
COLS = {}
for _n, _s in (("na_k", 512), ("na_v", 512), ("dn_k", 512), ("dn_v", 512), ("dn_a", 8), ("dn_b", 8),
               ("na_q", 512), ("dn_q", 512), ("dn_z", 512), ("fn_u", 512), ("gate", 3072)):
    COLS[_n] = (_o, _o + _s)
    _o += _s
IN_W = _o
FB_GROUPS = (("na_q", 4), ("na_k", 4), ("dn_q", 4), ("dn_k", 4), ("dn_v", 4), ("fn_u", 4), ("gate", 24))
FB_CHUNKS = [(g, i) for g, n in FB_GROUPS for i in range(n)]
NFB = len(FB_CHUNKS)
FA_COLS = np.concatenate([np.arange(*COLS["na_v"]), np.arange(*COLS["dn_z"]),
                          np.arange(*COLS["dn_a"]), np.arange(*COLS["dn_b"])])


def _fm(v, nchunk):
    return np.ascontiguousarray(np.asarray(v, np.float32).reshape(nchunk, 128).T)


def _lhs_chunks(w, cols):
    K = w.shape[0]
    sub = w[:, cols]
    nc_ = sub.shape[1] // 128
    a = sub.reshape(K // 128, 128, nc_, 128)
    return np.ascontiguousarray(a.transpose(2, 1, 0, 3))


def _rhs_rows(w):
    K, N = w.shape
    return np.ascontiguousarray(w.reshape(K // 128, 128, N).transpose(1, 0, 2))


def host_layout(inp):
    m = {}
    x = np.concatenate([inp["x"][0], inp["ctx"][0]], axis=0)
    m["xin"] = np.ascontiguousarray(x.T.reshape(8, 128, NT))
    m["cvec"] = np.ascontiguousarray(np.stack([_fm(inp["c"][0], 8), _fm(inp["c_ctx"], 8)], axis=2))
    for l in range(2):
        w_in = inp["w_in"][l]
        fbcols = np.concatenate([np.arange(COLS[g][0] + i * 128, COLS[g][0] + (i + 1) * 128) for g, i in FB_CHUNKS])
        m[f"w_inb{l}"] = _lhs_chunks(w_in, fbcols)
        m[f"w_ina{l}"] = _rhs_rows(w_in[:, FA_COLS])
        m[f"w_ada{l}"] = _lhs_chunks(inp["w_ada"][l], np.arange(6 * D))
        m[f"b_ada{l}"] = _fm(inp["b_ada"][l], 48)
        m[f"norm1_{l}"] = _fm(inp["norm1"][l], 8)
        m[f"norm2_{l}"] = _fm(inp["norm2"][l], 8)
        ab = np.concatenate([inp["a_log"][l].reshape(-1), inp["dt_bias"][l].reshape(-1)])
        m[f"abt{l}"] = np.ascontiguousarray(np.broadcast_to(ab[None, :], (128, 16))).astype(np.float32)
        m[f"convw{l}"] = np.ascontiguousarray(inp["conv_w"][l].T.reshape(12, 128, 5).transpose(1, 0, 2))
    cos, sin, rm = rope_tables()
    m["rope_cos"], m["rope_sin"], m["rope_rm"] = cos, sin, rm
    mk = dn_masks()
    m["dnmask0"], m["dnmask1"] = mk[0], mk[1]
    for l in range(2):
        ar = np.arange(D)
        m[f"w_nao{l}"] = _lhs_chunks(inp["w_na_o"][l], ar)
        m[f"w_dno{l}"] = _lhs_chunks(inp["w_dn_o"][l], ar)
        m[f"w_fno{l}"] = _lhs_chunks(inp["w_fn"][l], ar)
        m[f"w_out{l}"] = _lhs_chunks(inp["w_out"][l], ar)
        m[f"w_mlp1_{l}"] = _lhs_chunks(inp["w_mlp1"][l], np.arange(4 * D))
        m[f"w_mlp2_{l}"] = _lhs_chunks(inp["w_mlp2"][l], ar)
        m[f"dnw{l}"] = np.ascontiguousarray(np.broadcast_to(inp["dn_norm"][l][None, :], (128, 128))).astype(np.float32)
        m[f"natab{l}"] = na_table(inp["rpb"][l])
    m["fn_dft"], m["fn_tw"], m["fn_d256"] = fn_tables()
    m["norm_f"] = _fm(inp["norm_f"], 8)
    ident = np.eye(128, dtype=np.float32)
    m["ident"] = ident
    return m


class K:
    def __init__(self, nc, p, shapes):
        self.nc = nc
        self.p = p
        self.inp = {k: nc.dram_tensor(k, list(v), F32, kind="ExternalInput").ap() for k, v in shapes.items()}
        self.scr = {}
        self.sdep = {}
        self.banks = [p.psum(f"bank{i}", [128, 512]) for i in range(8)]
        self.bdep = [Dep(f"bank{i}") for i in range(8)]
        self.cdep = Dep("consts")
        self._ld = 0

    def scratch(self, name, shape, kind="Internal"):
        if name in self.inp and name not in self.scr:
            self.scr[name] = self.inp[name]
            self.sdep[name] = Dep(name)
        if name not in self.scr:
            self.scr[name] = self.nc.dram_tensor(name, list(shape), F32, kind=kind).ap()
            self.sdep[name] = Dep(name)
        return self.scr[name], self.sdep[name]

    def const(self, name, shape, src, eng="sp"):
        t = self.p.sbuf("c_" + name, shape)
        d = Dep("c_" + name)
        self.p.dma(eng, t[:], src, writes=[d], sem="ld_c_" + name)
        return t, d


def stage_mod(k, l):
    p = k.p
    cv, dcv = k.const("cvec", [128, 8, 2], k.inp["cvec"])
    cs = p.sbuf("cs", [128, 8, 2])
    dcs = Dep("cs")
    p.op("act", lambda e: e.activation(out=cs[:], in_=cv[:], func=AF.Silu), reads=[dcv], writes=[dcs])
    mod, dmod = k.modt[l]
    bt, dbt = k.const(f"b_ada{l}", [128, 48], k.inp[f"b_ada{l}"])
    ring = Ring(p, f"wada{l}_", [128, 8, 128], 3)
    wsrc = k.inp[f"w_ada{l}"]
    for j in range(48):
        wt, dw, sem = ring.next()
        p.dma("sp", wt[:], wsrc[j], writes=[dw], sem=sem)
        b = j % 2
        ps = k.banks[b]
        for kk in range(8):
            p.op("pe", lambda e, wt=wt, kk=kk, ps=ps: e.matmul(ps[:, 0:2], lhsT=wt[:, kk, :], rhs=cs[:, kk, :],
                                                              start=(kk == 0), stop=(kk == 7)),
                 reads=[dw, dcs], writes=[k.bdep[b]])
        p.op("dve", lambda e, ps=ps, j=j: e.tensor_scalar(out=mod[:, j, :], in0=ps[:, 0:2], scalar1=bt[:, j:j + 1],
                                                        scalar2=None, op0=ALU.add),
             reads=[k.bdep[b], dbt], writes=[dmod])
    return mod, dmod


def norm_mod(k, xt, dx, N, gam, dgam, mod, dmod, sh_j, sc_j, col, out, dout, tmp, dtmp, ones_t, dones, bank):
    p = k.p
    ps = k.banks[bank]
    p.op("act", lambda e: e.activation(out=tmp[:, :, 0:N], in_=xt[:, :, 0:N], func=AF.Square), reads=[dx], writes=[dtmp])
    for kk in range(8):
        p.op("pe", lambda e, kk=kk: e.matmul(ps[:, 0:N], lhsT=ones_t[:], rhs=tmp[:, kk, 0:N], start=(kk == 0), stop=(kk == 7)),
             reads=[dtmp, dones], writes=[k.bdep[bank]])
    rstd = k.rstd
    p.op("dve", lambda e: e.tensor_scalar(out=rstd[:, 0:N], in0=ps[:, 0:N], scalar1=EPS, scalar2=None, op0=ALU.add),
         reads=[k.bdep[bank]], writes=[k.drstd])
    p.op("act", lambda e: e.activation(out=rstd[:, 0:N], in_=rstd[:, 0:N], func=AF.Sqrt), reads=[k.drstd], writes=[k.drstd])
    p.op("dve", lambda e: e.reciprocal(out=rstd[:, 0:N], in_=rstd[:, 0:N]), reads=[k.drstd], writes=[k.drstd])
    for kk in range(8):
        eng = "dve" if kk % 2 == 0 else "pool"
        p.op(eng, lambda e, kk=kk: e.tensor_tensor(out=tmp[:, kk, 0:N], in0=xt[:, kk, 0:N], in1=rstd[:, 0:N], op=ALU.mult),
             reads=[dx, k.drstd], writes=[dtmp])
    gm = k.gm
    for kk in range(8):
        p.op("act", lambda e, kk=kk: e.activation(out=out[:, kk, 0:N], in_=tmp[:, kk, 0:N], func=AF.Identity,
                                                  bias=mod[:, sh_j * 8 + kk, col:col + 1], scale=gm[:, kk, col:col + 1]),
             reads=[dtmp, k.dgm, dmod], writes=[dout])


def make_gm(k, gam, dgam, mod, dmod, sc_j):
    p = k.p
    gm = k.gm
    for col in range(2):
        p.op("dve", lambda e, col=col: e.scalar_tensor_tensor(out=gm[:, :, col], in0=mod[:, sc_j * 8:sc_j * 8 + 8, col], scalar=1.0,
                                                              in1=gam[:, :], op0=ALU.add, op1=ALU.mult),
             reads=[dmod, dgam], writes=[k.dgm])


def super_tiles():
    st = [(i * 256, 256, 0) for i in range(64)]
    st.append((T, 256, 1))
    return st


def stage_proj(k, l, xname, mod, dmod, tiles=None):
    p = k.p
    xsrc, dxs = k.scratch(xname, [8, 128, NT])
    QT, dQT = k.scratch(f"QT{l}", [8, 64, NT])
    KT, dKT = k.scratch(f"KT{l}", [8, 64, NT])
    V, dV = k.scratch(f"V{l}", [NT, 512])
    DQ, dDQ = k.scratch(f"DQ{l}", [4, 128, NT])
    DK, dDK = k.scratch(f"DK{l}", [4, 128, NT])
    DV, dDV = k.scratch(f"DV{l}", [4, 128, NT])
    Z, dZ = k.scratch(f"Z{l}", [NT, 512])
    GB, dGB = k.scratch(f"GB{l}", [NT, 16])
    U, dU = k.scratch(f"U{l}", [4, 128, NT])
    GATE, dGATE = k.scratch(f"GATE{l}", [24, 128, NT])
    gam, dgam = k.const(f"norm1_{l}", [128, 8], k.inp[f"norm1_{l}"])
    abt, dabt = k.const(f"abt{l}", [128, 16], k.inp[f"abt{l}"])
    nexp = p.sbuf(f"nexp{l}", [128, 8])
    dnexp = Dep("nexp")
    p.op("act", lambda e: e.activation(out=nexp[:], in_=abt[:, 0:8], func=AF.Exp), reads=[dabt], writes=[dnexp])
    p.op("dve", lambda e: e.tensor_scalar(out=nexp[:], in0=nexp[:], scalar1=-1.0, scalar2=None, op0=ALU.mult), reads=[dnexp], writes=[dnexp])
    make_gm(k, gam, dgam, mod, dmod, 1)
    wa, dwa = k.const(f"w_ina{l}", [128, 8, 1040], k.inp[f"w_ina{l}"], eng="act")
    xr = Ring(p, f"px{l}_", [128, 8, 256], 2)
    hr = Ring(p, f"ph{l}_", [128, 8, 256], 8)
    tr = Ring(p, f"pt{l}_", [128, 8, 256], 1)
    wr = Ring(p, f"pw{l}_", [128, 8, 128], 3)
    orr = Ring(p, f"po{l}_", [128, 512], 4)
    gr = Ring(p, f"pg{l}_", [128, 48], 2)
    wsrc = k.inp[f"w_inb{l}"]
    fb_dst = {"na_q": None, "na_k": None, "dn_q": (DQ, dDQ), "dn_k": (DK, dDK), "dn_v": (DV, dDV), "fn_u": (U, dU), "gate": (GATE, dGATE)}
    nbank = 0
    tl = list(tiles or super_tiles())
    pairs = [tl[i:i + 8] for i in range(0, len(tl), 8)]
    for pair in pairs:
      hts = []
      for (t0, N, col) in pair:
        xt, dx, sem = xr.next()
        p.dma("sp", xt[:, :, 0:N], xsrc[:, :, t0:t0 + N].rearrange("k p t -> p k t"), reads=[dxs], writes=[dx], sem=sem)
        ht, dh, _ = hr.next()
        tmp, dtmp, _ = tr.next()
        norm_mod(k, xt, dx, N, gam, dgam, mod, dmod, 0, 1, col, ht, dh, tmp, dtmp, k.ones_t, k.dones, 7)
        hts.append((ht, dh))
      for c, (g, gi) in enumerate(FB_CHUNKS):
        wt, dw, sem = wr.next()
        p.dma("sp" if c % 2 == 0 else "act", wt[:], wsrc[c], writes=[dw], sem=sem)
        for (t0, N, col), (ht, dh) in zip(pair, hts):
            b = nbank % 4
            nbank += 1
            ps = k.banks[b]
            for kk in range(8):
                p.op("pe", lambda e, wt=wt, kk=kk, ps=ps, ht=ht, N=N: e.matmul(ps[:, 0:N], lhsT=wt[:, kk, :], rhs=ht[:, kk, 0:N],
                                                                         start=(kk == 0), stop=(kk == 7)),
                     reads=[dw, dh], writes=[k.bdep[b]])
            ot, do, _ = orr.next()
            if g == "gate":
                p.op("act", lambda e, ot=ot, ps=ps, N=N: e.activation(out=ot[:, 0:N], in_=ps[:, 0:N], func=AF.Sigmoid),
                     reads=[k.bdep[b]], writes=[do])
            elif g == "na_q":
                p.op("act", lambda e, ot=ot, ps=ps, N=N: e.activation(out=ot[:, 0:N], in_=ps[:, 0:N], func=AF.Copy, scale=0.125),
                     reads=[k.bdep[b]], writes=[do])
            else:
                p.op("dve", lambda e, ot=ot, ps=ps, N=N: e.tensor_copy(out=ot[:, 0:N], in_=ps[:, 0:N]), reads=[k.bdep[b]], writes=[do])
            if g in ("na_q", "na_k"):
                dst, dd = (QT, dQT) if g == "na_q" else (KT, dKT)
                p.store("sp", dst[2 * gi:2 * gi + 2, :, t0:t0 + N].rearrange("h d t -> (h d) t"), ot[:, 0:N], [do], dd)
            else:
                dst, dd = fb_dst[g]
                p.store("sp", dst[gi, :, t0:t0 + N], ot[:, 0:N], [do], dd)
      for (t0, N, col), (ht, dh) in zip(pair, hts):
        for tb in range(N // 128):
            tok = t0 + tb * 128
            for gi, (c0, cn, dst, dd) in enumerate(((0, 512, V, dV), (512, 512, Z, dZ), (1024, 16, None, None))):
                b = nbank % 4
                nbank += 1
                ps = k.banks[b]
                for kk in range(8):
                    p.op("pe", lambda e, kk=kk, ps=ps, ht=ht, tb=tb, c0=c0, cn=cn: e.matmul(
                        ps[:, 0:cn], lhsT=ht[:, kk, tb * 128:(tb + 1) * 128], rhs=wa[:, kk, c0:c0 + cn], start=(kk == 0), stop=(kk == 7)),
                        reads=[dwa, dh], writes=[k.bdep[b]])
                if dst is not None:
                    ot, do, _ = orr.next()
                    p.op("act" if gi == 0 else "dve", (lambda e, ot=ot, ps=ps: e.activation(out=ot[:], in_=ps[:], func=AF.Copy)) if gi == 0 else
                         (lambda e, ot=ot, ps=ps: e.tensor_copy(out=ot[:], in_=ps[:])), reads=[k.bdep[b]], writes=[do])
                    p.store("sp", dst[tok:tok + 128, :], ot[:], [do], dd)
                else:
                    gt, dg, _ = gr.next()
                    p.op("dve", lambda e, gt=gt, ps=ps: e.tensor_tensor(out=gt[:, 0:8], in0=ps[:, 0:8], in1=abt[:, 8:16], op=ALU.add),
                         reads=[k.bdep[b], dabt], writes=[dg])
                    p.op("act", lambda e, gt=gt: e.activation(out=gt[:, 0:8], in_=gt[:, 0:8], func=AF.Exp), reads=[dg], writes=[dg])
                    p.op("act", lambda e, gt=gt: e.activation(out=gt[:, 0:8], in_=gt[:, 0:8], func=AF.Ln, bias=1.0), reads=[dg], writes=[dg])
                    p.op("dve", lambda e, gt=gt: e.tensor_tensor(out=gt[:, 0:8], in0=gt[:, 0:8], in1=nexp[:], op=ALU.mult),
                         reads=[dg, dnexp], writes=[dg])
                    p.op("act", lambda e, gt=gt, ps=ps: e.activation(out=gt[:, 8:16], in_=ps[:, 8:16], func=AF.Sigmoid),
                         reads=[k.bdep[b]], writes=[dg])
                    p.store("sp", GB[tok:tok + 128, :], gt[:, 0:16], [dg], dGB)


def common(k):
    p = k.p
    k.ident, k.dident = k.const("ident", [128, 128], k.inp["ident"])
    k.ones_t = p.sbuf("ones_t", [128, 128])
    k.dones = Dep("ones")
    p.op("pool", lambda e: e.memset(k.ones_t[:], 1.0 / D), writes=[k.dones])
    k.one1 = p.sbuf("one1", [128, 128])
    k.done1 = Dep("one1")
    p.op("pool", lambda e: e.memset(k.one1[:], 1.0), writes=[k.done1])
    k.rstd = p.sbuf("rstd", [128, 512])
    k.drstd = Dep("rstd")
    k.gm = p.sbuf("gm", [128, 8, 2])
    k.dgm = Dep("gm")
    k.modt = [(p.sbuf(f"mod{l}", [128, 48, 2]), Dep(f"mod{l}")) for l in range(2)]


def rope_tables():
    half = 64
    inv = (1.0 / (10000.0 ** (np.arange(0, half, 2, dtype=np.float32) / np.float32(half)))).astype(np.float32)
    t = np.arange(T)
    pos = np.stack([t // GW, t % GW], axis=-1).astype(np.float32)
    ang = pos[:, :, None] * inv[None, None, :]
    ang = np.concatenate([ang, ang], axis=-1).reshape(T, 128)
    cos = np.cos(ang).astype(np.float32).T
    sin = np.sin(ang).astype(np.float32).T
    rm = np.zeros((128, 128), np.float32)
    for o in (0, 64):
        for i in range(64):
            if i < 32:
                rm[o + i + 32, o + i] = -1.0
            else:
                rm[o + i - 32, o + i] = 1.0
    return np.ascontiguousarray(cos), np.ascontiguousarray(sin), rm


def dn_masks():
    i = np.arange(128)
    out = {}
    for d in range(2):
        le = (i[:, None] <= i[None, :]) if d == 0 else (i[:, None] >= i[None, :])
        gt = (i[:, None] > i[None, :]) if d == 0 else (i[:, None] < i[None, :])
        blk = lambda b: (i[:, None] // b) == (i[None, :] // b)
        ms = [le, gt, gt, ~le | (i[:, None] == i[None, :]), le, blk(16)]
        for b in (16, 32, 64):
            up = blk(2 * b) & ((i[:, None] // b) < (i[None, :] // b))
            mN = up if d == 0 else up.T
            ms += [mN, mN.T]
        out[d] = np.ascontiguousarray(np.stack(ms, axis=1).astype(np.float32))
    return out


def stage_dn_pre(k, l, tiles=None):
    p = k.p
    src = [k.scratch(f"D{x}{l}", [4, 128, NT]) for x in "QKV"]
    dst = [k.scratch(f"D{x}2_{l}", [4, 128, NT]) for x in "QKV"]
    cw, dcw = k.const(f"convw{l}", [128, 12, 5], k.inp[f"convw{l}"])
    k.rm, k.drm = k.const("rope_rm", [128, 128], k.inp["rope_rm"])
    ur = Ring(p, f"du{l}_", [128, 260], 3)
    ar = Ring(p, f"da{l}_", [128, 256], 3)
    yr = Ring(p, f"dy{l}_", [128, 256], 3)
    sr = Ring(p, f"ds{l}_", [128, 256], 2)
    rr = Ring(p, f"dr{l}_", [128, 256], 2)
    cr = Ring(p, f"dc{l}_", [128, 2, 256], 2)
    outr = Ring(p, f"do{l}_", [128, 256], 3)
    cosd, sind = k.inp["rope_cos"], k.inp["rope_sin"]
    nb = 0
    for (t0, N, col) in (tiles or super_tiles()):
        seg0, seg1 = (0, T) if col == 0 else (T, T + CT)
        ct, dct = None, None
        if col == 0:
            ct, dct, sem = cr.next()
            p.dma("sp", ct[:, 0, :], cosd[:, t0:t0 + N], writes=[dct], sem=sem)
            p.dma("sp", ct[:, 1, :], sind[:, t0:t0 + N], writes=[dct], sem=sem)
        for h in range(4):
            for xi in range(3):
                ut, du, sem = ur.next()
                lo, hi = max(t0 - 2, seg0), min(t0 + N + 2, seg1)
                if lo != t0 - 2 or hi != t0 + N + 2:
                    p.op("pool", lambda e, ut=ut: e.memset(ut[:], 0.0), writes=[du])
                p.dma("sp" if xi != 1 else "act", ut[:, lo - (t0 - 2):hi - (t0 - 2)], src[xi][0][h, :, lo:hi], reads=[src[xi][1]], writes=[du], sem=sem)
                at, da, _ = ar.next()
                eng = "dve"
                j = xi * 4 + h
                p.op(eng, lambda e, at=at, ut=ut, j=j: e.tensor_scalar(out=at[:, 0:N], in0=ut[:, 0:N], scalar1=cw[:, j, 0:1], scalar2=None, op0=ALU.mult),
                     reads=[du, dcw], writes=[da])
                for tap in range(1, 5):
                    p.op(eng, lambda e, at=at, ut=ut, j=j, tap=tap: e.scalar_tensor_tensor(out=at[:, 0:N], in0=ut[:, tap:tap + N], scalar=cw[:, j, tap:tap + 1],
                                                                                         in1=at[:, 0:N], op0=ALU.mult, op1=ALU.add),
                         reads=[du, dcw, da], writes=[da])
                yt, dy, _ = yr.next()
                p.op("act", lambda e, yt=yt, at=at: e.activation(out=yt[:, 0:N], in_=at[:, 0:N], func=AF.Silu), reads=[da], writes=[dy])
                if xi == 2:
                    p.store("sp", dst[2][0][h, :, t0:t0 + N], yt[:, 0:N], [dy], dst[2][1])
                    continue
                st, dsq, _ = sr.next()
                p.op("pool", lambda e, st=st, yt=yt: e.tensor_tensor(out=st[:, 0:N], in0=yt[:, 0:N], in1=yt[:, 0:N], op=ALU.mult), reads=[dy], writes=[dsq])
                b = 4 + (nb % 2)
                nb += 1
                ps = k.banks[b]
                p.op("pe", lambda e, ps=ps, st=st: e.matmul(ps[:, 0:N], lhsT=k.one1[:], rhs=st[:, 0:N], start=True, stop=True),
                     reads=[dsq, k.done1], writes=[k.bdep[b]])
                p.op("dve", lambda e, ps=ps, st=st: e.tensor_scalar(out=st[:, 0:N], in0=ps[:, 0:N], scalar1=EPS, scalar2=None, op0=ALU.add),
                     reads=[k.bdep[b]], writes=[dsq])
                p.op("act", lambda e, st=st: e.activation(out=st[:, 0:N], in_=st[:, 0:N], func=AF.Sqrt), reads=[dsq], writes=[dsq])
                p.op("dve", lambda e, st=st: e.reciprocal(out=st[:, 0:N], in_=st[:, 0:N]), reads=[dsq], writes=[dsq])
                ot, do, _ = outr.next()
                if col == 1:
                    p.op("dve", lambda e, ot=ot, yt=yt, st=st: e.tensor_tensor(out=ot[:, 0:N], in0=yt[:, 0:N], in1=st[:, 0:N], op=ALU.mult),
                         reads=[dy, dsq], writes=[do])
                else:
                    p.op("dve", lambda e, yt=yt, st=st: e.tensor_tensor(out=yt[:, 0:N], in0=yt[:, 0:N], in1=st[:, 0:N], op=ALU.mult),
                         reads=[dy, dsq], writes=[dy])
                    b2 = 6 + (nb % 2)
                    ps2 = k.banks[b2]
                    p.op("pe", lambda e, ps2=ps2, yt=yt: e.matmul(ps2[:, 0:N], lhsT=k.rm[:], rhs=yt[:, 0:N], start=True, stop=True),
                         reads=[dy, k.drm], writes=[k.bdep[b2]])
                    rt, dr, _ = rr.next()
                    p.op("dve", lambda e, rt=rt, ps2=ps2, ct=ct: e.tensor_tensor(out=rt[:, 0:N], in0=ps2[:, 0:N], in1=ct[:, 1, 0:N], op=ALU.mult),
                         reads=[k.bdep[b2], dct], writes=[dr])
                    p.op("pool", lambda e, ot=ot, yt=yt, ct=ct: e.tensor_tensor(out=ot[:, 0:N], in0=yt[:, 0:N], in1=ct[:, 0, 0:N], op=ALU.mult),
                         reads=[dy, dct], writes=[do])
                    p.op("pool", lambda e, ot=ot, rt=rt: e.tensor_tensor(out=ot[:, 0:N], in0=ot[:, 0:N], in1=rt[:, 0:N], op=ALU.add),
                         reads=[dr, do], writes=[do])
                p.store("sp", dst[xi][0][h, :, t0:t0 + N], ot[:, 0:N], [do], dst[xi][1])


class PS4:
    def __init__(self, k, si):
        bs = [2 * si, 2 * si + 1]
        self.t = [k.banks[b][:, 0:128] for b in bs]
        self.d = [k.bdep[b] for b in bs]
        self.i = 0

    def next(self):
        i = self.i
        self.i = (i + 1) % 2
        return self.t[i], self.d[i]


def dn_scan_gen(k, l, si, h, d, blocks, want_o):
    p = k.p
    SC = 128 ** -0.5
    QT, dQ = k.scratch(f"DQ2_{l}", [4, 128, NT])
    KT, dK = k.scratch(f"DK2_{l}", [4, 128, NT])
    VT, dV = k.scratch(f"DV2_{l}", [4, 128, NT])
    GB, dGB = k.scratch(f"GB{l}", [NT, 16])
    OD, dOD = k.scratch(f"OD{l}", [2, 4, NT, 128])
    mk, dmk = k.dnm[d]
    U, SL, MLs, MLD, MLDT = (mk[:, i, :] for i in range(5))
    BD16 = mk[:, 5, :]
    MRG = [(mk[:, 6 + 2 * j, :], mk[:, 7 + 2 * j, :]) for j in range(3)]
    B = k.dnbuf[si]
    ps = PS4(k, si)
    S = [B["S0"], B["S1"]]
    dS = [B["d_S0"], B["d_S1"]]
    p.op("pool", lambda e: e.memset(S[0][:], 0.0), writes=[dS[0]])
    cur = 0
    ident = k.ident

    def T_(name):
        return B[name], B["d_" + name]
    for bi, tok in enumerate(blocks):
        par = bi % 2
        qT, dq = T_(f"qT{par}"); kT, dk_ = T_(f"kT{par}"); vT, dv = T_(f"vT{par}"); gb, dgb = T_(f"gb{par}")
        lsem = f"ld_blk{par}_{si}"
        p.dma("sp", qT[:], QT[h, :, tok:tok + 128], reads=[dQ], writes=[dq], sem=lsem)
        p.dma("act", kT[:], KT[h, :, tok:tok + 128], reads=[dK], writes=[dk_], sem=lsem)
        p.dma("sp", vT[:], VT[h, :, tok:tok + 128], reads=[dV], writes=[dv], sem=lsem)
        p.dma("act", gb[:], GB[tok:tok + 128, :], reads=[dGB], writes=[dgb], sem=lsem)
        for dd_ in (dq, dk_, dv, dgb):
            dd_.w = dgb.w
        g = gb[:, d * 4 + h:d * 4 + h + 1]
        beta = gb[:, 8 + d * 4 + h:8 + d * 4 + h + 1]
        sc, dsc = T_("sc")
        ktok, dkt = T_("ktok"); vb, dvb = T_("vb"); SLg, dSLg = T_("SLg"); Dm, dDm = T_("Dm")
        Dstr, dDstr = T_("Dstr"); Dlow, dDlow = T_("Dlow"); DlowT, dDlowT = T_("DlowT")
        Na, dNa = T_("Na"); Nb, dNb = T_("Nb"); NTa, dNTa = T_("NTa"); NTb, dNTb = T_("NTb")
        Xa, dXa = T_("Xa"); Xb, dXb = T_("Xb")
        kbg, dkbg = T_("kbg"); ktail, dktail = T_("ktail"); u_sb, du = T_("u"); wT, dwT = T_("wT"); qkT, dqkT = T_("qkT")
        vnew, dvn = T_("vnew"); ob, dob = T_("ob"); o_sb, do = T_(f"o{par}")
        p1, dp1 = ps.next()
        p.op("pe", lambda e, p1=p1, kT=kT: e.transpose(out=p1, in_=kT[:], identity=ident[:]), reads=[dk_, k.dident], writes=[dp1])
        p.op("act", lambda e, p1=p1, ktok=ktok: e.activation(out=ktok[:], in_=p1, func=AF.Copy), reads=[dp1], writes=[dkt])
        p2, dp2 = ps.next()
        p.op("pe", lambda e, p2=p2, vT=vT: e.transpose(out=p2, in_=vT[:], identity=ident[:]), reads=[dv, k.dident], writes=[dp2])
        p.op("dve", lambda e, p2=p2, vb=vb, beta=beta: e.tensor_scalar(out=vb[:], in0=p2, scalar1=beta, scalar2=None, op0=ALU.mult),
             reads=[dp2, dgb], writes=[dvb])
        p.op("pool", lambda e, SLg=SLg, g=g: e.tensor_scalar(out=SLg[:], in0=SL, scalar1=g, scalar2=None, op0=ALU.mult),
             reads=[dmk, dgb], writes=[dSLg])
        p.op("pool", lambda e, sc=sc, beta=beta: e.tensor_scalar(out=sc[:, 0:1], in0=beta, scalar1=-1.0, scalar2=None, op0=ALU.mult),
             reads=[dgb], writes=[dsc])
        yield
        p3, dp3 = ps.next()
        p.op("pe", lambda e, p3=p3, SLg=SLg: e.matmul(p3, lhsT=U, rhs=SLg[:], start=True, stop=True), reads=[dSLg, dmk], writes=[dp3])
        p4, dp4 = ps.next()
        p.op("pe", lambda e, p4=p4, g=g: e.matmul(p4[:, 0:1], lhsT=U, rhs=g, start=True, stop=True), reads=[dgb, dmk], writes=[dp4])
        p.op("pe", lambda e, p4=p4, g=g: e.matmul(p4[:, 1:2], lhsT=k.one1[:], rhs=g, start=True, stop=True), reads=[dgb, k.done1], writes=[dp4])
        p.op("act", lambda e, p3=p3, Dm=Dm: e.activation(out=Dm[:], in_=p3, func=AF.Exp), reads=[dp3], writes=[dDm])
        p.op("dve", lambda e, p4=p4, sc=sc: e.tensor_copy(out=sc[:, 1:3], in_=p4[:, 0:2]), reads=[dp4], writes=[dsc])
        yield
        p.op("pool", lambda e, Dstr=Dstr, Dm=Dm: e.tensor_tensor(out=Dstr[:], in0=Dm[:], in1=MLs, op=ALU.mult), reads=[dDm, dmk], writes=[dDstr])
        p.op("pool", lambda e, Dlow=Dlow, Dm=Dm: e.tensor_tensor(out=Dlow[:], in0=Dm[:], in1=MLD, op=ALU.mult), reads=[dDm, dmk], writes=[dDlow])
        yield
        p.op("act", lambda e, sc=sc: e.activation(out=sc[:, 3:4], in_=sc[:, 1:2], func=AF.Exp), reads=[dsc], writes=[dsc])
        p.op("dve", lambda e, sc=sc: e.tensor_tensor(out=sc[:, 4:5], in0=sc[:, 2:3], in1=sc[:, 1:2], op=ALU.subtract), reads=[dsc], writes=[dsc])
        p.op("act", lambda e, sc=sc: e.activation(out=sc[:, 4:5], in_=sc[:, 4:5], func=AF.Exp), reads=[dsc], writes=[dsc])
        p.op("act", lambda e, sc=sc: e.activation(out=sc[:, 5:6], in_=sc[:, 2:3], func=AF.Exp), reads=[dsc], writes=[dsc])
        yield
        p.op("dve", lambda e, sc=sc, beta=beta: e.tensor_tensor(out=sc[:, 6:7], in0=sc[:, 3:4], in1=beta, op=ALU.mult), reads=[dsc, dgb], writes=[dsc])
        p.op("dve", lambda e, sc=sc: e.tensor_scalar(out=sc[:, 7:8], in0=sc[:, 3:4], scalar1=SC, scalar2=None, op0=ALU.mult), reads=[dsc], writes=[dsc])
        yield
        p5, dp5 = ps.next()
        kT2, dkT2 = T_("kT2")
        p.op("pool", lambda e, kT2=kT2, kT=kT: e.tensor_copy(out=kT2[:], in_=kT[:]), reads=[dk_], writes=[dkT2])
        p.op("pe", lambda e, p5=p5, kT=kT, kT2=kT2: e.matmul(p5, lhsT=kT[:], rhs=kT2[:], start=True, stop=True), reads=[dk_, dkT2], writes=[dp5])
        yield
        p.op("dve", lambda e, p5=p5, NTa=NTa, sc=sc: e.tensor_scalar(out=NTa[:], in0=p5, scalar1=sc[:, 0:1], scalar2=None, op0=ALU.mult),
             reads=[dp5, dsc], writes=[dNTa])
        p.op("pool", lambda e, NTa=NTa, Dstr=Dstr: e.tensor_tensor(out=NTa[:], in0=NTa[:], in1=Dstr[:], op=ALU.mult),
             reads=[dDstr, dNTa], writes=[dNTa])
        yield
        N0, dN0 = T_("N0"); NT0 = NTa; dNT0 = dNTa
        p6, dp6 = ps.next()
        p.op("pe", lambda e, p6=p6: e.transpose(out=p6, in_=NT0[:], identity=ident[:]), reads=[dNT0, k.dident], writes=[dp6])
        p.op("act", lambda e, p6=p6: e.activation(out=N0[:], in_=p6, func=AF.Copy), reads=[dp6], writes=[dN0])
        yield
        Xc, dXc = T_("Xa"); XTc, dXTc = T_("XTa"); Xn_, dXn_ = T_("Xb"); XTn_, dXTn_ = T_("XTb")
        Np, dNp = T_("Na"); NTp, dNTp = T_("NTb"); Nn, dNn = T_("Nb"); NTn, dNTn = T_("NTc")
        p.op("pool", lambda e, Np=Np, N0=N0: e.tensor_tensor(out=Np[:], in0=N0[:], in1=BD16, op=ALU.mult), reads=[dN0, dmk], writes=[dNp])
        p.op("pool", lambda e, NTp=NTp, NT0=NT0: e.tensor_tensor(out=NTp[:], in0=NT0[:], in1=BD16, op=ALU.mult), reads=[dNT0, dmk], writes=[dNTp])
        p.op("dve", lambda e, Xc=Xc, Np=Np: e.tensor_tensor(out=Xc[:], in0=Np[:], in1=ident[:], op=ALU.add), reads=[dNp, k.dident], writes=[dXc])
        p.op("dve", lambda e, XTc=XTc, NTp=NTp: e.tensor_tensor(out=XTc[:], in0=NTp[:], in1=ident[:], op=ALU.add), reads=[dNTp, k.dident], writes=[dXTc])
        yield
        for lvl in range(3):
            pa, dpa = ps.next()
            p.op("pe", lambda e, pa=pa, Np=Np, NTp=NTp: e.matmul(pa, lhsT=NTp[:], rhs=Np[:], start=True, stop=True), reads=[dNp, dNTp], writes=[dpa])
            pb, dpb = ps.next()
            p.op("pe", lambda e, pb=pb, Np=Np, NTp=NTp: e.matmul(pb, lhsT=Np[:], rhs=NTp[:], start=True, stop=True), reads=[dNp, dNTp], writes=[dpb])
            p.op("act", lambda e, pa=pa, Nn=Nn: e.activation(out=Nn[:], in_=pa, func=AF.Copy), reads=[dpa], writes=[dNn])
            p.op("act", lambda e, pb=pb, NTn=NTn: e.activation(out=NTn[:], in_=pb, func=AF.Copy), reads=[dpb], writes=[dNTn])
            yield
            pc, dpc = ps.next()
            p.op("pe", lambda e, pc=pc, NTn=NTn, Xc=Xc: e.matmul(pc, lhsT=NTn[:], rhs=Xc[:], start=True, stop=True), reads=[dNTn, dXc], writes=[dpc])
            pd, dpd = ps.next()
            p.op("pe", lambda e, pd=pd, Nn=Nn, XTc=XTc: e.matmul(pd, lhsT=Nn[:], rhs=XTc[:], start=True, stop=True), reads=[dNn, dXTc], writes=[dpd])
            p.op("dve", lambda e, pc=pc, Xc=Xc, Xn_=Xn_: e.tensor_tensor(out=Xn_[:], in0=pc, in1=Xc[:], op=ALU.add), reads=[dpc, dXc], writes=[dXn_])
            p.op("dve", lambda e, pd=pd, XTc=XTc, XTn_=XTn_: e.tensor_tensor(out=XTn_[:], in0=pd, in1=XTc[:], op=ALU.add), reads=[dpd, dXTc], writes=[dXTn_])
            (Xc, dXc, Xn_, dXn_) = (Xn_, dXn_, Xc, dXc)
            (XTc, dXTc, XTn_, dXTn_) = (XTn_, dXTn_, XTc, dXTc)
            (Np, dNp, Nn, dNn) = (Nn, dNn, Np, dNp)
            (NTp, dNTp, NTn, dNTn) = (NTn, dNTn, NTp, dNTp)
            yield
        Nm, dNm = T_("Nm"); NTm, dNTm = T_("NTm"); Y, dY = T_("Y"); Y2, dY2 = T_("Y2")
        for j in range(3):
            Mj, MjT = MRG[j]
            last = (j == 2)
            p.op("pool", lambda e, Mj=Mj: e.tensor_tensor(out=NTm[:], in0=NT0[:], in1=MjT, op=ALU.mult) if False else e.tensor_tensor(out=NTm[:], in0=NT0[:], in1=MjT, op=ALU.mult),
                 reads=[dNT0, dmk], writes=[dNTm]) if False else None
            p.op("pool", lambda e, MjT=MjT: e.tensor_tensor(out=NTm[:], in0=NT0[:], in1=MjT, op=ALU.mult), reads=[dNT0, dmk], writes=[dNTm])
            if not last:
                p.op("pool", lambda e, Mj=Mj: e.tensor_tensor(out=Nm[:], in0=N0[:], in1=Mj, op=ALU.mult), reads=[dN0, dmk], writes=[dNm])
            py, dpy = ps.next()
            p.op("pe", lambda e, py=py, Xc=Xc: e.matmul(py, lhsT=NTm[:], rhs=Xc[:], start=True, stop=True), reads=[dNTm, dXc], writes=[dpy])
            p.op("act", lambda e, py=py: e.activation(out=Y[:], in_=py, func=AF.Copy), reads=[dpy], writes=[dY])
            if not last:
                py2, dpy2 = ps.next()
                p.op("pe", lambda e, py2=py2, XTc=XTc: e.matmul(py2, lhsT=Nm[:], rhs=XTc[:], start=True, stop=True), reads=[dNm, dXTc], writes=[dpy2])
                p.op("act", lambda e, py2=py2: e.activation(out=Y2[:], in_=py2, func=AF.Copy), reads=[dpy2], writes=[dY2])
            yield
            pz, dpz = ps.next()
            p.op("pe", lambda e, pz=pz, XTc=XTc: e.matmul(pz, lhsT=XTc[:], rhs=Y[:], start=True, stop=True), reads=[dXTc, dY], writes=[dpz])
            if not last:
                pz2, dpz2 = ps.next()
                p.op("pe", lambda e, pz2=pz2, Xc=Xc: e.matmul(pz2, lhsT=Xc[:], rhs=Y2[:], start=True, stop=True), reads=[dXc, dY2], writes=[dpz2])
            p.op("dve", lambda e, pz=pz, Xc=Xc, Xn_=Xn_: e.tensor_tensor(out=Xn_[:], in0=pz, in1=Xc[:], op=ALU.add), reads=[dpz, dXc], writes=[dXn_])
            if not last:
                p.op("dve", lambda e, pz2=pz2, XTc=XTc, XTn_=XTn_: e.tensor_tensor(out=XTn_[:], in0=pz2, in1=XTc[:], op=ALU.add), reads=[dpz2, dXTc], writes=[dXTn_])
                (XTc, dXTc, XTn_, dXTn_) = (XTn_, dXTn_, XTc, dXTc)
            (Xc, dXc, Xn_, dXn_) = (Xn_, dXn_, Xc, dXc)
            yield
        Xs = [(Xc, dXc)]
        X, dX = Xs[0]
        p.op("pool", lambda e, kbg=kbg, ktok=ktok, sc=sc: e.tensor_scalar(out=kbg[:], in0=ktok[:], scalar1=sc[:, 6:7], scalar2=None, op0=ALU.mult),
             reads=[dkt, dsc], writes=[dkbg])
        p.op("pool", lambda e, ktail=ktail, ktok=ktok, sc=sc: e.tensor_scalar(out=ktail[:], in0=ktok[:], scalar1=sc[:, 4:5], scalar2=None, op0=ALU.mult),
             reads=[dkt, dsc], writes=[dktail])
        pu, dpu = ps.next()
        p.op("pe", lambda e, pu=pu, X=X, vb=vb: e.matmul(pu, lhsT=X[:], rhs=vb[:], start=True, stop=True), reads=[dX, dvb], writes=[dpu])
        p.op("act", lambda e, pu=pu, u_sb=u_sb: e.activation(out=u_sb[:], in_=pu, func=AF.Copy), reads=[dpu], writes=[du])
        pw, dpw = ps.next()
        p.op("pe", lambda e, pw=pw, X=X, kbg=kbg: e.matmul(pw, lhsT=kbg[:], rhs=X[:], start=True, stop=True), reads=[dX, dkbg], writes=[dpw])
        p.op("act", lambda e, pw=pw, wT=wT: e.activation(out=wT[:], in_=pw, func=AF.Copy), reads=[dpw], writes=[dwT])
        yield
        if want_o:
            pt, dpt = ps.next()
            p.op("pe", lambda e, pt=pt, Dlow=Dlow: e.transpose(out=pt, in_=Dlow[:], identity=ident[:]), reads=[dDlow, k.dident], writes=[dpt])
            p.op("act", lambda e, pt=pt, DlowT=DlowT: e.activation(out=DlowT[:], in_=pt, func=AF.Copy), reads=[dpt], writes=[dDlowT])
            pq, dpq = ps.next()
            p.op("pe", lambda e, pq=pq, kT=kT, qT=qT: e.matmul(pq, lhsT=kT[:], rhs=qT[:], start=True, stop=True), reads=[dk_, dq], writes=[dpq])
            p.op("dve", lambda e, pq=pq, qkT=qkT, DlowT=DlowT: e.scalar_tensor_tensor(out=qkT[:], in0=pq, scalar=SC, in1=DlowT[:], op0=ALU.mult, op1=ALU.mult),
                 reads=[dpq, dDlowT], writes=[dqkT])
            yield
        Sc, dSc = S[cur], dS[cur]
        Sn, dSn = S[1 - cur], dS[1 - cur]
        pv, dpv = ps.next()
        p.op("pe", lambda e, pv=pv, wT=wT, Sc=Sc: e.matmul(pv, lhsT=wT[:], rhs=Sc[:], start=True, stop=True), reads=[dwT, dSc], writes=[dpv])
        po, dpo = ps.next()
        if want_o:
            p.op("pe", lambda e, po=po, qT=qT, Sc=Sc: e.matmul(po, lhsT=qT[:], rhs=Sc[:], start=True, stop=True), reads=[dq, dSc], writes=[dpo])
        p.op("dve", lambda e, pv=pv, vnew=vnew, u_sb=u_sb: e.tensor_tensor(out=vnew[:], in0=u_sb[:], in1=pv, op=ALU.subtract), reads=[dpv, du], writes=[dvn])
        if want_o:
            p.op("dve", lambda e, po=po, o_sb=o_sb, sc=sc: e.tensor_scalar(out=o_sb[:], in0=po, scalar1=sc[:, 7:8], scalar2=None, op0=ALU.mult),
                 reads=[dpo, dsc], writes=[do])
        yield
        pn, dpn = ps.next()
        p.op("pe", lambda e, pn=pn, ktail=ktail, vnew=vnew: e.matmul(pn, lhsT=ktail[:], rhs=vnew[:], start=True, stop=True), reads=[dktail, dvn], writes=[dpn])
        pb2, dpb2 = ps.next()
        if want_o:
            p.op("pe", lambda e, pb2=pb2, qkT=qkT, vnew=vnew: e.matmul(pb2, lhsT=qkT[:], rhs=vnew[:], start=True, stop=True), reads=[dqkT, dvn], writes=[dpb2])
        p.op("dve", lambda e, pn=pn, Sn=Sn, Sc=Sc, sc=sc: e.scalar_tensor_tensor(out=Sn[:], in0=Sc[:], scalar=sc[:, 5:6], in1=pn, op0=ALU.mult, op1=ALU.add),
             reads=[dpn, dSc, dsc], writes=[dSn])
        cur = 1 - cur
        if want_o:
            p.op("dve", lambda e, pb2=pb2, o_sb=o_sb: e.tensor_tensor(out=o_sb[:], in0=o_sb[:], in1=pb2, op=ALU.add), reads=[dpb2, do], writes=[do])
            p.store("sp", OD[d, h, tok:tok + 128, :], o_sb[:], [do], dOD)
        yield


def dn_blocks(d, nlat=128, with_ctx=True):
    cb = [T, T + 128]
    lb = [i * 128 for i in range(nlat)]
    if d == 1:
        cb = cb[::-1]
        lb = [i * 128 for i in range(128)][::-1][:nlat]
    return (cb if with_ctx else []) + lb


def stage_dn_scan(k, l, nlat=128, scans=None):
    p = k.p
    if True:
        k.dnm = [k.const(f"dnmask{d}", [128, 12, 128], k.inp[f"dnmask{d}"]) for d in range(2)]
        names = ["ktok", "vb", "SLg", "Dm", "Dstr", "Dlow", "DlowT", "Na", "Nb", "NTa", "NTb", "Xa", "Xb", "kbg", "ktail", "u", "wT", "qkT",
                 "vnew", "ob", "o0", "o1", "kT2", "N0", "XTa", "XTb", "NTc", "Nm", "NTm", "Y", "Y2", "qT0", "qT1", "kT0", "kT1", "vT0", "vT1", "S0", "S1"]
        k.dnbuf = []
        for si in range(4):
            B = {}
            for n in names:
                B[n] = p.sbuf(f"dn{si}_{n}", [128, 128])
                B["d_" + n] = Dep(f"dn{si}_{n}")
            for n in ("gb0", "gb1"):
                B[n] = p.sbuf(f"dn{si}_{n}", [128, 16]); B["d_" + n] = Dep(f"dn{si}_{n}")
            B["sc"] = p.sbuf(f"dn{si}_sc", [128, 8]); B["d_sc"] = Dep(f"dn{si}_sc")
            k.dnbuf.append(B)
    import os
    maxsteps = int(os.environ.get("DN_STEPS", "1000000000"))
    nstep = 0
    allsc = list(scans or [(h, d) for h in range(4) for d in range(2)])
    GRP = 4
    for g0 in range(0, len(allsc), GRP):
        gens = []
        for si, (h, d) in enumerate(allsc[g0:g0 + GRP]):
            gens.append(dn_scan_gen(k, l, si, h, d, dn_blocks(d, nlat), True))
        while gens and nstep < maxsteps:
            for g in list(gens):
                nstep += 1
                try:
                    next(g)
                except StopIteration:
                    gens.remove(g)


def stage_dn_post(k, l, toks=None):
    p = k.p
    OD, dOD = k.scratch(f"OD{l}", [2, 4, NT, 128])
    Z, dZ = k.scratch(f"Z{l}", [NT, 512])
    ODT, dODT = k.scratch(f"ODT{l}", [4, 128, NT])
    dnw, ddnw = k.const("dnw", [128, 128], k.inp[f"dnw{l}"])
    fr = Ring(p, "qf", [128, 4, 128], 2); br = Ring(p, "qb", [128, 4, 128], 2); zr = Ring(p, "qz", [128, 512], 2)
    sr = Ring(p, "qs", [128, 4, 128], 2); yr = Ring(p, "qy", [128, 4, 128], 2); tr_ = Ring(p, "qt", [128, 4, 128], 2)
    cr = Ring(p, "qc", [128, 8], 2)
    nb = 0
    for tok in (toks if toks is not None else range(0, NT, 128)):
        ft, df, sem = fr.next()
        p.dma("sp", ft[:], OD[0, :, tok:tok + 128, :].rearrange("h t d -> t h d"), reads=[dOD], writes=[df], sem=sem)
        bt, db, sem = br.next()
        p.dma("act", bt[:], OD[1, :, tok:tok + 128, :].rearrange("h t d -> t h d"), reads=[dOD], writes=[db], sem=sem)
        zt, dz, sem = zr.next()
        p.dma("sp", zt[:], Z[tok:tok + 128, :], reads=[dZ], writes=[dz], sem=sem)
        st, ds, _ = sr.next(); yt, dy, _ = yr.next(); ct, dc, _ = cr.next()
        p.op("dve", lambda e, st=st, ft=ft, bt=bt: e.tensor_tensor(out=st[:], in0=ft[:], in1=bt[:], op=ALU.add), reads=[df, db], writes=[ds])
        p.op("pool", lambda e, st=st, yt=yt: e.tensor_tensor(out=yt[:], in0=st[:], in1=st[:], op=ALU.mult), reads=[ds], writes=[dy])
        p.op("dve", lambda e, ct=ct, yt=yt: e.reduce_sum(out=ct[:, 0:4], in_=yt[:], axis=AX.X), reads=[dy], writes=[dc])
        p.op("dve", lambda e, ct=ct: e.tensor_scalar(out=ct[:, 0:4], in0=ct[:, 0:4], scalar1=1.0 / 128, scalar2=EPS, op0=ALU.mult, op1=ALU.add), reads=[dc], writes=[dc])
        p.op("act", lambda e, ct=ct: e.activation(out=ct[:, 0:4], in_=ct[:, 0:4], func=AF.Sqrt), reads=[dc], writes=[dc])
        p.op("dve", lambda e, ct=ct: e.reciprocal(out=ct[:, 0:4], in_=ct[:, 0:4]), reads=[dc], writes=[dc])
        p.op("act", lambda e, zt=zt: e.activation(out=zt[:], in_=zt[:], func=AF.Silu), reads=[dz], writes=[dz])
        for h in range(4):
            p.op("dve", lambda e, yt=yt, st=st, ct=ct, h=h: e.scalar_tensor_tensor(out=yt[:, h, :], in0=st[:, h, :], scalar=ct[:, h:h + 1], in1=dnw[:],
                                                                               op0=ALU.mult, op1=ALU.mult), reads=[ds, dc, ddnw, dy], writes=[dy])
        p.op("pool", lambda e, yt=yt, zt=zt: e.tensor_tensor(out=yt[:].rearrange("p h d -> p (h d)"), in0=yt[:].rearrange("p h d -> p (h d)"), in1=zt[:], op=ALU.mult),
             reads=[dz, dy], writes=[dy])
        b = nb % 2
        nb += 1
        ps = k.banks[b]
        for h in range(4):
            p.op("pe", lambda e, ps=ps, yt=yt, h=h: e.transpose(out=ps[:, h * 128:(h + 1) * 128], in_=yt[:, h, :], identity=k.ident[:]),
                 reads=[dy, k.dident], writes=[k.bdep[b]])
        tt, dt, _ = tr_.next()
        p.op("act", lambda e, tt=tt, ps=ps: e.activation(out=tt[:].rearrange("p h d -> p (h d)"), in_=ps[:, 0:512], func=AF.Copy), reads=[k.bdep[b]], writes=[dt])
        p.store("sp", ODT[:, :, tok:tok + 128].rearrange("h p t -> p h t"), tt[:], [dt], dODT)


def stage_mlp(k, l, xname, mod, dmod, tiles=None):
    p = k.p
    xsrc, dxs = k.scratch(xname, [8, 128, NT])
    XO, dXO = k.scratch(f"XO{l}", [8, 128, NT])
    ONT, dONT = k.scratch(f"ONT{l}", [4, 128, NT])
    ODT, dODT = k.scratch(f"ODT{l}", [4, 128, NT])
    OF, dOF = k.scratch(f"OF{l}", [NT, 512])
    GATE, dGATE = k.scratch(f"GATE{l}", [24, 128, NT])
    gam, dgam = k.const("norm2", [128, 8], k.inp[f"norm2_{l}"])
    make_gm(k, gam, dgam, mod, dmod, 4)
    xr = Ring(p, "mx", [128, 8, 256], 1); br_ = [Ring(p, f"mo{b}", [128, 4, 256], 1) for b in range(3)]
    gr = Ring(p, "mg", [128, 24, 256], 1); yr = Ring(p, "my", [128, 8, 256], 1); mr = Ring(p, "mm", [128, 8, 256], 1)
    tr_ = Ring(p, "mt", [128, 8, 256], 1); hr = Ring(p, "mh", [128, 8, 256], 1); ar = Ring(p, "ma", [128, 32, 256], 1)
    tmr = Ring(p, "mtm", [128, 256], 3); ofr = Ring(p, "mof", [128, 2, 512], 1); outr = Ring(p, "mout", [128, 256], 3)
    wmr = Ring(p, "wm", [128, 4, 128], 3); wor = Ring(p, "wo", [128, 8, 128], 2); w1r = Ring(p, "w1", [128, 8, 128], 3); w2r = Ring(p, "w2", [128, 32, 128], 2)
    wm_src = [k.inp[f"w_nao{l}"], k.inp[f"w_dno{l}"], k.inp[f"w_fno{l}"]]
    nb = 0
    nq = 0

    def wq():
        nonlocal nq
        nq += 1
        return "sp" if nq % 2 else "act"
    for (t0, N, col) in (tiles or super_tiles()):
        xt, dx, sem = xr.next()
        p.dma("sp", xt[:, :, 0:N], xsrc[:, :, t0:t0 + N].rearrange("k p t -> p k t"), reads=[dxs], writes=[dx], sem=sem)
        obs = []
        for b, (src, dsrc) in enumerate(((ONT, dONT), (ODT, dODT))):
            ot, do, sem = br_[b].next()
            p.dma("act", ot[:, :, 0:N], src[:, :, t0:t0 + N].rearrange("k p t -> p k t"), reads=[dsrc], writes=[do], sem=sem)
            obs.append((ot, do))
        oft, dof, sem = ofr.next()
        p.dma("sp", oft[:], OF[t0:t0 + N, :].rearrange("(a p) c -> p a c", p=128), reads=[dOF], writes=[dof], sem=sem)
        ot, do, _ = br_[2].next()
        for a in range(2):
            b_ = 6 + a
            for g in range(4):
                p.op("pe", lambda e, a=a, g=g, b_=b_, oft=oft: e.transpose(out=k.banks[b_][:, g * 128:(g + 1) * 128], in_=oft[:, a, g * 128:(g + 1) * 128], identity=k.ident[:]),
                     reads=[dof, k.dident], writes=[k.bdep[b_]])
            p.op("act", lambda e, a=a, b_=b_, ot=ot: e.activation(out=ot[:, :, a * 128:(a + 1) * 128], in_=k.banks[b_][:, 0:512].rearrange("p (g t) -> p g t", g=4), func=AF.Copy),
                 reads=[k.bdep[b_]], writes=[do])
        obs.append((ot, do))
        gt, dg, sem = gr.next()
        for q4 in range(4):
            p.dma(wq(), gt[:, q4 * 6:(q4 + 1) * 6, 0:N], GATE[q4 * 6:(q4 + 1) * 6, :, t0:t0 + N].rearrange("k p t -> p k t"), reads=[dGATE], writes=[dg], sem=sem)
        yt, dy, _ = yr.next()
        for c in range(8):
            for b in range(3):
                wt, dw, sem = wmr.next()
                p.dma(wq(), wt[:], wm_src[b][c], writes=[dw], sem=sem)
                bk = nb % 4
                nb += 1
                ps = k.banks[bk]
                ot, do = obs[b]
                for kk in range(4):
                    p.op("pe", lambda e, ps=ps, wt=wt, ot=ot, kk=kk: e.matmul(ps[:, 0:N], lhsT=wt[:, kk, :], rhs=ot[:, kk, 0:N], start=(kk == 0), stop=(kk == 3)),
                         reads=[dw, do], writes=[k.bdep[bk]])
                if b == 0:
                    p.op("dve", lambda e, ps=ps, yt=yt, gt=gt, c=c: e.tensor_tensor(out=yt[:, c, 0:N], in0=ps[:, 0:N], in1=gt[:, c, 0:N], op=ALU.mult),
                         reads=[k.bdep[bk], dg], writes=[dy])
                else:
                    tm, dtm, _ = tmr.next()
                    p.op("dve", lambda e, ps=ps, tm=tm, gt=gt, c=c, b=b: e.tensor_tensor(out=tm[:, 0:N], in0=ps[:, 0:N], in1=gt[:, b * 8 + c, 0:N], op=ALU.mult),
                         reads=[k.bdep[bk], dg], writes=[dtm])
                    p.op("pool", lambda e, tm=tm, yt=yt, c=c: e.tensor_tensor(out=yt[:, c, 0:N], in0=yt[:, c, 0:N], in1=tm[:, 0:N], op=ALU.add),
                         reads=[dtm, dy], writes=[dy])
        mt, dm, _ = mr.next()
        for c in range(8):
            wt, dw, sem = wor.next()
            p.dma(wq(), wt[:], k.inp[f"w_out{l}"][c], writes=[dw], sem=sem)
            bk = nb % 4
            nb += 1
            ps = k.banks[bk]
            for kk in range(8):
                p.op("pe", lambda e, ps=ps, wt=wt, yt=yt, kk=kk: e.matmul(ps[:, 0:N], lhsT=wt[:, kk, :], rhs=yt[:, kk, 0:N], start=(kk == 0), stop=(kk == 7)),
                     reads=[dw, dy], writes=[k.bdep[bk]])
            p.op("dve", lambda e, ps=ps, mt=mt, xt=xt, c=c, col=col: e.scalar_tensor_tensor(out=mt[:, c, 0:N], in0=ps[:, 0:N], scalar=mod[:, 16 + c, col:col + 1], in1=xt[:, c, 0:N],
                                                                                        op0=ALU.mult, op1=ALU.add), reads=[k.bdep[bk], dx, dmod], writes=[dm])
        ht, dh, _ = hr.next(); tmp, dtmp, _ = tr_.next()
        norm_mod(k, mt, dm, N, gam, dgam, mod, dmod, 3, 4, col, ht, dh, tmp, dtmp, k.ones_t, k.dones, 7)
        at, da, _ = ar.next()
        for j in range(32):
            wt, dw, sem = w1r.next()
            p.dma(wq(), wt[:], k.inp[f"w_mlp1_{l}"][j], writes=[dw], sem=sem)
            bk = nb % 4
            nb += 1
            ps = k.banks[bk]
            for kk in range(8):
                p.op("pe", lambda e, ps=ps, wt=wt, ht=ht, kk=kk: e.matmul(ps[:, 0:N], lhsT=wt[:, kk, :], rhs=ht[:, kk, 0:N], start=(kk == 0), stop=(kk == 7)),
                     reads=[dw, dh], writes=[k.bdep[bk]])
            p.op("act", lambda e, ps=ps, at=at, j=j: e.activation(out=at[:, j, 0:N], in_=ps[:, 0:N], func=AF.Relu), reads=[k.bdep[bk]], writes=[da])
            p.op("pool", lambda e, at=at, j=j: e.tensor_tensor(out=at[:, j, 0:N], in0=at[:, j, 0:N], in1=at[:, j, 0:N], op=ALU.mult), reads=[da], writes=[da])
        for c in range(8):
            wt, dw, sem = w2r.next()
            p.dma(wq(), wt[:], k.inp[f"w_mlp2_{l}"][c], writes=[dw], sem=sem)
            bk = nb % 4
            nb += 1
            ps = k.banks[bk]
            for j in range(32):
                p.op("pe", lambda e, ps=ps, wt=wt, at=at, j=j: e.matmul(ps[:, 0:N], lhsT=wt[:, j, :], rhs=at[:, j, 0:N], start=(j == 0), stop=(j == 31)),
                     reads=[dw, da], writes=[k.bdep[bk]])
            ot, do, _ = outr.next()
            p.op("dve", lambda e, ps=ps, ot=ot, mt=mt, c=c, col=col: e.scalar_tensor_tensor(out=ot[:, 0:N], in0=ps[:, 0:N], scalar=mod[:, 40 + c, col:col + 1], in1=mt[:, c, 0:N],
                                                                                        op0=ALU.mult, op1=ALU.add), reads=[k.bdep[bk], dm, dmod], writes=[do])
            p.store("sp", XO[c, :, t0:t0 + N], ot[:, 0:N], [do], dXO)


def stage_final(k, xname, tiles=None):
    p = k.p
    xsrc, dxs = k.scratch(xname, [8, 128, NT])
    y = k.nc.dram_tensor("y", [T, D], F32, kind="ExternalOutput").ap()
    k.ydep = Dep("y")
    gam, dgam = k.const("normf", [128, 8], k.inp["norm_f"])
    xr = Ring(p, "fx", [128, 8, 256], 2); tr_ = Ring(p, "ft", [128, 8, 256], 2); outr = Ring(p, "fo", [128, 1024], 2)
    nb = 0
    for (t0, N, col) in (tiles or super_tiles()[:-1]):
        xt, dx, sem = xr.next()
        p.dma("sp", xt[:, :, 0:N], xsrc[:, :, t0:t0 + N].rearrange("k p t -> p k t"), reads=[dxs], writes=[dx], sem=sem)
        tmp, dtmp, _ = tr_.next()
        p.op("act", lambda e, tmp=tmp, xt=xt: e.activation(out=tmp[:], in_=xt[:], func=AF.Square), reads=[dx], writes=[dtmp])
        ps = k.banks[7]
        for kk in range(8):
            p.op("pe", lambda e, kk=kk, tmp=tmp: e.matmul(ps[:, 0:N], lhsT=k.ones_t[:], rhs=tmp[:, kk, 0:N], start=(kk == 0), stop=(kk == 7)),
                 reads=[dtmp, k.dones], writes=[k.bdep[7]])
        rstd = k.rstd
        p.op("dve", lambda e: e.tensor_scalar(out=rstd[:, 0:N], in0=ps[:, 0:N], scalar1=EPS, scalar2=None, op0=ALU.add), reads=[k.bdep[7]], writes=[k.drstd])
        p.op("act", lambda e: e.activation(out=rstd[:, 0:N], in_=rstd[:, 0:N], func=AF.Sqrt), reads=[k.drstd], writes=[k.drstd])
        p.op("dve", lambda e: e.reciprocal(out=rstd[:, 0:N], in_=rstd[:, 0:N]), reads=[k.drstd], writes=[k.drstd])
        for kk in range(8):
            p.op("dve", lambda e, kk=kk, tmp=tmp, xt=xt: e.scalar_tensor_tensor(out=tmp[:, kk, 0:N], in0=xt[:, kk, 0:N], scalar=gam[:, kk:kk + 1], in1=rstd[:, 0:N],
                                                                               op0=ALU.mult, op1=ALU.mult), reads=[dx, k.drstd, dgam, dtmp], writes=[dtmp])
        for tb in range(N // 128):
            ot, do, _ = outr.next()
            for half in range(2):
                b = (nb % 3) * 2
                nb += 1
                b = b if False else (nb % 6)
                ps2 = k.banks[b]
                for q in range(4):
                    kk = half * 4 + q
                    p.op("pe", lambda e, ps2=ps2, tmp=tmp, kk=kk, q=q, tb=tb: e.transpose(out=ps2[:, q * 128:(q + 1) * 128], in_=tmp[:, kk, tb * 128:(tb + 1) * 128], identity=k.ident[:]),
                         reads=[dtmp, k.dident], writes=[k.bdep[b]])
                if half == 0:
                    p.op("act", lambda e, ps2=ps2, ot=ot: e.activation(out=ot[:, 0:512], in_=ps2[:, 0:512], func=AF.Copy), reads=[k.bdep[b]], writes=[do])
                else:
                    p.op("dve", lambda e, ps2=ps2, ot=ot: e.tensor_copy(out=ot[:, 512:1024], in_=ps2[:, 0:512]), reads=[k.bdep[b]], writes=[do])
            tok = t0 + tb * 128
            p.store("sp", y[tok:tok + 128, :], ot[:], [do], k.ydep)


def na_table(rpb):
    w = np.arange(64)
    cs = np.clip(w - 8, 0, 48)
    j = np.arange(64)
    inwin = (j[:, None] >= cs[None, :]) & (j[:, None] < cs[None, :] + 16)
    idx = np.clip(j[:, None] - w[None, :] + 15, 0, 30)
    tab = rpb[:, :, idx]
    tab = np.where(inwin[None, None], tab, np.float32(-30000.0))
    return np.ascontiguousarray(tab.transpose(2, 0, 1, 3)).astype(np.float32)


def stage_na(k, l, rows=None, do_ctx=True):
    p = k.p
    QT, dQT = k.scratch(f"QT{l}", [8, 64, NT])
    KT, dKT = k.scratch(f"KT{l}", [8, 64, NT])
    V, dV = k.scratch(f"V{l}", [NT, 512])
    ONT, dONT = k.scratch(f"ONT{l}", [4, 128, NT])
    eb, deb = k.const("natab", [64, 8 * 15 * 64], k.inp[f"natab{l}"].rearrange("j h o w -> j (h o w)"))
    for h in range(8):
        p.op("act", lambda e, h=h: e.activation(out=eb[:, h * 960:(h + 1) * 960], in_=eb[:, h * 960:(h + 1) * 960], func=AF.Exp), reads=[deb], writes=[deb])
    kc = p.sbuf("na_kc", [64, 8, 256]); dkc = Dep("na_kc")
    p.dma("sp", kc[:], KT[:, :, T:T + CT].rearrange("h d t -> d h t"), reads=[dKT], writes=[dkc], sem="ld_kc")
    vc = p.sbuf("na_vc", [128, 2, 8, 65]); dvc = Dep("na_vc")
    p.op("pool", lambda e: e.memset(vc[:], 1.0), writes=[dvc])
    for c in range(2):
        p.dma("act", vc[:, c, :, 0:64], V[T + c * 128:T + (c + 1) * 128, :].rearrange("t (h d) -> t h d", d=64), reads=[dV], writes=[dvc], sem="ld_vc")
    NS = 10
    kring = p.sbuf("na_kr", [64, NS, 8, 64]); vring = p.sbuf("na_vr", [64, NS, 8, 65])
    dkr = [Dep(f"na_kr{i}") for i in range(NS)]; dvr = [Dep(f"na_vr{i}") for i in range(NS)]
    p.op("pool", lambda e: e.memset(vring[:], 1.0), writes=dvr)
    qr = Ring(p, "na_q", [64, 8, 256], 2)
    er = Ring(p, "na_e", [64, 512], 3); ecr = Ring(p, "na_ec", [128, 128], 3)
    orr = Ring(p, "na_o", [64, 512], 2); otr = Ring(p, "na_ot", [128, 4, 64], 2); rcr = Ring(p, "na_rc", [64, 8], 2)
    loaded = -1
    it = 0
    qt = None
    rows = list(rows if rows is not None else range(256))
    for r in rows:
        rs = min(max(r - 4, 0), 248)
        while loaded < rs + 7:
            loaded += 1
            if loaded < rs:
                continue
            sl = loaded % NS
            p.dma("sp", kring[:, sl, :, :], KT[:, :, loaded * 64:(loaded + 1) * 64].rearrange("h d t -> d h t"), reads=[dKT], writes=[dkr[sl]], sem=f"ld_kr{sl}")
            p.dma("act", vring[:, sl, :, 0:64], V[loaded * 64:(loaded + 1) * 64, :].rearrange("t (h d) -> t h d", d=64), reads=[dV], writes=[dvr[sl]], sem=f"ld_vr{sl}")
        if qt is None or r % 4 == 0 or r == rows[0]:
            qt, dq, sem = qr.next()
            q0 = (r // 4) * 4
            p.dma("sp", qt[:], QT[:, :, q0 * 64:(q0 + 4) * 64].rearrange("h d t -> d h t"), reads=[dQT], writes=[dq], sem=sem)
        rr = r % 4
        o0 = rs - r + 7
        ot, do, _ = orr.next()
        rc, drc, _ = rcr.next()
        for h in range(8):
            bA, bB, bC = (it % 2), 2 + (it % 2), 4 + (it % 2)
            it += 1
            A, Bk, C = k.banks[bA], k.banks[bB], k.banks[bC]
            qv = qt[:, h, rr * 64:(rr + 1) * 64]
            for i in range(8):
                sl = (rs + i) % NS
                p.op("pe", lambda e, A=A, i=i, sl=sl, h=h, qv=qv: e.matmul(A[0:64, i * 64:(i + 1) * 64], lhsT=kring[:, sl, h, :], rhs=qv, start=True, stop=True),
                     reads=[dkr[sl], dq], writes=[k.bdep[bA]])
            for c in range(2):
                p.op("pe", lambda e, Bk=Bk, c=c, h=h, qv=qv: e.matmul(Bk[:, c * 64:(c + 1) * 64], lhsT=kc[:, h, c * 128:(c + 1) * 128], rhs=qv, start=True, stop=True),
                     reads=[dkc, dq], writes=[k.bdep[bB]])
            et, de, _ = er.next(); ect, dec, _ = ecr.next()
            p.op("act", lambda e, et=et, A=A: e.activation(out=et[:], in_=A[0:64, :], func=AF.Exp), reads=[k.bdep[bA]], writes=[de])
            p.op("act", lambda e, ect=ect, Bk=Bk: e.activation(out=ect[:], in_=Bk[:, 0:128], func=AF.Exp), reads=[k.bdep[bB]], writes=[dec])
            e0 = h * 960 + o0 * 64
            p.op("pool", lambda e, et=et, e0=e0: e.tensor_tensor(out=et[:], in0=et[:], in1=eb[:, e0:e0 + 512], op=ALU.mult), reads=[de, deb], writes=[de])
            for i in range(8):
                sl = (rs + i) % NS
                p.op("pe", lambda e, C=C, et=et, i=i, sl=sl, h=h: e.matmul(C[0:64, 0:65], lhsT=et[:, i * 64:(i + 1) * 64], rhs=vring[:, sl, h, :], start=(i == 0), stop=False),
                     reads=[de, dvr[sl]], writes=[k.bdep[bC]])
            for c in range(2):
                p.op("pe", lambda e, C=C, ect=ect, c=c, h=h: e.matmul(C[0:64, 0:65], lhsT=ect[:, c * 64:(c + 1) * 64], rhs=vc[:, c, h, :], start=False, stop=(c == 1)),
                     reads=[dec, dvc], writes=[k.bdep[bC]])
            p.op("dve", lambda e, C=C, rc=rc, h=h: e.reciprocal(out=rc[:, h:h + 1], in_=C[0:64, 64:65]), reads=[k.bdep[bC]], writes=[drc])
            p.op("dve", lambda e, C=C, rc=rc, ot=ot, h=h: e.tensor_scalar(out=ot[:, h * 64:(h + 1) * 64], in0=C[0:64, 0:64], scalar1=rc[:, h:h + 1], scalar2=None, op0=ALU.mult),
                 reads=[k.bdep[bC], drc], writes=[do])
        bT = 6 + (r % 2)
        for c in range(4):
            p.op("pe", lambda e, bT=bT, ot=ot, c=c: e.transpose(out=k.banks[bT][:, c * 64:(c + 1) * 64], in_=ot[:, c * 128:(c + 1) * 128], identity=k.ident[0:64, 0:64]),
                 reads=[do, k.dident], writes=[k.bdep[bT]])
        tt, dt, _ = otr.next()
        p.op("act", lambda e, bT=bT, tt=tt: e.activation(out=tt[:].rearrange("p c t -> p (c t)"), in_=k.banks[bT][:, 0:256], func=AF.Copy), reads=[k.bdep[bT]], writes=[dt])
        p.store("sp", ONT[:, :, r * 64:(r + 1) * 64].rearrange("c p t -> p c t"), tt[:], [dt], dONT)
    if not do_ctx:
        return
    qc = p.sbuf("na_qc", [64, 8, 256]); dqc = Dep("na_qc")
    p.dma("sp", qc[:], QT[:, :, T:T + CT].rearrange("h d t -> d h t"), reads=[dQT], writes=[dqc], sem="ld_qc")
    ocr = Ring(p, "na_oc", [128, 512], 2); otc = Ring(p, "na_otc", [128, 4, 128], 2); rcc = Ring(p, "na_rcc", [128, 8], 2)
    e2r = Ring(p, "na_e2", [128, 2, 128], 3)
    for t in range(2):
        ot, do, _ = ocr.next(); rc, drc, _ = rcc.next()
        for h in range(8):
            bA, bC = (it % 2), 4 + (it % 2)
            it += 1
            A, C = k.banks[bA], k.banks[bC]
            for c in range(2):
                p.op("pe", lambda e, A=A, c=c, h=h, t=t: e.matmul(A[:, c * 128:(c + 1) * 128], lhsT=kc[:, h, c * 128:(c + 1) * 128], rhs=qc[:, h, t * 128:(t + 1) * 128], start=True, stop=True),
                     reads=[dkc, dqc], writes=[k.bdep[bA]])
            et, de, _ = e2r.next()
            p.op("act", lambda e, et=et, A=A: e.activation(out=et[:].rearrange("p c t -> p (c t)"), in_=A[:, 0:256], func=AF.Exp), reads=[k.bdep[bA]], writes=[de])
            for c in range(2):
                p.op("pe", lambda e, C=C, et=et, c=c, h=h: e.matmul(C[:, 0:65], lhsT=et[:, c, :], rhs=vc[:, c, h, :], start=(c == 0), stop=(c == 1)),
                     reads=[de, dvc], writes=[k.bdep[bC]])
            p.op("dve", lambda e, C=C, rc=rc, h=h: e.reciprocal(out=rc[:, h:h + 1], in_=C[:, 64:65]), reads=[k.bdep[bC]], writes=[drc])
            p.op("dve", lambda e, C=C, rc=rc, ot=ot, h=h: e.tensor_scalar(out=ot[:, h * 64:(h + 1) * 64], in0=C[:, 0:64], scalar1=rc[:, h:h + 1], scalar2=None, op0=ALU.mult),
                 reads=[k.bdep[bC], drc], writes=[do])
        bT = 6 + t
        for c in range(4):
            p.op("pe", lambda e, bT=bT, ot=ot, c=c: e.transpose(out=k.banks[bT][:, c * 128:(c + 1) * 128], in_=ot[:, c * 128:(c + 1) * 128], identity=k.ident[:]),
                 reads=[do, k.dident], writes=[k.bdep[bT]])
        tt, dt, _ = otc.next()
        p.op("act", lambda e, bT=bT, tt=tt: e.activation(out=tt[:].rearrange("p c t -> p (c t)"), in_=k.banks[bT][:, 0:512], func=AF.Copy), reads=[k.bdep[bT]], writes=[dt])
        p.store("sp", ONT[:, :, T + t * 128:T + (t + 1) * 128].rearrange("c p t -> p c t"), tt[:], [dt], dONT)


def fn_tables():
    n = np.arange(128, dtype=np.float64)
    a = 2 * np.pi * np.outer(n, n) / 128.0
    c128, s128 = np.cos(a), np.sin(a)
    tw = 2 * np.pi * np.outer(n, n) / float(T)
    m = np.arange(256, dtype=np.float64)
    a2 = 2 * np.pi * np.outer(m, m) / 256.0
    c256, s256 = np.cos(a2), np.sin(a2)
    f = lambda x: np.ascontiguousarray(x.astype(np.float32))
    dft = np.stack([c128, s128, -c128, -s128], axis=1)
    twt = np.stack([np.cos(tw), np.sin(tw), -np.sin(tw)], axis=1)
    d256 = np.stack([c256, -s256], axis=0).reshape(2, 2, 128, 256).transpose(2, 0, 1, 3)
    return f(dft), f(twt), f(d256)


def stage_fn(k, l, do_ctx=True, t2s=None, t1s=None):
    p = k.p
    U, dU = k.scratch(f"U{l}", [4, 128, NT])
    PQ, dPQ = k.scratch(f"PQ{l}", [NT, 1024])
    YD, dYD = k.scratch(f"YD{l}", [128, 128, 1024])
    OF, dOF = k.scratch(f"OF{l}", [NT, 512])
    dft, ddft = k.const("dft", [128, 4, 128], k.inp["fn_dft"])
    twt, dtw = k.const("twt", [128, 3, 128], k.inp["fn_tw"])
    C, S, NC_, NS_ = (dft[:, i, :] for i in range(4))
    ur = Ring(p, "fu", [128, 4, 128], 2); pqr = Ring(p, "fpq", [128, 1024], 2)
    nb = 0
    for tok in range(0, NT if do_ctx else T, 128):
        ut, du, sem = ur.next()
        p.dma("sp", ut[:], U[:, :, tok:tok + 128].rearrange("g c t -> c g t"), reads=[dU], writes=[du], sem=sem)
        b0 = (nb % 2) * 2
        nb += 1
        for g in range(4):
            p.op("pe", lambda e, ut=ut, g=g, b0=b0: e.matmul(k.banks[b0][:, g * 128:(g + 1) * 128], lhsT=ut[:, g, :], rhs=C, start=True, stop=True),
                 reads=[du, ddft], writes=[k.bdep[b0]])
            p.op("pe", lambda e, ut=ut, g=g, b0=b0: e.matmul(k.banks[b0 + 1][:, g * 128:(g + 1) * 128], lhsT=ut[:, g, :], rhs=S, start=True, stop=True),
                 reads=[du, ddft], writes=[k.bdep[b0 + 1]])
        pt, dp, _ = pqr.next()
        p.op("act", lambda e, pt=pt, b0=b0: e.activation(out=pt[:, 0:512], in_=k.banks[b0][:, 0:512], func=AF.Copy), reads=[k.bdep[b0]], writes=[dp])
        p.op("dve", lambda e, pt=pt, b0=b0: e.tensor_copy(out=pt[:, 512:1024], in_=k.banks[b0 + 1][:, 0:512]), reads=[k.bdep[b0 + 1]], writes=[dp])
        p.store("sp", PQ[tok:tok + 128, :], pt[:], [dp], dPQ)
    xr = Ring(p, "fx", [128, 1024], 2); yr = Ring(p, "fy", [128, 1024], 2); tr_ = Ring(p, "ftm", [128, 2, 512], 2)
    PQv = PQ[0:T, :].rearrange("(t1 t2) c -> t2 t1 c", t2=128)
    for t2 in (t2s if t2s is not None else range(128)):
        xt, dx, sem = xr.next()
        for q4 in range(4):
            p.dma("sp" if q4 % 2 == 0 else "act", xt[:, q4 * 256:(q4 + 1) * 256], PQv[t2, :, q4 * 256:(q4 + 1) * 256], reads=[dPQ], writes=[dx], sem=sem)
        b0 = (nb % 2) * 2
        nb += 1
        Yr, Yi = k.banks[b0], k.banks[b0 + 1]
        p.op("pe", lambda e, Yr=Yr, xt=xt: e.matmul(Yr[:, 0:512], lhsT=C, rhs=xt[:, 0:512], start=True, stop=False), reads=[dx, ddft], writes=[k.bdep[b0]])
        p.op("pe", lambda e, Yr=Yr, xt=xt: e.matmul(Yr[:, 0:512], lhsT=NS_, rhs=xt[:, 512:1024], start=False, stop=True), reads=[dx, ddft], writes=[k.bdep[b0]])
        p.op("pe", lambda e, Yi=Yi, xt=xt: e.matmul(Yi[:, 0:512], lhsT=NC_, rhs=xt[:, 512:1024], start=True, stop=False), reads=[dx, ddft], writes=[k.bdep[b0 + 1]])
        p.op("pe", lambda e, Yi=Yi, xt=xt: e.matmul(Yi[:, 0:512], lhsT=NS_, rhs=xt[:, 0:512], start=False, stop=True), reads=[dx, ddft], writes=[k.bdep[b0 + 1]])
        tm, dtm, _ = tr_.next(); yt, dy, _ = yr.next()
        cf, sf, nsf = twt[:, 0, t2:t2 + 1], twt[:, 1, t2:t2 + 1], twt[:, 2, t2:t2 + 1]
        p.op("dve", lambda e, tm=tm, Yr=Yr, cf=cf: e.tensor_scalar(out=tm[:, 0, :], in0=Yr[:, 0:512], scalar1=cf, scalar2=None, op0=ALU.mult), reads=[k.bdep[b0], dtw], writes=[dtm])
        p.op("dve", lambda e, tm=tm, Yi=Yi, cf=cf: e.tensor_scalar(out=tm[:, 1, :], in0=Yi[:, 0:512], scalar1=cf, scalar2=None, op0=ALU.mult), reads=[k.bdep[b0 + 1], dtw], writes=[dtm])
        p.op("dve", lambda e, tm=tm, yt=yt, Yi=Yi, sf=sf: e.scalar_tensor_tensor(out=yt[:, 0:512], in0=Yi[:, 0:512], scalar=sf, in1=tm[:, 0, :], op0=ALU.mult, op1=ALU.add),
             reads=[k.bdep[b0 + 1], dtw, dtm], writes=[dy])
        p.op("dve", lambda e, tm=tm, yt=yt, Yr=Yr, nsf=nsf: e.scalar_tensor_tensor(out=yt[:, 512:1024], in0=Yr[:, 0:512], scalar=nsf, in1=tm[:, 1, :], op0=ALU.mult, op1=ALU.add),
             reads=[k.bdep[b0], dtw, dtm], writes=[dy])
        for q4 in range(4):
            p.store("sp", YD[:, t2, q4 * 256:(q4 + 1) * 256], yt[:, q4 * 256:(q4 + 1) * 256], [dy], dYD)
    sc_lat = 1.0 / math.sqrt(T * 128.0)
    zr = Ring(p, "fz", [128, 1024], 2); orr = Ring(p, "fo", [128, 512], 2)
    OFv = OF[0:T, :].rearrange("(b a) c -> a b c", a=128)
    for t1 in (t1s if t1s is not None else range(128)):
        zt, dz, sem = zr.next()
        p.dma("sp", zt[:], YD[t1], reads=[dYD], writes=[dz], sem=sem)
        b = 4 + (nb % 2)
        nb += 1
        ps = k.banks[b]
        p.op("pe", lambda e, ps=ps, zt=zt: e.matmul(ps[:, 0:512], lhsT=C, rhs=zt[:, 0:512], start=True, stop=False), reads=[dz, ddft], writes=[k.bdep[b]])
        p.op("pe", lambda e, ps=ps, zt=zt: e.matmul(ps[:, 0:512], lhsT=S, rhs=zt[:, 512:1024], start=False, stop=True), reads=[dz, ddft], writes=[k.bdep[b]])
        ot, do, _ = orr.next()
        p.op("act", lambda e, ps=ps, ot=ot: e.activation(out=ot[:], in_=ps[:, 0:512], func=AF.Copy, scale=sc_lat), reads=[k.bdep[b]], writes=[do])
        for q2 in range(2):
            p.store("sp", OFv[t1, :, q2 * 256:(q2 + 1) * 256], ot[:, q2 * 256:(q2 + 1) * 256], [do], dOF)
    if not do_ctx:
        return
    d256, dd256 = k.const("d256", [128, 2, 2, 256], k.inp["fn_d256"])
    sc_ctx = 1.0 / math.sqrt(256.0 * 128.0)
    cx = p.sbuf("fcx", [128, 2, 1024]); dcx = Dep("fcx")
    p.dma("sp", cx[:], PQ[T:T + 256, :].rearrange("(a p) c -> p a c", p=128), reads=[dPQ], writes=[dcx], sem="ld_fcx")
    for tt_ in range(2):
        b = 4 + (nb % 2)
        nb += 1
        ps = k.banks[b]
        n_ = 0
        for cs_ in range(2):
            for kt in range(2):
                p.op("pe", lambda e, ps=ps, cs_=cs_, kt=kt, tt_=tt_, n_=n_: e.matmul(ps[:, 0:512], lhsT=d256[:, cs_, kt, tt_ * 128:(tt_ + 1) * 128],
                                                                                 rhs=cx[:, kt, cs_ * 512:(cs_ + 1) * 512], start=(n_ == 0), stop=(n_ == 3)),
                     reads=[dcx, dd256], writes=[k.bdep[b]])
                n_ += 1
        ot, do, _ = orr.next()
        p.op("act", lambda e, ps=ps, ot=ot: e.activation(out=ot[:], in_=ps[:, 0:512], func=AF.Copy, scale=sc_ctx), reads=[k.bdep[b]], writes=[do])
        p.store("sp", OF[T + tt_ * 128:T + (tt_ + 1) * 128, :], ot[:], [do], dOF)


def build_program(shapes, debug=None):
    nc = bass.Bass("TRN2", target_bir_lowering=False)
    p = Prog(nc)
    k = K(nc, p, shapes)
    common(k)
    if debug is not None:
        fin = debug(k)
        p.build(final_deps=fin)
        return nc
    for l in range(2):
        xname = "xin" if l == 0 else "XO0"
        mod, dmod = k.modt[l]
        lat = super_tiles()[:-1]
        for fn in (lambda: stage_mod(k, l),
                   lambda: stage_proj(k, l, xname, mod, dmod),
                   lambda: stage_dn_pre(k, l),
                   lambda: stage_dn_scan(k, l),
                   lambda: stage_dn_post(k, l, toks=None if l == 0 else range(0, T, 128)),
                   lambda: stage_na(k, l, do_ctx=(l == 0)),
                   lambda: stage_fn(k, l, do_ctx=(l == 0)),
                   lambda: stage_mlp(k, l, xname, mod, dmod, tiles=None if l == 0 else lat)):
            p.scope_begin()
            fn()
            p.scope_end()
    p.scope_begin()
    stage_final(k, "XO1")
    p.scope_end()
    p.build(final_deps=[k.ydep])
    return nc


def kernel(**inputs):
    inp = {k_: np.asarray(v, np.float32) for k_, v in inputs.items()}
    m = host_layout(inp)
    shapes = {k_: v.shape for k_, v in m.items()}
    nc = build_program(shapes)
    res = run_bass_kernel_spmd(nc, [m], core_ids=[0])
    y = np.asarray(res.results[0]["y"], np.float32)
    return y.reshape(1, T, D)
```
